# Optimizing a Trainium2 kernel written in Bass

```python
import math
import jax, jax.numpy as jnp
from jax import lax
import numpy as np

D_MODEL = 1024
BATCH = 2
SEQ = 8192
DEPTH = 1

N_META = 16
D_FF = 2816
D_SSM = D_MODEL // 2
GROUP_CH = 16
N_GROUPS = D_SSM // GROUP_CH
STATE = 64
D_CONV = D_MODEL // 2
CONV_W = 3
N_BRANCH = 2
IN_COLS = D_SSM + 3 * D_CONV + N_BRANCH * D_MODEL
EPS = 1e-6

kernel_name = "hybrid_s5_shortconv_gated_macaron"


def rmsnorm(x, g):
    xf = x.astype(jnp.float32)
    y = xf * lax.rsqrt(jnp.mean(xf * xf, axis=-1, keepdims=True) + EPS)
    return (y * g.astype(jnp.float32)).astype(x.dtype)


def swiglu(x, w_gate, w_up, w_down):
    return (jax.nn.silu(x @ w_gate) * (x @ w_up)) @ w_down


def s5_mixer(u, a_re, a_im, log_dt, b_re, b_im, c_re, c_im, d_skip):
    bsz, length, _ = u.shape
    f32 = jnp.float32
    ug = u.astype(f32).reshape(bsz, length, N_GROUPS, GROUP_CH)
    a_re = a_re.astype(f32); a_im = a_im.astype(f32)
    dt = jnp.exp(log_dt.astype(f32))[:, None]
    mag = jnp.exp(a_re * dt)
    lam_re = mag * jnp.cos(a_im * dt)
    lam_im = mag * jnp.sin(a_im * dt)
    den = a_re * a_re + a_im * a_im
    q_re = ((lam_re - 1.0) * a_re + lam_im * a_im) / den
    q_im = (lam_im * a_re - (lam_re - 1.0) * a_im) / den
    b_re = b_re.astype(f32); b_im = b_im.astype(f32)
    bb_re = q_re[..., None] * b_re - q_im[..., None] * b_im
    bb_im = q_re[..., None] * b_im + q_im[..., None] * b_re
    bu_re = jnp.einsum('blgc,gpc->blgp', ug, bb_re)
    bu_im = jnp.einsum('blgc,gpc->blgp', ug, bb_im)
    la_re = jnp.broadcast_to(lam_re, bu_re.shape)
    la_im = jnp.broadcast_to(lam_im, bu_re.shape)

    def combine(e1, e2):
        a1r, a1i, b1r, b1i = e1
        a2r, a2i, b2r, b2i = e2
        return (a1r * a2r - a1i * a2i,
                a1r * a2i + a1i * a2r,
                a2r * b1r - a2i * b1i + b2r,
                a2r * b1i + a2i * b1r + b2i)

    _, _, h_re, h_im = lax.associative_scan(combine, (la_re, la_im, bu_re, bu_im), axis=1)
    y = (jnp.einsum('blgp,gcp->blgc', h_re, c_re.astype(f32))
         - jnp.einsum('blgp,gcp->blgc', h_im, c_im.astype(f32))
         + d_skip.astype(f32).reshape(N_GROUPS, GROUP_CH) * ug)
    return y.reshape(bsz, length, D_SSM).astype(u.dtype)


def short_conv(v, w_conv):
    return lax.conv_general_dilated(
        v, w_conv.astype(v.dtype), window_strides=(1,), padding=[(CONV_W - 1, 0)],
        dimension_numbers=('NWC', 'WIO', 'NWC'), feature_group_count=v.shape[-1])


def setup_inputs(seed: int = 0) -> dict:
    key = jax.random.key(seed)
    ks = jax.random.split(key, 32)
    D = D_MODEL
    nrm = lambda k, shape, s: jax.random.normal(k, shape, jnp.float32) * s
    gain = lambda k, shape: 1.0 + 0.02 * jax.random.normal(k, shape, jnp.float32)
    a_im_base = jnp.pi * jnp.arange(STATE, dtype=jnp.float32)
    return {
        "x": nrm(ks[0], (BATCH, SEQ, D), 1.0),
        "meta_tokens": nrm(ks[1], (N_META, D), 1.0),
        "g_ffn1": gain(ks[2], (DEPTH, D)),
        "ffn1_w_gate": nrm(ks[3], (DEPTH, D, D_FF), D ** -0.5),
        "ffn1_w_up": nrm(ks[4], (DEPTH, D, D_FF), D ** -0.5),
        "ffn1_w_down": nrm(ks[5], (DEPTH, D_FF, D), D_FF ** -0.5),
        "g_mix": gain(ks[6], (DEPTH, D)),
        "w_in": nrm(ks[7], (DEPTH, D, IN_COLS), D ** -0.5),
        "b_gate": nrm(ks[8], (DEPTH, N_BRANCH * D), 0.01),
        "ssm_a_re": -0.5 + nrm(ks[9], (DEPTH, N_GROUPS, STATE), 0.01),
        "ssm_a_im": a_im_base + nrm(ks[10], (DEPTH, N_GROUPS, STATE), 0.01),
        "ssm_log_dt": jax.random.uniform(ks[11], (DEPTH, N_GROUPS), jnp.float32,
                                         math.log(1e-3), math.log(1e-1)),
        "ssm_b_re": nrm(ks[12], (DEPTH, N_GROUPS, STATE, GROUP_CH), GROUP_CH ** -0.5),
        "ssm_b_im": nrm(ks[13], (DEPTH, N_GROUPS, STATE, GROUP_CH), GROUP_CH ** -0.5),
        "ssm_c_re": nrm(ks[14], (DEPTH, N_GROUPS, GROUP_CH, STATE), (2 * STATE) ** -0.5),
        "ssm_c_im": nrm(ks[15], (DEPTH, N_GROUPS, GROUP_CH, STATE), (2 * STATE) ** -0.5),
        "ssm_d": nrm(ks[16], (DEPTH, D_SSM), 1.0),
        "ssm_w_glu": nrm(ks[17], (DEPTH, D_SSM, 2 * D), D_SSM ** -0.5),
        "conv_w": nrm(ks[18], (DEPTH, CONV_W, 1, D_CONV), CONV_W ** -0.5),
        "conv_w_out": nrm(ks[19], (DEPTH, D_CONV, D), D_CONV ** -0.5),
        "w_o": nrm(ks[20], (DEPTH, D, D), D ** -0.5),
        "g_ffn2": gain(ks[21], (DEPTH, D)),
        "ffn2_w_gate": nrm(ks[22], (DEPTH, D, D_FF), D ** -0.5),
        "ffn2_w_up": nrm(ks[23], (DEPTH, D, D_FF), D ** -0.5),
        "ffn2_w_down": nrm(ks[24], (DEPTH, D_FF, D), D_FF ** -0.5),
        "g_final": gain(ks[25], (D,)),
    }


def reference(x, meta_tokens, g_ffn1, ffn1_w_gate, ffn1_w_up, ffn1_w_down, g_mix, w_in,
              b_gate, ssm_a_re, ssm_a_im, ssm_log_dt, ssm_b_re, ssm_b_im, ssm_c_re,
              ssm_c_im, ssm_d, ssm_w_glu, conv_w, conv_w_out, w_o, g_ffn2,
              ffn2_w_gate, ffn2_w_up, ffn2_w_down, g_final):
    bsz = x.shape[0]
    meta = jnp.broadcast_to(meta_tokens.astype(x.dtype)[None], (bsz, N_META, D_MODEL))
    h = jnp.concatenate([meta, x], axis=1)
    s0 = D_SSM
    s1 = s0 + D_CONV
    s2 = s1 + D_CONV
    s3 = s2 + D_CONV
    s4 = s3 + D_MODEL
    for l in range(DEPTH):
        h = h + 0.5 * swiglu(rmsnorm(h, g_ffn1[l]), ffn1_w_gate[l], ffn1_w_up[l], ffn1_w_down[l])
        u = rmsnorm(h, g_mix[l])
        p = u @ w_in[l]
        gates = jax.nn.sigmoid(p[..., s3:] + b_gate[l])
        gate_ssm, gate_conv = gates[..., :D_MODEL], gates[..., D_MODEL:]
        y_ssm = s5_mixer(p[..., :s0], ssm_a_re[l], ssm_a_im[l], ssm_log_dt[l], ssm_b_re[l],
                         ssm_b_im[l], ssm_c_re[l], ssm_c_im[l], ssm_d[l])
        z = jax.nn.gelu(y_ssm) @ ssm_w_glu[l]
        y_ssm = z[..., :D_MODEL] * jax.nn.sigmoid(z[..., D_MODEL:])
        v, gb, gc = p[..., s0:s1], p[..., s1:s2], p[..., s2:s3]
        y_conv = (gb * short_conv(gc * v, conv_w[l])) @ conv_w_out[l]
        mixed = gate_ssm * y_ssm + gate_conv * y_conv
        h = h + mixed @ w_o[l]
        h = h + 0.5 * swiglu(rmsnorm(h, g_ffn2[l]), ffn2_w_gate[l], ffn2_w_up[l], ffn2_w_down[l])
    out = rmsnorm(h, g_final)
    return out[:, N_META:]
```

```python
import contextlib
import math
import numpy as np
import concourse.bass as bass
import concourse.mybir as mybir
from concourse.bass_utils import run_bass_kernel_spmd

F32 = mybir.dt.float32
BF16 = mybir.dt.bfloat16
I32 = mybir.dt.int32
AF = mybir.ActivationFunctionType
ALU = mybir.AluOpType

D = 1024
DFF = 2816
NT = 2064
NBLK = 4
NSB = 6
SB = 344
TC = 172
NF = 22
GROUPS = [(0, 3), (3, 3), (6, 3), (9, 3), (12, 3), (15, 3), (18, 2), (20, 2)]
EPS = 1e-6
ENGS = ("pe", "act", "dve", "pool", "sp")
PI = math.pi


class Prog:
    def __init__(self, nc, dma_ring=8):
        self.nc = nc
        self.ops = []
        self.dma_ring = dma_ring

    def op(self, eng, fn, reads=(), writes=(), dma=False, epoch=True):
        self.ops.append(dict(eng=eng, fn=fn, reads=tuple(reads) + (("EPOCH",) if epoch else ()), writes=tuple(writes), dma=dma))

    def barrier(self, dummy):
        self.ops.append(dict(eng="dve", fn=lambda e: e.memset(dummy, 0.0), reads=(), writes=("EPOCH",), dma=False))

    def pe(self, fn, r=(), w=()):
        self.op("pe", fn, r, w)

    def act(self, fn, r=(), w=()):
        self.op("act", fn, r, w)

    def dve(self, fn, r=(), w=()):
        self.op("dve", fn, r, w)

    def pool(self, fn, r=(), w=()):
        self.op("pool", fn, r, w)

    def dma(self, q, fn, r=(), w=(), epoch=True):
        self.op(q, fn, r, w, dma=True, epoch=epoch)

    def emit(self):
        nc = self.nc
        ops = self.ops
        last_writer = {}
        readers = {}
        for i, o in enumerate(ops):
            deps = set()
            for r in o["reads"]:
                if r in last_writer:
                    deps.add(last_writer[r])
            for w in o["writes"]:
                if w in last_writer:
                    deps.add(last_writer[w])
                deps.update(readers.get(w, ()))
            deps.discard(i)
            o["deps"] = deps
            for r in o["reads"]:
                readers.setdefault(r, []).append(i)
            for w in o["writes"]:
                last_writer[w] = i
                readers[w] = []
        eidx = {e: 0 for e in ENGS}
        dcount = {e: 0 for e in ENGS}
        for o in ops:
            o["eidx"] = eidx[o["eng"]]
            eidx[o["eng"]] += 1
            o["marked"] = False
            if o["dma"]:
                o["dma_n"] = dcount[o["eng"]]
                dcount[o["eng"]] += 1
        R_ = self.dma_ring
        for i, o in enumerate(ops):
            E = o["eng"]
            need = []
            best = {}
            bestd = {}
            for j in o["deps"]:
                p = ops[j]
                if p["dma"]:
                    k = (p["eng"], p["dma_n"] % R_)
                    if k not in bestd or ops[bestd[k]]["dma_n"] < p["dma_n"]:
                        bestd[k] = j
                else:
                    k = p["eng"]
                    if k not in best or ops[best[k]]["eidx"] < p["eidx"]:
                        best[k] = j
            need.extend(bestd.values())
            for k, j in best.items():
                p = ops[j]
                if k != E:
                    need.append(j)
                    p["marked"] = True
                else:
                    if E == "pe" and not o["dma"]:
                        continue
                    need.append(j)
                    p["marked"] = True
            o["need"] = need
        cnt = {e: 0 for e in ENGS}
        for o in ops:
            if o["marked"]:
                cnt[o["eng"]] += 1
                o["cnt"] = cnt[o["eng"]]
        R = self.dma_ring
        with contextlib.ExitStack() as st:
            esem = {e: st.enter_context(nc.semaphore("s_" + e)) for e in ENGS}
            dsem = {e: [st.enter_context(nc.semaphore("d_%s_%d" % (e, k))) for k in range(R)]
                    for e in ("sp", "act", "pool") if dcount[e] > 0}
            block = st.enter_context(nc.Block())

            def run_engine(E, eng):
                waited = {}

                def wait(sem, val):
                    key = id(sem)
                    if waited.get(key, 0) >= val:
                        return
                    waited[key] = val
                    eng.wait_ge(sem, val)

                for o in ops:
                    if o["eng"] != E:
                        continue
                    for j in o["need"]:
                        p = ops[j]
                        if p["dma"]:
                            n = p["dma_n"]
                            wait(dsem[p["eng"]][n % R], 16 * (n // R + 1))
                        else:
                            wait(esem[p["eng"]], p["cnt"])
                    if o["dma"]:
                        n = o["dma_n"]
                        s = dsem[E][n % R]
                        if n // R > 0:
                            wait(s, 16 * (n // R))
                        o["fn"](eng).then_inc(s, 16)
                    else:
                        ins = o["fn"](eng)
                        if o["marked"]:
                            ins.then_inc(esem[E], 1)
                if E in dsem:
                    tot = dcount[E]
                    for k in range(R):
                        uses = (tot - k + R - 1) // R if tot > k else 0
                        if uses > 0:
                            wait(dsem[E][k], 16 * uses)

            @block.tensor
            def _(eng):
                run_engine("pe", eng)

            @block.scalar
            def _(eng):
                run_engine("act", eng)

            @block.vector
            def _(eng):
                run_engine("dve", eng)

            @block.gpsimd
            def _(eng):
                run_engine("pool", eng)

            @block.sync
            def _(eng):
                run_engine("sp", eng)


DEBUG = False


def build_nc():
    nc = bass.Bass("TRN2", target_bir_lowering=False)
    dr = lambda name, shape, kind="ExternalInput", dt=F32: nc.dram_tensor(name, shape, dt, kind=kind).ap()
    xs = dr("xs", [NBLK * NT, D])
    out = dr("out", [2048, D], kind="ExternalOutput")
    W = {}
    for nm, shp in [("g_ffn1", [D]), ("ffn1_w_gate", [D, DFF]), ("ffn1_w_up", [D, DFF]), ("ffn1_w_down", [DFF, D]),
                    ("g_mix", [D]), ("w_in", [D, 4096]), ("b_gate", [2048]), ("ssm_a_re", [32, 64]),
                    ("ssm_a_im", [32, 64]), ("ssm_log_dt", [32]), ("ssm_b_re", [32, 64, 16]),
                    ("ssm_b_im", [32, 64, 16]), ("ssm_c_re", [512, 64]), ("ssm_c_im", [512, 64]),
                    ("ssm_d", [512]), ("ssm_w_glu", [512, 2048]), ("conv_w", [3, 512]),
                    ("conv_w_out", [512, D]), ("w_o", [D, D]), ("g_ffn2", [D]), ("ffn2_w_gate", [D, DFF]),
                    ("ffn2_w_up", [D, DFF]), ("ffn2_w_down", [DFF, D]), ("g_final", [D])]:
        W[nm] = dr(nm, shp)

    P = Prog(nc)
    nc_allow = nc.allow_non_contiguous_dma(reason="small parameter loads")
    with contextlib.ExitStack() as st:
        st.enter_context(nc_allow)

        uid = [0]

        def sb(name, shape, dt, stack=st):
            uid[0] += 1
            return stack.enter_context(nc.sbuf_tensor("%s_%d" % (name, uid[0]), shape, dt))

        ps = [st.enter_context(nc.psum_tensor("ps%d" % i, [128, 512], F32)) for i in range(8)]

        def tap(name, ap, keys, dt=F32):
            if not DEBUG:
                return
            shape = list(ap.shape)
            d = nc.dram_tensor("dbg_" + name, shape, dt, kind="ExternalOutput").ap()
            P.dma("sp", lambda e: e.dma_start(out=d, in_=ap), r=keys, w=[("dbg", name)])

        HTK = [("hT", dt_, s_) for dt_ in range(8) for s_ in range(NSB)]
        psk = lambda i: ("ps", i)

        hT = sb("hT", [128, 8, NT], F32)
        ident = sb("ident", [128, 128], F32)
        dummy = sb("dummy", [128, 1], F32)
        dummy2 = sb("dummy2", [128, 1], F32)
        SLOT = 9216
        WS = sb("WS", [128, 2 * SLOT], BF16)
        wsg = [WS[:, i * SLOT: i * SLOT + 3072].rearrange("p (k n) -> p k n", k=8) for i in range(2)]
        wsu = [WS[:, i * SLOT + 3072: i * SLOT + 6144].rearrange("p (k n) -> p k n", k=8) for i in range(2)]
        wsd = [WS[:, i * SLOT + 6144: i * SLOT + 9216].rearrange("p (f n) -> p f n", f=3) for i in range(2)]
        ALLWS = [("WS", i, j) for i in range(2) for j in range(3)]
        uT = sb("uT", [128, 4, NT], BF16)
        junk = [sb("junk%d" % i, [128, TC], F32) for i in range(2)]
        acc = sb("acc", [128, 4, 16], F32)
        ctm = sb("ctm", [128, 6, 16], F32)
        rows3 = lambda ap: ap.rearrange("(kc p) n -> p kc n", p=128)
        ones_bf = sb("ones_bf", [128, 128], BF16)
        gains = sb("gains", [128, 4, 8], F32)
        bgate = sb("bgate", [128, 16], F32)
        dsk = sb("dsk", [128, 4], F32)
        cw = sb("cw", [128, 4, 3], F32)
        mag = sb("mag", [128, 16], F32)
        carry = sb("carry", [128, 2, 16], F32)
        glast = sb("glast", [128, 2, 16], F32)
        th = sb("th", [128, 16], F32)
        adr = sb("adr", [128, 16], F32)
        lamN = sb("lamN", [128, 2, 16], F32)
        cosN = sb("cosN", [128, 16], F32)
        sinN = sb("sinN", [128, 16], F32)
        Ec = sb("Ec", [128, 16, TC], F32)
        Es = sb("Es", [128, 16, TC], F32)
        BqPad = sb("BqPad", [128, 16, 2, 128], BF16)
        CPad = sb("CPad", [128, 16, 2, 128], BF16)

        P.pool(lambda e: e.memset(ident[:], 0.0), w=["ident"])
        P.pool(lambda e: e.affine_select(out=ident[:], in_=ident[:], compare_op=ALU.not_equal, fill=1.0,
                                         base=0, pattern=[[-1, 128]], channel_multiplier=1), r=["ident"], w=["ident"])
        P.dve(lambda e: e.memset(ones_bf[:], 1.0), w=["ones"])
        for i, nm in enumerate(["g_ffn1", "g_mix", "g_ffn2", "g_final"]):
            P.dma("sp", lambda e, i=i, nm=nm: e.dma_start(out=gains[:, i, :], in_=W[nm].rearrange("(t p) -> p t", p=128)),
                  w=[("gains", i)])
        P.dma("sp", lambda e: e.dma_start(out=bgate[:], in_=W["b_gate"].rearrange("(t p) -> p t", p=128)), w=["bgate"])
        P.dma("sp", lambda e: e.dma_start(out=dsk[:], in_=W["ssm_d"].rearrange("(t p) -> p t", p=128)), w=["dsk"])
        for j in range(3):
            P.dma("sp", lambda e, j=j: e.dma_start(out=cw[:, :, j], in_=W["conv_w"][j, :].rearrange("(t p) -> p t", p=128)), w=["cw"])

        with contextlib.ExitStack() as s2:
            t = lambda name, shape, dt=F32: sb(name, shape, dt, s2)
            are = t("are", [128, 16]); aim = t("aim", [128, 16]); ldt = t("ldt", [128, 16])
            bre = t("bre", [128, 16, 16]); bim = t("bim", [128, 16, 16])
            cn = t("cn", [128, 2, 4, 2, 64])
            CP = t("CP", [128, 2, 16, 16])
            dtv = t("dtv", [128, 16])
            cs = t("cs", [128, 16]); sn = t("sn", [128, 16])
            lre = t("lre", [128, 16]); lim = t("lim", [128, 16])
            den = t("den", [128, 16]); tmpa = t("tmpa", [128, 16]); tmpb = t("tmpb", [128, 16])
            qre = t("qre", [128, 16]); qim = t("qim", [128, 16]); lm1 = t("lm1", [128, 16])
            bq = t("bq", [128, 2, 16, 16]); tb = t("tb", [128, 16])
            angN = t("angN", [128, 16])
            wkn1 = t("wkn1", [128, 16]); wkn2 = t("wkn2", [128, 16]); wkni = t("wkni", [128, 16], I32)

            pair_view = lambda ap: ap.rearrange("(gp two) p -> (two p) gp", two=2)
            P.dma("sp", lambda e: e.dma_start(out=are[:], in_=pair_view(W["ssm_a_re"])), w=["are"])
            P.dma("sp", lambda e: e.dma_start(out=aim[:], in_=pair_view(W["ssm_a_im"])), w=["aim"])
            for gpar in range(2):
                P.dma("sp", lambda e, gpar=gpar: e.dma_start(
                    out=ldt[gpar * 64:(gpar + 1) * 64, :],
                    in_=W["ssm_log_dt"].rearrange("(g two) -> two g", two=2)[gpar:gpar + 1, :].to_broadcast([64, 16])),
                    w=[("ldt", gpar)])
            bview = lambda ap: ap.rearrange("(gp two) p c -> (two p) gp c", two=2)
            P.dma("sp", lambda e: e.dma_start(out=bre[:], in_=bview(W["ssm_b_re"])), w=["bre"])
            P.dma("sp", lambda e: e.dma_start(out=bim[:], in_=bview(W["ssm_b_im"])), w=["bim"])
            for ri, nm in enumerate(["ssm_c_re", "ssm_c_im"]):
                for dup in range(2):
                    P.dma("sp", lambda e, ri=ri, nm=nm, dup=dup: e.dma_start(
                        out=cn[:, ri, :, dup, :], in_=W[nm].rearrange("(t p) s -> p t s", p=128)), w=[("cn", ri, dup)])
            for ri in range(2):
                for ct in range(4):
                    bank = 6 + (ct % 2)
                    P.pe(lambda e, ri=ri, ct=ct, bank=bank: e.transpose(
                        out=ps[bank][:, 0:128], in_=cn[:, ri, ct, :, :].rearrange("p a b -> p (a b)"), identity=ident[:]),
                        r=[("cn", ri, 0), ("cn", ri, 1), "ident"], w=[psk(bank)])
                    for gpar in range(2):
                        P.act(lambda e, ri=ri, ct=ct, bank=bank, gpar=gpar: e.activation(
                            out=CP[gpar * 64:(gpar + 1) * 64, ri, ct * 4:(ct + 1) * 4, :],
                            in_=ps[bank][gpar * 64:(gpar + 1) * 64, 0:128].rearrange(
                                "p (gl two c) -> p gl two c", two=2, c=16)[:, :, gpar, :],
                            func=AF.Identity, scale=(1.0 if ri == 0 else -1.0)),
                            r=[psk(bank)], w=[("CP", ri, ct, gpar)])
            cp_keys = [("CP", ri, ct, gpar) for ri in range(2) for ct in range(4) for gpar in range(2)]
            ldk = [("ldt", 0), ("ldt", 1)]
            P.act(lambda e: e.activation(out=dtv[:], in_=ldt[:], func=AF.Exp), r=ldk, w=["dtv"])
            P.dve(lambda e: e.tensor_tensor(out=adr[:], in0=are[:], in1=dtv[:], op=ALU.mult), r=["are", "dtv"], w=["adr"])
            P.dve(lambda e: e.tensor_tensor(out=th[:], in0=aim[:], in1=dtv[:], op=ALU.mult), r=["aim", "dtv"], w=["th"])
            P.act(lambda e: e.activation(out=mag[:], in_=adr[:], func=AF.Exp), r=["adr"], w=["mag"])

            def sincos(x, xk, o_sin, o_cos, w1, w2, wi, keys):
                for off, o in ((0.0, o_sin), (PI / 2, o_cos)):
                    ok = keys[0] if o is o_sin else keys[1]
                    P.dve(lambda e, off=off: e.tensor_scalar(out=w1, in0=x, scalar1=off, scalar2=1.0 / (2 * PI),
                                                             op0=ALU.add, op1=ALU.mult), r=[xk], w=["w1" + keys[2]])
                    P.dve(lambda e: e.tensor_copy(out=wi, in_=w1), r=["w1" + keys[2]], w=["wi" + keys[2]])
                    P.dve(lambda e: e.tensor_copy(out=w1, in_=wi), r=["wi" + keys[2]], w=["w1" + keys[2]])
                    P.dve(lambda e: e.scalar_tensor_tensor(out=w2, in0=w1, scalar=-6.25, in1=x,
                                                           op0=ALU.mult, op1=ALU.add), r=["w1" + keys[2], xk], w=["w2" + keys[2]])
                    P.dve(lambda e: e.scalar_tensor_tensor(out=w2, in0=w1, scalar=-(2 * PI - 6.25), in1=w2,
                                                           op0=ALU.mult, op1=ALU.add), r=["w1" + keys[2], "w2" + keys[2]], w=["w2" + keys[2]])
                    P.dve(lambda e, off=off: e.tensor_scalar(out=w2, in0=w2, scalar1=off, scalar2=None, op0=ALU.add),
                          r=["w2" + keys[2]], w=["w2" + keys[2]])
                    for lim_, sgn in ((PI, -1.0), (-PI, 1.0)):
                        cmp = ALU.is_gt if sgn < 0 else ALU.is_lt
                        P.dve(lambda e, lim_=lim_, cmp=cmp: e.tensor_scalar(out=w1, in0=w2, scalar1=lim_, scalar2=None, op0=cmp),
                              r=["w2" + keys[2]], w=["w1" + keys[2]])
                        P.dve(lambda e, sgn=sgn: e.scalar_tensor_tensor(out=w2, in0=w1, scalar=sgn * 2 * PI, in1=w2,
                                                                        op0=ALU.mult, op1=ALU.add),
                              r=["w1" + keys[2], "w2" + keys[2]], w=["w2" + keys[2]])
                    P.act(lambda e, o=o: e.activation(out=o, in_=w2, func=AF.Sin), r=["w2" + keys[2]], w=[ok])

            sincos(th[:], "th", sn[:], cs[:], wkn1[:], wkn2[:], wkni[:], ("sn", "cs", "n"))
            P.dve(lambda e: e.tensor_tensor(out=lre[:], in0=mag[:], in1=cs[:], op=ALU.mult), r=["mag", "cs"], w=["lre"])
            P.dve(lambda e: e.tensor_tensor(out=lim[:], in0=mag[:], in1=sn[:], op=ALU.mult), r=["mag", "sn"], w=["lim"])
            P.dve(lambda e: e.tensor_tensor(out=den[:], in0=are[:], in1=are[:], op=ALU.mult), r=["are"], w=["den"])
            P.dve(lambda e: e.tensor_tensor(out=tmpa[:], in0=aim[:], in1=aim[:], op=ALU.mult), r=["aim"], w=["tmpa"])
            P.dve(lambda e: e.tensor_tensor(out=den[:], in0=den[:], in1=tmpa[:], op=ALU.add), r=["den", "tmpa"], w=["den"])
            P.dve(lambda e: e.reciprocal(out=den[:], in_=den[:]), r=["den"], w=["den"])
            P.dve(lambda e: e.tensor_scalar(out=lm1[:], in0=lre[:], scalar1=-1.0, scalar2=None, op0=ALU.add), r=["lre"], w=["lm1"])
            P.dve(lambda e: e.tensor_tensor(out=tmpa[:], in0=lm1[:], in1=are[:], op=ALU.mult), r=["lm1", "are", "den"], w=["tmpa"])
            P.dve(lambda e: e.tensor_tensor(out=tmpb[:], in0=lim[:], in1=aim[:], op=ALU.mult), r=["lim", "aim"], w=["tmpb"])
            P.dve(lambda e: e.tensor_tensor(out=tmpa[:], in0=tmpa[:], in1=tmpb[:], op=ALU.add), r=["tmpa", "tmpb"], w=["tmpa"])
            P.dve(lambda e: e.tensor_tensor(out=qre[:], in0=tmpa[:], in1=den[:], op=ALU.mult), r=["tmpa", "den"], w=["qre"])
            P.dve(lambda e: e.tensor_tensor(out=tmpa[:], in0=lim[:], in1=are[:], op=ALU.mult), r=["lim", "are", "qre"], w=["tmpa"])
            P.dve(lambda e: e.tensor_tensor(out=tmpb[:], in0=lm1[:], in1=aim[:], op=ALU.mult), r=["lm1", "aim"], w=["tmpb"])
            P.dve(lambda e: e.tensor_tensor(out=tmpa[:], in0=tmpa[:], in1=tmpb[:], op=ALU.subtract), r=["tmpa", "tmpb"], w=["tmpa"])
            P.dve(lambda e: e.tensor_tensor(out=qim[:], in0=tmpa[:], in1=den[:], op=ALU.mult), r=["tmpa", "den"], w=["qim"])
            qb = lambda q: q[:, :].unsqueeze(2).to_broadcast([128, 16, 16])
            P.dve(lambda e: e.tensor_tensor(out=bq[:, 0], in0=bre[:], in1=qb(qre), op=ALU.mult), r=["bre", "qre"], w=["bq0"])
            P.dve(lambda e: e.tensor_tensor(out=bq[:, 1], in0=bim[:], in1=qb(qim), op=ALU.mult), r=["bim", "qim"], w=["bq1"])
            P.dve(lambda e: e.tensor_tensor(out=bq[:, 0], in0=bq[:, 0], in1=bq[:, 1], op=ALU.subtract), r=["bq0", "bq1"], w=["bq0"])
            P.dve(lambda e: e.tensor_tensor(out=bq[:, 1], in0=bim[:], in1=qb(qre), op=ALU.mult), r=["bim", "qre", "bq0"], w=["bq1"])
            P.dve(lambda e: e.tensor_tensor(out=bre[:], in0=bre[:], in1=qb(qim), op=ALU.mult), r=["bre", "qim", "bq0"], w=["bre"])
            P.dve(lambda e: e.tensor_tensor(out=bq[:, 1], in0=bq[:, 1], in1=bre[:], op=ALU.add), r=["bq1", "bre"], w=["bq1"])
            sz = contextlib.ExitStack()
            sz.__enter__()
            Z = sb("Z", [128, 16, 2, 128], F32, sz)
            P.pool(lambda e: e.memset(Z[:], 0.0), w=["Z"])
            P.pool(lambda e: e.memset(CPad[:], 0.0), w=["CPad"])
            for ri in range(2):
                for gpar in range(2):
                    for gl in range(4):
                        col = (2 * gl + gpar) * 16
                        rows = slice(gpar * 64, (gpar + 1) * 64)
                        P.dve(lambda e, ri=ri, rows=rows, gl=gl, col=col: e.tensor_copy(
                            out=Z[rows, gl::4, ri, col:col + 16], in_=bq[rows, ri, gl::4, :]),
                            r=["bq0", "bq1", "Z"], w=["Z"])
                        P.act(lambda e, ri=ri, rows=rows, gl=gl, col=col: e.activation(
                            out=CPad[rows, gl::4, ri, col:col + 16], in_=CP[rows, ri, gl::4, :], func=AF.Copy),
                            r=cp_keys + ["CPad"], w=["CPad"])
            for gp in range(16):
                for ri in range(2):
                    bank = 6 + ((gp * 2 + ri) % 2)
                    P.pe(lambda e, gp=gp, ri=ri, bank=bank: e.transpose(out=ps[bank][:, 0:128], in_=Z[:, gp, ri, :], identity=ident[:]),
                         r=["Z", "ident"], w=[psk(bank)])
                    P.act(lambda e, gp=gp, ri=ri, bank=bank: e.activation(out=BqPad[:, gp, ri, :], in_=ps[bank][:, 0:128], func=AF.Copy),
                          r=[psk(bank)], w=["BqPad"])
            sz.__exit__(None, None, None)
            P.barrier(dummy[:])
            P.dve(lambda e: e.tensor_scalar(out=angN[:], in0=th[:], scalar1=float(TC), scalar2=None, op0=ALU.mult), r=["th"], w=["angN"])
            sincos(angN[:], "angN", sinN[:], cosN[:], wkn1[:], wkn2[:], wkni[:], ("sinN", "cosN", "n"))
            P.act(lambda e: e.activation(out=tmpa[:], in_=adr[:], func=AF.Exp, scale=float(TC)), r=["adr", "qre", "qim"], w=["tmpa"])
            P.dve(lambda e: e.tensor_tensor(out=lamN[:, 0, :], in0=tmpa[:], in1=cosN[:], op=ALU.mult), r=["tmpa", "cosN"], w=["lamN"])
            P.dve(lambda e: e.tensor_tensor(out=lamN[:, 1, :], in0=tmpa[:], in1=sinN[:], op=ALU.mult), r=["tmpa", "sinN"], w=["lamN"])
            P.dve(lambda e: e.memset(carry[:], 0.0), w=["carry"])
            tap("mag", mag[:], ["mag"]); tap("lre", lre[:], ["lre"]); tap("lim", lim[:], ["lim"])
            tap("qre", qre[:], ["qre"]); tap("qim", qim[:], ["qim"]); tap("th", th[:], ["th"])
            tap("BqPad", BqPad[:], ["BqPad"], BF16); tap("CPad", CPad[:], ["CPad"], BF16)
            tap("CP", CP[:], cp_keys); tap("bq", bq[:], ["bq0", "bq1"]); tap("ident", ident[:], ["ident"])

        def make_tables(mode):
            P.barrier(dummy[:])
            with contextlib.ExitStack() as sx:
                iot = sb("iot", [128, TC], F32, sx); ioti = sb("ioti", [128, TC], I32, sx)
                ang = sb("ang", [128, 16, TC], F32, sx); wk1 = sb("wk1", [128, 16, TC], F32, sx)
                wk2 = sb("wk2", [128, 16, TC], F32, sx); wki = sb("wki", [128, 16, TC], I32, sx)
                if mode == "E":
                    P.pool(lambda e: e.iota(ioti[:], pattern=[[1, TC]], base=1, channel_multiplier=0), w=["ioti"])
                else:
                    P.pool(lambda e: e.iota(ioti[:], pattern=[[-1, TC]], base=TC - 1, channel_multiplier=0), w=["ioti"])
                P.dve(lambda e: e.tensor_copy(out=iot[:], in_=ioti[:]), r=["ioti"], w=["iot"])
                for gp in range(16):
                    P.dve(lambda e, gp=gp: e.tensor_scalar(out=ang[:, gp, :], in0=iot[:], scalar1=th[:, gp:gp + 1], scalar2=None,
                                                           op0=ALU.mult), r=["iot", "th"], w=["ang"])
                sincos(ang[:], "ang", Es[:], Ec[:], wk1[:], wk2[:], wki[:], ("Es", "Ec", "t"))
                if mode == "D":
                    for gp in range(16):
                        P.act(lambda e, gp=gp: e.activation(out=ang[:, gp, :], in_=iot[:], func=AF.Exp, scale=adr[:, gp:gp + 1]),
                              r=["iot", "adr", "Es", "Ec"], w=[("magp", gp)])
                    mk = [("magp", gp) for gp in range(16)]
                    P.dve(lambda e: e.tensor_tensor(out=Ec[:], in0=Ec[:], in1=ang[:], op=ALU.mult), r=["Ec"] + mk, w=["Ec"])
                    P.dve(lambda e: e.tensor_tensor(out=Es[:], in0=Es[:], in1=ang[:], op=ALU.mult), r=["Es"] + mk, w=["Es"])
                tap("Ec_" + mode, Ec[:], ["Ec"]); tap("Es_" + mode, Es[:], ["Es"])

        make_tables("D")

        cw_scr = nc.dram_tensor("cw_scr", [8, 128, 3584], BF16).ap()
        glu_v = W["ssm_w_glu"].rearrange("(kc p) n -> p kc n", p=128)
        co_v = W["conv_w_out"].rearrange("(kc p) n -> p kc n", p=128)
        win_v = W["w_in"].rearrange("(kc p) n -> p kc n", p=128)
        for dt_ in range(8):
            for j in range(2):
                P.dma("pool", lambda e, j=j, dt_=dt_: e.dma_start(
                    out=cw_scr[dt_, :, 0:1024].rearrange("p (k j n) -> p k j n", k=4, j=2)[:, :, j, :],
                    in_=glu_v[:, :, j * 1024 + dt_ * 128: j * 1024 + (dt_ + 1) * 128]), w=[("cwscr", dt_, "gl", j)], epoch=False)
                P.dma("pool", lambda e, j=j, dt_=dt_: e.dma_start(
                    out=cw_scr[dt_, :, 1536:3584].rearrange("p (k j n) -> p k j n", k=8, j=2)[:, :, j, :],
                    in_=win_v[:, :, 2048 + j * 1024 + dt_ * 128: 2048 + j * 1024 + (dt_ + 1) * 128]), w=[("cwscr", dt_, "ga", j)], epoch=False)
            P.dma("pool", lambda e, dt_=dt_: e.dma_start(
                out=cw_scr[dt_, :, 1024:1536].rearrange("p (k n) -> p k n", k=4),
                in_=co_v[:, :, dt_ * 128:(dt_ + 1) * 128]), w=[("cwscr", dt_, "co")], epoch=False)

        def norm_sb(s_, gi, xn_ap_fn, sq, sd, rstd, keyfn, pbank=6, tag=0):
            tok = slice(s_ * SB, (s_ + 1) * SB)
            for dt_ in range(8):
                P.act(lambda e, dt_=dt_: e.activation(out=sq[dt_ % 2][:], in_=hT[:, dt_, tok], func=AF.Square),
                      r=[("hT", dt_, s_)], w=[("sq", dt_ % 2)])
                P.pe(lambda e, dt_=dt_: e.matmul(ps[pbank][:, 0:SB], lhsT=ones_bf[:], rhs=sq[dt_ % 2][:],
                                                 start=(dt_ == 0), stop=(dt_ == 7)),
                     r=[("sq", dt_ % 2), "ones"], w=[psk(pbank)])
            P.act(lambda e: e.activation(out=sd[:], in_=ps[pbank][:, 0:SB], func=AF.Sqrt, scale=1.0 / D, bias=EPS),
                  r=[psk(pbank)], w=[("sd", tag)])
            P.dve(lambda e: e.reciprocal(out=rstd[:], in_=sd[:]), r=[("sd", tag)], w=[("rstd", tag)])
            for dt_ in range(8):
                P.dve(lambda e, dt_=dt_: e.scalar_tensor_tensor(out=xn_ap_fn(dt_), in0=hT[:, dt_, tok],
                                                                scalar=gains[:, gi, dt_:dt_ + 1], in1=rstd[:],
                                                                op0=ALU.mult, op1=ALU.mult),
                      r=[("hT", dt_, s_), ("rstd", tag), ("gains", gi)], w=[keyfn(dt_)])

        def ffn(blk, gi, wg, wu, wd, bg=None):
            P.barrier(dummy[:])
            with contextlib.ExitStack() as s3:
                xn = sb("xn", [128, 8, NT], BF16, s3)
                sq = [sb("sq%d" % i, [128, SB], BF16, s3) for i in range(2)]
                sd = sb("sd", [128, SB], F32, s3)
                rstd = sb("rstd", [128, SB], F32, s3)
                actb = [sb("actb%d" % i, [128, 3, SB], BF16, s3) for i in range(2)]
                sg = [sb("sg%d" % i, [128, SB], F32, s3) for i in range(2)]
                for s_ in range(NSB):
                    norm_sb(s_, gi, lambda dt_, s_=s_: xn[:, dt_, s_ * SB:(s_ + 1) * SB], sq, sd, rstd,
                            lambda dt_, s_=s_: ("xn", dt_, s_))
                cnt = 0
                prev_down = [None]

                def emit_down(nf, sl, ab, s_, tok):
                    for dt_ in range(8):
                        pb = 4 + (dt_ % 2)
                        for fi in range(nf):
                            P.pe(lambda e, fi=fi, dt_=dt_, pb=pb, sl=sl, ab=ab, nf=nf: e.matmul(
                                ps[pb][:, 0:SB], lhsT=wsd[sl][:, fi, dt_ * 128:(dt_ + 1) * 128], rhs=actb[ab][:, fi, :],
                                start=(fi == 0), stop=(fi == nf - 1)),
                                r=[("WS", sl, 2), ("actb", ab, fi)], w=[psk(pb)])
                        P.dve(lambda e, dt_=dt_, pb=pb, tok=tok: e.scalar_tensor_tensor(
                            out=hT[:, dt_, tok], in0=ps[pb][:, 0:SB], scalar=0.5, in1=hT[:, dt_, tok],
                            op0=ALU.mult, op1=ALU.add),
                            r=[psk(pb), ("hT", dt_, s_)], w=[("hT", dt_, s_)])
                        if bg is not None and dt_ % 4 == 3:
                            next(bg, None)

                for gix, (f0, nf) in enumerate(GROUPS):
                    sl = gix % 2
                    P.dma("pool", lambda e, f0=f0, nf=nf, sl=sl: e.dma_start(
                        out=wsg[sl][:, :, 0:nf * 128], in_=rows3(wg)[:, :, f0 * 128:(f0 + nf) * 128]),
                        w=[("WS", sl, 0)], epoch=False)
                    P.dma("pool", lambda e, f0=f0, nf=nf, sl=sl: e.dma_start(
                        out=wsu[sl][:, :, 0:nf * 128], in_=rows3(wu)[:, :, f0 * 128:(f0 + nf) * 128]),
                        w=[("WS", sl, 1)], epoch=False)
                    P.dma("pool", lambda e, f0=f0, nf=nf, sl=sl: e.dma_start(
                        out=wsd[sl][:, 0:nf, :], in_=rows3(wd)[:, f0:f0 + nf, :]),
                        w=[("WS", sl, 2)], epoch=False)
                    for s_ in range(NSB):
                        tok = slice(s_ * SB, (s_ + 1) * SB)
                        ab = (gix * NSB + s_) % 2
                        for fi in range(nf):
                            par = cnt % 2
                            cnt += 1
                            for kc in range(8):
                                P.pe(lambda e, kc=kc, fi=fi, par=par, sl=sl, tok=tok: e.matmul(
                                    ps[par][:, 0:SB], lhsT=wsg[sl][:, kc, fi * 128:(fi + 1) * 128], rhs=xn[:, kc, tok],
                                    start=(kc == 0), stop=(kc == 7)),
                                    r=[("WS", sl, 0), ("xn", kc, s_)], w=[psk(par)])
                            for kc in range(8):
                                P.pe(lambda e, kc=kc, fi=fi, par=par, sl=sl, tok=tok: e.matmul(
                                    ps[2 + par][:, 0:SB], lhsT=wsu[sl][:, kc, fi * 128:(fi + 1) * 128], rhs=xn[:, kc, tok],
                                    start=(kc == 0), stop=(kc == 7)),
                                    r=[("WS", sl, 1), ("xn", kc, s_)], w=[psk(2 + par)])
                            P.act(lambda e, par=par: e.activation(out=sg[par][:], in_=ps[par][:, 0:SB], func=AF.Silu),
                                  r=[psk(par)], w=[("sg", par)])
                            P.dve(lambda e, par=par, ab=ab, fi=fi: e.tensor_tensor(
                                out=actb[ab][:, fi, :], in0=ps[2 + par][:, 0:SB], in1=sg[par][:], op=ALU.mult),
                                r=[psk(2 + par), ("sg", par)], w=[("actb", ab, fi)])
                            if bg is not None:
                                next(bg, None)
                        if prev_down[0] is not None:
                            prev_down[0]()
                        prev_down[0] = (lambda nf=nf, sl=sl, ab=ab, s_=s_, tok=tok: emit_down(nf, sl, ab, s_, tok))
                if prev_down[0] is not None:
                    prev_down[0]()
                if bg is not None:
                    for _ in bg:
                        pass

        pending = []

        def ssm_hist():
            for c in range(NT // TC):
                tk = slice(c * TC, (c + 1) * TC)
                s_ = (c * TC) // SB
                for gp in range(16):
                    ct = gp // 4
                    bank = 6 + (gp % 2)
                    PA, PB = ps[bank][:, 0:TC], ps[bank][:, TC:2 * TC]
                    P.pe(lambda e, gp=gp, ct=ct, PA=PA, tk=tk: e.matmul(PA, lhsT=BqPad[:, gp, 0, :], rhs=uT[:, ct, tk], start=True, stop=True),
                         r=["BqPad", ("uT", ct, s_)], w=[psk(bank)])
                    P.pe(lambda e, gp=gp, ct=ct, PB=PB, tk=tk: e.matmul(PB, lhsT=BqPad[:, gp, 1, :], rhs=uT[:, ct, tk], start=True, stop=True),
                         r=["BqPad", ("uT", ct, s_)], w=[psk(bank)])
                    for k, (src, tab, tkey) in enumerate([(PA, Ec, "Ec"), (PB, Es, "Es"), (PA, Es, "Es"), (PB, Ec, "Ec")]):
                        P.dve(lambda e, gp=gp, k=k, src=src, tab=tab: e.scalar_tensor_tensor(
                            out=junk[k % 2][:], in0=src, scalar=1.0, in1=tab[:, gp, :],
                            op0=ALU.mult, op1=ALU.mult, accum_out=acc[:, k, gp:gp + 1]),
                            r=[psk(bank), tkey], w=[("junk", k % 2), ("acc", k, gp)])
                    yield
                ak = [("acc", k, gp) for k in range(4) for gp in range(16)]
                TT = lambda o, a_, b_, op: (lambda e: e.tensor_tensor(out=o, in0=a_, in1=b_, op=op))
                P.dve(TT(ctm[:, 0, :], acc[:, 0, :], acc[:, 1, :], ALU.subtract), r=ak, w=["ctm0"])
                P.dve(TT(ctm[:, 1, :], acc[:, 2, :], acc[:, 3, :], ALU.add), r=ak, w=["ctm1"])
                P.dve(TT(ctm[:, 2, :], lamN[:, 0, :], carry[:, 0, :], ALU.mult), r=["lamN", "carry"], w=["ctm2"])
                P.dve(TT(ctm[:, 3, :], lamN[:, 1, :], carry[:, 1, :], ALU.mult), r=["lamN", "carry"], w=["ctm3"])
                P.dve(TT(ctm[:, 4, :], lamN[:, 0, :], carry[:, 1, :], ALU.mult), r=["lamN", "carry"], w=["ctm4"])
                P.dve(TT(ctm[:, 5, :], lamN[:, 1, :], carry[:, 0, :], ALU.mult), r=["lamN", "carry"], w=["ctm5"])
                P.dve(TT(ctm[:, 2, :], ctm[:, 2, :], ctm[:, 3, :], ALU.subtract), r=["ctm2", "ctm3"], w=["ctm2"])
                P.dve(TT(ctm[:, 4, :], ctm[:, 4, :], ctm[:, 5, :], ALU.add), r=["ctm4", "ctm5"], w=["ctm4"])
                P.dve(TT(carry[:, 0, :], ctm[:, 2, :], ctm[:, 0, :], ALU.add), r=["ctm2", "ctm0", "ctm4"], w=["carry"])
                P.dve(TT(carry[:, 1, :], ctm[:, 4, :], ctm[:, 1, :], ALU.add), r=["ctm4", "ctm1"], w=["carry"])
                yield

        def do_block(blk):
            full = (blk == NBLK - 1)
            P.barrier(dummy[:])
            with contextlib.ExitStack() as s3:
                xst = [sb("xst%d" % i, [128, D], F32, s3) for i in range(2)]
                ntile = (NT + 127) // 128
                for i in range(ntile):
                    rows = min(128, NT - i * 128)
                    xb = i % 2
                    P.dma("sp", lambda e, i=i, rows=rows, xb=xb: e.dma_start(
                        out=xst[xb][0:rows, :], in_=xs[blk * NT + i * 128: blk * NT + i * 128 + rows, :]),
                        w=[("xst", xb)])
                    sbs = sorted(set([(i * 128) // SB, (i * 128 + rows - 1) // SB]))
                    for h in range(2):
                        bank = 6 + h
                        for q in range(4):
                            dt_ = h * 4 + q
                            P.pe(lambda e, dt_=dt_, q=q, rows=rows, xb=xb, bank=bank: e.transpose(
                                out=ps[bank][:, q * 128:q * 128 + rows], in_=xst[xb][0:rows, dt_ * 128:(dt_ + 1) * 128],
                                identity=ident[0:rows, 0:rows]),
                                r=[("xst", xb), "ident"], w=[psk(bank)])
                        eng = P.act if h == 0 else P.dve
                        if h == 0:
                            P.act(lambda e, h=h, i=i, rows=rows, bank=bank: e.activation(
                                out=hT[:, h * 4:(h + 1) * 4, i * 128:i * 128 + rows],
                                in_=ps[bank][:, :].rearrange("p (q c) -> p q c", q=4)[:, :, 0:rows], func=AF.Copy),
                                r=[psk(bank)], w=[("hT", dt_, s_) for dt_ in range(h * 4, h * 4 + 4) for s_ in sbs])
                        else:
                            P.dve(lambda e, h=h, i=i, rows=rows, bank=bank: e.tensor_copy(
                                out=hT[:, h * 4:(h + 1) * 4, i * 128:i * 128 + rows],
                                in_=ps[bank][:, :].rearrange("p (q c) -> p q c", q=4)[:, :, 0:rows]),
                                r=[psk(bank)], w=[("hT", dt_, s_) for dt_ in range(h * 4, h * 4 + 4) for s_ in sbs])
            if full:
                tap("hT_load", hT[:], HTK)
            ffn(blk, 0, W["ffn1_w_gate"], W["ffn1_w_up"], W["ffn1_w_down"], bg=(pending.pop() if pending else None))
            if full:
                make_tables("E")
            if full:
                tap("hT_ffn1", hT[:], HTK)
            P.barrier(dummy[:])
            with contextlib.ExitStack() as s3:
                cvT = sb("cvT", [128, 4, NT], BF16, s3) if full else None
                ga = uT
                P.barrier(dummy[:])
                with contextlib.ExitStack() as s4:
                    unA = [sb("un%d" % i, [128, 8, SB], BF16, s4) for i in range(2)]
                    sqA = [sb("sq%d" % i, [128, SB], BF16, s4) for i in range(2)]
                    sdA = [sb("sd%d" % i, [128, SB], F32, s4) for i in range(2)]
                    rstdA = [sb("rstd%d" % i, [128, SB], F32, s4) for i in range(2)]
                    wu_ = WS[:, 0:4096].rearrange("p (k n) -> p k n", k=8)
                    P.dma("pool", lambda e: e.dma_start(out=wu_, in_=rows3(W["w_in"])[:, :, 0:512]), w=ALLWS, epoch=False)
                    if full:
                        wv = WS[:, 4096:16384].rearrange("p (k n) -> p k n", k=8)
                        zs = sb("zs", [128, 4, SB + 2], F32, s4)
                        vv = sb("vv", [128, SB], F32, s4)
                        c1 = sb("c1", [128, SB], F32, s4)
                        P.dma("pool", lambda e: e.dma_start(out=wv, in_=rows3(W["w_in"])[:, :, 512:2048]),
                              w=ALLWS, epoch=False)
                        P.dve(lambda e: e.memset(zs[:], 0.0), w=[("zs", c) for c in range(4)])
                    for s_ in range(NSB):
                        tok = slice(s_ * SB, (s_ + 1) * SB)
                        npar = s_ % 2
                        norm_sb(s_, 1, lambda dt_, npar=npar: unA[npar][:, dt_, :], sqA, sdA[npar], rstdA[npar],
                                lambda dt_, npar=npar: ("un", npar, dt_), pbank=6 + npar, tag=npar)
                        for ot in range(4):
                            pb = 4 + (ot % 2)
                            for kc in range(8):
                                P.pe(lambda e, kc=kc, ot=ot, pb=pb, npar=npar: e.matmul(
                                    ps[pb][:, 0:SB], lhsT=wu_[:, kc, ot * 128:(ot + 1) * 128], rhs=unA[npar][:, kc, :],
                                    start=(kc == 0), stop=(kc == 7)), r=ALLWS + [("un", npar, kc)], w=[psk(pb)])
                            P.act(lambda e, ot=ot, pb=pb, tok=tok: e.activation(out=uT[:, ot, tok], in_=ps[pb][:, 0:SB], func=AF.Copy),
                                  r=[psk(pb)], w=[("uT", ot, s_)])
                        if full:
                            for ot in range(4):
                                for j, bank in ((0, 0), (2, 1), (1, 2)):
                                    for kc in range(8):
                                        P.pe(lambda e, kc=kc, ot=ot, j=j, bank=bank, npar=npar: e.matmul(
                                            ps[bank][:, 0:SB], lhsT=wv[:, kc, j * 512 + ot * 128: j * 512 + (ot + 1) * 128],
                                            rhs=unA[npar][:, kc, :], start=(kc == 0), stop=(kc == 7)),
                                            r=ALLWS + [("un", npar, kc)], w=[psk(bank)])
                                P.act(lambda e: e.activation(out=vv[:], in_=ps[0][:, 0:SB], func=AF.Copy), r=[psk(0)], w=["vv"])
                                P.dve(lambda e, ot=ot: e.tensor_copy(out=zs[:, ot, 0:2], in_=zs[:, ot, SB:SB + 2]),
                                      r=[("zs", ot)], w=[("zs", ot)])
                                P.dve(lambda e, ot=ot: e.tensor_tensor(out=zs[:, ot, 2:SB + 2], in0=ps[1][:, 0:SB], in1=vv[:], op=ALU.mult),
                                      r=[psk(1), "vv", ("zs", ot)], w=[("zs", ot)])
                                P.dve(lambda e, ot=ot: e.tensor_scalar(out=c1[:], in0=zs[:, ot, 0:SB], scalar1=cw[:, ot, 0:1], scalar2=None,
                                                                       op0=ALU.mult), r=[("zs", ot), "cw"], w=["c1"])
                                P.dve(lambda e, ot=ot: e.scalar_tensor_tensor(out=c1[:], in0=zs[:, ot, 1:SB + 1], scalar=cw[:, ot, 1:2], in1=c1[:],
                                                                              op0=ALU.mult, op1=ALU.add), r=[("zs", ot), "cw", "c1"], w=["c1"])
                                P.dve(lambda e, ot=ot: e.scalar_tensor_tensor(out=c1[:], in0=zs[:, ot, 2:SB + 2], scalar=cw[:, ot, 2:3], in1=c1[:],
                                                                              op0=ALU.mult, op1=ALU.add), r=[("zs", ot), "cw", "c1"], w=["c1"])
                                P.dve(lambda e, ot=ot, tok=tok: e.tensor_tensor(out=cvT[:, ot, tok], in0=ps[2][:, 0:SB], in1=c1[:], op=ALU.mult),
                                      r=[psk(2), "c1"], w=[("cvT", ot, s_)])
                if full:
                    tap("uT_A", uT[:], [("uT", c_, s_) for c_ in range(4) for s_ in range(NSB)], BF16)
                if not full:
                    pending.append(ssm_hist())
                    return
                P.barrier(dummy[:])
                with contextlib.ExitStack() as s4:
                    NBUF = 3
                    TN = ["t1", "t2", "t3", "t4", "wre", "wim", "gre", "gim"]
                    T = {}
                    for b_ in range(NBUF):
                        for n_ in TN:
                            T[(n_, b_)] = sb("%s_%d" % (n_, b_), [128, TC], F32, s4)
                        for n_ in ["d1", "d2", "d3", "d4"]:
                            T[(n_, b_)] = sb("%s_%d" % (n_, b_), [128, TC], BF16, s4)
                    ctmp = sb("ctmp", [128, 4, 16], F32, s4)
                    Hh = [sb("Hh%d" % i, [128, 4, 2, TC], BF16, s4) for i in range(2)]
                    ya = [sb("ya%d" % i, [128, TC], F32, s4) for i in range(2)]
                    TTf = lambda o, a_, b_, op: (lambda e: e.tensor_tensor(out=o, in0=a_, in1=b_, op=op))
                    NCH = NT // TC
                    NPAIR = NCH * 16

                    def ctx(i):
                        c, gp = divmod(i, 16)
                        return dict(c=c, gp=gp, ct=gp // 4, b=i % NBUF, pbk=i % NBUF, tk=slice(c * TC, (c + 1) * TC),
                                    s_=(c * TC) // SB, hb=(c * 4 + gp // 4) % 2)

                    def st1(i):
                        q = ctx(i); gp, ct, b, pbk, tk, s_ = q["gp"], q["ct"], q["b"], q["pbk"], q["tk"], q["s_"]
                        PA, PB = ps[pbk][:, 0:TC], ps[pbk][:, TC:2 * TC]
                        K = lambda n_: (n_, b)
                        X = lambda n_: T[(n_, b)][:]
                        P.pe(lambda e: e.matmul(PA, lhsT=BqPad[:, gp, 0, :], rhs=uT[:, ct, tk], start=True, stop=True),
                             r=["BqPad", ("uT", ct, s_)], w=[psk(pbk)])
                        P.pe(lambda e: e.matmul(PB, lhsT=BqPad[:, gp, 1, :], rhs=uT[:, ct, tk], start=True, stop=True),
                             r=["BqPad", ("uT", ct, s_)], w=[psk(pbk)])
                        P.dve(TTf(X("t1"), PA, Ec[:, gp, :], ALU.mult), r=[psk(pbk), "Ec"], w=[K("t1")])
                        P.dve(TTf(X("t2"), PB, Es[:, gp, :], ALU.mult), r=[psk(pbk), "Es"], w=[K("t2")])
                        P.dve(TTf(X("t3"), PB, Ec[:, gp, :], ALU.mult), r=[psk(pbk), "Ec"], w=[K("t3")])
                        P.dve(TTf(X("t4"), PA, Es[:, gp, :], ALU.mult), r=[psk(pbk), "Es"], w=[K("t4")])
                        P.pool(TTf(X("wre"), X("t1"), X("t2"), ALU.add), r=[K("t1"), K("t2")], w=[K("wre")])
                        P.pool(TTf(X("wim"), X("t3"), X("t4"), ALU.subtract), r=[K("t3"), K("t4")], w=[K("wim")])

                    def st2(i):
                        q = ctx(i); gp, b, c = q["gp"], q["b"], q["c"]
                        K = lambda n_: (n_, b)
                        X = lambda n_: T[(n_, b)][:]
                        gre_, gim_, wre_, wim_ = X("gre"), X("gim"), X("wre"), X("wim")
                        P.dve(lambda e: e.tensor_tensor_scan(out=gre_, data0=mag[:, gp:gp + 1].to_broadcast([128, TC]), data1=wre_,
                                                             initial=carry[:, 0, gp:gp + 1], op0=ALU.mult, op1=ALU.add),
                              r=[K("wre"), "carry", "mag"], w=[K("gre")])
                        P.dve(lambda e: e.tensor_tensor_scan(out=gim_, data0=mag[:, gp:gp + 1].to_broadcast([128, TC]), data1=wim_,
                                                             initial=carry[:, 1, gp:gp + 1], op0=ALU.mult, op1=ALU.add),
                              r=[K("wim"), "carry", "mag"], w=[K("gim")])
                        P.act(lambda e: e.activation(out=glast[:, 0, gp:gp + 1], in_=gre_[:, TC - 1:TC], func=AF.Copy),
                              r=[K("gre")], w=[("glast", 0, gp)])
                        P.act(lambda e: e.activation(out=glast[:, 1, gp:gp + 1], in_=gim_[:, TC - 1:TC], func=AF.Copy),
                              r=[K("gim")], w=[("glast", 1, gp)])
                        if gp == 15:
                            gk = [("glast", ri, g_) for ri in range(2) for g_ in range(16)]
                            P.dve(TTf(ctmp[:, 0, :], glast[:, 0, :], cosN[:], ALU.mult), r=gk + ["cosN"], w=["ctmp0"])
                            P.dve(TTf(ctmp[:, 1, :], glast[:, 1, :], sinN[:], ALU.mult), r=gk + ["sinN"], w=["ctmp1"])
                            P.dve(TTf(ctmp[:, 2, :], glast[:, 0, :], sinN[:], ALU.mult), r=gk + ["sinN"], w=["ctmp2"])
                            P.dve(TTf(ctmp[:, 3, :], glast[:, 1, :], cosN[:], ALU.mult), r=gk + ["cosN"], w=["ctmp3"])
                            P.dve(TTf(carry[:, 0, :], ctmp[:, 0, :], ctmp[:, 1, :], ALU.subtract), r=["ctmp0", "ctmp1"], w=["carry"])
                            P.dve(TTf(carry[:, 1, :], ctmp[:, 2, :], ctmp[:, 3, :], ALU.add), r=["ctmp2", "ctmp3"], w=["carry"])

                    def st3(i):
                        q = ctx(i); gp, ct, b, tk, s_, hb = q["gp"], q["ct"], q["b"], q["tk"], q["s_"], q["hb"]
                        K = lambda n_: (n_, b)
                        X = lambda n_: T[(n_, b)][:]
                        P.pool(TTf(X("d1"), X("gre"), Ec[:, gp, :], ALU.mult), r=[K("gre"), "Ec"], w=[K("d1")])
                        P.pool(TTf(X("d2"), X("gim"), Es[:, gp, :], ALU.mult), r=[K("gim"), "Es"], w=[K("d2")])
                        P.pool(TTf(Hh[hb][:, gp % 4, 0, :], X("d1"), X("d2"), ALU.subtract), r=[K("d1"), K("d2")], w=[("Hh", hb, gp % 4, 0)])
                        P.dve(TTf(X("d3"), X("gre"), Es[:, gp, :], ALU.mult), r=[K("gre"), "Es"], w=[K("d3")])
                        P.dve(TTf(X("d4"), X("gim"), Ec[:, gp, :], ALU.mult), r=[K("gim"), "Ec"], w=[K("d4")])
                        P.dve(TTf(Hh[hb][:, gp % 4, 1, :], X("d3"), X("d4"), ALU.add), r=[K("d3"), K("d4")], w=[("Hh", hb, gp % 4, 1)])
                        if gp % 4 == 3:
                            for k in range(8):
                                gl, ri = k // 2, k % 2
                                P.pe(lambda e, gl=gl, ri=ri, k=k: e.matmul(
                                    ps[4 + hb][:, 0:TC], lhsT=CPad[:, ct * 4 + gl, ri, :], rhs=Hh[hb][:, gl, ri, :],
                                    start=(k == 0), stop=(k == 7)),
                                    r=["CPad", ("Hh", hb, gl, ri)], w=[psk(4 + hb)])
                            P.dve(lambda e: e.scalar_tensor_tensor(
                                out=ya[hb][:], in0=uT[:, ct, tk], scalar=dsk[:, ct:ct + 1], in1=ps[4 + hb][:, 0:TC],
                                op0=ALU.mult, op1=ALU.add), r=[psk(4 + hb), ("uT", ct, s_), "dsk"], w=[("ya", hb)])
                            P.act(lambda e: e.activation(out=ga[:, ct, tk], in_=ya[hb][:], func=AF.Gelu_apprx_tanh),
                                  r=[("ya", hb)], w=[("ga", ct, s_)])

                    for i in range(NPAIR + 2):
                        if i < NPAIR:
                            st1(i)
                        if 0 <= i - 1 < NPAIR:
                            st2(i - 1)
                        if 0 <= i - 2 < NPAIR:
                            st3(i - 2)
                if full:
                    tap("uT", uT[:], [("uT", c_, s_) for c_ in range(4) for s_ in range(NSB)], BF16)
                    tap("cvT", cvT[:], [("cvT", c_, s_) for c_ in range(4) for s_ in range(NSB)], BF16)
                    tap("ga", ga[:], [("ga", c_, s_) for c_ in range(4) for s_ in range(NSB)], BF16)
                    tap("carry", carry[:], ["carry"])
                if not full:
                    return
                P.barrier(dummy[:])
                with contextlib.ExitStack() as s4:
                    unC = sb("un", [128, 8, SB], BF16, s4)
                    sqC = [sb("sq%d" % i, [128, SB], BF16, s4) for i in range(2)]
                    sdC = sb("sd", [128, SB], F32, s4)
                    rstdC = sb("rstd", [128, SB], F32, s4)
                    wgl = [WS[:, i * 3584: i * 3584 + 1024].rearrange("p (k j n) -> p k j n", k=4, j=2) for i in range(2)]
                    wco = [WS[:, i * 3584 + 1024: i * 3584 + 1536].rearrange("p (k n) -> p k n", k=4) for i in range(2)]
                    wga = [WS[:, i * 3584 + 1536: i * 3584 + 3584].rearrange("p (k j n) -> p k j n", k=8, j=2) for i in range(2)]
                    glu_v = W["ssm_w_glu"].rearrange("(kc p) n -> p kc n", p=128)
                    co_v = W["conv_w_out"].rearrange("(kc p) n -> p kc n", p=128)
                    win_v = W["w_in"].rearrange("(kc p) n -> p kc n", p=128)
                    wo = WS[:, SLOT:SLOT + 8192].rearrange("p (k n) -> p k n", k=8)
                    mixed = sb("mixed", [128, 8, SB], BF16, s4)
                    ysv = sb("ysv", [128, SB], F32, s4)
                    g1 = sb("g1", [128, SB], F32, s4)
                    g2 = sb("g2", [128, SB], F32, s4)
                    P.dma("pool", lambda e: e.dma_start(out=wo, in_=rows3(W["w_o"])), w=[("WS", 1, 0), ("WS", 1, 1), ("WS", 1, 2)], epoch=False)
                    P.op("pool", lambda e: e.memset(dummy2[:], 0.0), (), [("WS", 0, 0), ("WS", 0, 1), ("WS", 0, 2)], epoch=False)
                    for s_ in range(NSB):
                        tok = slice(s_ * SB, (s_ + 1) * SB)
                        norm_sb(s_, 1, lambda dt_: unC[:, dt_, :], sqC, sdC, rstdC, lambda dt_: ("un", dt_))
                        for dt_ in range(8):
                            ws_ = (s_ * 8 + dt_) % 2
                            P.dma("sp", lambda e, dt_=dt_, ws_=ws_: e.dma_start(
                                out=WS[:, ws_ * 3584:(ws_ + 1) * 3584], in_=cw_scr[dt_, :, :]),
                                r=[("WS", 0, 0), ("WS", 0, 1), ("WS", 0, 2), ("cwscr", dt_, "co")] + [("cwscr", dt_, a_, j_) for a_ in ("gl", "ga") for j_ in range(2)],
                                w=[("WSC", ws_, "gl", 0), ("WSC", ws_, "gl", 1), ("WSC", ws_, "ga", 0), ("WSC", ws_, "ga", 1), ("WSC", ws_, "co")], epoch=False)
                            for kc in range(4):
                                P.pe(lambda e, kc=kc, dt_=dt_, tok=tok, ws_=ws_: e.matmul(ps[0][:, 0:SB], lhsT=wgl[ws_][:, kc, 0, :],
                                                                                  rhs=ga[:, kc, tok], start=(kc == 0), stop=(kc == 3)),
                                     r=[("WS", 0, 0), ("WS", 0, 1), ("WS", 0, 2), ("WSC", ws_, "gl", 0), ("ga", kc, s_)], w=[psk(0)])
                            for kc in range(4):
                                P.pe(lambda e, kc=kc, dt_=dt_, tok=tok, ws_=ws_: e.matmul(ps[1][:, 0:SB], lhsT=wgl[ws_][:, kc, 1, :],
                                                                                  rhs=ga[:, kc, tok], start=(kc == 0), stop=(kc == 3)),
                                     r=[("WS", 0, 0), ("WS", 0, 1), ("WS", 0, 2), ("WSC", ws_, "gl", 1), ("ga", kc, s_)], w=[psk(1)])
                            for kc in range(4):
                                P.pe(lambda e, kc=kc, dt_=dt_, tok=tok, ws_=ws_: e.matmul(ps[2][:, 0:SB], lhsT=wco[ws_][:, kc, :],
                                                                                  rhs=cvT[:, kc, tok], start=(kc == 0), stop=(kc == 3)),
                                     r=[("WS", 0, 0), ("WS", 0, 1), ("WS", 0, 2), ("WSC", ws_, "co"), ("cvT", kc, s_)], w=[psk(2)])
                            for j in range(2):
                                for kc in range(8):
                                    P.pe(lambda e, kc=kc, dt_=dt_, j=j, ws_=ws_: e.matmul(ps[3 + j][:, 0:SB],
                                                                                  lhsT=wga[ws_][:, kc, j, :],
                                                                                  rhs=unC[:, kc, :], start=(kc == 0), stop=(kc == 7)),
                                         r=[("WS", 0, 0), ("WS", 0, 1), ("WS", 0, 2), ("WSC", ws_, "ga", j), ("un", kc)], w=[psk(3 + j)])
                            P.act(lambda e: e.activation(out=g1[:], in_=ps[1][:, 0:SB], func=AF.Sigmoid), r=[psk(1)], w=["g1"])
                            P.dve(lambda e: e.tensor_tensor(out=ysv[:], in0=ps[0][:, 0:SB], in1=g1[:], op=ALU.mult), r=[psk(0), "g1"], w=["ysv"])
                            P.act(lambda e, dt_=dt_: e.activation(out=g1[:], in_=ps[3][:, 0:SB], func=AF.Sigmoid, bias=bgate[:, dt_:dt_ + 1]),
                                  r=[psk(3), "bgate"], w=["g1"])
                            P.act(lambda e, dt_=dt_: e.activation(out=g2[:], in_=ps[4][:, 0:SB], func=AF.Sigmoid, bias=bgate[:, 8 + dt_:9 + dt_]),
                                  r=[psk(4), "bgate"], w=["g2"])
                            P.dve(lambda e: e.tensor_tensor(out=ysv[:], in0=ysv[:], in1=g1[:], op=ALU.mult), r=["ysv", "g1"], w=["ysv"])
                            P.dve(lambda e: e.tensor_tensor(out=g2[:], in0=ps[2][:, 0:SB], in1=g2[:], op=ALU.mult), r=[psk(2), "g2"], w=["g2"])
                            P.dve(lambda e, dt_=dt_: e.tensor_tensor(out=mixed[:, dt_, :], in0=ysv[:], in1=g2[:], op=ALU.add),
                                  r=["ysv", "g2"], w=[("mixed", dt_)])
                        for dt_ in range(8):
                            pb = 5 + (dt_ % 2)
                            for kc in range(8):
                                P.pe(lambda e, kc=kc, dt_=dt_, pb=pb: e.matmul(ps[pb][:, 0:SB], lhsT=wo[:, kc, dt_ * 128:(dt_ + 1) * 128],
                                                                                rhs=mixed[:, kc, :], start=(kc == 0), stop=(kc == 7)),
                                     r=[("WS", 1, 0), ("WS", 1, 1), ("WS", 1, 2), ("mixed", kc)], w=[psk(pb)])
                            P.dve(lambda e, dt_=dt_, pb=pb, tok=tok: e.tensor_tensor(out=hT[:, dt_, tok], in0=ps[pb][:, 0:SB], in1=hT[:, dt_, tok], op=ALU.add),
                                  r=[psk(pb), ("hT", dt_, s_)], w=[("hT", dt_, s_)])
            tap("hT_mix", hT[:], HTK)
            ffn(blk, 2, W["ffn2_w_gate"], W["ffn2_w_up"], W["ffn2_w_down"])
            tap("hT_ffn2", hT[:], HTK)
            P.barrier(dummy[:])
            with contextlib.ExitStack() as s3:
                sqF = [sb("sq%d" % i, [128, SB], BF16, s3) for i in range(2)]
                sdF = sb("sd", [128, SB], F32, s3)
                rstdF = sb("rstd", [128, SB], F32, s3)
                ost = [sb("ost%d" % i, [128, D], F32, s3) for i in range(2)]
                for s_ in range(NSB):
                    norm_sb(s_, 3, lambda dt_, s_=s_: hT[:, dt_, s_ * SB:(s_ + 1) * SB], sqF, sdF, rstdF,
                            lambda dt_, s_=s_: ("hT", dt_, s_))
                for i in range(16):
                    t0 = 16 + i * 128
                    ob = i % 2
                    sbs = sorted(set([t0 // SB, (t0 + 127) // SB]))
                    for h in range(2):
                        bank = 6 + h
                        for q in range(4):
                            dt_ = h * 4 + q
                            P.pe(lambda e, dt_=dt_, q=q, t0=t0, bank=bank: e.transpose(
                                out=ps[bank][:, q * 128:(q + 1) * 128], in_=hT[:, dt_, t0:t0 + 128], identity=ident[:]),
                                r=[("hT", dt_, s_) for s_ in sbs] + ["ident"], w=[psk(bank)])
                        if h == 0:
                            P.act(lambda e, ob=ob, h=h, bank=bank: e.activation(out=ost[ob][:, h * 512:(h + 1) * 512], in_=ps[bank][:, :], func=AF.Copy),
                                  r=[psk(bank)], w=[("ost", ob, h)])
                        else:
                            P.dve(lambda e, ob=ob, h=h, bank=bank: e.tensor_copy(out=ost[ob][:, h * 512:(h + 1) * 512], in_=ps[bank][:, :]),
                                  r=[psk(bank)], w=[("ost", ob, h)])
                    P.dma("sp", lambda e, i=i, ob=ob: e.dma_start(out=out[i * 128:(i + 1) * 128, :], in_=ost[ob][:]),
                          r=[("ost", ob, 0), ("ost", ob, 1)], w=[("out", i)])
        for blk in range(NBLK):
            do_block(blk)
        P.emit()
    return nc


_NC_CACHE = {}


def kernel(**inputs):
    x = np.asarray(inputs["x"], dtype=np.float32)
    meta = np.asarray(inputs["meta_tokens"], dtype=np.float32)
    B, S, _ = x.shape
    n = 8
    if "nc" not in _NC_CACHE:
        _NC_CACHE["nc"] = build_nc()
    nc = _NC_CACHE["nc"]
    f = lambda k: np.ascontiguousarray(np.asarray(inputs[k], dtype=np.float32))
    shared = {
        "g_ffn1": f("g_ffn1")[0], "ffn1_w_gate": f("ffn1_w_gate")[0], "ffn1_w_up": f("ffn1_w_up")[0],
        "ffn1_w_down": f("ffn1_w_down")[0], "g_mix": f("g_mix")[0], "w_in": f("w_in")[0], "b_gate": f("b_gate")[0],
        "ssm_a_re": f("ssm_a_re")[0], "ssm_a_im": f("ssm_a_im")[0], "ssm_log_dt": f("ssm_log_dt")[0],
        "ssm_b_re": f("ssm_b_re")[0], "ssm_b_im": f("ssm_b_im")[0],
        "ssm_c_re": f("ssm_c_re")[0].reshape(512, 64), "ssm_c_im": f("ssm_c_im")[0].reshape(512, 64),
        "ssm_d": f("ssm_d")[0], "ssm_w_glu": f("ssm_w_glu")[0], "conv_w": f("conv_w")[0].reshape(3, 512),
        "conv_w_out": f("conv_w_out")[0], "w_o": f("w_o")[0], "g_ffn2": f("g_ffn2")[0],
        "ffn2_w_gate": f("ffn2_w_gate")[0], "ffn2_w_up": f("ffn2_w_up")[0], "ffn2_w_down": f("ffn2_w_down")[0],
        "g_final": f("g_final"),
    }
    in_maps = []
    total = NBLK * NT
    for c in range(n):
        b, q = c // 4, c % 4
        seq = np.concatenate([meta, x[b, : 2048 * (q + 1)]], axis=0)
        stream = np.zeros((total, D), np.float32)
        stream[total - seq.shape[0]:] = seq
        m = dict(shared)
        m["xs"] = stream
        in_maps.append(m)
    res = run_bass_kernel_spmd(nc, in_maps, core_ids=list(range(n)))
    outp = np.zeros((B, S, D), np.float32)
    for c in range(n):
        b, q = c // 4, c % 4
        outp[b, q * 2048:(q + 1) * 2048] = np.asarray(res.results[c]["out"], dtype=np.float32)
    return outp
```

```python
import contextlib
import math
import numpy as np
import concourse.bass as bass
import concourse.mybir as mybir
from concourse.bass_utils import run_bass_kernel_spmd

F32 = mybir.dt.float32
BF16 = mybir.dt.bfloat16
I32 = mybir.dt.int32
AF = mybir.ActivationFunctionType
ALU = mybir.AluOpType

D = 1024
DFF = 2816
NT = 2064
NBLK = 4
NSB = 6
SB = 344
TC = 172
NF = 22
GROUPS = [(0, 3), (3, 3), (6, 3), (9, 3), (12, 3), (15, 3), (18, 2), (20, 2)]
EPS = 1e-6
ENGS = ("pe", "act", "dve", "pool", "sp")
PI = math.pi


class Prog:
    def __init__(self, nc, dma_ring=8):
        self.nc = nc
        self.ops = []
        self.dma_ring = dma_ring

    def op(self, eng, fn, reads=(), writes=(), dma=False, epoch=True):
        self.ops.append(dict(eng=eng, fn=fn, reads=tuple(reads) + (("EPOCH",) if epoch else ()), writes=tuple(writes), dma=dma))

    def barrier(self, dummy):
        self.ops.append(dict(eng="dve", fn=lambda e: e.memset(dummy, 0.0), reads=(), writes=("EPOCH",), dma=False))

    def pe(self, fn, r=(), w=()):
        self.op("pe", fn, r, w)

    def act(self, fn, r=(), w=()):
        self.op("act", fn, r, w)

    def dve(self, fn, r=(), w=()):
        self.op("dve", fn, r, w)

    def pool(self, fn, r=(), w=()):
        self.op("pool", fn, r, w)

    def dma(self, q, fn, r=(), w=(), epoch=True):
        self.op(q, fn, r, w, dma=True, epoch=epoch)

    def emit(self):
        nc = self.nc
        ops = self.ops
        last_writer = {}
        readers = {}
        for i, o in enumerate(ops):
            deps = set()
            for r in o["reads"]:
                if r in last_writer:
                    deps.add(last_writer[r])
            for w in o["writes"]:
                if w in last_writer:
                    deps.add(last_writer[w])
                deps.update(readers.get(w, ()))
            deps.discard(i)
            o["deps"] = deps
            for r in o["reads"]:
                readers.setdefault(r, []).append(i)
            for w in o["writes"]:
                last_writer[w] = i
                readers[w] = []
        eidx = {e: 0 for e in ENGS}
        dcount = {e: 0 for e in ENGS}
        for o in ops:
            o["eidx"] = eidx[o["eng"]]
            eidx[o["eng"]] += 1
            o["marked"] = False
            if o["dma"]:
                o["dma_n"] = dcount[o["eng"]]
                dcount[o["eng"]] += 1
        R_ = self.dma_ring
        for i, o in enumerate(ops):
            E = o["eng"]
            need = []
            best = {}
            bestd = {}
            for j in o["deps"]:
                p = ops[j]
                if p["dma"]:
                    k = (p["eng"], p["dma_n"] % R_)
                    if k not in bestd or ops[bestd[k]]["dma_n"] < p["dma_n"]:
                        bestd[k] = j
                else:
                    k = p["eng"]
                    if k not in best or ops[best[k]]["eidx"] < p["eidx"]:
                        best[k] = j
            need.extend(bestd.values())
            for k, j in best.items():
                p = ops[j]
                if k != E:
                    need.append(j)
                    p["marked"] = True
                else:
                    if E == "pe" and not o["dma"]:
                        continue
                    need.append(j)
                    p["marked"] = True
            o["need"] = need
        cnt = {e: 0 for e in ENGS}
        for o in ops:
            if o["marked"]:
                cnt[o["eng"]] += 1
                o["cnt"] = cnt[o["eng"]]
        R = self.dma_ring
        with contextlib.ExitStack() as st:
            esem = {e: st.enter_context(nc.semaphore("s_" + e)) for e in ENGS}
            dsem = {e: [st.enter_context(nc.semaphore("d_%s_%d" % (e, k))) for k in range(R)]
                    for e in ("sp", "act", "pool") if dcount[e] > 0}
            block = st.enter_context(nc.Block())

            def run_engine(E, eng):
                waited = {}

                def wait(sem, val):
                    key = id(sem)
                    if waited.get(key, 0) >= val:
                        return
                    waited[key] = val
                    eng.wait_ge(sem, val)

                for o in ops:
                    if o["eng"] != E:
                        continue
                    for j in o["need"]:
                        p = ops[j]
                        if p["dma"]:
                            n = p["dma_n"]
                            wait(dsem[p["eng"]][n % R], 16 * (n // R + 1))
                        else:
                            wait(esem[p["eng"]], p["cnt"])
                    if o["dma"]:
                        n = o["dma_n"]
                        s = dsem[E][n % R]
                        if n // R > 0:
                            wait(s, 16 * (n // R))
                        o["fn"](eng).then_inc(s, 16)
                    else:
                        ins = o["fn"](eng)
                        if o["marked"]:
                            ins.then_inc(esem[E], 1)
                if E in dsem:
                    tot = dcount[E]
                    for k in range(R):
                        uses = (tot - k + R - 1) // R if tot > k else 0
                        if uses > 0:
                            wait(dsem[E][k], 16 * uses)

            @block.tensor
            def _(eng):
                run_engine("pe", eng)

            @block.scalar
            def _(eng):
                run_engine("act", eng)

            @block.vector
            def _(eng):
                run_engine("dve", eng)

            @block.gpsimd
            def _(eng):
                run_engine("pool", eng)

            @block.sync
            def _(eng):
                run_engine("sp", eng)


DEBUG = False


def build_nc():
    nc = bass.Bass("TRN2", target_bir_lowering=False)
    dr = lambda name, shape, kind="ExternalInput", dt=F32: nc.dram_tensor(name, shape, dt, kind=kind).ap()
    xs = dr("xs", [NBLK * NT, D])
    out = dr("out", [2048, D], kind="ExternalOutput")
    W = {}
    for nm, shp in [("g_ffn1", [D]), ("ffn1_w_gate", [D, DFF]), ("ffn1_w_up", [D, DFF]), ("ffn1_w_down", [DFF, D]),
                    ("g_mix", [D]), ("w_in", [D, 4096]), ("b_gate", [2048]), ("ssm_a_re", [32, 64]),
                    ("ssm_a_im", [32, 64]), ("ssm_log_dt", [32]), ("ssm_b_re", [32, 64, 16]),
                    ("ssm_b_im", [32, 64, 16]), ("ssm_c_re", [512, 64]), ("ssm_c_im", [512, 64]),
                    ("ssm_d", [512]), ("ssm_w_glu", [512, 2048]), ("conv_w", [3, 512]),
                    ("conv_w_out", [512, D]), ("w_o", [D, D]), ("g_ffn2", [D]), ("ffn2_w_gate", [D, DFF]),
                    ("ffn2_w_up", [D, DFF]), ("ffn2_w_down", [DFF, D]), ("g_final", [D])]:
        W[nm] = dr(nm, shp)

    P = Prog(nc)
    nc_allow = nc.allow_non_contiguous_dma(reason="small parameter loads")
    with contextlib.ExitStack() as st:
        st.enter_context(nc_allow)

        uid = [0]

        def sb(name, shape, dt, stack=st):
            uid[0] += 1
            return stack.enter_context(nc.sbuf_tensor("%s_%d" % (name, uid[0]), shape, dt))

        ps = [st.enter_context(nc.psum_tensor("ps%d" % i, [128, 512], F32)) for i in range(8)]

        def tap(name, ap, keys, dt=F32):
            if not DEBUG:
                return
            shape = list(ap.shape)
            d = nc.dram_tensor("dbg_" + name, shape, dt, kind="ExternalOutput").ap()
            P.dma("sp", lambda e: e.dma_start(out=d, in_=ap), r=keys, w=[("dbg", name)])

        HTK = [("hT", dt_, s_) for dt_ in range(8) for s_ in range(NSB)]
        psk = lambda i: ("ps", i)

        hT = sb("hT", [128, 8, NT], F32)
        ident = sb("ident", [128, 128], F32)
        dummy = sb("dummy", [128, 1], F32)
        dummy2 = sb("dummy2", [128, 1], F32)
        SLOT = 9216
        WS = sb("WS", [128, 2 * SLOT], BF16)
        wsg = [WS[:, i * SLOT: i * SLOT + 3072].rearrange("p (k n) -> p k n", k=8) for i in range(2)]
        wsu = [WS[:, i * SLOT + 3072: i * SLOT + 6144].rearrange("p (k n) -> p k n", k=8) for i in range(2)]
        wsd = [WS[:, i * SLOT + 6144: i * SLOT + 9216].rearrange("p (f n) -> p f n", f=3) for i in range(2)]
        ALLWS = [("WS", i, j) for i in range(2) for j in range(3)]
        uT = sb("uT", [128, 4, NT], BF16)
        junk = [sb("junk%d" % i, [128, TC], F32) for i in range(2)]
        acc = sb("acc", [128, 4, 16], F32)
        ctm = sb("ctm", [128, 6, 16], F32)
        rows3 = lambda ap: ap.rearrange("(kc p) n -> p kc n", p=128)
        ones_bf = sb("ones_bf", [128, 128], BF16)
        gains = sb("gains", [128, 4, 8], F32)
        bgate = sb("bgate", [128, 16], F32)
        dsk = sb("dsk", [128, 4], F32)
        cw = sb("cw", [128, 4, 3], F32)
        mag = sb("mag", [128, 16], F32)
        carry = sb("carry", [128, 2, 16], F32)
        glast = sb("glast", [128, 2, 16], F32)
        th = sb("th", [128, 16], F32)
        adr = sb("adr", [128, 16], F32)
        lamN = sb("lamN", [128, 2, 16], F32)
        lamP = sb("lamP", [128, 2, 16], F32)
        rotP = sb("rotP", [128, 2, 16], F32)
        cosN = sb("cosN", [128, 16], F32)
        sinN = sb("sinN", [128, 16], F32)
        Ec = sb("Ec", [128, 16, TC], F32)
        Es = sb("Es", [128, 16, TC], F32)
        BqPad = sb("BqPad", [128, 16, 2, 128], BF16)
        CPad = sb("CPad", [128, 16, 2, 128], BF16)

        P.pool(lambda e: e.memset(ident[:], 0.0), w=["ident"])
        P.pool(lambda e: e.affine_select(out=ident[:], in_=ident[:], compare_op=ALU.not_equal, fill=1.0,
                                         base=0, pattern=[[-1, 128]], channel_multiplier=1), r=["ident"], w=["ident"])
        P.dve(lambda e: e.memset(ones_bf[:], 1.0), w=["ones"])
        for i, nm in enumerate(["g_ffn1", "g_mix", "g_ffn2", "g_final"]):
            P.dma("sp", lambda e, i=i, nm=nm: e.dma_start(out=gains[:, i, :], in_=W[nm].rearrange("(t p) -> p t", p=128)),
                  w=[("gains", i)])
        P.dma("sp", lambda e: e.dma_start(out=bgate[:], in_=W["b_gate"].rearrange("(t p) -> p t", p=128)), w=["bgate"])
        P.dma("sp", lambda e: e.dma_start(out=dsk[:], in_=W["ssm_d"].rearrange("(t p) -> p t", p=128)), w=["dsk"])
        for j in range(3):
            P.dma("sp", lambda e, j=j: e.dma_start(out=cw[:, :, j], in_=W["conv_w"][j, :].rearrange("(t p) -> p t", p=128)), w=["cw"])

        with contextlib.ExitStack() as s2:
            t = lambda name, shape, dt=F32: sb(name, shape, dt, s2)
            are = t("are", [128, 16]); aim = t("aim", [128, 16]); ldt = t("ldt", [128, 16])
            bre = t("bre", [128, 16, 16]); bim = t("bim", [128, 16, 16])
            cn = t("cn", [128, 2, 4, 2, 64])
            CP = t("CP", [128, 2, 16, 16])
            dtv = t("dtv", [128, 16])
            cs = t("cs", [128, 16]); sn = t("sn", [128, 16])
            lre = t("lre", [128, 16]); lim = t("lim", [128, 16])
            den = t("den", [128, 16]); tmpa = t("tmpa", [128, 16]); tmpb = t("tmpb", [128, 16])
            qre = t("qre", [128, 16]); qim = t("qim", [128, 16]); lm1 = t("lm1", [128, 16])
            bq = t("bq", [128, 2, 16, 16]); tb = t("tb", [128, 16])
            angN = t("angN", [128, 16])
            wkn1 = t("wkn1", [128, 16]); wkn2 = t("wkn2", [128, 16]); wkni = t("wkni", [128, 16], I32)

            pair_view = lambda ap: ap.rearrange("(gp two) p -> (two p) gp", two=2)
            P.dma("sp", lambda e: e.dma_start(out=are[:], in_=pair_view(W["ssm_a_re"])), w=["are"])
            P.dma("sp", lambda e: e.dma_start(out=aim[:], in_=pair_view(W["ssm_a_im"])), w=["aim"])
            for gpar in range(2):
                P.dma("sp", lambda e, gpar=gpar: e.dma_start(
                    out=ldt[gpar * 64:(gpar + 1) * 64, :],
                    in_=W["ssm_log_dt"].rearrange("(g two) -> two g", two=2)[gpar:gpar + 1, :].to_broadcast([64, 16])),
                    w=[("ldt", gpar)])
            bview = lambda ap: ap.rearrange("(gp two) p c -> (two p) gp c", two=2)
            P.dma("sp", lambda e: e.dma_start(out=bre[:], in_=bview(W["ssm_b_re"])), w=["bre"])
            P.dma("sp", lambda e: e.dma_start(out=bim[:], in_=bview(W["ssm_b_im"])), w=["bim"])
            for ri, nm in enumerate(["ssm_c_re", "ssm_c_im"]):
                for dup in range(2):
                    P.dma("sp", lambda e, ri=ri, nm=nm, dup=dup: e.dma_start(
                        out=cn[:, ri, :, dup, :], in_=W[nm].rearrange("(t p) s -> p t s", p=128)), w=[("cn", ri, dup)])
            for ri in range(2):
                for ct in range(4):
                    bank = 6 + (ct % 2)
                    P.pe(lambda e, ri=ri, ct=ct, bank=bank: e.transpose(
                        out=ps[bank][:, 0:128], in_=cn[:, ri, ct, :, :].rearrange("p a b -> p (a b)"), identity=ident[:]),
                        r=[("cn", ri, 0), ("cn", ri, 1), "ident"], w=[psk(bank)])
                    for gpar in range(2):
                        P.act(lambda e, ri=ri, ct=ct, bank=bank, gpar=gpar: e.activation(
                            out=CP[gpar * 64:(gpar + 1) * 64, ri, ct * 4:(ct + 1) * 4, :],
                            in_=ps[bank][gpar * 64:(gpar + 1) * 64, 0:128].rearrange(
                                "p (gl two c) -> p gl two c", two=2, c=16)[:, :, gpar, :],
                            func=AF.Identity, scale=(1.0 if ri == 0 else -1.0)),
                            r=[psk(bank)], w=[("CP", ri, ct, gpar)])
            cp_keys = [("CP", ri, ct, gpar) for ri in range(2) for ct in range(4) for gpar in range(2)]
            ldk = [("ldt", 0), ("ldt", 1)]
            P.act(lambda e: e.activation(out=dtv[:], in_=ldt[:], func=AF.Exp), r=ldk, w=["dtv"])
            P.dve(lambda e: e.tensor_tensor(out=adr[:], in0=are[:], in1=dtv[:], op=ALU.mult), r=["are", "dtv"], w=["adr"])
            P.dve(lambda e: e.tensor_tensor(out=th[:], in0=aim[:], in1=dtv[:], op=ALU.mult), r=["aim", "dtv"], w=["th"])
            P.act(lambda e: e.activation(out=mag[:], in_=adr[:], func=AF.Exp), r=["adr"], w=["mag"])

            def sincos(x, xk, o_sin, o_cos, w1, w2, wi, keys):
                for off, o in ((0.0, o_sin), (PI / 2, o_cos)):
                    ok = keys[0] if o is o_sin else keys[1]
                    P.dve(lambda e, off=off: e.tensor_scalar(out=w1, in0=x, scalar1=off, scalar2=1.0 / (2 * PI),
                                                             op0=ALU.add, op1=ALU.mult), r=[xk], w=["w1" + keys[2]])
                    P.dve(lambda e: e.tensor_copy(out=wi, in_=w1), r=["w1" + keys[2]], w=["wi" + keys[2]])
                    P.dve(lambda e: e.tensor_copy(out=w1, in_=wi), r=["wi" + keys[2]], w=["w1" + keys[2]])
                    P.dve(lambda e: e.scalar_tensor_tensor(out=w2, in0=w1, scalar=-6.25, in1=x,
                                                           op0=ALU.mult, op1=ALU.add), r=["w1" + keys[2], xk], w=["w2" + keys[2]])
                    P.dve(lambda e: e.scalar_tensor_tensor(out=w2, in0=w1, scalar=-(2 * PI - 6.25), in1=w2,
                                                           op0=ALU.mult, op1=ALU.add), r=["w1" + keys[2], "w2" + keys[2]], w=["w2" + keys[2]])
                    P.dve(lambda e, off=off: e.tensor_scalar(out=w2, in0=w2, scalar1=off, scalar2=None, op0=ALU.add),
                          r=["w2" + keys[2]], w=["w2" + keys[2]])
                    for lim_, sgn in ((PI, -1.0), (-PI, 1.0)):
                        cmp = ALU.is_gt if sgn < 0 else ALU.is_lt
                        P.dve(lambda e, lim_=lim_, cmp=cmp: e.tensor_scalar(out=w1, in0=w2, scalar1=lim_, scalar2=None, op0=cmp),
                              r=["w2" + keys[2]], w=["w1" + keys[2]])
                        P.dve(lambda e, sgn=sgn: e.scalar_tensor_tensor(out=w2, in0=w1, scalar=sgn * 2 * PI, in1=w2,
                                                                        op0=ALU.mult, op1=ALU.add),
                              r=["w1" + keys[2], "w2" + keys[2]], w=["w2" + keys[2]])
                    P.act(lambda e, o=o: e.activation(out=o, in_=w2, func=AF.Sin), r=["w2" + keys[2]], w=[ok])

            HPB = sb("halfpi", [128, 1], F32, s2)
            P.dve(lambda e: e.memset(HPB[:], PI / 2), w=["halfpi"])
            P.act(lambda e: e.activation(out=sn[:], in_=th[:], func=AF.Sin, scale=1.0 / 16), r=["th"], w=["sn"])
            P.act(lambda e: e.activation(out=cs[:], in_=th[:], func=AF.Sin, scale=1.0 / 16, bias=HPB[:]), r=["th", "halfpi"], w=["cs"])
            for _ in range(4):
                P.dve(lambda e: e.tensor_tensor(out=wkn1[:], in0=cs[:], in1=cs[:], op=ALU.mult), r=["cs"], w=["wkn1"])
                P.dve(lambda e: e.tensor_tensor(out=wkn2[:], in0=sn[:], in1=sn[:], op=ALU.mult), r=["sn"], w=["wkn2"])
                P.dve(lambda e: e.tensor_tensor(out=sn[:], in0=sn[:], in1=cs[:], op=ALU.mult), r=["sn", "cs", "wkn2"], w=["sn"])
                P.dve(lambda e: e.tensor_tensor(out=cs[:], in0=wkn1[:], in1=wkn2[:], op=ALU.subtract), r=["wkn1", "wkn2", "sn"], w=["cs"])
                P.dve(lambda e: e.tensor_scalar(out=sn[:], in0=sn[:], scalar1=2.0, scalar2=None, op0=ALU.mult), r=["sn"], w=["sn"])
            P.dve(lambda e: e.tensor_copy(out=rotP[:, 0, :], in_=cs[:]), r=["cs"], w=["rotP"])
            P.dve(lambda e: e.tensor_copy(out=rotP[:, 1, :], in_=sn[:]), r=["sn"], w=["rotP"])
            P.dve(lambda e: e.tensor_tensor(out=lre[:], in0=mag[:], in1=cs[:], op=ALU.mult), r=["mag", "cs"], w=["lre"])
            P.dve(lambda e: e.tensor_tensor(out=lim[:], in0=mag[:], in1=sn[:], op=ALU.mult), r=["mag", "sn"], w=["lim"])
            P.dve(lambda e: e.tensor_copy(out=lamP[:, 0, :], in_=lre[:]), r=["lre"], w=["lamP"])
            P.dve(lambda e: e.tensor_copy(out=lamP[:, 1, :], in_=lim[:]), r=["lim"], w=["lamP"])
            P.dve(lambda e: e.tensor_tensor(out=den[:], in0=are[:], in1=are[:], op=ALU.mult), r=["are"], w=["den"])
            P.dve(lambda e: e.tensor_tensor(out=tmpa[:], in0=aim[:], in1=aim[:], op=ALU.mult), r=["aim"], w=["tmpa"])
            P.dve(lambda e: e.tensor_tensor(out=den[:], in0=den[:], in1=tmpa[:], op=ALU.add), r=["den", "tmpa"], w=["den"])
            P.dve(lambda e: e.reciprocal(out=den[:], in_=den[:]), r=["den"], w=["den"])
            P.dve(lambda e: e.tensor_scalar(out=lm1[:], in0=lre[:], scalar1=-1.0, scalar2=None, op0=ALU.add), r=["lre"], w=["lm1"])
            P.dve(lambda e: e.tensor_tensor(out=tmpa[:], in0=lm1[:], in1=are[:], op=ALU.mult), r=["lm1", "are", "den"], w=["tmpa"])
            P.dve(lambda e: e.tensor_tensor(out=tmpb[:], in0=lim[:], in1=aim[:], op=ALU.mult), r=["lim", "aim"], w=["tmpb"])
            P.dve(lambda e: e.tensor_tensor(out=tmpa[:], in0=tmpa[:], in1=tmpb[:], op=ALU.add), r=["tmpa", "tmpb"], w=["tmpa"])
            P.dve(lambda e: e.tensor_tensor(out=qre[:], in0=tmpa[:], in1=den[:], op=ALU.mult), r=["tmpa", "den"], w=["qre"])
            P.dve(lambda e: e.tensor_tensor(out=tmpa[:], in0=lim[:], in1=are[:], op=ALU.mult), r=["lim", "are", "qre"], w=["tmpa"])
            P.dve(lambda e: e.tensor_tensor(out=tmpb[:], in0=lm1[:], in1=aim[:], op=ALU.mult), r=["lm1", "aim"], w=["tmpb"])
            P.dve(lambda e: e.tensor_tensor(out=tmpa[:], in0=tmpa[:], in1=tmpb[:], op=ALU.subtract), r=["tmpa", "tmpb"], w=["tmpa"])
            P.dve(lambda e: e.tensor_tensor(out=qim[:], in0=tmpa[:], in1=den[:], op=ALU.mult), r=["tmpa", "den"], w=["qim"])
            qb = lambda q: q[:, :].unsqueeze(2).to_broadcast([128, 16, 16])
            P.dve(lambda e: e.tensor_tensor(out=bq[:, 0], in0=bre[:], in1=qb(qre), op=ALU.mult), r=["bre", "qre"], w=["bq0"])
            P.dve(lambda e: e.tensor_tensor(out=bq[:, 1], in0=bim[:], in1=qb(qim), op=ALU.mult), r=["bim", "qim"], w=["bq1"])
            P.dve(lambda e: e.tensor_tensor(out=bq[:, 0], in0=bq[:, 0], in1=bq[:, 1], op=ALU.subtract), r=["bq0", "bq1"], w=["bq0"])
            P.dve(lambda e: e.tensor_tensor(out=bq[:, 1], in0=bim[:], in1=qb(qre), op=ALU.mult), r=["bim", "qre", "bq0"], w=["bq1"])
            P.dve(lambda e: e.tensor_tensor(out=bre[:], in0=bre[:], in1=qb(qim), op=ALU.mult), r=["bre", "qim", "bq0"], w=["bre"])
            P.dve(lambda e: e.tensor_tensor(out=bq[:, 1], in0=bq[:, 1], in1=bre[:], op=ALU.add), r=["bq1", "bre"], w=["bq1"])
            sz = contextlib.ExitStack()
            sz.__enter__()
            Z = sb("Z", [128, 16, 2, 128], F32, sz)
            P.pool(lambda e: e.memset(Z[:], 0.0), w=["Z"])
            P.pool(lambda e: e.memset(CPad[:], 0.0), w=["CPad"])
            for ri in range(2):
                for gpar in range(2):
                    for gl in range(4):
                        col = (2 * gl + gpar) * 16
                        rows = slice(gpar * 64, (gpar + 1) * 64)
                        P.dve(lambda e, ri=ri, rows=rows, gl=gl, col=col: e.tensor_copy(
                            out=Z[rows, gl::4, ri, col:col + 16], in_=bq[rows, ri, gl::4, :]),
                            r=["bq0", "bq1", "Z"], w=["Z"])
                        P.act(lambda e, ri=ri, rows=rows, gl=gl, col=col: e.activation(
                            out=CPad[rows, gl::4, ri, col:col + 16], in_=CP[rows, ri, gl::4, :], func=AF.Copy),
                            r=cp_keys + ["CPad"], w=["CPad"])
            for gp in range(16):
                for ri in range(2):
                    bank = 6 + ((gp * 2 + ri) % 2)
                    P.pe(lambda e, gp=gp, ri=ri, bank=bank: e.transpose(out=ps[bank][:, 0:128], in_=Z[:, gp, ri, :], identity=ident[:]),
                         r=["Z", "ident"], w=[psk(bank)])
                    P.act(lambda e, gp=gp, ri=ri, bank=bank: e.activation(out=BqPad[:, gp, ri, :], in_=ps[bank][:, 0:128], func=AF.Copy),
                          r=[psk(bank)], w=["BqPad"])
            sz.__exit__(None, None, None)
            P.barrier(dummy[:])
            P.dve(lambda e: e.memset(carry[:], 0.0), w=["carry"])
            tap("mag", mag[:], ["mag"]); tap("lre", lre[:], ["lre"]); tap("lim", lim[:], ["lim"])
            tap("qre", qre[:], ["qre"]); tap("qim", qim[:], ["qim"]); tap("th", th[:], ["th"])
            tap("BqPad", BqPad[:], ["BqPad"], BF16); tap("CPad", CPad[:], ["CPad"], BF16)
            tap("CP", CP[:], cp_keys); tap("bq", bq[:], ["bq0", "bq1"]); tap("ident", ident[:], ["ident"])

        def make_tables(mode):
            P.barrier(dummy[:])
            with contextlib.ExitStack() as sx:
                HB = 128
                pa = sb("pa", [128, 16, HB], F32, sx); pb_ = sb("pb", [128, 16, HB], F32, sx)
                pc = sb("pc", [128, 16, HB], F32, sx); pd = sb("pd", [128, 16, HB], F32, sx)
                Pw = sb("Pw", [128, 2, 16], F32, sx)
                pt = sb("pt", [128, 4, 16], F32, sx)
                base = rotP if mode == "E" else lamP
                TT = lambda o, a_, b_, op: (lambda e: e.tensor_tensor(out=o, in0=a_, in1=b_, op=op))
                P.dve(lambda e: e.tensor_copy(out=Pw[:], in_=base[:]), r=["rotP", "lamP"], w=["Pw"])
                if mode == "E":
                    P.dve(lambda e: e.tensor_copy(out=Ec[:, :, 0], in_=base[:, 0, :]), r=["rotP"], w=["Ec"])
                    P.dve(lambda e: e.tensor_copy(out=Es[:, :, 0], in_=base[:, 1, :]), r=["rotP"], w=["Es"])
                else:
                    P.dve(lambda e: e.memset(Ec[:, :, TC - 1:TC], 1.0), w=["Ec"])
                    P.dve(lambda e: e.memset(Es[:, :, TC - 1:TC], 0.0), w=["Es"])
                f = 1
                while f < TC:
                    n = min(f, TC - f)
                    if mode == "E":
                        src = slice(0, n); dst = slice(f, f + n)
                    else:
                        src = slice(TC - n, TC); dst = slice(TC - f - n, TC - f)
                    bre = Pw[:, 0, :].unsqueeze(2).to_broadcast([128, 16, n])
                    bim = Pw[:, 1, :].unsqueeze(2).to_broadcast([128, 16, n])
                    P.dve(TT(pa[:, :, 0:n], Ec[:, :, src], bre, ALU.mult), r=["Ec", "Pw"], w=["pa"])
                    P.dve(TT(pb_[:, :, 0:n], Es[:, :, src], bim, ALU.mult), r=["Es", "Pw"], w=["pb"])
                    P.dve(TT(pc[:, :, 0:n], Ec[:, :, src], bim, ALU.mult), r=["Ec", "Pw"], w=["pc"])
                    P.dve(TT(pd[:, :, 0:n], Es[:, :, src], bre, ALU.mult), r=["Es", "Pw"], w=["pd"])
                    P.dve(TT(Ec[:, :, dst], pa[:, :, 0:n], pb_[:, :, 0:n], ALU.subtract), r=["pa", "pb", "pc", "pd"], w=["Ec"])
                    P.dve(TT(Es[:, :, dst], pc[:, :, 0:n], pd[:, :, 0:n], ALU.add), r=["pc", "pd", "Ec"], w=["Es"])
                    P.dve(TT(pt[:, 0, :], Pw[:, 0, :], Pw[:, 0, :], ALU.mult), r=["Pw", "Es"], w=["pt0"])
                    P.dve(TT(pt[:, 1, :], Pw[:, 1, :], Pw[:, 1, :], ALU.mult), r=["Pw"], w=["pt1"])
                    P.dve(TT(pt[:, 2, :], Pw[:, 0, :], Pw[:, 1, :], ALU.mult), r=["Pw"], w=["pt2"])
                    P.dve(TT(Pw[:, 0, :], pt[:, 0, :], pt[:, 1, :], ALU.subtract), r=["pt0", "pt1", "pt2"], w=["Pw"])
                    P.dve(lambda e: e.tensor_scalar(out=Pw[:, 1, :], in0=pt[:, 2, :], scalar1=2.0, scalar2=None, op0=ALU.mult),
                          r=["pt2", "Pw"], w=["Pw"])
                    f *= 2
                if mode == "E":
                    P.dve(lambda e: e.tensor_copy(out=cosN[:], in_=Ec[:, :, TC - 1]), r=["Ec"], w=["cosN"])
                    P.dve(lambda e: e.tensor_copy(out=sinN[:], in_=Es[:, :, TC - 1]), r=["Es"], w=["sinN"])
                else:
                    P.dve(TT(pt[:, 0, :], Ec[:, :, 0], lamP[:, 0, :], ALU.mult), r=["Ec", "lamP", "Pw"], w=["pt0"])
                    P.dve(TT(pt[:, 1, :], Es[:, :, 0], lamP[:, 1, :], ALU.mult), r=["Es", "lamP"], w=["pt1"])
                    P.dve(TT(pt[:, 2, :], Ec[:, :, 0], lamP[:, 1, :], ALU.mult), r=["Ec", "lamP"], w=["pt2"])
                    P.dve(TT(pt[:, 3, :], Es[:, :, 0], lamP[:, 0, :], ALU.mult), r=["Es", "lamP"], w=["pt3"])
                    P.dve(TT(lamN[:, 0, :], pt[:, 0, :], pt[:, 1, :], ALU.subtract), r=["pt0", "pt1"], w=["lamN"])
                    P.dve(TT(lamN[:, 1, :], pt[:, 2, :], pt[:, 3, :], ALU.add), r=["pt2", "pt3"], w=["lamN"])
                tap("Ec_" + mode, Ec[:], ["Ec"]); tap("Es_" + mode, Es[:], ["Es"])

        make_tables("D")

        cw_scr = nc.dram_tensor("cw_scr", [8, 128, 3584], BF16).ap()
        glu_v = W["ssm_w_glu"].rearrange("(kc p) n -> p kc n", p=128)
        co_v = W["conv_w_out"].rearrange("(kc p) n -> p kc n", p=128)
        win_v = W["w_in"].rearrange("(kc p) n -> p kc n", p=128)
        for dt_ in range(8):
            for j in range(2):
                P.dma("pool", lambda e, j=j, dt_=dt_: e.dma_start(
                    out=cw_scr[dt_, :, 0:1024].rearrange("p (k j n) -> p k j n", k=4, j=2)[:, :, j, :],
                    in_=glu_v[:, :, j * 1024 + dt_ * 128: j * 1024 + (dt_ + 1) * 128]), w=[("cwscr", dt_, "gl", j)], epoch=False)
                P.dma("pool", lambda e, j=j, dt_=dt_: e.dma_start(
                    out=cw_scr[dt_, :, 1536:3584].rearrange("p (k j n) -> p k j n", k=8, j=2)[:, :, j, :],
                    in_=win_v[:, :, 2048 + j * 1024 + dt_ * 128: 2048 + j * 1024 + (dt_ + 1) * 128]), w=[("cwscr", dt_, "ga", j)], epoch=False)
            P.dma("pool", lambda e, dt_=dt_: e.dma_start(
                out=cw_scr[dt_, :, 1024:1536].rearrange("p (k n) -> p k n", k=4),
                in_=co_v[:, :, dt_ * 128:(dt_ + 1) * 128]), w=[("cwscr", dt_, "co")], epoch=False)

        def norm_sb(s_, gi, xn_ap_fn, sq, sd, rstd, keyfn, pbank=6, tag=0):
            tok = slice(s_ * SB, (s_ + 1) * SB)
            for dt_ in range(8):
                P.act(lambda e, dt_=dt_: e.activation(out=sq[dt_ % 2][:], in_=hT[:, dt_, tok], func=AF.Square),
                      r=[("hT", dt_, s_)], w=[("sq", dt_ % 2)])
                P.pe(lambda e, dt_=dt_: e.matmul(ps[pbank][:, 0:SB], lhsT=ones_bf[:], rhs=sq[dt_ % 2][:],
                                                 start=(dt_ == 0), stop=(dt_ == 7)),
                     r=[("sq", dt_ % 2), "ones"], w=[psk(pbank)])
            P.act(lambda e: e.activation(out=sd[:], in_=ps[pbank][:, 0:SB], func=AF.Sqrt, scale=1.0 / D, bias=EPS),
                  r=[psk(pbank)], w=[("sd", tag)])
            P.dve(lambda e: e.reciprocal(out=rstd[:], in_=sd[:]), r=[("sd", tag)], w=[("rstd", tag)])
            for dt_ in range(8):
                P.dve(lambda e, dt_=dt_: e.scalar_tensor_tensor(out=xn_ap_fn(dt_), in0=hT[:, dt_, tok],
                                                                scalar=gains[:, gi, dt_:dt_ + 1], in1=rstd[:],
                                                                op0=ALU.mult, op1=ALU.mult),
                      r=[("hT", dt_, s_), ("rstd", tag), ("gains", gi)], w=[keyfn(dt_)])

        def ffn(blk, gi, wg, wu, wd, bg=None):
            P.barrier(dummy[:])
            with contextlib.ExitStack() as s3:
                xn = sb("xn", [128, 8, NT], BF16, s3)
                sq = [sb("sq%d" % i, [128, SB], BF16, s3) for i in range(2)]
                sd = sb("sd", [128, SB], F32, s3)
                rstd = sb("rstd", [128, SB], F32, s3)
                actb = [sb("actb%d" % i, [128, 3, SB], BF16, s3) for i in range(2)]
                sg = [sb("sg%d" % i, [128, SB], F32, s3) for i in range(2)]
                for s_ in range(NSB):
                    norm_sb(s_, gi, lambda dt_, s_=s_: xn[:, dt_, s_ * SB:(s_ + 1) * SB], sq, sd, rstd,
                            lambda dt_, s_=s_: ("xn", dt_, s_))
                cnt = 0
                prev_down = [None]

                def emit_down(nf, sl, ab, s_, tok):
                    for dt_ in range(8):
                        pb = 4 + (dt_ % 2)
                        for fi in range(nf):
                            P.pe(lambda e, fi=fi, dt_=dt_, pb=pb, sl=sl, ab=ab, nf=nf: e.matmul(
                                ps[pb][:, 0:SB], lhsT=wsd[sl][:, fi, dt_ * 128:(dt_ + 1) * 128], rhs=actb[ab][:, fi, :],
                                start=(fi == 0), stop=(fi == nf - 1)),
                                r=[("WS", sl, 2), ("actb", ab, fi)], w=[psk(pb)])
                        P.dve(lambda e, dt_=dt_, pb=pb, tok=tok: e.scalar_tensor_tensor(
                            out=hT[:, dt_, tok], in0=ps[pb][:, 0:SB], scalar=0.5, in1=hT[:, dt_, tok],
                            op0=ALU.mult, op1=ALU.add),
                            r=[psk(pb), ("hT", dt_, s_)], w=[("hT", dt_, s_)])
                        if bg is not None and dt_ % 4 == 3:
                            next(bg, None)

                for gix, (f0, nf) in enumerate(GROUPS):
                    sl = gix % 2
                    P.dma("pool", lambda e, f0=f0, nf=nf, sl=sl: e.dma_start(
                        out=wsg[sl][:, :, 0:nf * 128], in_=rows3(wg)[:, :, f0 * 128:(f0 + nf) * 128]),
                        w=[("WS", sl, 0)], epoch=False)
                    P.dma("pool", lambda e, f0=f0, nf=nf, sl=sl: e.dma_start(
                        out=wsu[sl][:, :, 0:nf * 128], in_=rows3(wu)[:, :, f0 * 128:(f0 + nf) * 128]),
                        w=[("WS", sl, 1)], epoch=False)
                    P.dma("pool", lambda e, f0=f0, nf=nf, sl=sl: e.dma_start(
                        out=wsd[sl][:, 0:nf, :], in_=rows3(wd)[:, f0:f0 + nf, :]),
                        w=[("WS", sl, 2)], epoch=False)
                    for s_ in range(NSB):
                        tok = slice(s_ * SB, (s_ + 1) * SB)
                        ab = (gix * NSB + s_) % 2
                        for fi in range(nf):
                            par = cnt % 2
                            cnt += 1
                            for kc in range(8):
                                P.pe(lambda e, kc=kc, fi=fi, par=par, sl=sl, tok=tok: e.matmul(
                                    ps[par][:, 0:SB], lhsT=wsg[sl][:, kc, fi * 128:(fi + 1) * 128], rhs=xn[:, kc, tok],
                                    start=(kc == 0), stop=(kc == 7)),
                                    r=[("WS", sl, 0), ("xn", kc, s_)], w=[psk(par)])
                            for kc in range(8):
                                P.pe(lambda e, kc=kc, fi=fi, par=par, sl=sl, tok=tok: e.matmul(
                                    ps[2 + par][:, 0:SB], lhsT=wsu[sl][:, kc, fi * 128:(fi + 1) * 128], rhs=xn[:, kc, tok],
                                    start=(kc == 0), stop=(kc == 7)),
                                    r=[("WS", sl, 1), ("xn", kc, s_)], w=[psk(2 + par)])
                            P.act(lambda e, par=par: e.activation(out=sg[par][:], in_=ps[par][:, 0:SB], func=AF.Silu),
                                  r=[psk(par)], w=[("sg", par)])
                            P.dve(lambda e, par=par, ab=ab, fi=fi: e.tensor_tensor(
                                out=actb[ab][:, fi, :], in0=ps[2 + par][:, 0:SB], in1=sg[par][:], op=ALU.mult),
                                r=[psk(2 + par), ("sg", par)], w=[("actb", ab, fi)])
                            if bg is not None:
                                next(bg, None)
                        if prev_down[0] is not None:
                            prev_down[0]()
                        prev_down[0] = (lambda nf=nf, sl=sl, ab=ab, s_=s_, tok=tok: emit_down(nf, sl, ab, s_, tok))
                if prev_down[0] is not None:
                    prev_down[0]()
                if bg is not None:
                    for _ in bg:
                        pass

        pending = []

        def ssm_hist():
            for c in range(NT // TC):
                tk = slice(c * TC, (c + 1) * TC)
                s_ = (c * TC) // SB
                for gp in range(16):
                    ct = gp // 4
                    bank = 6 + (gp % 2)
                    PA, PB = ps[bank][:, 0:TC], ps[bank][:, TC:2 * TC]
                    P.pe(lambda e, gp=gp, ct=ct, PA=PA, tk=tk: e.matmul(PA, lhsT=BqPad[:, gp, 0, :], rhs=uT[:, ct, tk], start=True, stop=True),
                         r=["BqPad", ("uT", ct, s_)], w=[psk(bank)])
                    P.pe(lambda e, gp=gp, ct=ct, PB=PB, tk=tk: e.matmul(PB, lhsT=BqPad[:, gp, 1, :], rhs=uT[:, ct, tk], start=True, stop=True),
                         r=["BqPad", ("uT", ct, s_)], w=[psk(bank)])
                    for k, (src, tab, tkey) in enumerate([(PA, Ec, "Ec"), (PB, Es, "Es"), (PA, Es, "Es"), (PB, Ec, "Ec")]):
                        P.dve(lambda e, gp=gp, k=k, src=src, tab=tab: e.scalar_tensor_tensor(
                            out=junk[k % 2][:], in0=src, scalar=1.0, in1=tab[:, gp, :],
                            op0=ALU.mult, op1=ALU.mult, accum_out=acc[:, k, gp:gp + 1]),
                            r=[psk(bank), tkey], w=[("junk", k % 2), ("acc", k, gp)])
                    yield
                ak = [("acc", k, gp) for k in range(4) for gp in range(16)]
                TT = lambda o, a_, b_, op: (lambda e: e.tensor_tensor(out=o, in0=a_, in1=b_, op=op))
                P.dve(TT(ctm[:, 0, :], acc[:, 0, :], acc[:, 1, :], ALU.subtract), r=ak, w=["ctm0"])
                P.dve(TT(ctm[:, 1, :], acc[:, 2, :], acc[:, 3, :], ALU.add), r=ak, w=["ctm1"])
                P.dve(TT(ctm[:, 2, :], lamN[:, 0, :], carry[:, 0, :], ALU.mult), r=["lamN", "carry"], w=["ctm2"])
                P.dve(TT(ctm[:, 3, :], lamN[:, 1, :], carry[:, 1, :], ALU.mult), r=["lamN", "carry"], w=["ctm3"])
                P.dve(TT(ctm[:, 4, :], lamN[:, 0, :], carry[:, 1, :], ALU.mult), r=["lamN", "carry"], w=["ctm4"])
                P.dve(TT(ctm[:, 5, :], lamN[:, 1, :], carry[:, 0, :], ALU.mult), r=["lamN", "carry"], w=["ctm5"])
                P.dve(TT(ctm[:, 2, :], ctm[:, 2, :], ctm[:, 3, :], ALU.subtract), r=["ctm2", "ctm3"], w=["ctm2"])
                P.dve(TT(ctm[:, 4, :], ctm[:, 4, :], ctm[:, 5, :], ALU.add), r=["ctm4", "ctm5"], w=["ctm4"])
                P.dve(TT(carry[:, 0, :], ctm[:, 2, :], ctm[:, 0, :], ALU.add), r=["ctm2", "ctm0", "ctm4"], w=["carry"])
                P.dve(TT(carry[:, 1, :], ctm[:, 4, :], ctm[:, 1, :], ALU.add), r=["ctm4", "ctm1"], w=["carry"])
                yield

        def do_block(blk):
            full = (blk == NBLK - 1)
            P.barrier(dummy[:])
            with contextlib.ExitStack() as s3:
                xst = [sb("xst%d" % i, [128, D], F32, s3) for i in range(2)]
                ntile = (NT + 127) // 128
                for i in range(ntile):
                    rows = min(128, NT - i * 128)
                    xb = i % 2
                    P.dma("sp", lambda e, i=i, rows=rows, xb=xb: e.dma_start(
                        out=xst[xb][0:rows, :], in_=xs[blk * NT + i * 128: blk * NT + i * 128 + rows, :]),
                        w=[("xst", xb)])
                    sbs = sorted(set([(i * 128) // SB, (i * 128 + rows - 1) // SB]))
                    for h in range(2):
                        bank = 6 + h
                        for q in range(4):
                            dt_ = h * 4 + q
                            P.pe(lambda e, dt_=dt_, q=q, rows=rows, xb=xb, bank=bank: e.transpose(
                                out=ps[bank][:, q * 128:q * 128 + rows], in_=xst[xb][0:rows, dt_ * 128:(dt_ + 1) * 128],
                                identity=ident[0:rows, 0:rows]),
                                r=[("xst", xb), "ident"], w=[psk(bank)])
                        eng = P.act if h == 0 else P.dve
                        if h == 0:
                            P.act(lambda e, h=h, i=i, rows=rows, bank=bank: e.activation(
                                out=hT[:, h * 4:(h + 1) * 4, i * 128:i * 128 + rows],
                                in_=ps[bank][:, :].rearrange("p (q c) -> p q c", q=4)[:, :, 0:rows], func=AF.Copy),
                                r=[psk(bank)], w=[("hT", dt_, s_) for dt_ in range(h * 4, h * 4 + 4) for s_ in sbs])
                        else:
                            P.dve(lambda e, h=h, i=i, rows=rows, bank=bank: e.tensor_copy(
                                out=hT[:, h * 4:(h + 1) * 4, i * 128:i * 128 + rows],
                                in_=ps[bank][:, :].rearrange("p (q c) -> p q c", q=4)[:, :, 0:rows]),
                                r=[psk(bank)], w=[("hT", dt_, s_) for dt_ in range(h * 4, h * 4 + 4) for s_ in sbs])
            if full:
                tap("hT_load", hT[:], HTK)
            ffn(blk, 0, W["ffn1_w_gate"], W["ffn1_w_up"], W["ffn1_w_down"], bg=(pending.pop() if pending else None))
            if full:
                make_tables("E")
            if full:
                tap("hT_ffn1", hT[:], HTK)
            P.barrier(dummy[:])
            with contextlib.ExitStack() as s3:
                cvT = sb("cvT", [128, 4, NT], BF16, s3) if full else None
                ga = uT
                P.barrier(dummy[:])
                with contextlib.ExitStack() as s4:
                    unA = [sb("un%d" % i, [128, 8, SB], BF16, s4) for i in range(2)]
                    sqA = [sb("sq%d" % i, [128, SB], BF16, s4) for i in range(2)]
                    sdA = [sb("sd%d" % i, [128, SB], F32, s4) for i in range(2)]
                    rstdA = [sb("rstd%d" % i, [128, SB], F32, s4) for i in range(2)]
                    wu_ = WS[:, 0:4096].rearrange("p (k n) -> p k n", k=8)
                    P.dma("pool", lambda e: e.dma_start(out=wu_, in_=rows3(W["w_in"])[:, :, 0:512]), w=ALLWS, epoch=False)
                    if full:
                        wv = WS[:, 4096:16384].rearrange("p (k n) -> p k n", k=8)
                        zs = sb("zs", [128, 4, SB + 2], F32, s4)
                        vv = sb("vv", [128, SB], F32, s4)
                        c1 = sb("c1", [128, SB], F32, s4)
                        P.dma("pool", lambda e: e.dma_start(out=wv, in_=rows3(W["w_in"])[:, :, 512:2048]),
                              w=ALLWS, epoch=False)
                        P.dve(lambda e: e.memset(zs[:], 0.0), w=[("zs", c) for c in range(4)])
                    for s_ in range(NSB):
                        tok = slice(s_ * SB, (s_ + 1) * SB)
                        npar = s_ % 2
                        norm_sb(s_, 1, lambda dt_, npar=npar: unA[npar][:, dt_, :], sqA, sdA[npar], rstdA[npar],
                                lambda dt_, npar=npar: ("un", npar, dt_), pbank=6 + npar, tag=npar)
                        for ot in range(4):
                            pb = 4 + (ot % 2)
                            for kc in range(8):
                                P.pe(lambda e, kc=kc, ot=ot, pb=pb, npar=npar: e.matmul(
                                    ps[pb][:, 0:SB], lhsT=wu_[:, kc, ot * 128:(ot + 1) * 128], rhs=unA[npar][:, kc, :],
                                    start=(kc == 0), stop=(kc == 7)), r=ALLWS + [("un", npar, kc)], w=[psk(pb)])
                            P.act(lambda e, ot=ot, pb=pb, tok=tok: e.activation(out=uT[:, ot, tok], in_=ps[pb][:, 0:SB], func=AF.Copy),
                                  r=[psk(pb)], w=[("uT", ot, s_)])
                        if full:
                            for ot in range(4):
                                for j, bank in ((0, 0), (2, 1), (1, 2)):
                                    for kc in range(8):
                                        P.pe(lambda e, kc=kc, ot=ot, j=j, bank=bank, npar=npar: e.matmul(
                                            ps[bank][:, 0:SB], lhsT=wv[:, kc, j * 512 + ot * 128: j * 512 + (ot + 1) * 128],
                                            rhs=unA[npar][:, kc, :], start=(kc == 0), stop=(kc == 7)),
                                            r=ALLWS + [("un", npar, kc)], w=[psk(bank)])
                                P.act(lambda e: e.activation(out=vv[:], in_=ps[0][:, 0:SB], func=AF.Copy), r=[psk(0)], w=["vv"])
                                P.dve(lambda e, ot=ot: e.tensor_copy(out=zs[:, ot, 0:2], in_=zs[:, ot, SB:SB + 2]),
                                      r=[("zs", ot)], w=[("zs", ot)])
                                P.dve(lambda e, ot=ot: e.tensor_tensor(out=zs[:, ot, 2:SB + 2], in0=ps[1][:, 0:SB], in1=vv[:], op=ALU.mult),
                                      r=[psk(1), "vv", ("zs", ot)], w=[("zs", ot)])
                                P.dve(lambda e, ot=ot: e.tensor_scalar(out=c1[:], in0=zs[:, ot, 0:SB], scalar1=cw[:, ot, 0:1], scalar2=None,
                                                                       op0=ALU.mult), r=[("zs", ot), "cw"], w=["c1"])
                                P.dve(lambda e, ot=ot: e.scalar_tensor_tensor(out=c1[:], in0=zs[:, ot, 1:SB + 1], scalar=cw[:, ot, 1:2], in1=c1[:],
                                                                              op0=ALU.mult, op1=ALU.add), r=[("zs", ot), "cw", "c1"], w=["c1"])
                                P.dve(lambda e, ot=ot: e.scalar_tensor_tensor(out=c1[:], in0=zs[:, ot, 2:SB + 2], scalar=cw[:, ot, 2:3], in1=c1[:],
                                                                              op0=ALU.mult, op1=ALU.add), r=[("zs", ot), "cw", "c1"], w=["c1"])
                                P.dve(lambda e, ot=ot, tok=tok: e.tensor_tensor(out=cvT[:, ot, tok], in0=ps[2][:, 0:SB], in1=c1[:], op=ALU.mult),
                                      r=[psk(2), "c1"], w=[("cvT", ot, s_)])
                if full:
                    tap("uT_A", uT[:], [("uT", c_, s_) for c_ in range(4) for s_ in range(NSB)], BF16)
                if not full:
                    pending.append(ssm_hist())
                    return
                P.barrier(dummy[:])
                with contextlib.ExitStack() as s4:
                    NBUF = 3
                    TN = ["t1", "t2", "t3", "t4", "wre", "wim", "gre", "gim"]
                    T = {}
                    for b_ in range(NBUF):
                        for n_ in TN:
                            T[(n_, b_)] = sb("%s_%d" % (n_, b_), [128, TC], F32, s4)
                        for n_ in ["d1", "d2", "d3", "d4"]:
                            T[(n_, b_)] = sb("%s_%d" % (n_, b_), [128, TC], BF16, s4)
                    ctmp = sb("ctmp", [128, 4, 16], F32, s4)
                    Hh = [sb("Hh%d" % i, [128, 4, 2, TC], BF16, s4) for i in range(2)]
                    ya = [sb("ya%d" % i, [128, TC], F32, s4) for i in range(2)]
                    TTf = lambda o, a_, b_, op: (lambda e: e.tensor_tensor(out=o, in0=a_, in1=b_, op=op))
                    npc = 0
                    for c in range(NT // TC):
                        tk = slice(c * TC, (c + 1) * TC)
                        s_ = (c * TC) // SB
                        for gp in range(16):
                            ct = gp // 4
                            b = npc % NBUF
                            pbk = npc % NBUF
                            npc += 1
                            PA, PB = ps[pbk][:, 0:TC], ps[pbk][:, TC:2 * TC]
                            hb = (c * 4 + ct) % 2
                            K = lambda n_: (n_, b)
                            X = lambda n_: T[(n_, b)][:]
                            P.pe(lambda e, gp=gp, ct=ct, PA=PA, tk=tk: e.matmul(PA, lhsT=BqPad[:, gp, 0, :], rhs=uT[:, ct, tk], start=True, stop=True),
                                 r=["BqPad", ("uT", ct, s_)], w=[psk(pbk)])
                            P.pe(lambda e, gp=gp, ct=ct, PB=PB, tk=tk: e.matmul(PB, lhsT=BqPad[:, gp, 1, :], rhs=uT[:, ct, tk], start=True, stop=True),
                                 r=["BqPad", ("uT", ct, s_)], w=[psk(pbk)])
                            P.dve(TTf(X("t1"), PA, Ec[:, gp, :], ALU.mult), r=[psk(pbk), "Ec"], w=[K("t1")])
                            P.dve(TTf(X("t2"), PB, Es[:, gp, :], ALU.mult), r=[psk(pbk), "Es"], w=[K("t2")])
                            P.dve(TTf(X("t3"), PB, Ec[:, gp, :], ALU.mult), r=[psk(pbk), "Ec"], w=[K("t3")])
                            P.dve(TTf(X("t4"), PA, Es[:, gp, :], ALU.mult), r=[psk(pbk), "Es"], w=[K("t4")])
                            P.pool(TTf(X("wre"), X("t1"), X("t2"), ALU.add), r=[K("t1"), K("t2")], w=[K("wre")])
                            P.pool(TTf(X("wim"), X("t3"), X("t4"), ALU.subtract), r=[K("t3"), K("t4")], w=[K("wim")])
                            P.dve(lambda e, gp=gp, o=X("gre"), d=X("wre"): e.tensor_tensor_scan(
                                out=o, data0=mag[:, gp:gp + 1].to_broadcast([128, TC]), data1=d,
                                initial=carry[:, 0, gp:gp + 1], op0=ALU.mult, op1=ALU.add),
                                r=[K("wre"), "carry", "mag"], w=[K("gre")])
                            P.dve(lambda e, gp=gp, o=X("gim"), d=X("wim"): e.tensor_tensor_scan(
                                out=o, data0=mag[:, gp:gp + 1].to_broadcast([128, TC]), data1=d,
                                initial=carry[:, 1, gp:gp + 1], op0=ALU.mult, op1=ALU.add),
                                r=[K("wim"), "carry", "mag"], w=[K("gim")])
                            P.act(lambda e, gp=gp, g_=T[("gre", b)]: e.activation(out=glast[:, 0, gp:gp + 1], in_=g_[:, TC - 1:TC], func=AF.Copy),
                                  r=[K("gre")], w=[("glast", 0, gp)])
                            P.act(lambda e, gp=gp, g_=T[("gim", b)]: e.activation(out=glast[:, 1, gp:gp + 1], in_=g_[:, TC - 1:TC], func=AF.Copy),
                                  r=[K("gim")], w=[("glast", 1, gp)])
                            P.pool(TTf(X("d1"), X("gre"), Ec[:, gp, :], ALU.mult), r=[K("gre"), "Ec"], w=[K("d1")])
                            P.pool(TTf(X("d2"), X("gim"), Es[:, gp, :], ALU.mult), r=[K("gim"), "Es"], w=[K("d2")])
                            P.pool(TTf(Hh[hb][:, gp % 4, 0, :], X("d1"), X("d2"), ALU.subtract), r=[K("d1"), K("d2")], w=[("Hh", hb, gp % 4, 0)])
                            P.dve(TTf(X("d3"), X("gre"), Es[:, gp, :], ALU.mult), r=[K("gre"), "Es"], w=[K("d3")])
                            P.dve(TTf(X("d4"), X("gim"), Ec[:, gp, :], ALU.mult), r=[K("gim"), "Ec"], w=[K("d4")])
                            P.dve(TTf(Hh[hb][:, gp % 4, 1, :], X("d3"), X("d4"), ALU.add), r=[K("d3"), K("d4")], w=[("Hh", hb, gp % 4, 1)])
                            if gp % 4 == 3:
                                for k in range(8):
                                    gl, ri = k // 2, k % 2
                                    P.pe(lambda e, gl=gl, ri=ri, ct=ct, hb=hb, k=k: e.matmul(
                                        ps[4 + hb][:, 0:TC], lhsT=CPad[:, ct * 4 + gl, ri, :], rhs=Hh[hb][:, gl, ri, :],
                                        start=(k == 0), stop=(k == 7)),
                                        r=["CPad", ("Hh", hb, gl, ri)], w=[psk(4 + hb)])
                                P.dve(lambda e, ct=ct, hb=hb, tk=tk: e.scalar_tensor_tensor(
                                    out=ya[hb][:], in0=uT[:, ct, tk], scalar=dsk[:, ct:ct + 1], in1=ps[4 + hb][:, 0:TC],
                                    op0=ALU.mult, op1=ALU.add), r=[psk(4 + hb), ("uT", ct, s_), "dsk"], w=[("ya", hb)])
                                P.act(lambda e, ct=ct, tk=tk, hb=hb: e.activation(out=ga[:, ct, tk], in_=ya[hb][:], func=AF.Gelu_apprx_tanh),
                                      r=[("ya", hb)], w=[("ga", ct, s_)])
                        gk = [("glast", ri, gp) for ri in range(2) for gp in range(16)]
                        P.dve(TTf(ctmp[:, 0, :], glast[:, 0, :], cosN[:], ALU.mult), r=gk + ["cosN"], w=["ctmp0"])
                        P.dve(TTf(ctmp[:, 1, :], glast[:, 1, :], sinN[:], ALU.mult), r=gk + ["sinN"], w=["ctmp1"])
                        P.dve(TTf(ctmp[:, 2, :], glast[:, 0, :], sinN[:], ALU.mult), r=gk + ["sinN"], w=["ctmp2"])
                        P.dve(TTf(ctmp[:, 3, :], glast[:, 1, :], cosN[:], ALU.mult), r=gk + ["cosN"], w=["ctmp3"])
                        P.dve(TTf(carry[:, 0, :], ctmp[:, 0, :], ctmp[:, 1, :], ALU.subtract), r=["ctmp0", "ctmp1"], w=["carry"])
                        P.dve(TTf(carry[:, 1, :], ctmp[:, 2, :], ctmp[:, 3, :], ALU.add), r=["ctmp2", "ctmp3"], w=["carry"])
                if full:
                    tap("uT", uT[:], [("uT", c_, s_) for c_ in range(4) for s_ in range(NSB)], BF16)
                    tap("cvT", cvT[:], [("cvT", c_, s_) for c_ in range(4) for s_ in range(NSB)], BF16)
                    tap("ga", ga[:], [("ga", c_, s_) for c_ in range(4) for s_ in range(NSB)], BF16)
                    tap("carry", carry[:], ["carry"])
                if not full:
                    return
                P.barrier(dummy[:])
                with contextlib.ExitStack() as s4:
                    unC = sb("un", [128, 8, SB], BF16, s4)
                    sqC = [sb("sq%d" % i, [128, SB], BF16, s4) for i in range(2)]
                    sdC = sb("sd", [128, SB], F32, s4)
                    rstdC = sb("rstd", [128, SB], F32, s4)
                    wgl = [WS[:, i * 3584: i * 3584 + 1024].rearrange("p (k j n) -> p k j n", k=4, j=2) for i in range(2)]
                    wco = [WS[:, i * 3584 + 1024: i * 3584 + 1536].rearrange("p (k n) -> p k n", k=4) for i in range(2)]
                    wga = [WS[:, i * 3584 + 1536: i * 3584 + 3584].rearrange("p (k j n) -> p k j n", k=8, j=2) for i in range(2)]
                    glu_v = W["ssm_w_glu"].rearrange("(kc p) n -> p kc n", p=128)
                    co_v = W["conv_w_out"].rearrange("(kc p) n -> p kc n", p=128)
                    win_v = W["w_in"].rearrange("(kc p) n -> p kc n", p=128)
                    wo = WS[:, SLOT:SLOT + 8192].rearrange("p (k n) -> p k n", k=8)
                    mixed = sb("mixed", [128, 8, SB], BF16, s4)
                    ysv = sb("ysv", [128, SB], F32, s4)
                    g1 = sb("g1", [128, SB], F32, s4)
                    g2 = sb("g2", [128, SB], F32, s4)
                    P.dma("pool", lambda e: e.dma_start(out=wo, in_=rows3(W["w_o"])), w=[("WS", 1, 0), ("WS", 1, 1), ("WS", 1, 2)], epoch=False)
                    P.op("pool", lambda e: e.memset(dummy2[:], 0.0), (), [("WS", 0, 0), ("WS", 0, 1), ("WS", 0, 2)], epoch=False)
                    for s_ in range(NSB):
                        tok = slice(s_ * SB, (s_ + 1) * SB)
                        norm_sb(s_, 1, lambda dt_: unC[:, dt_, :], sqC, sdC, rstdC, lambda dt_: ("un", dt_))
                        for dt_ in range(8):
                            ws_ = (s_ * 8 + dt_) % 2
                            P.dma("sp", lambda e, dt_=dt_, ws_=ws_: e.dma_start(
                                out=WS[:, ws_ * 3584:(ws_ + 1) * 3584], in_=cw_scr[dt_, :, :]),
                                r=[("WS", 0, 0), ("WS", 0, 1), ("WS", 0, 2), ("cwscr", dt_, "co")] + [("cwscr", dt_, a_, j_) for a_ in ("gl", "ga") for j_ in range(2)],
                                w=[("WSC", ws_, "gl", 0), ("WSC", ws_, "gl", 1), ("WSC", ws_, "ga", 0), ("WSC", ws_, "ga", 1), ("WSC", ws_, "co")], epoch=False)
                            for kc in range(4):
                                P.pe(lambda e, kc=kc, dt_=dt_, tok=tok, ws_=ws_: e.matmul(ps[0][:, 0:SB], lhsT=wgl[ws_][:, kc, 0, :],
                                                                                  rhs=ga[:, kc, tok], start=(kc == 0), stop=(kc == 3)),
                                     r=[("WS", 0, 0), ("WS", 0, 1), ("WS", 0, 2), ("WSC", ws_, "gl", 0), ("ga", kc, s_)], w=[psk(0)])
                            for kc in range(4):
                                P.pe(lambda e, kc=kc, dt_=dt_, tok=tok, ws_=ws_: e.matmul(ps[1][:, 0:SB], lhsT=wgl[ws_][:, kc, 1, :],
                                                                                  rhs=ga[:, kc, tok], start=(kc == 0), stop=(kc == 3)),
                                     r=[("WS", 0, 0), ("WS", 0, 1), ("WS", 0, 2), ("WSC", ws_, "gl", 1), ("ga", kc, s_)], w=[psk(1)])
                            for kc in range(4):
                                P.pe(lambda e, kc=kc, dt_=dt_, tok=tok, ws_=ws_: e.matmul(ps[2][:, 0:SB], lhsT=wco[ws_][:, kc, :],
                                                                                  rhs=cvT[:, kc, tok], start=(kc == 0), stop=(kc == 3)),
                                     r=[("WS", 0, 0), ("WS", 0, 1), ("WS", 0, 2), ("WSC", ws_, "co"), ("cvT", kc, s_)], w=[psk(2)])
                            for j in range(2):
                                for kc in range(8):
                                    P.pe(lambda e, kc=kc, dt_=dt_, j=j, ws_=ws_: e.matmul(ps[3 + j][:, 0:SB],
                                                                                  lhsT=wga[ws_][:, kc, j, :],
                                                                                  rhs=unC[:, kc, :], start=(kc == 0), stop=(kc == 7)),
                                         r=[("WS", 0, 0), ("WS", 0, 1), ("WS", 0, 2), ("WSC", ws_, "ga", j), ("un", kc)], w=[psk(3 + j)])
                            P.act(lambda e: e.activation(out=g1[:], in_=ps[1][:, 0:SB], func=AF.Sigmoid), r=[psk(1)], w=["g1"])
                            P.dve(lambda e: e.tensor_tensor(out=ysv[:], in0=ps[0][:, 0:SB], in1=g1[:], op=ALU.mult), r=[psk(0), "g1"], w=["ysv"])
                            P.act(lambda e, dt_=dt_: e.activation(out=g1[:], in_=ps[3][:, 0:SB], func=AF.Sigmoid, bias=bgate[:, dt_:dt_ + 1]),
                                  r=[psk(3), "bgate"], w=["g1"])
                            P.act(lambda e, dt_=dt_: e.activation(out=g2[:], in_=ps[4][:, 0:SB], func=AF.Sigmoid, bias=bgate[:, 8 + dt_:9 + dt_]),
                                  r=[psk(4), "bgate"], w=["g2"])
                            P.dve(lambda e: e.tensor_tensor(out=ysv[:], in0=ysv[:], in1=g1[:], op=ALU.mult), r=["ysv", "g1"], w=["ysv"])
                            P.dve(lambda e: e.tensor_tensor(out=g2[:], in0=ps[2][:, 0:SB], in1=g2[:], op=ALU.mult), r=[psk(2), "g2"], w=["g2"])
                            P.dve(lambda e, dt_=dt_: e.tensor_tensor(out=mixed[:, dt_, :], in0=ysv[:], in1=g2[:], op=ALU.add),
                                  r=["ysv", "g2"], w=[("mixed", dt_)])
                        for dt_ in range(8):
                            pb = 5 + (dt_ % 2)
                            for kc in range(8):
                                P.pe(lambda e, kc=kc, dt_=dt_, pb=pb: e.matmul(ps[pb][:, 0:SB], lhsT=wo[:, kc, dt_ * 128:(dt_ + 1) * 128],
                                                                                rhs=mixed[:, kc, :], start=(kc == 0), stop=(kc == 7)),
                                     r=[("WS", 1, 0), ("WS", 1, 1), ("WS", 1, 2), ("mixed", kc)], w=[psk(pb)])
                            P.dve(lambda e, dt_=dt_, pb=pb, tok=tok: e.tensor_tensor(out=hT[:, dt_, tok], in0=ps[pb][:, 0:SB], in1=hT[:, dt_, tok], op=ALU.add),
                                  r=[psk(pb), ("hT", dt_, s_)], w=[("hT", dt_, s_)])
            tap("hT_mix", hT[:], HTK)
            ffn(blk, 2, W["ffn2_w_gate"], W["ffn2_w_up"], W["ffn2_w_down"])
            tap("hT_ffn2", hT[:], HTK)
            P.barrier(dummy[:])
            with contextlib.ExitStack() as s3:
                sqF = [sb("sq%d" % i, [128, SB], BF16, s3) for i in range(2)]
                sdF = sb("sd", [128, SB], F32, s3)
                rstdF = sb("rstd", [128, SB], F32, s3)
                ost = [sb("ost%d" % i, [128, D], F32, s3) for i in range(2)]
                for s_ in range(NSB):
                    norm_sb(s_, 3, lambda dt_, s_=s_: hT[:, dt_, s_ * SB:(s_ + 1) * SB], sqF, sdF, rstdF,
                            lambda dt_, s_=s_: ("hT", dt_, s_))
                for i in range(16):
                    t0 = 16 + i * 128
                    ob = i % 2
                    sbs = sorted(set([t0 // SB, (t0 + 127) // SB]))
                    for h in range(2):
                        bank = 6 + h
                        for q in range(4):
                            dt_ = h * 4 + q
                            P.pe(lambda e, dt_=dt_, q=q, t0=t0, bank=bank: e.transpose(
                                out=ps[bank][:, q * 128:(q + 1) * 128], in_=hT[:, dt_, t0:t0 + 128], identity=ident[:]),
                                r=[("hT", dt_, s_) for s_ in sbs] + ["ident"], w=[psk(bank)])
                        if h == 0:
                            P.act(lambda e, ob=ob, h=h, bank=bank: e.activation(out=ost[ob][:, h * 512:(h + 1) * 512], in_=ps[bank][:, :], func=AF.Copy),
                                  r=[psk(bank)], w=[("ost", ob, h)])
                        else:
                            P.dve(lambda e, ob=ob, h=h, bank=bank: e.tensor_copy(out=ost[ob][:, h * 512:(h + 1) * 512], in_=ps[bank][:, :]),
                                  r=[psk(bank)], w=[("ost", ob, h)])
                    P.dma("sp", lambda e, i=i, ob=ob: e.dma_start(out=out[i * 128:(i + 1) * 128, :], in_=ost[ob][:]),
                          r=[("ost", ob, 0), ("ost", ob, 1)], w=[("out", i)])
        for blk in range(NBLK):
            do_block(blk)
        P.emit()
    return nc


_NC_CACHE = {}


def kernel(**inputs):
    x = np.asarray(inputs["x"], dtype=np.float32)
    meta = np.asarray(inputs["meta_tokens"], dtype=np.float32)
    B, S, _ = x.shape
    n = 8
    if "nc" not in _NC_CACHE:
        _NC_CACHE["nc"] = build_nc()
    nc = _NC_CACHE["nc"]
    f = lambda k: np.ascontiguousarray(np.asarray(inputs[k], dtype=np.float32))
    shared = {
        "g_ffn1": f("g_ffn1")[0], "ffn1_w_gate": f("ffn1_w_gate")[0], "ffn1_w_up": f("ffn1_w_up")[0],
        "ffn1_w_down": f("ffn1_w_down")[0], "g_mix": f("g_mix")[0], "w_in": f("w_in")[0], "b_gate": f("b_gate")[0],
        "ssm_a_re": f("ssm_a_re")[0], "ssm_a_im": f("ssm_a_im")[0], "ssm_log_dt": f("ssm_log_dt")[0],
        "ssm_b_re": f("ssm_b_re")[0], "ssm_b_im": f("ssm_b_im")[0],
        "ssm_c_re": f("ssm_c_re")[0].reshape(512, 64), "ssm_c_im": f("ssm_c_im")[0].reshape(512, 64),
        "ssm_d": f("ssm_d")[0], "ssm_w_glu": f("ssm_w_glu")[0], "conv_w": f("conv_w")[0].reshape(3, 512),
        "conv_w_out": f("conv_w_out")[0], "w_o": f("w_o")[0], "g_ffn2": f("g_ffn2")[0],
        "ffn2_w_gate": f("ffn2_w_gate")[0], "ffn2_w_up": f("ffn2_w_up")[0], "ffn2_w_down": f("ffn2_w_down")[0],
        "g_final": f("g_final"),
    }
    in_maps = []
    total = NBLK * NT
    for c in range(n):
        b, q = c // 4, c % 4
        seq = np.concatenate([meta, x[b, : 2048 * (q + 1)]], axis=0)
        stream = np.zeros((total, D), np.float32)
        stream[total - seq.shape[0]:] = seq
        m = dict(shared)
        m["xs"] = stream
        in_maps.append(m)
    res = run_bass_kernel_spmd(nc, in_maps, core_ids=list(range(n)))
    outp = np.zeros((B, S, D), np.float32)
    for c in range(n):
        b, q = c // 4, c % 4
        outp[b, q * 2048:(q + 1) * 2048] = np.asarray(res.results[c]["out"], dtype=np.float32)
    return outp
```

```python
import contextlib
import math
import numpy as np
import concourse.bass as bass
import concourse.mybir as mybir
from concourse.bass_utils import run_bass_kernel_spmd

F32 = mybir.dt.float32
BF16 = mybir.dt.bfloat16
I32 = mybir.dt.int32
AF = mybir.ActivationFunctionType
ALU = mybir.AluOpType

D = 1024
DFF = 2816
NT = 2064
NBLK = 4
NSB = 6
SB = 344
TC = 172
NF = 22
GROUPS = [(0, 3), (3, 3), (6, 3), (9, 3), (12, 3), (15, 3), (18, 2), (20, 2)]
EPS = 1e-6
ENGS = ("pe", "act", "dve", "pool", "sp")
PI = math.pi


class Prog:
    def __init__(self, nc, dma_ring=8):
        self.nc = nc
        self.ops = []
        self.dma_ring = dma_ring

    def op(self, eng, fn, reads=(), writes=(), dma=False, epoch=True):
        self.ops.append(dict(eng=eng, fn=fn, reads=tuple(reads) + (("EPOCH",) if epoch else ()), writes=tuple(writes), dma=dma))

    def barrier(self, dummy):
        self.ops.append(dict(eng="dve", fn=lambda e: e.memset(dummy, 0.0), reads=(), writes=("EPOCH",), dma=False))

    def pe(self, fn, r=(), w=()):
        self.op("pe", fn, r, w)

    def act(self, fn, r=(), w=()):
        self.op("act", fn, r, w)

    def dve(self, fn, r=(), w=()):
        self.op("dve", fn, r, w)

    def pool(self, fn, r=(), w=()):
        self.op("pool", fn, r, w)

    def dma(self, q, fn, r=(), w=(), epoch=True):
        self.op(q, fn, r, w, dma=True, epoch=epoch)

    def emit(self):
        nc = self.nc
        ops = self.ops
        last_writer = {}
        readers = {}
        for i, o in enumerate(ops):
            deps = set()
            for r in o["reads"]:
                if r in last_writer:
                    deps.add(last_writer[r])
            for w in o["writes"]:
                if w in last_writer:
                    deps.add(last_writer[w])
                deps.update(readers.get(w, ()))
            deps.discard(i)
            o["deps"] = deps
            for r in o["reads"]:
                readers.setdefault(r, []).append(i)
            for w in o["writes"]:
                last_writer[w] = i
                readers[w] = []
        eidx = {e: 0 for e in ENGS}
        dcount = {e: 0 for e in ENGS}
        for o in ops:
            o["eidx"] = eidx[o["eng"]]
            eidx[o["eng"]] += 1
            o["marked"] = False
            if o["dma"]:
                o["dma_n"] = dcount[o["eng"]]
                dcount[o["eng"]] += 1
        R_ = self.dma_ring
        for i, o in enumerate(ops):
            E = o["eng"]
            need = []
            best = {}
            bestd = {}
            for j in o["deps"]:
                p = ops[j]
                if p["dma"]:
                    k = (p["eng"], p["dma_n"] % R_)
                    if k not in bestd or ops[bestd[k]]["dma_n"] < p["dma_n"]:
                        bestd[k] = j
                else:
                    k = p["eng"]
                    if k not in best or ops[best[k]]["eidx"] < p["eidx"]:
                        best[k] = j
            need.extend(bestd.values())
            for k, j in best.items():
                p = ops[j]
                if k != E:
                    need.append(j)
                    p["marked"] = True
                else:
                    if E == "pe" and not o["dma"]:
                        continue
                    need.append(j)
                    p["marked"] = True
            o["need"] = need
        cnt = {e: 0 for e in ENGS}
        for o in ops:
            if o["marked"]:
                cnt[o["eng"]] += 1
                o["cnt"] = cnt[o["eng"]]
        R = self.dma_ring
        with contextlib.ExitStack() as st:
            esem = {e: st.enter_context(nc.semaphore("s_" + e)) for e in ENGS}
            dsem = {e: [st.enter_context(nc.semaphore("d_%s_%d" % (e, k))) for k in range(R)]
                    for e in ("sp", "act", "pool") if dcount[e] > 0}
            block = st.enter_context(nc.Block())

            def run_engine(E, eng):
                waited = {}

                def wait(sem, val):
                    key = id(sem)
                    if waited.get(key, 0) >= val:
                        return
                    waited[key] = val
                    eng.wait_ge(sem, val)

                for o in ops:
                    if o["eng"] != E:
                        continue
                    for j in o["need"]:
                        p = ops[j]
                        if p["dma"]:
                            n = p["dma_n"]
                            wait(dsem[p["eng"]][n % R], 16 * (n // R + 1))
                        else:
                            wait(esem[p["eng"]], p["cnt"])
                    if o["dma"]:
                        n = o["dma_n"]
                        s = dsem[E][n % R]
                        if n // R > 0:
                            wait(s, 16 * (n // R))
                        o["fn"](eng).then_inc(s, 16)
                    else:
                        ins = o["fn"](eng)
                        if o["marked"]:
                            ins.then_inc(esem[E], 1)
                if E in dsem:
                    tot = dcount[E]
                    for k in range(R):
                        uses = (tot - k + R - 1) // R if tot > k else 0
                        if uses > 0:
                            wait(dsem[E][k], 16 * uses)

            @block.tensor
            def _(eng):
                run_engine("pe", eng)

            @block.scalar
            def _(eng):
                run_engine("act", eng)

            @block.vector
            def _(eng):
                run_engine("dve", eng)

            @block.gpsimd
            def _(eng):
                run_engine("pool", eng)

            @block.sync
            def _(eng):
                run_engine("sp", eng)


DEBUG = False


def build_nc():
    nc = bass.Bass("TRN2", target_bir_lowering=False)
    dr = lambda name, shape, kind="ExternalInput", dt=F32: nc.dram_tensor(name, shape, dt, kind=kind).ap()
    xs = dr("xs", [NBLK * NT, D])
    out = dr("out", [2048, D], kind="ExternalOutput")
    W = {}
    for nm, shp in [("g_ffn1", [D]), ("ffn1_w_gate", [D, DFF]), ("ffn1_w_up", [D, DFF]), ("ffn1_w_down", [DFF, D]),
                    ("g_mix", [D]), ("w_in", [D, 4096]), ("b_gate", [2048]), ("ssm_a_re", [32, 64]),
                    ("ssm_a_im", [32, 64]), ("ssm_log_dt", [32]), ("ssm_b_re", [32, 64, 16]),
                    ("ssm_b_im", [32, 64, 16]), ("ssm_c_re", [512, 64]), ("ssm_c_im", [512, 64]),
                    ("ssm_d", [512]), ("ssm_w_glu", [512, 2048]), ("conv_w", [3, 512]),
                    ("conv_w_out", [512, D]), ("w_o", [D, D]), ("g_ffn2", [D]), ("ffn2_w_gate", [D, DFF]),
                    ("ffn2_w_up", [D, DFF]), ("ffn2_w_down", [DFF, D]), ("g_final", [D])]:
        W[nm] = dr(nm, shp)

    P = Prog(nc)
    nc_allow = nc.allow_non_contiguous_dma(reason="small parameter loads")
    with contextlib.ExitStack() as st:
        st.enter_context(nc_allow)

        uid = [0]

        def sb(name, shape, dt, stack=st):
            uid[0] += 1
            return stack.enter_context(nc.sbuf_tensor("%s_%d" % (name, uid[0]), shape, dt))

        ps = [st.enter_context(nc.psum_tensor("ps%d" % i, [128, 512], F32)) for i in range(8)]

        def tap(name, ap, keys, dt=F32):
            if not DEBUG:
                return
            shape = list(ap.shape)
            d = nc.dram_tensor("dbg_" + name, shape, dt, kind="ExternalOutput").ap()
            P.dma("sp", lambda e: e.dma_start(out=d, in_=ap), r=keys, w=[("dbg", name)])

        HTK = [("hT", dt_, s_) for dt_ in range(8) for s_ in range(NSB)]
        psk = lambda i: ("ps", i)

        hT = sb("hT", [128, 8, NT], F32)
        ident = sb("ident", [128, 128], F32)
        dummy = sb("dummy", [128, 1], F32)
        dummy2 = sb("dummy2", [128, 1], F32)
        SLOT = 9216
        WS = sb("WS", [128, 2 * SLOT], BF16)
        wsg = [WS[:, i * SLOT: i * SLOT + 3072].rearrange("p (k n) -> p k n", k=8) for i in range(2)]
        wsu = [WS[:, i * SLOT + 3072: i * SLOT + 6144].rearrange("p (k n) -> p k n", k=8) for i in range(2)]
        wsd = [WS[:, i * SLOT + 6144: i * SLOT + 9216].rearrange("p (f n) -> p f n", f=3) for i in range(2)]
        ALLWS = [("WS", i, j) for i in range(2) for j in range(3)]
        uT = sb("uT", [128, 4, NT], BF16)
        junk = [sb("junk%d" % i, [128, TC], F32) for i in range(2)]
        acc = sb("acc", [128, 4, 16], F32)
        ctm = sb("ctm", [128, 6, 16], F32)
        rows3 = lambda ap: ap.rearrange("(kc p) n -> p kc n", p=128)
        ones_bf = sb("ones_bf", [128, 128], BF16)
        gains = sb("gains", [128, 4, 8], F32)
        bgate = sb("bgate", [128, 16], F32)
        dsk = sb("dsk", [128, 4], F32)
        cw = sb("cw", [128, 4, 3], F32)
        mag = sb("mag", [128, 16], F32)
        carry = sb("carry", [128, 2, 16], F32)
        glast = sb("glast", [128, 2, 16], F32)
        th = sb("th", [128, 16], F32)
        adr = sb("adr", [128, 16], F32)
        lamN = sb("lamN", [128, 2, 16], F32)
        cosN = sb("cosN", [128, 16], F32)
        sinN = sb("sinN", [128, 16], F32)
        Ec = sb("Ec", [128, 16, TC], F32)
        Es = sb("Es", [128, 16, TC], F32)
        BqPad = sb("BqPad", [128, 16, 2, 128], BF16)
        CPad = sb("CPad", [128, 16, 2, 128], BF16)

        P.pool(lambda e: e.memset(ident[:], 0.0), w=["ident"])
        P.pool(lambda e: e.affine_select(out=ident[:], in_=ident[:], compare_op=ALU.not_equal, fill=1.0,
                                         base=0, pattern=[[-1, 128]], channel_multiplier=1), r=["ident"], w=["ident"])
        P.dve(lambda e: e.memset(ones_bf[:], 1.0), w=["ones"])
        for i, nm in enumerate(["g_ffn1", "g_mix", "g_ffn2", "g_final"]):
            P.dma("sp", lambda e, i=i, nm=nm: e.dma_start(out=gains[:, i, :], in_=W[nm].rearrange("(t p) -> p t", p=128)),
                  w=[("gains", i)])
        P.dma("sp", lambda e: e.dma_start(out=bgate[:], in_=W["b_gate"].rearrange("(t p) -> p t", p=128)), w=["bgate"])
        P.dma("sp", lambda e: e.dma_start(out=dsk[:], in_=W["ssm_d"].rearrange("(t p) -> p t", p=128)), w=["dsk"])
        for j in range(3):
            P.dma("sp", lambda e, j=j: e.dma_start(out=cw[:, :, j], in_=W["conv_w"][j, :].rearrange("(t p) -> p t", p=128)), w=["cw"])

        with contextlib.ExitStack() as s2:
            t = lambda name, shape, dt=F32: sb(name, shape, dt, s2)
            are = t("are", [128, 16]); aim = t("aim", [128, 16]); ldt = t("ldt", [128, 16])
            bre = t("bre", [128, 16, 16]); bim = t("bim", [128, 16, 16])
            cn = t("cn", [128, 2, 4, 2, 64])
            CP = t("CP", [128, 2, 16, 16])
            dtv = t("dtv", [128, 16])
            cs = t("cs", [128, 16]); sn = t("sn", [128, 16])
            lre = t("lre", [128, 16]); lim = t("lim", [128, 16])
            den = t("den", [128, 16]); tmpa = t("tmpa", [128, 16]); tmpb = t("tmpb", [128, 16])
            qre = t("qre", [128, 16]); qim = t("qim", [128, 16]); lm1 = t("lm1", [128, 16])
            bq = t("bq", [128, 2, 16, 16]); tb = t("tb", [128, 16])
            angN = t("angN", [128, 16])
            wkn1 = t("wkn1", [128, 16]); wkn2 = t("wkn2", [128, 16]); wkni = t("wkni", [128, 16], I32)

            pair_view = lambda ap: ap.rearrange("(gp two) p -> (two p) gp", two=2)
            P.dma("sp", lambda e: e.dma_start(out=are[:], in_=pair_view(W["ssm_a_re"])), w=["are"])
            P.dma("sp", lambda e: e.dma_start(out=aim[:], in_=pair_view(W["ssm_a_im"])), w=["aim"])
            for gpar in range(2):
                P.dma("sp", lambda e, gpar=gpar: e.dma_start(
                    out=ldt[gpar * 64:(gpar + 1) * 64, :],
                    in_=W["ssm_log_dt"].rearrange("(g two) -> two g", two=2)[gpar:gpar + 1, :].to_broadcast([64, 16])),
                    w=[("ldt", gpar)])
            bview = lambda ap: ap.rearrange("(gp two) p c -> (two p) gp c", two=2)
            P.dma("sp", lambda e: e.dma_start(out=bre[:], in_=bview(W["ssm_b_re"])), w=["bre"])
            P.dma("sp", lambda e: e.dma_start(out=bim[:], in_=bview(W["ssm_b_im"])), w=["bim"])
            for ri, nm in enumerate(["ssm_c_re", "ssm_c_im"]):
                for dup in range(2):
                    P.dma("sp", lambda e, ri=ri, nm=nm, dup=dup: e.dma_start(
                        out=cn[:, ri, :, dup, :], in_=W[nm].rearrange("(t p) s -> p t s", p=128)), w=[("cn", ri, dup)])
            for ri in range(2):
                for ct in range(4):
                    bank = 6 + (ct % 2)
                    P.pe(lambda e, ri=ri, ct=ct, bank=bank: e.transpose(
                        out=ps[bank][:, 0:128], in_=cn[:, ri, ct, :, :].rearrange("p a b -> p (a b)"), identity=ident[:]),
                        r=[("cn", ri, 0), ("cn", ri, 1), "ident"], w=[psk(bank)])
                    for gpar in range(2):
                        P.act(lambda e, ri=ri, ct=ct, bank=bank, gpar=gpar: e.activation(
                            out=CP[gpar * 64:(gpar + 1) * 64, ri, ct * 4:(ct + 1) * 4, :],
                            in_=ps[bank][gpar * 64:(gpar + 1) * 64, 0:128].rearrange(
                                "p (gl two c) -> p gl two c", two=2, c=16)[:, :, gpar, :],
                            func=AF.Identity, scale=(1.0 if ri == 0 else -1.0)),
                            r=[psk(bank)], w=[("CP", ri, ct, gpar)])
            cp_keys = [("CP", ri, ct, gpar) for ri in range(2) for ct in range(4) for gpar in range(2)]
            ldk = [("ldt", 0), ("ldt", 1)]
            P.act(lambda e: e.activation(out=dtv[:], in_=ldt[:], func=AF.Exp), r=ldk, w=["dtv"])
            P.dve(lambda e: e.tensor_tensor(out=adr[:], in0=are[:], in1=dtv[:], op=ALU.mult), r=["are", "dtv"], w=["adr"])
            P.dve(lambda e: e.tensor_tensor(out=th[:], in0=aim[:], in1=dtv[:], op=ALU.mult), r=["aim", "dtv"], w=["th"])
            P.act(lambda e: e.activation(out=mag[:], in_=adr[:], func=AF.Exp), r=["adr"], w=["mag"])

            def sincos(x, xk, o_sin, o_cos, w1, w2, wi, keys):
                for off, o in ((0.0, o_sin), (PI / 2, o_cos)):
                    ok = keys[0] if o is o_sin else keys[1]
                    P.dve(lambda e, off=off: e.tensor_scalar(out=w1, in0=x, scalar1=off, scalar2=1.0 / (2 * PI),
                                                             op0=ALU.add, op1=ALU.mult), r=[xk], w=["w1" + keys[2]])
                    P.dve(lambda e: e.tensor_copy(out=wi, in_=w1), r=["w1" + keys[2]], w=["wi" + keys[2]])
                    P.dve(lambda e: e.tensor_copy(out=w1, in_=wi), r=["wi" + keys[2]], w=["w1" + keys[2]])
                    P.dve(lambda e: e.scalar_tensor_tensor(out=w2, in0=w1, scalar=-6.25, in1=x,
                                                           op0=ALU.mult, op1=ALU.add), r=["w1" + keys[2], xk], w=["w2" + keys[2]])
                    P.dve(lambda e: e.scalar_tensor_tensor(out=w2, in0=w1, scalar=-(2 * PI - 6.25), in1=w2,
                                                           op0=ALU.mult, op1=ALU.add), r=["w1" + keys[2], "w2" + keys[2]], w=["w2" + keys[2]])
                    P.dve(lambda e, off=off: e.tensor_scalar(out=w2, in0=w2, scalar1=off, scalar2=None, op0=ALU.add),
                          r=["w2" + keys[2]], w=["w2" + keys[2]])
                    for lim_, sgn in ((PI, -1.0), (-PI, 1.0)):
                        cmp = ALU.is_gt if sgn < 0 else ALU.is_lt
                        P.dve(lambda e, lim_=lim_, cmp=cmp: e.tensor_scalar(out=w1, in0=w2, scalar1=lim_, scalar2=None, op0=cmp),
                              r=["w2" + keys[2]], w=["w1" + keys[2]])
                        P.dve(lambda e, sgn=sgn: e.scalar_tensor_tensor(out=w2, in0=w1, scalar=sgn * 2 * PI, in1=w2,
                                                                        op0=ALU.mult, op1=ALU.add),
                              r=["w1" + keys[2], "w2" + keys[2]], w=["w2" + keys[2]])
                    P.act(lambda e, o=o: e.activation(out=o, in_=w2, func=AF.Sin), r=["w2" + keys[2]], w=[ok])

            sincos(th[:], "th", sn[:], cs[:], wkn1[:], wkn2[:], wkni[:], ("sn", "cs", "n"))
            P.dve(lambda e: e.tensor_tensor(out=lre[:], in0=mag[:], in1=cs[:], op=ALU.mult), r=["mag", "cs"], w=["lre"])
            P.dve(lambda e: e.tensor_tensor(out=lim[:], in0=mag[:], in1=sn[:], op=ALU.mult), r=["mag", "sn"], w=["lim"])
            P.dve(lambda e: e.tensor_tensor(out=den[:], in0=are[:], in1=are[:], op=ALU.mult), r=["are"], w=["den"])
            P.dve(lambda e: e.tensor_tensor(out=tmpa[:], in0=aim[:], in1=aim[:], op=ALU.mult), r=["aim"], w=["tmpa"])
            P.dve(lambda e: e.tensor_tensor(out=den[:], in0=den[:], in1=tmpa[:], op=ALU.add), r=["den", "tmpa"], w=["den"])
            P.dve(lambda e: e.reciprocal(out=den[:], in_=den[:]), r=["den"], w=["den"])
            P.dve(lambda e: e.tensor_scalar(out=lm1[:], in0=lre[:], scalar1=-1.0, scalar2=None, op0=ALU.add), r=["lre"], w=["lm1"])
            P.dve(lambda e: e.tensor_tensor(out=tmpa[:], in0=lm1[:], in1=are[:], op=ALU.mult), r=["lm1", "are", "den"], w=["tmpa"])
            P.dve(lambda e: e.tensor_tensor(out=tmpb[:], in0=lim[:], in1=aim[:], op=ALU.mult), r=["lim", "aim"], w=["tmpb"])
            P.dve(lambda e: e.tensor_tensor(out=tmpa[:], in0=tmpa[:], in1=tmpb[:], op=ALU.add), r=["tmpa", "tmpb"], w=["tmpa"])
            P.dve(lambda e: e.tensor_tensor(out=qre[:], in0=tmpa[:], in1=den[:], op=ALU.mult), r=["tmpa", "den"], w=["qre"])
            P.dve(lambda e: e.tensor_tensor(out=tmpa[:], in0=lim[:], in1=are[:], op=ALU.mult), r=["lim", "are", "qre"], w=["tmpa"])
            P.dve(lambda e: e.tensor_tensor(out=tmpb[:], in0=lm1[:], in1=aim[:], op=ALU.mult), r=["lm1", "aim"], w=["tmpb"])
            P.dve(lambda e: e.tensor_tensor(out=tmpa[:], in0=tmpa[:], in1=tmpb[:], op=ALU.subtract), r=["tmpa", "tmpb"], w=["tmpa"])
            P.dve(lambda e: e.tensor_tensor(out=qim[:], in0=tmpa[:], in1=den[:], op=ALU.mult), r=["tmpa", "den"], w=["qim"])
            qb = lambda q: q[:, :].unsqueeze(2).to_broadcast([128, 16, 16])
            P.dve(lambda e: e.tensor_tensor(out=bq[:, 0], in0=bre[:], in1=qb(qre), op=ALU.mult), r=["bre", "qre"], w=["bq0"])
            P.dve(lambda e: e.tensor_tensor(out=bq[:, 1], in0=bim[:], in1=qb(qim), op=ALU.mult), r=["bim", "qim"], w=["bq1"])
            P.dve(lambda e: e.tensor_tensor(out=bq[:, 0], in0=bq[:, 0], in1=bq[:, 1], op=ALU.subtract), r=["bq0", "bq1"], w=["bq0"])
            P.dve(lambda e: e.tensor_tensor(out=bq[:, 1], in0=bim[:], in1=qb(qre), op=ALU.mult), r=["bim", "qre", "bq0"], w=["bq1"])
            P.dve(lambda e: e.tensor_tensor(out=bre[:], in0=bre[:], in1=qb(qim), op=ALU.mult), r=["bre", "qim", "bq0"], w=["bre"])
            P.dve(lambda e: e.tensor_tensor(out=bq[:, 1], in0=bq[:, 1], in1=bre[:], op=ALU.add), r=["bq1", "bre"], w=["bq1"])
            sz = contextlib.ExitStack()
            sz.__enter__()
            Z = sb("Z", [128, 16, 2, 128], F32, sz)
            P.pool(lambda e: e.memset(Z[:], 0.0), w=["Z"])
            P.pool(lambda e: e.memset(CPad[:], 0.0), w=["CPad"])
            for ri in range(2):
                for gpar in range(2):
                    for gl in range(4):
                        col = (2 * gl + gpar) * 16
                        rows = slice(gpar * 64, (gpar + 1) * 64)
                        P.dve(lambda e, ri=ri, rows=rows, gl=gl, col=col: e.tensor_copy(
                            out=Z[rows, gl::4, ri, col:col + 16], in_=bq[rows, ri, gl::4, :]),
                            r=["bq0", "bq1", "Z"], w=["Z"])
                        P.act(lambda e, ri=ri, rows=rows, gl=gl, col=col: e.activation(
                            out=CPad[rows, gl::4, ri, col:col + 16], in_=CP[rows, ri, gl::4, :], func=AF.Copy),
                            r=cp_keys + ["CPad"], w=["CPad"])
            for gp in range(16):
                for ri in range(2):
                    bank = 6 + ((gp * 2 + ri) % 2)
                    P.pe(lambda e, gp=gp, ri=ri, bank=bank: e.transpose(out=ps[bank][:, 0:128], in_=Z[:, gp, ri, :], identity=ident[:]),
                         r=["Z", "ident"], w=[psk(bank)])
                    P.act(lambda e, gp=gp, ri=ri, bank=bank: e.activation(out=BqPad[:, gp, ri, :], in_=ps[bank][:, 0:128], func=AF.Copy),
                          r=[psk(bank)], w=["BqPad"])
            sz.__exit__(None, None, None)
            P.barrier(dummy[:])
            P.dve(lambda e: e.tensor_scalar(out=angN[:], in0=th[:], scalar1=float(TC), scalar2=None, op0=ALU.mult), r=["th"], w=["angN"])
            sincos(angN[:], "angN", sinN[:], cosN[:], wkn1[:], wkn2[:], wkni[:], ("sinN", "cosN", "n"))
            P.act(lambda e: e.activation(out=tmpa[:], in_=adr[:], func=AF.Exp, scale=float(TC)), r=["adr", "qre", "qim"], w=["tmpa"])
            P.dve(lambda e: e.tensor_tensor(out=lamN[:, 0, :], in0=tmpa[:], in1=cosN[:], op=ALU.mult), r=["tmpa", "cosN"], w=["lamN"])
            P.dve(lambda e: e.tensor_tensor(out=lamN[:, 1, :], in0=tmpa[:], in1=sinN[:], op=ALU.mult), r=["tmpa", "sinN"], w=["lamN"])
            P.dve(lambda e: e.memset(carry[:], 0.0), w=["carry"])
            tap("mag", mag[:], ["mag"]); tap("lre", lre[:], ["lre"]); tap("lim", lim[:], ["lim"])
            tap("qre", qre[:], ["qre"]); tap("qim", qim[:], ["qim"]); tap("th", th[:], ["th"])
            tap("BqPad", BqPad[:], ["BqPad"], BF16); tap("CPad", CPad[:], ["CPad"], BF16)
            tap("CP", CP[:], cp_keys); tap("bq", bq[:], ["bq0", "bq1"]); tap("ident", ident[:], ["ident"])

        def make_tables(mode):
            P.barrier(dummy[:])
            with contextlib.ExitStack() as sx:
                iot = sb("iot", [128, TC], F32, sx); ioti = sb("ioti", [128, TC], I32, sx)
                ang = sb("ang", [128, 16, TC], F32, sx); wk1 = sb("wk1", [128, 16, TC], F32, sx)
                wk2 = sb("wk2", [128, 16, TC], F32, sx); wki = sb("wki", [128, 16, TC], I32, sx)
                if mode == "E":
                    P.pool(lambda e: e.iota(ioti[:], pattern=[[1, TC]], base=1, channel_multiplier=0), w=["ioti"])
                else:
                    P.pool(lambda e: e.iota(ioti[:], pattern=[[-1, TC]], base=TC - 1, channel_multiplier=0), w=["ioti"])
                P.dve(lambda e: e.tensor_copy(out=iot[:], in_=ioti[:]), r=["ioti"], w=["iot"])
                for gp in range(16):
                    P.dve(lambda e, gp=gp: e.tensor_scalar(out=ang[:, gp, :], in0=iot[:], scalar1=th[:, gp:gp + 1], scalar2=None,
                                                           op0=ALU.mult), r=["iot", "th"], w=["ang"])
                sincos(ang[:], "ang", Es[:], Ec[:], wk1[:], wk2[:], wki[:], ("Es", "Ec", "t"))
                if mode == "D":
                    for gp in range(16):
                        P.act(lambda e, gp=gp: e.activation(out=ang[:, gp, :], in_=iot[:], func=AF.Exp, scale=adr[:, gp:gp + 1]),
                              r=["iot", "adr", "Es", "Ec"], w=[("magp", gp)])
                    mk = [("magp", gp) for gp in range(16)]
                    P.dve(lambda e: e.tensor_tensor(out=Ec[:], in0=Ec[:], in1=ang[:], op=ALU.mult), r=["Ec"] + mk, w=["Ec"])
                    P.dve(lambda e: e.tensor_tensor(out=Es[:], in0=Es[:], in1=ang[:], op=ALU.mult), r=["Es"] + mk, w=["Es"])
                tap("Ec_" + mode, Ec[:], ["Ec"]); tap("Es_" + mode, Es[:], ["Es"])

        make_tables("D")

        cw_scr = nc.dram_tensor("cw_scr", [8, 128, 3584], BF16).ap()
        glu_v = W["ssm_w_glu"].rearrange("(kc p) n -> p kc n", p=128)
        co_v = W["conv_w_out"].rearrange("(kc p) n -> p kc n", p=128)
        win_v = W["w_in"].rearrange("(kc p) n -> p kc n", p=128)
        for dt_ in range(8):
            for j in range(2):
                P.dma("pool", lambda e, j=j, dt_=dt_: e.dma_start(
                    out=cw_scr[dt_, :, 0:1024].rearrange("p (k j n) -> p k j n", k=4, j=2)[:, :, j, :],
                    in_=glu_v[:, :, j * 1024 + dt_ * 128: j * 1024 + (dt_ + 1) * 128]), w=[("cwscr", dt_, "gl", j)], epoch=False)
                P.dma("pool", lambda e, j=j, dt_=dt_: e.dma_start(
                    out=cw_scr[dt_, :, 1536:3584].rearrange("p (k j n) -> p k j n", k=8, j=2)[:, :, j, :],
                    in_=win_v[:, :, 2048 + j * 1024 + dt_ * 128: 2048 + j * 1024 + (dt_ + 1) * 128]), w=[("cwscr", dt_, "ga", j)], epoch=False)
            P.dma("pool", lambda e, dt_=dt_: e.dma_start(
                out=cw_scr[dt_, :, 1024:1536].rearrange("p (k n) -> p k n", k=4),
                in_=co_v[:, :, dt_ * 128:(dt_ + 1) * 128]), w=[("cwscr", dt_, "co")], epoch=False)

        def norm_sb(s_, gi, xn_ap_fn, sq, sd, rstd, keyfn, pbank=6, tag=0):
            tok = slice(s_ * SB, (s_ + 1) * SB)
            for dt_ in range(8):
                P.act(lambda e, dt_=dt_: e.activation(out=sq[dt_ % 2][:], in_=hT[:, dt_, tok], func=AF.Square),
                      r=[("hT", dt_, s_)], w=[("sq", dt_ % 2)])
                P.pe(lambda e, dt_=dt_: e.matmul(ps[pbank][:, 0:SB], lhsT=ones_bf[:], rhs=sq[dt_ % 2][:],
                                                 start=(dt_ == 0), stop=(dt_ == 7)),
                     r=[("sq", dt_ % 2), "ones"], w=[psk(pbank)])
            P.act(lambda e: e.activation(out=sd[:], in_=ps[pbank][:, 0:SB], func=AF.Sqrt, scale=1.0 / D, bias=EPS),
                  r=[psk(pbank)], w=[("sd", tag)])
            P.dve(lambda e: e.reciprocal(out=rstd[:], in_=sd[:]), r=[("sd", tag)], w=[("rstd", tag)])
            for dt_ in range(8):
                P.dve(lambda e, dt_=dt_: e.scalar_tensor_tensor(out=xn_ap_fn(dt_), in0=hT[:, dt_, tok],
                                                                scalar=gains[:, gi, dt_:dt_ + 1], in1=rstd[:],
                                                                op0=ALU.mult, op1=ALU.mult),
                      r=[("hT", dt_, s_), ("rstd", tag), ("gains", gi)], w=[keyfn(dt_)])

        def ffn(blk, gi, wg, wu, wd, bg=None):
            P.barrier(dummy[:])
            with contextlib.ExitStack() as s3:
                xn = sb("xn", [128, 8, NT], BF16, s3)
                sq = [sb("sq%d" % i, [128, SB], BF16, s3) for i in range(2)]
                sd = sb("sd", [128, SB], F32, s3)
                rstd = sb("rstd", [128, SB], F32, s3)
                actb = [sb("actb%d" % i, [128, 3, SB], BF16, s3) for i in range(2)]
                sg = [sb("sg%d" % i, [128, SB], F32, s3) for i in range(2)]
                for s_ in range(NSB):
                    norm_sb(s_, gi, lambda dt_, s_=s_: xn[:, dt_, s_ * SB:(s_ + 1) * SB], sq, sd, rstd,
                            lambda dt_, s_=s_: ("xn", dt_, s_))
                cnt = 0
                prev_down = [None]

                def emit_down(nf, sl, ab, s_, tok):
                    for dt_ in range(8):
                        pb = 4 + (dt_ % 2)
                        for fi in range(nf):
                            P.pe(lambda e, fi=fi, dt_=dt_, pb=pb, sl=sl, ab=ab, nf=nf: e.matmul(
                                ps[pb][:, 0:SB], lhsT=wsd[sl][:, fi, dt_ * 128:(dt_ + 1) * 128], rhs=actb[ab][:, fi, :],
                                start=(fi == 0), stop=(fi == nf - 1)),
                                r=[("WS", sl, 2), ("actb", ab, fi)], w=[psk(pb)])
                        P.dve(lambda e, dt_=dt_, pb=pb, tok=tok: e.scalar_tensor_tensor(
                            out=hT[:, dt_, tok], in0=ps[pb][:, 0:SB], scalar=0.5, in1=hT[:, dt_, tok],
                            op0=ALU.mult, op1=ALU.add),
                            r=[psk(pb), ("hT", dt_, s_)], w=[("hT", dt_, s_)])
                        if bg is not None and dt_ % 4 == 3:
                            next(bg, None)

                for gix, (f0, nf) in enumerate(GROUPS):
                    sl = gix % 2
                    P.dma("pool", lambda e, f0=f0, nf=nf, sl=sl: e.dma_start(
                        out=wsg[sl][:, :, 0:nf * 128], in_=rows3(wg)[:, :, f0 * 128:(f0 + nf) * 128]),
                        w=[("WS", sl, 0)], epoch=False)
                    P.dma("pool", lambda e, f0=f0, nf=nf, sl=sl: e.dma_start(
                        out=wsu[sl][:, :, 0:nf * 128], in_=rows3(wu)[:, :, f0 * 128:(f0 + nf) * 128]),
                        w=[("WS", sl, 1)], epoch=False)
                    P.dma("pool", lambda e, f0=f0, nf=nf, sl=sl: e.dma_start(
                        out=wsd[sl][:, 0:nf, :], in_=rows3(wd)[:, f0:f0 + nf, :]),
                        w=[("WS", sl, 2)], epoch=False)
                    for s_ in range(NSB):
                        tok = slice(s_ * SB, (s_ + 1) * SB)
                        ab = (gix * NSB + s_) % 2
                        for fi in range(nf):
                            par = cnt % 2
                            cnt += 1
                            for kc in range(8):
                                P.pe(lambda e, kc=kc, fi=fi, par=par, sl=sl, tok=tok: e.matmul(
                                    ps[par][:, 0:SB], lhsT=wsg[sl][:, kc, fi * 128:(fi + 1) * 128], rhs=xn[:, kc, tok],
                                    start=(kc == 0), stop=(kc == 7)),
                                    r=[("WS", sl, 0), ("xn", kc, s_)], w=[psk(par)])
                            for kc in range(8):
                                P.pe(lambda e, kc=kc, fi=fi, par=par, sl=sl, tok=tok: e.matmul(
                                    ps[2 + par][:, 0:SB], lhsT=wsu[sl][:, kc, fi * 128:(fi + 1) * 128], rhs=xn[:, kc, tok],
                                    start=(kc == 0), stop=(kc == 7)),
                                    r=[("WS", sl, 1), ("xn", kc, s_)], w=[psk(2 + par)])
                            P.act(lambda e, par=par: e.activation(out=sg[par][:], in_=ps[par][:, 0:SB], func=AF.Silu),
                                  r=[psk(par)], w=[("sg", par)])
                            P.dve(lambda e, par=par, ab=ab, fi=fi: e.tensor_tensor(
                                out=actb[ab][:, fi, :], in0=ps[2 + par][:, 0:SB], in1=sg[par][:], op=ALU.mult),
                                r=[psk(2 + par), ("sg", par)], w=[("actb", ab, fi)])
                            if bg is not None:
                                next(bg, None)
                        if prev_down[0] is not None:
                            prev_down[0]()
                        prev_down[0] = (lambda nf=nf, sl=sl, ab=ab, s_=s_, tok=tok: emit_down(nf, sl, ab, s_, tok))
                if prev_down[0] is not None:
                    prev_down[0]()
                if bg is not None:
                    for _ in bg:
                        pass

        pending = []

        def ssm_hist():
            for c in range(NT // TC):
                tk = slice(c * TC, (c + 1) * TC)
                s_ = (c * TC) // SB
                for gp in range(16):
                    ct = gp // 4
                    bank = 6 + (gp % 2)
                    PA, PB = ps[bank][:, 0:TC], ps[bank][:, TC:2 * TC]
                    P.pe(lambda e, gp=gp, ct=ct, PA=PA, tk=tk: e.matmul(PA, lhsT=BqPad[:, gp, 0, :], rhs=uT[:, ct, tk], start=True, stop=True),
                         r=["BqPad", ("uT", ct, s_)], w=[psk(bank)])
                    P.pe(lambda e, gp=gp, ct=ct, PB=PB, tk=tk: e.matmul(PB, lhsT=BqPad[:, gp, 1, :], rhs=uT[:, ct, tk], start=True, stop=True),
                         r=["BqPad", ("uT", ct, s_)], w=[psk(bank)])
                    for k, (src, tab, tkey) in enumerate([(PA, Ec, "Ec"), (PB, Es, "Es"), (PA, Es, "Es"), (PB, Ec, "Ec")]):
                        P.dve(lambda e, gp=gp, k=k, src=src, tab=tab: e.scalar_tensor_tensor(
                            out=junk[k % 2][:], in0=src, scalar=1.0, in1=tab[:, gp, :],
                            op0=ALU.mult, op1=ALU.mult, accum_out=acc[:, k, gp:gp + 1]),
                            r=[psk(bank), tkey], w=[("junk", k % 2), ("acc", k, gp)])
                    yield
                ak = [("acc", k, gp) for k in range(4) for gp in range(16)]
                TT = lambda o, a_, b_, op: (lambda e: e.tensor_tensor(out=o, in0=a_, in1=b_, op=op))
                P.dve(TT(ctm[:, 0, :], acc[:, 0, :], acc[:, 1, :], ALU.subtract), r=ak, w=["ctm0"])
                P.dve(TT(ctm[:, 1, :], acc[:, 2, :], acc[:, 3, :], ALU.add), r=ak, w=["ctm1"])
                P.dve(TT(ctm[:, 2, :], lamN[:, 0, :], carry[:, 0, :], ALU.mult), r=["lamN", "carry"], w=["ctm2"])
                P.dve(TT(ctm[:, 3, :], lamN[:, 1, :], carry[:, 1, :], ALU.mult), r=["lamN", "carry"], w=["ctm3"])
                P.dve(TT(ctm[:, 4, :], lamN[:, 0, :], carry[:, 1, :], ALU.mult), r=["lamN", "carry"], w=["ctm4"])
                P.dve(TT(ctm[:, 5, :], lamN[:, 1, :], carry[:, 0, :], ALU.mult), r=["lamN", "carry"], w=["ctm5"])
                P.dve(TT(ctm[:, 2, :], ctm[:, 2, :], ctm[:, 3, :], ALU.subtract), r=["ctm2", "ctm3"], w=["ctm2"])
                P.dve(TT(ctm[:, 4, :], ctm[:, 4, :], ctm[:, 5, :], ALU.add), r=["ctm4", "ctm5"], w=["ctm4"])
                P.dve(TT(carry[:, 0, :], ctm[:, 2, :], ctm[:, 0, :], ALU.add), r=["ctm2", "ctm0", "ctm4"], w=["carry"])
                P.dve(TT(carry[:, 1, :], ctm[:, 4, :], ctm[:, 1, :], ALU.add), r=["ctm4", "ctm1"], w=["carry"])
                yield

        def do_block(blk):
            full = (blk == NBLK - 1)
            P.barrier(dummy[:])
            with contextlib.ExitStack() as s3:
                xst = [sb("xst%d" % i, [128, D], F32, s3) for i in range(6)]
                ntile = (NT + 127) // 128
                for i in range(ntile):
                    rows = min(128, NT - i * 128)
                    xb = i % 6
                    P.dma("sp", lambda e, i=i, rows=rows, xb=xb: e.dma_start(
                        out=xst[xb][0:rows, :], in_=xs[blk * NT + i * 128: blk * NT + i * 128 + rows, :]),
                        w=[("xst", xb)])
                    sbs = sorted(set([(i * 128) // SB, (i * 128 + rows - 1) // SB]))
                    for h in range(2):
                        bank = 6 + h
                        for q in range(4):
                            dt_ = h * 4 + q
                            P.pe(lambda e, dt_=dt_, q=q, rows=rows, xb=xb, bank=bank: e.transpose(
                                out=ps[bank][:, q * 128:q * 128 + rows], in_=xst[xb][0:rows, dt_ * 128:(dt_ + 1) * 128],
                                identity=ident[0:rows, 0:rows]),
                                r=[("xst", xb), "ident"], w=[psk(bank)])
                        eng = P.act if h == 0 else P.dve
                        if h == 0:
                            P.act(lambda e, h=h, i=i, rows=rows, bank=bank: e.activation(
                                out=hT[:, h * 4:(h + 1) * 4, i * 128:i * 128 + rows],
                                in_=ps[bank][:, :].rearrange("p (q c) -> p q c", q=4)[:, :, 0:rows], func=AF.Copy),
                                r=[psk(bank)], w=[("hT", dt_, s_) for dt_ in range(h * 4, h * 4 + 4) for s_ in sbs])
                        else:
                            P.dve(lambda e, h=h, i=i, rows=rows, bank=bank: e.tensor_copy(
                                out=hT[:, h * 4:(h + 1) * 4, i * 128:i * 128 + rows],
                                in_=ps[bank][:, :].rearrange("p (q c) -> p q c", q=4)[:, :, 0:rows]),
                                r=[psk(bank)], w=[("hT", dt_, s_) for dt_ in range(h * 4, h * 4 + 4) for s_ in sbs])
            if full:
                tap("hT_load", hT[:], HTK)
            ffn(blk, 0, W["ffn1_w_gate"], W["ffn1_w_up"], W["ffn1_w_down"], bg=(pending.pop() if pending else None))
            if full:
                make_tables("E")
            if full:
                tap("hT_ffn1", hT[:], HTK)
            P.barrier(dummy[:])
            with contextlib.ExitStack() as s3:
                cvT = sb("cvT", [128, 4, NT], BF16, s3) if full else None
                ga = uT
                P.barrier(dummy[:])
                with contextlib.ExitStack() as s4:
                    unA = [sb("un%d" % i, [128, 8, SB], BF16, s4) for i in range(2)]
                    sqA = [sb("sq%d" % i, [128, SB], BF16, s4) for i in range(2)]
                    sdA = [sb("sd%d" % i, [128, SB], F32, s4) for i in range(2)]
                    rstdA = [sb("rstd%d" % i, [128, SB], F32, s4) for i in range(2)]
                    wu_ = WS[:, 0:4096].rearrange("p (k n) -> p k n", k=8)
                    P.dma("pool", lambda e: e.dma_start(out=wu_, in_=rows3(W["w_in"])[:, :, 0:512]), w=ALLWS, epoch=False)
                    if full:
                        wv = WS[:, 4096:16384].rearrange("p (k n) -> p k n", k=8)
                        zs = sb("zs", [128, 4, SB + 2], F32, s4)
                        vv = sb("vv", [128, SB], F32, s4)
                        c1 = sb("c1", [128, SB], F32, s4)
                        P.dma("pool", lambda e: e.dma_start(out=wv, in_=rows3(W["w_in"])[:, :, 512:2048]),
                              w=ALLWS, epoch=False)
                        P.dve(lambda e: e.memset(zs[:], 0.0), w=[("zs", c) for c in range(4)])
                    for s_ in range(NSB):
                        tok = slice(s_ * SB, (s_ + 1) * SB)
                        npar = s_ % 2
                        norm_sb(s_, 1, lambda dt_, npar=npar: unA[npar][:, dt_, :], sqA, sdA[npar], rstdA[npar],
                                lambda dt_, npar=npar: ("un", npar, dt_), pbank=6 + npar, tag=npar)
                        for ot in range(4):
                            pb = 4 + (ot % 2)
                            for kc in range(8):
                                P.pe(lambda e, kc=kc, ot=ot, pb=pb, npar=npar: e.matmul(
                                    ps[pb][:, 0:SB], lhsT=wu_[:, kc, ot * 128:(ot + 1) * 128], rhs=unA[npar][:, kc, :],
                                    start=(kc == 0), stop=(kc == 7)), r=ALLWS + [("un", npar, kc)], w=[psk(pb)])
                            P.act(lambda e, ot=ot, pb=pb, tok=tok: e.activation(out=uT[:, ot, tok], in_=ps[pb][:, 0:SB], func=AF.Copy),
                                  r=[psk(pb)], w=[("uT", ot, s_)])
                        if full:
                            for ot in range(4):
                                for j, bank in ((0, 0), (2, 1), (1, 2)):
                                    for kc in range(8):
                                        P.pe(lambda e, kc=kc, ot=ot, j=j, bank=bank, npar=npar: e.matmul(
                                            ps[bank][:, 0:SB], lhsT=wv[:, kc, j * 512 + ot * 128: j * 512 + (ot + 1) * 128],
                                            rhs=unA[npar][:, kc, :], start=(kc == 0), stop=(kc == 7)),
                                            r=ALLWS + [("un", npar, kc)], w=[psk(bank)])
                                P.act(lambda e: e.activation(out=vv[:], in_=ps[0][:, 0:SB], func=AF.Copy), r=[psk(0)], w=["vv"])
                                P.dve(lambda e, ot=ot: e.tensor_copy(out=zs[:, ot, 0:2], in_=zs[:, ot, SB:SB + 2]),
                                      r=[("zs", ot)], w=[("zs", ot)])
                                P.dve(lambda e, ot=ot: e.tensor_tensor(out=zs[:, ot, 2:SB + 2], in0=ps[1][:, 0:SB], in1=vv[:], op=ALU.mult),
                                      r=[psk(1), "vv", ("zs", ot)], w=[("zs", ot)])
                                P.dve(lambda e, ot=ot: e.tensor_scalar(out=c1[:], in0=zs[:, ot, 0:SB], scalar1=cw[:, ot, 0:1], scalar2=None,
                                                                       op0=ALU.mult), r=[("zs", ot), "cw"], w=["c1"])
                                P.dve(lambda e, ot=ot: e.scalar_tensor_tensor(out=c1[:], in0=zs[:, ot, 1:SB + 1], scalar=cw[:, ot, 1:2], in1=c1[:],
                                                                              op0=ALU.mult, op1=ALU.add), r=[("zs", ot), "cw", "c1"], w=["c1"])
                                P.dve(lambda e, ot=ot: e.scalar_tensor_tensor(out=c1[:], in0=zs[:, ot, 2:SB + 2], scalar=cw[:, ot, 2:3], in1=c1[:],
                                                                              op0=ALU.mult, op1=ALU.add), r=[("zs", ot), "cw", "c1"], w=["c1"])
                                P.dve(lambda e, ot=ot, tok=tok: e.tensor_tensor(out=cvT[:, ot, tok], in0=ps[2][:, 0:SB], in1=c1[:], op=ALU.mult),
                                      r=[psk(2), "c1"], w=[("cvT", ot, s_)])
                if full:
                    tap("uT_A", uT[:], [("uT", c_, s_) for c_ in range(4) for s_ in range(NSB)], BF16)
                if not full:
                    pending.append(ssm_hist())
                    return
                P.barrier(dummy[:])
                with contextlib.ExitStack() as s4:
                    NBUF = 3
                    TN = ["t1", "t2", "t3", "t4", "wre", "wim", "gre", "gim"]
                    T = {}
                    for b_ in range(NBUF):
                        for n_ in TN:
                            T[(n_, b_)] = sb("%s_%d" % (n_, b_), [128, TC], F32, s4)
                        for n_ in ["d1", "d2", "d3", "d4"]:
                            T[(n_, b_)] = sb("%s_%d" % (n_, b_), [128, TC], BF16, s4)
                    ctmp = sb("ctmp", [128, 4, 16], F32, s4)
                    Hh = [sb("Hh%d" % i, [128, 4, 2, TC], BF16, s4) for i in range(2)]
                    ya = [sb("ya%d" % i, [128, TC], F32, s4) for i in range(2)]
                    TTf = lambda o, a_, b_, op: (lambda e: e.tensor_tensor(out=o, in0=a_, in1=b_, op=op))
                    npc = 0
                    for c in range(NT // TC):
                        tk = slice(c * TC, (c + 1) * TC)
                        s_ = (c * TC) // SB
                        for gp in range(16):
                            ct = gp // 4
                            b = npc % NBUF
                            pbk = npc % NBUF
                            npc += 1
                            PA, PB = ps[pbk][:, 0:TC], ps[pbk][:, TC:2 * TC]
                            hb = (c * 4 + ct) % 2
                            K = lambda n_: (n_, b)
                            X = lambda n_: T[(n_, b)][:]
                            P.pe(lambda e, gp=gp, ct=ct, PA=PA, tk=tk: e.matmul(PA, lhsT=BqPad[:, gp, 0, :], rhs=uT[:, ct, tk], start=True, stop=True),
                                 r=["BqPad", ("uT", ct, s_)], w=[psk(pbk)])
                            P.pe(lambda e, gp=gp, ct=ct, PB=PB, tk=tk: e.matmul(PB, lhsT=BqPad[:, gp, 1, :], rhs=uT[:, ct, tk], start=True, stop=True),
                                 r=["BqPad", ("uT", ct, s_)], w=[psk(pbk)])
                            P.dve(TTf(X("t1"), PA, Ec[:, gp, :], ALU.mult), r=[psk(pbk), "Ec"], w=[K("t1")])
                            P.dve(TTf(X("t2"), PB, Es[:, gp, :], ALU.mult), r=[psk(pbk), "Es"], w=[K("t2")])
                            P.dve(TTf(X("t3"), PB, Ec[:, gp, :], ALU.mult), r=[psk(pbk), "Ec"], w=[K("t3")])
                            P.dve(TTf(X("t4"), PA, Es[:, gp, :], ALU.mult), r=[psk(pbk), "Es"], w=[K("t4")])
                            P.pool(TTf(X("wre"), X("t1"), X("t2"), ALU.add), r=[K("t1"), K("t2")], w=[K("wre")])
                            P.pool(TTf(X("wim"), X("t3"), X("t4"), ALU.subtract), r=[K("t3"), K("t4")], w=[K("wim")])
                            P.dve(lambda e, gp=gp, o=X("gre"), d=X("wre"): e.tensor_tensor_scan(
                                out=o, data0=mag[:, gp:gp + 1].to_broadcast([128, TC]), data1=d,
                                initial=carry[:, 0, gp:gp + 1], op0=ALU.mult, op1=ALU.add),
                                r=[K("wre"), "carry", "mag"], w=[K("gre")])
                            P.dve(lambda e, gp=gp, o=X("gim"), d=X("wim"): e.tensor_tensor_scan(
                                out=o, data0=mag[:, gp:gp + 1].to_broadcast([128, TC]), data1=d,
                                initial=carry[:, 1, gp:gp + 1], op0=ALU.mult, op1=ALU.add),
                                r=[K("wim"), "carry", "mag"], w=[K("gim")])
                            P.act(lambda e, gp=gp, g_=T[("gre", b)]: e.activation(out=glast[:, 0, gp:gp + 1], in_=g_[:, TC - 1:TC], func=AF.Copy),
                                  r=[K("gre")], w=[("glast", 0, gp)])
                            P.act(lambda e, gp=gp, g_=T[("gim", b)]: e.activation(out=glast[:, 1, gp:gp + 1], in_=g_[:, TC - 1:TC], func=AF.Copy),
                                  r=[K("gim")], w=[("glast", 1, gp)])
                            P.pool(TTf(X("d1"), X("gre"), Ec[:, gp, :], ALU.mult), r=[K("gre"), "Ec"], w=[K("d1")])
                            P.pool(TTf(X("d2"), X("gim"), Es[:, gp, :], ALU.mult), r=[K("gim"), "Es"], w=[K("d2")])
                            P.pool(TTf(Hh[hb][:, gp % 4, 0, :], X("d1"), X("d2"), ALU.subtract), r=[K("d1"), K("d2")], w=[("Hh", hb, gp % 4, 0)])
                            P.dve(TTf(X("d3"), X("gre"), Es[:, gp, :], ALU.mult), r=[K("gre"), "Es"], w=[K("d3")])
                            P.dve(TTf(X("d4"), X("gim"), Ec[:, gp, :], ALU.mult), r=[K("gim"), "Ec"], w=[K("d4")])
                            P.dve(TTf(Hh[hb][:, gp % 4, 1, :], X("d3"), X("d4"), ALU.add), r=[K("d3"), K("d4")], w=[("Hh", hb, gp % 4, 1)])
                            if gp % 4 == 3:
                                for k in range(8):
                                    gl, ri = k // 2, k % 2
                                    P.pe(lambda e, gl=gl, ri=ri, ct=ct, hb=hb, k=k: e.matmul(
                                        ps[4 + hb][:, 0:TC], lhsT=CPad[:, ct * 4 + gl, ri, :], rhs=Hh[hb][:, gl, ri, :],
                                        start=(k == 0), stop=(k == 7)),
                                        r=["CPad", ("Hh", hb, gl, ri)], w=[psk(4 + hb)])
                                P.dve(lambda e, ct=ct, hb=hb, tk=tk: e.scalar_tensor_tensor(
                                    out=ya[hb][:], in0=uT[:, ct, tk], scalar=dsk[:, ct:ct + 1], in1=ps[4 + hb][:, 0:TC],
                                    op0=ALU.mult, op1=ALU.add), r=[psk(4 + hb), ("uT", ct, s_), "dsk"], w=[("ya", hb)])
                                P.act(lambda e, ct=ct, tk=tk, hb=hb: e.activation(out=ga[:, ct, tk], in_=ya[hb][:], func=AF.Gelu_apprx_tanh),
                                      r=[("ya", hb)], w=[("ga", ct, s_)])
                        gk = [("glast", ri, gp) for ri in range(2) for gp in range(16)]
                        P.dve(TTf(ctmp[:, 0, :], glast[:, 0, :], cosN[:], ALU.mult), r=gk + ["cosN"], w=["ctmp0"])
                        P.dve(TTf(ctmp[:, 1, :], glast[:, 1, :], sinN[:], ALU.mult), r=gk + ["sinN"], w=["ctmp1"])
                        P.dve(TTf(ctmp[:, 2, :], glast[:, 0, :], sinN[:], ALU.mult), r=gk + ["sinN"], w=["ctmp2"])
                        P.dve(TTf(ctmp[:, 3, :], glast[:, 1, :], cosN[:], ALU.mult), r=gk + ["cosN"], w=["ctmp3"])
                        P.dve(TTf(carry[:, 0, :], ctmp[:, 0, :], ctmp[:, 1, :], ALU.subtract), r=["ctmp0", "ctmp1"], w=["carry"])
                        P.dve(TTf(carry[:, 1, :], ctmp[:, 2, :], ctmp[:, 3, :], ALU.add), r=["ctmp2", "ctmp3"], w=["carry"])
                if full:
                    tap("uT", uT[:], [("uT", c_, s_) for c_ in range(4) for s_ in range(NSB)], BF16)
                    tap("cvT", cvT[:], [("cvT", c_, s_) for c_ in range(4) for s_ in range(NSB)], BF16)
                    tap("ga", ga[:], [("ga", c_, s_) for c_ in range(4) for s_ in range(NSB)], BF16)
                    tap("carry", carry[:], ["carry"])
                if not full:
                    return
                P.barrier(dummy[:])
                with contextlib.ExitStack() as s4:
                    unC = sb("un", [128, 8, SB], BF16, s4)
                    sqC = [sb("sq%d" % i, [128, SB], BF16, s4) for i in range(2)]
                    sdC = sb("sd", [128, SB], F32, s4)
                    rstdC = sb("rstd", [128, SB], F32, s4)
                    wgl = [WS[:, i * 3584: i * 3584 + 1024].rearrange("p (k j n) -> p k j n", k=4, j=2) for i in range(2)]
                    wco = [WS[:, i * 3584 + 1024: i * 3584 + 1536].rearrange("p (k n) -> p k n", k=4) for i in range(2)]
                    wga = [WS[:, i * 3584 + 1536: i * 3584 + 3584].rearrange("p (k j n) -> p k j n", k=8, j=2) for i in range(2)]
                    glu_v = W["ssm_w_glu"].rearrange("(kc p) n -> p kc n", p=128)
                    co_v = W["conv_w_out"].rearrange("(kc p) n -> p kc n", p=128)
                    win_v = W["w_in"].rearrange("(kc p) n -> p kc n", p=128)
                    wo = WS[:, SLOT:SLOT + 8192].rearrange("p (k n) -> p k n", k=8)
                    mixed = sb("mixed", [128, 8, SB], BF16, s4)
                    ysv = sb("ysv", [128, SB], F32, s4)
                    g1 = sb("g1", [128, SB], F32, s4)
                    g2 = sb("g2", [128, SB], F32, s4)
                    P.dma("pool", lambda e: e.dma_start(out=wo, in_=rows3(W["w_o"])), w=[("WS", 1, 0), ("WS", 1, 1), ("WS", 1, 2)], epoch=False)
                    P.op("pool", lambda e: e.memset(dummy2[:], 0.0), (), [("WS", 0, 0), ("WS", 0, 1), ("WS", 0, 2)], epoch=False)
                    for s_ in range(NSB):
                        tok = slice(s_ * SB, (s_ + 1) * SB)
                        norm_sb(s_, 1, lambda dt_: unC[:, dt_, :], sqC, sdC, rstdC, lambda dt_: ("un", dt_))
                        for dt_ in range(8):
                            ws_ = (s_ * 8 + dt_) % 2
                            P.dma("sp", lambda e, dt_=dt_, ws_=ws_: e.dma_start(
                                out=WS[:, ws_ * 3584:(ws_ + 1) * 3584], in_=cw_scr[dt_, :, :]),
                                r=[("WS", 0, 0), ("WS", 0, 1), ("WS", 0, 2), ("cwscr", dt_, "co")] + [("cwscr", dt_, a_, j_) for a_ in ("gl", "ga") for j_ in range(2)],
                                w=[("WSC", ws_, "gl", 0), ("WSC", ws_, "gl", 1), ("WSC", ws_, "ga", 0), ("WSC", ws_, "ga", 1), ("WSC", ws_, "co")], epoch=False)
                            b0 = 0 if dt_ % 2 == 0 else 7
                            for kc in range(4):
                                P.pe(lambda e, kc=kc, dt_=dt_, tok=tok, ws_=ws_, b0=b0: e.matmul(ps[b0][:, 0:SB], lhsT=wgl[ws_][:, kc, 0, :],
                                                                                  rhs=ga[:, kc, tok], start=(kc == 0), stop=(kc == 3)),
                                     r=[("WS", 0, 0), ("WS", 0, 1), ("WS", 0, 2), ("WSC", ws_, "gl", 0), ("ga", kc, s_)], w=[psk(b0)])
                            for kc in range(4):
                                P.pe(lambda e, kc=kc, dt_=dt_, tok=tok, ws_=ws_: e.matmul(ps[1][:, 0:SB], lhsT=wgl[ws_][:, kc, 1, :],
                                                                                  rhs=ga[:, kc, tok], start=(kc == 0), stop=(kc == 3)),
                                     r=[("WS", 0, 0), ("WS", 0, 1), ("WS", 0, 2), ("WSC", ws_, "gl", 1), ("ga", kc, s_)], w=[psk(1)])
                            for j in range(2):
                                for kc in range(8):
                                    P.pe(lambda e, kc=kc, dt_=dt_, j=j, ws_=ws_: e.matmul(ps[3 + j][:, 0:SB],
                                                                                  lhsT=wga[ws_][:, kc, j, :],
                                                                                  rhs=unC[:, kc, :], start=(kc == 0), stop=(kc == 7)),
                                         r=[("WS", 0, 0), ("WS", 0, 1), ("WS", 0, 2), ("WSC", ws_, "ga", j), ("un", kc)], w=[psk(3 + j)])
                            for kc in range(4):
                                P.pe(lambda e, kc=kc, dt_=dt_, tok=tok, ws_=ws_: e.matmul(ps[2][:, 0:SB], lhsT=wco[ws_][:, kc, :],
                                                                                  rhs=cvT[:, kc, tok], start=(kc == 0), stop=(kc == 3)),
                                     r=[("WS", 0, 0), ("WS", 0, 1), ("WS", 0, 2), ("WSC", ws_, "co"), ("cvT", kc, s_)], w=[psk(2)])
                            P.act(lambda e: e.activation(out=g1[:], in_=ps[1][:, 0:SB], func=AF.Sigmoid), r=[psk(1)], w=["g1"])
                            P.dve(lambda e, b0=b0: e.tensor_tensor(out=ysv[:], in0=ps[b0][:, 0:SB], in1=g1[:], op=ALU.mult), r=[psk(b0), "g1"], w=["ysv"])
                            P.act(lambda e, dt_=dt_: e.activation(out=g1[:], in_=ps[3][:, 0:SB], func=AF.Sigmoid, bias=bgate[:, dt_:dt_ + 1]),
                                  r=[psk(3), "bgate"], w=["g1"])
                            P.act(lambda e, dt_=dt_: e.activation(out=g2[:], in_=ps[4][:, 0:SB], func=AF.Sigmoid, bias=bgate[:, 8 + dt_:9 + dt_]),
                                  r=[psk(4), "bgate"], w=["g2"])
                            P.dve(lambda e: e.tensor_tensor(out=ysv[:], in0=ysv[:], in1=g1[:], op=ALU.mult), r=["ysv", "g1"], w=["ysv"])
                            P.dve(lambda e: e.tensor_tensor(out=g2[:], in0=ps[2][:, 0:SB], in1=g2[:], op=ALU.mult), r=[psk(2), "g2"], w=["g2"])
                            P.dve(lambda e, dt_=dt_: e.tensor_tensor(out=mixed[:, dt_, :], in0=ysv[:], in1=g2[:], op=ALU.add),
                                  r=["ysv", "g2"], w=[("mixed", dt_)])
                        for dt_ in range(8):
                            pb = 5 + (dt_ % 2)
                            for kc in range(8):
                                P.pe(lambda e, kc=kc, dt_=dt_, pb=pb: e.matmul(ps[pb][:, 0:SB], lhsT=wo[:, kc, dt_ * 128:(dt_ + 1) * 128],
                                                                                rhs=mixed[:, kc, :], start=(kc == 0), stop=(kc == 7)),
                                     r=[("WS", 1, 0), ("WS", 1, 1), ("WS", 1, 2), ("mixed", kc)], w=[psk(pb)])
                            P.dve(lambda e, dt_=dt_, pb=pb, tok=tok: e.tensor_tensor(out=hT[:, dt_, tok], in0=ps[pb][:, 0:SB], in1=hT[:, dt_, tok], op=ALU.add),
                                  r=[psk(pb), ("hT", dt_, s_)], w=[("hT", dt_, s_)])
            tap("hT_mix", hT[:], HTK)
            ffn(blk, 2, W["ffn2_w_gate"], W["ffn2_w_up"], W["ffn2_w_down"])
            tap("hT_ffn2", hT[:], HTK)
            P.barrier(dummy[:])
            with contextlib.ExitStack() as s3:
                sqF = [sb("sq%d" % i, [128, SB], BF16, s3) for i in range(2)]
                sdF = sb("sd", [128, SB], F32, s3)
                rstdF = sb("rstd", [128, SB], F32, s3)
                ost = [sb("ost%d" % i, [128, D], F32, s3) for i in range(2)]
                for s_ in range(NSB):
                    norm_sb(s_, 3, lambda dt_, s_=s_: hT[:, dt_, s_ * SB:(s_ + 1) * SB], sqF, sdF, rstdF,
                            lambda dt_, s_=s_: ("hT", dt_, s_))
                for i in range(16):
                    t0 = 16 + i * 128
                    ob = i % 2
                    sbs = sorted(set([t0 // SB, (t0 + 127) // SB]))
                    for h in range(2):
                        bank = 6 + h
                        for q in range(4):
                            dt_ = h * 4 + q
                            P.pe(lambda e, dt_=dt_, q=q, t0=t0, bank=bank: e.transpose(
                                out=ps[bank][:, q * 128:(q + 1) * 128], in_=hT[:, dt_, t0:t0 + 128], identity=ident[:]),
                                r=[("hT", dt_, s_) for s_ in sbs] + ["ident"], w=[psk(bank)])
                        if h == 0:
                            P.act(lambda e, ob=ob, h=h, bank=bank: e.activation(out=ost[ob][:, h * 512:(h + 1) * 512], in_=ps[bank][:, :], func=AF.Copy),
                                  r=[psk(bank)], w=[("ost", ob, h)])
                        else:
                            P.dve(lambda e, ob=ob, h=h, bank=bank: e.tensor_copy(out=ost[ob][:, h * 512:(h + 1) * 512], in_=ps[bank][:, :]),
                                  r=[psk(bank)], w=[("ost", ob, h)])
                    P.dma("sp", lambda e, i=i, ob=ob: e.dma_start(out=out[i * 128:(i + 1) * 128, :], in_=ost[ob][:]),
                          r=[("ost", ob, 0), ("ost", ob, 1)], w=[("out", i)])
        for blk in range(NBLK):
            do_block(blk)
        P.emit()
    return nc


_NC_CACHE = {}


def kernel(**inputs):
    x = np.asarray(inputs["x"], dtype=np.float32)
    meta = np.asarray(inputs["meta_tokens"], dtype=np.float32)
    B, S, _ = x.shape
    n = 8
    if "nc" not in _NC_CACHE:
        _NC_CACHE["nc"] = build_nc()
    nc = _NC_CACHE["nc"]
    f = lambda k: np.ascontiguousarray(np.asarray(inputs[k], dtype=np.float32))
    shared = {
        "g_ffn1": f("g_ffn1")[0], "ffn1_w_gate": f("ffn1_w_gate")[0], "ffn1_w_up": f("ffn1_w_up")[0],
        "ffn1_w_down": f("ffn1_w_down")[0], "g_mix": f("g_mix")[0], "w_in": f("w_in")[0], "b_gate": f("b_gate")[0],
        "ssm_a_re": f("ssm_a_re")[0], "ssm_a_im": f("ssm_a_im")[0], "ssm_log_dt": f("ssm_log_dt")[0],
        "ssm_b_re": f("ssm_b_re")[0], "ssm_b_im": f("ssm_b_im")[0],
        "ssm_c_re": f("ssm_c_re")[0].reshape(512, 64), "ssm_c_im": f("ssm_c_im")[0].reshape(512, 64),
        "ssm_d": f("ssm_d")[0], "ssm_w_glu": f("ssm_w_glu")[0], "conv_w": f("conv_w")[0].reshape(3, 512),
        "conv_w_out": f("conv_w_out")[0], "w_o": f("w_o")[0], "g_ffn2": f("g_ffn2")[0],
        "ffn2_w_gate": f("ffn2_w_gate")[0], "ffn2_w_up": f("ffn2_w_up")[0], "ffn2_w_down": f("ffn2_w_down")[0],
        "g_final": f("g_final"),
    }
    in_maps = []
    total = NBLK * NT
    for c in range(n):
        b, q = c // 4, c % 4
        seq = np.concatenate([meta, x[b, : 2048 * (q + 1)]], axis=0)
        stream = np.zeros((total, D), np.float32)
        stream[total - seq.shape[0]:] = seq
        m = dict(shared)
        m["xs"] = stream
        in_maps.append(m)
    res = run_bass_kernel_spmd(nc, in_maps, core_ids=list(range(n)))
    outp = np.zeros((B, S, D), np.float32)
    for c in range(n):
        b, q = c // 4, c % 4
        outp[b, q * 2048:(q + 1) * 2048] = np.asarray(res.results[c]["out"], dtype=np.float32)
    return outp
```

```python
import contextlib
import math
import numpy as np
import concourse.bass as bass
import concourse.mybir as mybir
from concourse.bass_utils import run_bass_kernel_spmd

F32 = mybir.dt.float32
BF16 = mybir.dt.bfloat16
I32 = mybir.dt.int32
AF = mybir.ActivationFunctionType
ALU = mybir.AluOpType

D = 1024
DFF = 2816
NT = 2064
NBLK = 4
NSB = 6
SB = 344
TC = 172
NF = 22
GROUPS = [(0, 3), (3, 3), (6, 3), (9, 3), (12, 3), (15, 3), (18, 2), (20, 2)]
EPS = 1e-6
ENGS = ("pe", "act", "dve", "pool", "sp")
PI = math.pi


class Prog:
    def __init__(self, nc, dma_ring=8):
        self.nc = nc
        self.ops = []
        self.dma_ring = dma_ring

    def op(self, eng, fn, reads=(), writes=(), dma=False, epoch=True):
        self.ops.append(dict(eng=eng, fn=fn, reads=tuple(reads) + (("EPOCH",) if epoch else ()), writes=tuple(writes), dma=dma))

    def barrier(self, dummy):
        self.ops.append(dict(eng="dve", fn=lambda e: e.memset(dummy, 0.0), reads=(), writes=("EPOCH",), dma=False))

    def pe(self, fn, r=(), w=()):
        self.op("pe", fn, r, w)

    def act(self, fn, r=(), w=()):
        self.op("act", fn, r, w)

    def dve(self, fn, r=(), w=()):
        self.op("dve", fn, r, w)

    def pool(self, fn, r=(), w=()):
        self.op("pool", fn, r, w)

    def dma(self, q, fn, r=(), w=(), epoch=True):
        self.op(q, fn, r, w, dma=True, epoch=epoch)

    def emit(self):
        nc = self.nc
        ops = self.ops
        last_writer = {}
        readers = {}
        for i, o in enumerate(ops):
            deps = set()
            for r in o["reads"]:
                if r in last_writer:
                    deps.add(last_writer[r])
            for w in o["writes"]:
                if w in last_writer:
                    deps.add(last_writer[w])
                deps.update(readers.get(w, ()))
            deps.discard(i)
            o["deps"] = deps
            for r in o["reads"]:
                readers.setdefault(r, []).append(i)
            for w in o["writes"]:
                last_writer[w] = i
                readers[w] = []
        eidx = {e: 0 for e in ENGS}
        dcount = {e: 0 for e in ENGS}
        for o in ops:
            o["eidx"] = eidx[o["eng"]]
            eidx[o["eng"]] += 1
            o["marked"] = False
            if o["dma"]:
                o["dma_n"] = dcount[o["eng"]]
                dcount[o["eng"]] += 1
        R_ = self.dma_ring
        for i, o in enumerate(ops):
            E = o["eng"]
            need = []
            best = {}
            bestd = {}
            for j in o["deps"]:
                p = ops[j]
                if p["dma"]:
                    k = (p["eng"], p["dma_n"] % R_)
                    if k not in bestd or ops[bestd[k]]["dma_n"] < p["dma_n"]:
                        bestd[k] = j
                else:
                    k = p["eng"]
                    if k not in best or ops[best[k]]["eidx"] < p["eidx"]:
                        best[k] = j
            need.extend(bestd.values())
            for k, j in best.items():
                p = ops[j]
                if k != E:
                    need.append(j)
                    p["marked"] = True
                else:
                    if E == "pe" and not o["dma"]:
                        continue
                    need.append(j)
                    p["marked"] = True
            o["need"] = need
        cnt = {e: 0 for e in ENGS}
        for o in ops:
            if o["marked"]:
                cnt[o["eng"]] += 1
                o["cnt"] = cnt[o["eng"]]
        R = self.dma_ring
        with contextlib.ExitStack() as st:
            esem = {e: st.enter_context(nc.semaphore("s_" + e)) for e in ENGS}
            dsem = {e: [st.enter_context(nc.semaphore("d_%s_%d" % (e, k))) for k in range(R)]
                    for e in ("sp", "act", "pool") if dcount[e] > 0}
            block = st.enter_context(nc.Block())

            def run_engine(E, eng):
                waited = {}

                def wait(sem, val):
                    key = id(sem)
                    if waited.get(key, 0) >= val:
                        return
                    waited[key] = val
                    eng.wait_ge(sem, val)

                for o in ops:
                    if o["eng"] != E:
                        continue
                    for j in o["need"]:
                        p = ops[j]
                        if p["dma"]:
                            n = p["dma_n"]
                            wait(dsem[p["eng"]][n % R], 16 * (n // R + 1))
                        else:
                            wait(esem[p["eng"]], p["cnt"])
                    if o["dma"]:
                        n = o["dma_n"]
                        s = dsem[E][n % R]
                        if n // R > 0:
                            wait(s, 16 * (n // R))
                        o["fn"](eng).then_inc(s, 16)
                    else:
                        ins = o["fn"](eng)
                        if o["marked"]:
                            ins.then_inc(esem[E], 1)
                if E in dsem:
                    tot = dcount[E]
                    for k in range(R):
                        uses = (tot - k + R - 1) // R if tot > k else 0
                        if uses > 0:
                            wait(dsem[E][k], 16 * uses)

            @block.tensor
            def _(eng):
                run_engine("pe", eng)

            @block.scalar
            def _(eng):
                run_engine("act", eng)

            @block.vector
            def _(eng):
                run_engine("dve", eng)

            @block.gpsimd
            def _(eng):
                run_engine("pool", eng)

            @block.sync
            def _(eng):
                run_engine("sp", eng)


DEBUG = False


def build_nc():
    nc = bass.Bass("TRN2", target_bir_lowering=False)
    dr = lambda name, shape, kind="ExternalInput", dt=F32: nc.dram_tensor(name, shape, dt, kind=kind).ap()
    xs = dr("xs", [NBLK * NT, D])
    out = dr("out", [2048, D], kind="ExternalOutput")
    W = {}
    for nm, shp in [("g_ffn1", [D]), ("ffn1_w_gate", [D, DFF]), ("ffn1_w_up", [D, DFF]), ("ffn1_w_down", [DFF, D]),
                    ("g_mix", [D]), ("w_in", [D, 4096]), ("b_gate", [2048]), ("ssm_a_re", [32, 64]),
                    ("ssm_a_im", [32, 64]), ("ssm_log_dt", [32]), ("ssm_b_re", [32, 64, 16]),
                    ("ssm_b_im", [32, 64, 16]), ("ssm_c_re", [512, 64]), ("ssm_c_im", [512, 64]),
                    ("ssm_d", [512]), ("ssm_w_glu", [512, 2048]), ("conv_w", [3, 512]),
                    ("conv_w_out", [512, D]), ("w_o", [D, D]), ("g_ffn2", [D]), ("ffn2_w_gate", [D, DFF]),
                    ("ffn2_w_up", [D, DFF]), ("ffn2_w_down", [DFF, D]), ("g_final", [D])]:
        W[nm] = dr(nm, shp)

    P = Prog(nc)
    nc_allow = nc.allow_non_contiguous_dma(reason="small parameter loads")
    with contextlib.ExitStack() as st:
        st.enter_context(nc_allow)

        uid = [0]

        def sb(name, shape, dt, stack=st):
            uid[0] += 1
            return stack.enter_context(nc.sbuf_tensor("%s_%d" % (name, uid[0]), shape, dt))

        ps = [st.enter_context(nc.psum_tensor("ps%d" % i, [128, 512], F32)) for i in range(8)]

        def tap(name, ap, keys, dt=F32):
            if not DEBUG:
                return
            shape = list(ap.shape)
            d = nc.dram_tensor("dbg_" + name, shape, dt, kind="ExternalOutput").ap()
            P.dma("sp", lambda e: e.dma_start(out=d, in_=ap), r=keys, w=[("dbg", name)])

        HTK = [("hT", dt_, s_) for dt_ in range(8) for s_ in range(NSB)]
        psk = lambda i: ("ps", i)

        hT = sb("hT", [128, 8, NT], F32)
        ident = sb("ident", [128, 128], F32)
        dummy = sb("dummy", [128, 1], F32)
        dummy2 = sb("dummy2", [128, 1], F32)
        SLOT = 9216
        WS = sb("WS", [128, 2 * SLOT], BF16)
        wsg = [WS[:, i * SLOT: i * SLOT + 3072].rearrange("p (k n) -> p k n", k=8) for i in range(2)]
        wsu = [WS[:, i * SLOT + 3072: i * SLOT + 6144].rearrange("p (k n) -> p k n", k=8) for i in range(2)]
        wsd = [WS[:, i * SLOT + 6144: i * SLOT + 9216].rearrange("p (f n) -> p f n", f=3) for i in range(2)]
        ALLWS = [("WS", i, j) for i in range(2) for j in range(3)]
        uT = sb("uT", [128, 4, NT], BF16)
        junk = [sb("junk%d" % i, [128, TC], F32) for i in range(2)]
        acc = sb("acc", [128, 4, 16], F32)
        ctm = sb("ctm", [128, 6, 16], F32)
        rows3 = lambda ap: ap.rearrange("(kc p) n -> p kc n", p=128)
        ones_bf = sb("ones_bf", [128, 128], BF16)
        gains = sb("gains", [128, 4, 8], F32)
        bgate = sb("bgate", [128, 16], F32)
        dsk = sb("dsk", [128, 4], F32)
        cw = sb("cw", [128, 4, 3], F32)
        mag = sb("mag", [128, 16], F32)
        carry = sb("carry", [128, 2, 16], F32)
        glast = sb("glast", [128, 2, 16], F32)
        th = sb("th", [128, 16], F32)
        adr = sb("adr", [128, 16], F32)
        lamN = sb("lamN", [128, 2, 16], F32)
        cosN = sb("cosN", [128, 16], F32)
        sinN = sb("sinN", [128, 16], F32)
        Ec = sb("Ec", [128, 16, TC], F32)
        Es = sb("Es", [128, 16, TC], F32)
        BqPad = sb("BqPad", [128, 16, 2, 128], BF16)
        CPad = sb("CPad", [128, 16, 2, 128], BF16)

        P.pool(lambda e: e.memset(ident[:], 0.0), w=["ident"])
        P.pool(lambda e: e.affine_select(out=ident[:], in_=ident[:], compare_op=ALU.not_equal, fill=1.0,
                                         base=0, pattern=[[-1, 128]], channel_multiplier=1), r=["ident"], w=["ident"])
        P.dve(lambda e: e.memset(ones_bf[:], 1.0), w=["ones"])
        for i, nm in enumerate(["g_ffn1", "g_mix", "g_ffn2", "g_final"]):
            P.dma("sp", lambda e, i=i, nm=nm: e.dma_start(out=gains[:, i, :], in_=W[nm].rearrange("(t p) -> p t", p=128)),
                  w=[("gains", i)])
        P.dma("sp", lambda e: e.dma_start(out=bgate[:], in_=W["b_gate"].rearrange("(t p) -> p t", p=128)), w=["bgate"])
        P.dma("sp", lambda e: e.dma_start(out=dsk[:], in_=W["ssm_d"].rearrange("(t p) -> p t", p=128)), w=["dsk"])
        for j in range(3):
            P.dma("sp", lambda e, j=j: e.dma_start(out=cw[:, :, j], in_=W["conv_w"][j, :].rearrange("(t p) -> p t", p=128)), w=["cw"])

        with contextlib.ExitStack() as s2:
            t = lambda name, shape, dt=F32: sb(name, shape, dt, s2)
            are = t("are", [128, 16]); aim = t("aim", [128, 16]); ldt = t("ldt", [128, 16])
            bre = t("bre", [128, 16, 16]); bim = t("bim", [128, 16, 16])
            cn = t("cn", [128, 2, 4, 2, 64])
            CP = t("CP", [128, 2, 16, 16])
            dtv = t("dtv", [128, 16])
            cs = t("cs", [128, 16]); sn = t("sn", [128, 16])
            lre = t("lre", [128, 16]); lim = t("lim", [128, 16])
            den = t("den", [128, 16]); tmpa = t("tmpa", [128, 16]); tmpb = t("tmpb", [128, 16])
            qre = t("qre", [128, 16]); qim = t("qim", [128, 16]); lm1 = t("lm1", [128, 16])
            bq = t("bq", [128, 2, 16, 16]); tb = t("tb", [128, 16])
            angN = t("angN", [128, 16])
            wkn1 = t("wkn1", [128, 16]); wkn2 = t("wkn2", [128, 16]); wkni = t("wkni", [128, 16], I32)

            pair_view = lambda ap: ap.rearrange("(gp two) p -> (two p) gp", two=2)
            P.dma("sp", lambda e: e.dma_start(out=are[:], in_=pair_view(W["ssm_a_re"])), w=["are"])
            P.dma("sp", lambda e: e.dma_start(out=aim[:], in_=pair_view(W["ssm_a_im"])), w=["aim"])
            for gpar in range(2):
                P.dma("sp", lambda e, gpar=gpar: e.dma_start(
                    out=ldt[gpar * 64:(gpar + 1) * 64, :],
                    in_=W["ssm_log_dt"].rearrange("(g two) -> two g", two=2)[gpar:gpar + 1, :].to_broadcast([64, 16])),
                    w=[("ldt", gpar)])
            bview = lambda ap: ap.rearrange("(gp two) p c -> (two p) gp c", two=2)
            P.dma("sp", lambda e: e.dma_start(out=bre[:], in_=bview(W["ssm_b_re"])), w=["bre"])
            P.dma("sp", lambda e: e.dma_start(out=bim[:], in_=bview(W["ssm_b_im"])), w=["bim"])
            for ri, nm in enumerate(["ssm_c_re", "ssm_c_im"]):
                for dup in range(2):
                    P.dma("sp", lambda e, ri=ri, nm=nm, dup=dup: e.dma_start(
                        out=cn[:, ri, :, dup, :], in_=W[nm].rearrange("(t p) s -> p t s", p=128)), w=[("cn", ri, dup)])
            for ri in range(2):
                for ct in range(4):
                    bank = 6 + (ct % 2)
                    P.pe(lambda e, ri=ri, ct=ct, bank=bank: e.transpose(
                        out=ps[bank][:, 0:128], in_=cn[:, ri, ct, :, :].rearrange("p a b -> p (a b)"), identity=ident[:]),
                        r=[("cn", ri, 0), ("cn", ri, 1), "ident"], w=[psk(bank)])
                    for gpar in range(2):
                        P.act(lambda e, ri=ri, ct=ct, bank=bank, gpar=gpar: e.activation(
                            out=CP[gpar * 64:(gpar + 1) * 64, ri, ct * 4:(ct + 1) * 4, :],
                            in_=ps[bank][gpar * 64:(gpar + 1) * 64, 0:128].rearrange(
                                "p (gl two c) -> p gl two c", two=2, c=16)[:, :, gpar, :],
                            func=AF.Identity, scale=(1.0 if ri == 0 else -1.0)),
                            r=[psk(bank)], w=[("CP", ri, ct, gpar)])
            cp_keys = [("CP", ri, ct, gpar) for ri in range(2) for ct in range(4) for gpar in range(2)]
            ldk = [("ldt", 0), ("ldt", 1)]
            P.act(lambda e: e.activation(out=dtv[:], in_=ldt[:], func=AF.Exp), r=ldk, w=["dtv"])
            P.dve(lambda e: e.tensor_tensor(out=adr[:], in0=are[:], in1=dtv[:], op=ALU.mult), r=["are", "dtv"], w=["adr"])
            P.dve(lambda e: e.tensor_tensor(out=th[:], in0=aim[:], in1=dtv[:], op=ALU.mult), r=["aim", "dtv"], w=["th"])
            P.act(lambda e: e.activation(out=mag[:], in_=adr[:], func=AF.Exp), r=["adr"], w=["mag"])

            def sincos(x, xk, o_sin, o_cos, w1, w2, wi, keys):
                for off, o in ((0.0, o_sin), (PI / 2, o_cos)):
                    ok = keys[0] if o is o_sin else keys[1]
                    P.dve(lambda e, off=off: e.tensor_scalar(out=w1, in0=x, scalar1=off, scalar2=1.0 / (2 * PI),
                                                             op0=ALU.add, op1=ALU.mult), r=[xk], w=["w1" + keys[2]])
                    P.dve(lambda e: e.tensor_copy(out=wi, in_=w1), r=["w1" + keys[2]], w=["wi" + keys[2]])
                    P.dve(lambda e: e.tensor_copy(out=w1, in_=wi), r=["wi" + keys[2]], w=["w1" + keys[2]])
                    P.dve(lambda e: e.scalar_tensor_tensor(out=w2, in0=w1, scalar=-6.25, in1=x,
                                                           op0=ALU.mult, op1=ALU.add), r=["w1" + keys[2], xk], w=["w2" + keys[2]])
                    P.dve(lambda e: e.scalar_tensor_tensor(out=w2, in0=w1, scalar=-(2 * PI - 6.25), in1=w2,
                                                           op0=ALU.mult, op1=ALU.add), r=["w1" + keys[2], "w2" + keys[2]], w=["w2" + keys[2]])
                    P.dve(lambda e, off=off: e.tensor_scalar(out=w2, in0=w2, scalar1=off, scalar2=None, op0=ALU.add),
                          r=["w2" + keys[2]], w=["w2" + keys[2]])
                    for lim_, sgn in ((PI, -1.0), (-PI, 1.0)):
                        cmp = ALU.is_gt if sgn < 0 else ALU.is_lt
                        P.dve(lambda e, lim_=lim_, cmp=cmp: e.tensor_scalar(out=w1, in0=w2, scalar1=lim_, scalar2=None, op0=cmp),
                              r=["w2" + keys[2]], w=["w1" + keys[2]])
                        P.dve(lambda e, sgn=sgn: e.scalar_tensor_tensor(out=w2, in0=w1, scalar=sgn * 2 * PI, in1=w2,
                                                                        op0=ALU.mult, op1=ALU.add),
                              r=["w1" + keys[2], "w2" + keys[2]], w=["w2" + keys[2]])
                    P.act(lambda e, o=o: e.activation(out=o, in_=w2, func=AF.Sin), r=["w2" + keys[2]], w=[ok])

            sincos(th[:], "th", sn[:], cs[:], wkn1[:], wkn2[:], wkni[:], ("sn", "cs", "n"))
            P.dve(lambda e: e.tensor_tensor(out=lre[:], in0=mag[:], in1=cs[:], op=ALU.mult), r=["mag", "cs"], w=["lre"])
            P.dve(lambda e: e.tensor_tensor(out=lim[:], in0=mag[:], in1=sn[:], op=ALU.mult), r=["mag", "sn"], w=["lim"])
            P.dve(lambda e: e.tensor_tensor(out=den[:], in0=are[:], in1=are[:], op=ALU.mult), r=["are"], w=["den"])
            P.dve(lambda e: e.tensor_tensor(out=tmpa[:], in0=aim[:], in1=aim[:], op=ALU.mult), r=["aim"], w=["tmpa"])
            P.dve(lambda e: e.tensor_tensor(out=den[:], in0=den[:], in1=tmpa[:], op=ALU.add), r=["den", "tmpa"], w=["den"])
            P.dve(lambda e: e.reciprocal(out=den[:], in_=den[:]), r=["den"], w=["den"])
            P.dve(lambda e: e.tensor_scalar(out=lm1[:], in0=lre[:], scalar1=-1.0, scalar2=None, op0=ALU.add), r=["lre"], w=["lm1"])
            P.dve(lambda e: e.tensor_tensor(out=tmpa[:], in0=lm1[:], in1=are[:], op=ALU.mult), r=["lm1", "are", "den"], w=["tmpa"])
            P.dve(lambda e: e.tensor_tensor(out=tmpb[:], in0=lim[:], in1=aim[:], op=ALU.mult), r=["lim", "aim"], w=["tmpb"])
            P.dve(lambda e: e.tensor_tensor(out=tmpa[:], in0=tmpa[:], in1=tmpb[:], op=ALU.add), r=["tmpa", "tmpb"], w=["tmpa"])
            P.dve(lambda e: e.tensor_tensor(out=qre[:], in0=tmpa[:], in1=den[:], op=ALU.mult), r=["tmpa", "den"], w=["qre"])
            P.dve(lambda e: e.tensor_tensor(out=tmpa[:], in0=lim[:], in1=are[:], op=ALU.mult), r=["lim", "are", "qre"], w=["tmpa"])
            P.dve(lambda e: e.tensor_tensor(out=tmpb[:], in0=lm1[:], in1=aim[:], op=ALU.mult), r=["lm1", "aim"], w=["tmpb"])
            P.dve(lambda e: e.tensor_tensor(out=tmpa[:], in0=tmpa[:], in1=tmpb[:], op=ALU.subtract), r=["tmpa", "tmpb"], w=["tmpa"])
            P.dve(lambda e: e.tensor_tensor(out=qim[:], in0=tmpa[:], in1=den[:], op=ALU.mult), r=["tmpa", "den"], w=["qim"])
            qb = lambda q: q[:, :].unsqueeze(2).to_broadcast([128, 16, 16])
            P.dve(lambda e: e.tensor_tensor(out=bq[:, 0], in0=bre[:], in1=qb(qre), op=ALU.mult), r=["bre", "qre"], w=["bq0"])
            P.dve(lambda e: e.tensor_tensor(out=bq[:, 1], in0=bim[:], in1=qb(qim), op=ALU.mult), r=["bim", "qim"], w=["bq1"])
            P.dve(lambda e: e.tensor_tensor(out=bq[:, 0], in0=bq[:, 0], in1=bq[:, 1], op=ALU.subtract), r=["bq0", "bq1"], w=["bq0"])
            P.dve(lambda e: e.tensor_tensor(out=bq[:, 1], in0=bim[:], in1=qb(qre), op=ALU.mult), r=["bim", "qre", "bq0"], w=["bq1"])
            P.dve(lambda e: e.tensor_tensor(out=bre[:], in0=bre[:], in1=qb(qim), op=ALU.mult), r=["bre", "qim", "bq0"], w=["bre"])
            P.dve(lambda e: e.tensor_tensor(out=bq[:, 1], in0=bq[:, 1], in1=bre[:], op=ALU.add), r=["bq1", "bre"], w=["bq1"])
            sz = contextlib.ExitStack()
            sz.__enter__()
            Z = sb("Z", [128, 16, 2, 128], F32, sz)
            P.pool(lambda e: e.memset(Z[:], 0.0), w=["Z"])
            P.pool(lambda e: e.memset(CPad[:], 0.0), w=["CPad"])
            for ri in range(2):
                for gpar in range(2):
                    for gl in range(4):
                        col = (2 * gl + gpar) * 16
                        rows = slice(gpar * 64, (gpar + 1) * 64)
                        P.dve(lambda e, ri=ri, rows=rows, gl=gl, col=col: e.tensor_copy(
                            out=Z[rows, gl::4, ri, col:col + 16], in_=bq[rows, ri, gl::4, :]),
                            r=["bq0", "bq1", "Z"], w=["Z"])
                        P.act(lambda e, ri=ri, rows=rows, gl=gl, col=col: e.activation(
                            out=CPad[rows, gl::4, ri, col:col + 16], in_=CP[rows, ri, gl::4, :], func=AF.Copy),
                            r=cp_keys + ["CPad"], w=["CPad"])
            for gp in range(16):
                for ri in range(2):
                    bank = 6 + ((gp * 2 + ri) % 2)
                    P.pe(lambda e, gp=gp, ri=ri, bank=bank: e.transpose(out=ps[bank][:, 0:128], in_=Z[:, gp, ri, :], identity=ident[:]),
                         r=["Z", "ident"], w=[psk(bank)])
                    P.act(lambda e, gp=gp, ri=ri, bank=bank: e.activation(out=BqPad[:, gp, ri, :], in_=ps[bank][:, 0:128], func=AF.Copy),
                          r=[psk(bank)], w=["BqPad"])
            sz.__exit__(None, None, None)
            P.barrier(dummy[:])
            P.dve(lambda e: e.tensor_scalar(out=angN[:], in0=th[:], scalar1=float(TC), scalar2=None, op0=ALU.mult), r=["th"], w=["angN"])
            sincos(angN[:], "angN", sinN[:], cosN[:], wkn1[:], wkn2[:], wkni[:], ("sinN", "cosN", "n"))
            P.act(lambda e: e.activation(out=tmpa[:], in_=adr[:], func=AF.Exp, scale=float(TC)), r=["adr", "qre", "qim"], w=["tmpa"])
            P.dve(lambda e: e.tensor_tensor(out=lamN[:, 0, :], in0=tmpa[:], in1=cosN[:], op=ALU.mult), r=["tmpa", "cosN"], w=["lamN"])
            P.dve(lambda e: e.tensor_tensor(out=lamN[:, 1, :], in0=tmpa[:], in1=sinN[:], op=ALU.mult), r=["tmpa", "sinN"], w=["lamN"])
            P.dve(lambda e: e.memset(carry[:], 0.0), w=["carry"])
            tap("mag", mag[:], ["mag"]); tap("lre", lre[:], ["lre"]); tap("lim", lim[:], ["lim"])
            tap("qre", qre[:], ["qre"]); tap("qim", qim[:], ["qim"]); tap("th", th[:], ["th"])
            tap("BqPad", BqPad[:], ["BqPad"], BF16); tap("CPad", CPad[:], ["CPad"], BF16)
            tap("CP", CP[:], cp_keys); tap("bq", bq[:], ["bq0", "bq1"]); tap("ident", ident[:], ["ident"])

        def make_tables(mode):
            P.barrier(dummy[:])
            with contextlib.ExitStack() as sx:
                iot = sb("iot", [128, TC], F32, sx); ioti = sb("ioti", [128, TC], I32, sx)
                ang = sb("ang", [128, 16, TC], F32, sx); wk1 = sb("wk1", [128, 16, TC], F32, sx)
                wk2 = sb("wk2", [128, 16, TC], F32, sx); wki = sb("wki", [128, 16, TC], I32, sx)
                if mode == "E":
                    P.pool(lambda e: e.iota(ioti[:], pattern=[[1, TC]], base=1, channel_multiplier=0), w=["ioti"])
                else:
                    P.pool(lambda e: e.iota(ioti[:], pattern=[[-1, TC]], base=TC - 1, channel_multiplier=0), w=["ioti"])
                P.dve(lambda e: e.tensor_copy(out=iot[:], in_=ioti[:]), r=["ioti"], w=["iot"])
                for gp in range(16):
                    P.dve(lambda e, gp=gp: e.tensor_scalar(out=ang[:, gp, :], in0=iot[:], scalar1=th[:, gp:gp + 1], scalar2=None,
                                                           op0=ALU.mult), r=["iot", "th"], w=["ang"])
                sincos(ang[:], "ang", Es[:], Ec[:], wk1[:], wk2[:], wki[:], ("Es", "Ec", "t"))
                if mode == "D":
                    for gp in range(16):
                        P.act(lambda e, gp=gp: e.activation(out=ang[:, gp, :], in_=iot[:], func=AF.Exp, scale=adr[:, gp:gp + 1]),
                              r=["iot", "adr", "Es", "Ec"], w=[("magp", gp)])
                    mk = [("magp", gp) for gp in range(16)]
                    P.dve(lambda e: e.tensor_tensor(out=Ec[:], in0=Ec[:], in1=ang[:], op=ALU.mult), r=["Ec"] + mk, w=["Ec"])
                    P.dve(lambda e: e.tensor_tensor(out=Es[:], in0=Es[:], in1=ang[:], op=ALU.mult), r=["Es"] + mk, w=["Es"])
                tap("Ec_" + mode, Ec[:], ["Ec"]); tap("Es_" + mode, Es[:], ["Es"])

        make_tables("D")

        cw_scr = nc.dram_tensor("cw_scr", [8, 128, 3584], BF16).ap()
        glu_v = W["ssm_w_glu"].rearrange("(kc p) n -> p kc n", p=128)
        co_v = W["conv_w_out"].rearrange("(kc p) n -> p kc n", p=128)
        win_v = W["w_in"].rearrange("(kc p) n -> p kc n", p=128)
        def precast_phaseC():
            for dt_ in range(8):
                for j in range(2):
                    P.dma("pool", lambda e, j=j, dt_=dt_: e.dma_start(
                        out=cw_scr[dt_, :, 0:1024].rearrange("p (k j n) -> p k j n", k=4, j=2)[:, :, j, :],
                        in_=glu_v[:, :, j * 1024 + dt_ * 128: j * 1024 + (dt_ + 1) * 128]), w=[("cwscr", dt_, "gl", j)], epoch=False)
                    P.dma("pool", lambda e, j=j, dt_=dt_: e.dma_start(
                        out=cw_scr[dt_, :, 1536:3584].rearrange("p (k j n) -> p k j n", k=8, j=2)[:, :, j, :],
                        in_=win_v[:, :, 2048 + j * 1024 + dt_ * 128: 2048 + j * 1024 + (dt_ + 1) * 128]), w=[("cwscr", dt_, "ga", j)], epoch=False)
                P.dma("pool", lambda e, dt_=dt_: e.dma_start(
                    out=cw_scr[dt_, :, 1024:1536].rearrange("p (k n) -> p k n", k=4),
                    in_=co_v[:, :, dt_ * 128:(dt_ + 1) * 128]), w=[("cwscr", dt_, "co")], epoch=False)

        def norm_sb(s_, gi, xn_ap_fn, sq, sd, rstd, keyfn, pbank=6, tag=0):
            tok = slice(s_ * SB, (s_ + 1) * SB)
            for dt_ in range(8):
                P.act(lambda e, dt_=dt_: e.activation(out=sq[dt_ % 2][:], in_=hT[:, dt_, tok], func=AF.Square),
                      r=[("hT", dt_, s_)], w=[("sq", dt_ % 2)])
                P.pe(lambda e, dt_=dt_: e.matmul(ps[pbank][:, 0:SB], lhsT=ones_bf[:], rhs=sq[dt_ % 2][:],
                                                 start=(dt_ == 0), stop=(dt_ == 7)),
                     r=[("sq", dt_ % 2), "ones"], w=[psk(pbank)])
            P.act(lambda e: e.activation(out=sd[:], in_=ps[pbank][:, 0:SB], func=AF.Sqrt, scale=1.0 / D, bias=EPS),
                  r=[psk(pbank)], w=[("sd", tag)])
            P.dve(lambda e: e.reciprocal(out=rstd[:], in_=sd[:]), r=[("sd", tag)], w=[("rstd", tag)])
            for dt_ in range(8):
                P.dve(lambda e, dt_=dt_: e.scalar_tensor_tensor(out=xn_ap_fn(dt_), in0=hT[:, dt_, tok],
                                                                scalar=gains[:, gi, dt_:dt_ + 1], in1=rstd[:],
                                                                op0=ALU.mult, op1=ALU.mult),
                      r=[("hT", dt_, s_), ("rstd", tag), ("gains", gi)], w=[keyfn(dt_)])

        def ffn(blk, gi, wg, wu, wd, bg=None):
            P.barrier(dummy[:])
            with contextlib.ExitStack() as s3:
                xn = sb("xn", [128, 8, NT], BF16, s3)
                sq = [sb("sq%d" % i, [128, SB], BF16, s3) for i in range(2)]
                sd = sb("sd", [128, SB], F32, s3)
                rstd = sb("rstd", [128, SB], F32, s3)
                actb = [sb("actb%d" % i, [128, 3, SB], BF16, s3) for i in range(2)]
                sg = [sb("sg%d" % i, [128, SB], F32, s3) for i in range(2)]
                for s_ in range(NSB):
                    norm_sb(s_, gi, lambda dt_, s_=s_: xn[:, dt_, s_ * SB:(s_ + 1) * SB], sq, sd, rstd,
                            lambda dt_, s_=s_: ("xn", dt_, s_))
                cnt = 0
                prev_down = [None]

                def emit_down(nf, sl, ab, s_, tok):
                    for dt_ in range(8):
                        pb = 4 + (dt_ % 2)
                        for fi in range(nf):
                            P.pe(lambda e, fi=fi, dt_=dt_, pb=pb, sl=sl, ab=ab, nf=nf: e.matmul(
                                ps[pb][:, 0:SB], lhsT=wsd[sl][:, fi, dt_ * 128:(dt_ + 1) * 128], rhs=actb[ab][:, fi, :],
                                start=(fi == 0), stop=(fi == nf - 1)),
                                r=[("WS", sl, 2), ("actb", ab, fi)], w=[psk(pb)])
                        P.dve(lambda e, dt_=dt_, pb=pb, tok=tok: e.scalar_tensor_tensor(
                            out=hT[:, dt_, tok], in0=ps[pb][:, 0:SB], scalar=0.5, in1=hT[:, dt_, tok],
                            op0=ALU.mult, op1=ALU.add),
                            r=[psk(pb), ("hT", dt_, s_)], w=[("hT", dt_, s_)])
                        if bg is not None and dt_ % 4 == 3:
                            next(bg, None)

                for gix, (f0, nf) in enumerate(GROUPS):
                    sl = gix % 2
                    P.dma("pool", lambda e, f0=f0, nf=nf, sl=sl: e.dma_start(
                        out=wsg[sl][:, :, 0:nf * 128], in_=rows3(wg)[:, :, f0 * 128:(f0 + nf) * 128]),
                        w=[("WS", sl, 0)], epoch=False)
                    P.dma("pool", lambda e, f0=f0, nf=nf, sl=sl: e.dma_start(
                        out=wsu[sl][:, :, 0:nf * 128], in_=rows3(wu)[:, :, f0 * 128:(f0 + nf) * 128]),
                        w=[("WS", sl, 1)], epoch=False)
                    P.dma("pool", lambda e, f0=f0, nf=nf, sl=sl: e.dma_start(
                        out=wsd[sl][:, 0:nf, :], in_=rows3(wd)[:, f0:f0 + nf, :]),
                        w=[("WS", sl, 2)], epoch=False)
                    for s_ in range(NSB):
                        tok = slice(s_ * SB, (s_ + 1) * SB)
                        ab = (gix * NSB + s_) % 2
                        for fi in range(nf):
                            par = cnt % 2
                            cnt += 1
                            for kc in range(8):
                                P.pe(lambda e, kc=kc, fi=fi, par=par, sl=sl, tok=tok: e.matmul(
                                    ps[par][:, 0:SB], lhsT=wsg[sl][:, kc, fi * 128:(fi + 1) * 128], rhs=xn[:, kc, tok],
                                    start=(kc == 0), stop=(kc == 7)),
                                    r=[("WS", sl, 0), ("xn", kc, s_)], w=[psk(par)])
                            for kc in range(8):
                                P.pe(lambda e, kc=kc, fi=fi, par=par, sl=sl, tok=tok: e.matmul(
                                    ps[2 + par][:, 0:SB], lhsT=wsu[sl][:, kc, fi * 128:(fi + 1) * 128], rhs=xn[:, kc, tok],
                                    start=(kc == 0), stop=(kc == 7)),
                                    r=[("WS", sl, 1), ("xn", kc, s_)], w=[psk(2 + par)])
                            P.act(lambda e, par=par: e.activation(out=sg[par][:], in_=ps[par][:, 0:SB], func=AF.Silu),
                                  r=[psk(par)], w=[("sg", par)])
                            P.dve(lambda e, par=par, ab=ab, fi=fi: e.tensor_tensor(
                                out=actb[ab][:, fi, :], in0=ps[2 + par][:, 0:SB], in1=sg[par][:], op=ALU.mult),
                                r=[psk(2 + par), ("sg", par)], w=[("actb", ab, fi)])
                            if bg is not None:
                                next(bg, None)
                        if prev_down[0] is not None:
                            prev_down[0]()
                        prev_down[0] = (lambda nf=nf, sl=sl, ab=ab, s_=s_, tok=tok: emit_down(nf, sl, ab, s_, tok))
                if prev_down[0] is not None:
                    prev_down[0]()
                if bg is not None:
                    for _ in bg:
                        pass

        pending = []

        def ssm_hist():
            for c in range(NT // TC):
                tk = slice(c * TC, (c + 1) * TC)
                s_ = (c * TC) // SB
                for gp in range(16):
                    ct = gp // 4
                    bank = 6 + (gp % 2)
                    PA, PB = ps[bank][:, 0:TC], ps[bank][:, TC:2 * TC]
                    P.pe(lambda e, gp=gp, ct=ct, PA=PA, tk=tk: e.matmul(PA, lhsT=BqPad[:, gp, 0, :], rhs=uT[:, ct, tk], start=True, stop=True),
                         r=["BqPad", ("uT", ct, s_)], w=[psk(bank)])
                    P.pe(lambda e, gp=gp, ct=ct, PB=PB, tk=tk: e.matmul(PB, lhsT=BqPad[:, gp, 1, :], rhs=uT[:, ct, tk], start=True, stop=True),
                         r=["BqPad", ("uT", ct, s_)], w=[psk(bank)])
                    for k, (src, tab, tkey) in enumerate([(PA, Ec, "Ec"), (PB, Es, "Es"), (PA, Es, "Es"), (PB, Ec, "Ec")]):
                        P.dve(lambda e, gp=gp, k=k, src=src, tab=tab: e.scalar_tensor_tensor(
                            out=junk[k % 2][:], in0=src, scalar=1.0, in1=tab[:, gp, :],
                            op0=ALU.mult, op1=ALU.mult, accum_out=acc[:, k, gp:gp + 1]),
                            r=[psk(bank), tkey], w=[("junk", k % 2), ("acc", k, gp)])
                    yield
                ak = [("acc", k, gp) for k in range(4) for gp in range(16)]
                TT = lambda o, a_, b_, op: (lambda e: e.tensor_tensor(out=o, in0=a_, in1=b_, op=op))
                P.dve(TT(ctm[:, 0, :], acc[:, 0, :], acc[:, 1, :], ALU.subtract), r=ak, w=["ctm0"])
                P.dve(TT(ctm[:, 1, :], acc[:, 2, :], acc[:, 3, :], ALU.add), r=ak, w=["ctm1"])
                P.dve(TT(ctm[:, 2, :], lamN[:, 0, :], carry[:, 0, :], ALU.mult), r=["lamN", "carry"], w=["ctm2"])
                P.dve(TT(ctm[:, 3, :], lamN[:, 1, :], carry[:, 1, :], ALU.mult), r=["lamN", "carry"], w=["ctm3"])
                P.dve(TT(ctm[:, 4, :], lamN[:, 0, :], carry[:, 1, :], ALU.mult), r=["lamN", "carry"], w=["ctm4"])
                P.dve(TT(ctm[:, 5, :], lamN[:, 1, :], carry[:, 0, :], ALU.mult), r=["lamN", "carry"], w=["ctm5"])
                P.dve(TT(ctm[:, 2, :], ctm[:, 2, :], ctm[:, 3, :], ALU.subtract), r=["ctm2", "ctm3"], w=["ctm2"])
                P.dve(TT(ctm[:, 4, :], ctm[:, 4, :], ctm[:, 5, :], ALU.add), r=["ctm4", "ctm5"], w=["ctm4"])
                P.dve(TT(carry[:, 0, :], ctm[:, 2, :], ctm[:, 0, :], ALU.add), r=["ctm2", "ctm0", "ctm4"], w=["carry"])
                P.dve(TT(carry[:, 1, :], ctm[:, 4, :], ctm[:, 1, :], ALU.add), r=["ctm4", "ctm1"], w=["carry"])
                yield

        def do_block(blk):
            full = (blk == NBLK - 1)
            if blk == 1:
                precast_phaseC()
            P.barrier(dummy[:])
            with contextlib.ExitStack() as s3:
                xst = [sb("xst%d" % i, [128, D], F32, s3) for i in range(6)]
                ntile = (NT + 127) // 128
                for i in range(ntile):
                    rows = min(128, NT - i * 128)
                    xb = i % 6
                    P.dma("sp", lambda e, i=i, rows=rows, xb=xb: e.dma_start(
                        out=xst[xb][0:rows, :], in_=xs[blk * NT + i * 128: blk * NT + i * 128 + rows, :]),
                        w=[("xst", xb)])
                    sbs = sorted(set([(i * 128) // SB, (i * 128 + rows - 1) // SB]))
                    for h in range(2):
                        bank = 6 + h
                        for q in range(4):
                            dt_ = h * 4 + q
                            P.pe(lambda e, dt_=dt_, q=q, rows=rows, xb=xb, bank=bank: e.transpose(
                                out=ps[bank][:, q * 128:q * 128 + rows], in_=xst[xb][0:rows, dt_ * 128:(dt_ + 1) * 128],
                                identity=ident[0:rows, 0:rows]),
                                r=[("xst", xb), "ident"], w=[psk(bank)])
                        eng = P.act if h == 0 else P.dve
                        if h == 0:
                            P.act(lambda e, h=h, i=i, rows=rows, bank=bank: e.activation(
                                out=hT[:, h * 4:(h + 1) * 4, i * 128:i * 128 + rows],
                                in_=ps[bank][:, :].rearrange("p (q c) -> p q c", q=4)[:, :, 0:rows], func=AF.Copy),
                                r=[psk(bank)], w=[("hT", dt_, s_) for dt_ in range(h * 4, h * 4 + 4) for s_ in sbs])
                        else:
                            P.dve(lambda e, h=h, i=i, rows=rows, bank=bank: e.tensor_copy(
                                out=hT[:, h * 4:(h + 1) * 4, i * 128:i * 128 + rows],
                                in_=ps[bank][:, :].rearrange("p (q c) -> p q c", q=4)[:, :, 0:rows]),
                                r=[psk(bank)], w=[("hT", dt_, s_) for dt_ in range(h * 4, h * 4 + 4) for s_ in sbs])
            if full:
                tap("hT_load", hT[:], HTK)
            ffn(blk, 0, W["ffn1_w_gate"], W["ffn1_w_up"], W["ffn1_w_down"], bg=(pending.pop() if pending else None))
            if full:
                make_tables("E")
            if full:
                tap("hT_ffn1", hT[:], HTK)
            P.barrier(dummy[:])
            with contextlib.ExitStack() as s3:
                cvT = sb("cvT", [128, 4, NT], BF16, s3) if full else None
                ga = uT
                P.barrier(dummy[:])
                with contextlib.ExitStack() as s4:
                    unA = [sb("un%d" % i, [128, 8, SB], BF16, s4) for i in range(2)]
                    sqA = [sb("sq%d" % i, [128, SB], BF16, s4) for i in range(2)]
                    sdA = [sb("sd%d" % i, [128, SB], F32, s4) for i in range(2)]
                    rstdA = [sb("rstd%d" % i, [128, SB], F32, s4) for i in range(2)]
                    wu_ = WS[:, 0:4096].rearrange("p (k n) -> p k n", k=8)
                    P.dma("pool", lambda e: e.dma_start(out=wu_, in_=rows3(W["w_in"])[:, :, 0:512]), w=ALLWS, epoch=False)
                    if full:
                        wv = WS[:, 4096:16384].rearrange("p (k n) -> p k n", k=8)
                        zs = sb("zs", [128, 4, SB + 2], F32, s4)
                        vv = sb("vv", [128, SB], F32, s4)
                        c1 = sb("c1", [128, SB], F32, s4)
                        P.dma("pool", lambda e: e.dma_start(out=wv, in_=rows3(W["w_in"])[:, :, 512:2048]),
                              w=ALLWS, epoch=False)
                        P.dve(lambda e: e.memset(zs[:], 0.0), w=[("zs", c) for c in range(4)])
                    for s_ in range(NSB):
                        tok = slice(s_ * SB, (s_ + 1) * SB)
                        npar = s_ % 2
                        norm_sb(s_, 1, lambda dt_, npar=npar: unA[npar][:, dt_, :], sqA, sdA[npar], rstdA[npar],
                                lambda dt_, npar=npar: ("un", npar, dt_), pbank=6 + npar, tag=npar)
                        for ot in range(4):
                            pb = 4 + (ot % 2)
                            for kc in range(8):
                                P.pe(lambda e, kc=kc, ot=ot, pb=pb, npar=npar: e.matmul(
                                    ps[pb][:, 0:SB], lhsT=wu_[:, kc, ot * 128:(ot + 1) * 128], rhs=unA[npar][:, kc, :],
                                    start=(kc == 0), stop=(kc == 7)), r=ALLWS + [("un", npar, kc)], w=[psk(pb)])
                            P.act(lambda e, ot=ot, pb=pb, tok=tok: e.activation(out=uT[:, ot, tok], in_=ps[pb][:, 0:SB], func=AF.Copy),
                                  r=[psk(pb)], w=[("uT", ot, s_)])
                        if full:
                            for ot in range(4):
                                for j, bank in ((0, 0), (2, 1), (1, 2)):
                                    for kc in range(8):
                                        P.pe(lambda e, kc=kc, ot=ot, j=j, bank=bank, npar=npar: e.matmul(
                                            ps[bank][:, 0:SB], lhsT=wv[:, kc, j * 512 + ot * 128: j * 512 + (ot + 1) * 128],
                                            rhs=unA[npar][:, kc, :], start=(kc == 0), stop=(kc == 7)),
                                            r=ALLWS + [("un", npar, kc)], w=[psk(bank)])
                                P.act(lambda e: e.activation(out=vv[:], in_=ps[0][:, 0:SB], func=AF.Copy), r=[psk(0)], w=["vv"])
                                P.dve(lambda e, ot=ot: e.tensor_copy(out=zs[:, ot, 0:2], in_=zs[:, ot, SB:SB + 2]),
                                      r=[("zs", ot)], w=[("zs", ot)])
                                P.dve(lambda e, ot=ot: e.tensor_tensor(out=zs[:, ot, 2:SB + 2], in0=ps[1][:, 0:SB], in1=vv[:], op=ALU.mult),
                                      r=[psk(1), "vv", ("zs", ot)], w=[("zs", ot)])
                                P.dve(lambda e, ot=ot: e.tensor_scalar(out=c1[:], in0=zs[:, ot, 0:SB], scalar1=cw[:, ot, 0:1], scalar2=None,
                                                                       op0=ALU.mult), r=[("zs", ot), "cw"], w=["c1"])
                                P.dve(lambda e, ot=ot: e.scalar_tensor_tensor(out=c1[:], in0=zs[:, ot, 1:SB + 1], scalar=cw[:, ot, 1:2], in1=c1[:],
                                                                              op0=ALU.mult, op1=ALU.add), r=[("zs", ot), "cw", "c1"], w=["c1"])
                                P.dve(lambda e, ot=ot: e.scalar_tensor_tensor(out=c1[:], in0=zs[:, ot, 2:SB + 2], scalar=cw[:, ot, 2:3], in1=c1[:],
                                                                              op0=ALU.mult, op1=ALU.add), r=[("zs", ot), "cw", "c1"], w=["c1"])
                                P.dve(lambda e, ot=ot, tok=tok: e.tensor_tensor(out=cvT[:, ot, tok], in0=ps[2][:, 0:SB], in1=c1[:], op=ALU.mult),
                                      r=[psk(2), "c1"], w=[("cvT", ot, s_)])
                if full:
                    tap("uT_A", uT[:], [("uT", c_, s_) for c_ in range(4) for s_ in range(NSB)], BF16)
                if not full:
                    pending.append(ssm_hist())
                    return
                P.barrier(dummy[:])
                with contextlib.ExitStack() as s4:
                    NBUF = 3
                    TN = ["t1", "t2", "t3", "t4", "wre", "wim", "gre", "gim"]
                    T = {}
                    for b_ in range(NBUF):
                        for n_ in TN:
                            T[(n_, b_)] = sb("%s_%d" % (n_, b_), [128, TC], F32, s4)
                        for n_ in ["d1", "d2", "d3", "d4"]:
                            T[(n_, b_)] = sb("%s_%d" % (n_, b_), [128, TC], BF16, s4)
                    ctmp = sb("ctmp", [128, 4, 16], F32, s4)
                    Hh = [sb("Hh%d" % i, [128, 4, 2, TC], BF16, s4) for i in range(2)]
                    ya = [sb("ya%d" % i, [128, TC], F32, s4) for i in range(2)]
                    TTf = lambda o, a_, b_, op: (lambda e: e.tensor_tensor(out=o, in0=a_, in1=b_, op=op))
                    npc = 0
                    for c in range(NT // TC):
                        tk = slice(c * TC, (c + 1) * TC)
                        s_ = (c * TC) // SB
                        for gp in range(16):
                            ct = gp // 4
                            b = npc % NBUF
                            pbk = npc % NBUF
                            npc += 1
                            PA, PB = ps[pbk][:, 0:TC], ps[pbk][:, TC:2 * TC]
                            hb = (c * 4 + ct) % 2
                            K = lambda n_: (n_, b)
                            X = lambda n_: T[(n_, b)][:]
                            P.pe(lambda e, gp=gp, ct=ct, PA=PA, tk=tk: e.matmul(PA, lhsT=BqPad[:, gp, 0, :], rhs=uT[:, ct, tk], start=True, stop=True),
                                 r=["BqPad", ("uT", ct, s_)], w=[psk(pbk)])
                            P.pe(lambda e, gp=gp, ct=ct, PB=PB, tk=tk: e.matmul(PB, lhsT=BqPad[:, gp, 1, :], rhs=uT[:, ct, tk], start=True, stop=True),
                                 r=["BqPad", ("uT", ct, s_)], w=[psk(pbk)])
                            P.dve(TTf(X("t1"), PA, Ec[:, gp, :], ALU.mult), r=[psk(pbk), "Ec"], w=[K("t1")])
                            P.dve(TTf(X("t2"), PB, Es[:, gp, :], ALU.mult), r=[psk(pbk), "Es"], w=[K("t2")])
                            P.dve(TTf(X("t3"), PB, Ec[:, gp, :], ALU.mult), r=[psk(pbk), "Ec"], w=[K("t3")])
                            P.dve(TTf(X("t4"), PA, Es[:, gp, :], ALU.mult), r=[psk(pbk), "Es"], w=[K("t4")])
                            P.pool(TTf(X("wre"), X("t1"), X("t2"), ALU.add), r=[K("t1"), K("t2")], w=[K("wre")])
                            P.pool(TTf(X("wim"), X("t3"), X("t4"), ALU.subtract), r=[K("t3"), K("t4")], w=[K("wim")])
                            P.dve(lambda e, gp=gp, o=X("gre"), d=X("wre"): e.tensor_tensor_scan(
                                out=o, data0=mag[:, gp:gp + 1].to_broadcast([128, TC]), data1=d,
                                initial=carry[:, 0, gp:gp + 1], op0=ALU.mult, op1=ALU.add),
                                r=[K("wre"), "carry", "mag"], w=[K("gre")])
                            P.dve(lambda e, gp=gp, o=X("gim"), d=X("wim"): e.tensor_tensor_scan(
                                out=o, data0=mag[:, gp:gp + 1].to_broadcast([128, TC]), data1=d,
                                initial=carry[:, 1, gp:gp + 1], op0=ALU.mult, op1=ALU.add),
                                r=[K("wim"), "carry", "mag"], w=[K("gim")])
                            P.act(lambda e, gp=gp, g_=T[("gre", b)]: e.activation(out=glast[:, 0, gp:gp + 1], in_=g_[:, TC - 1:TC], func=AF.Copy),
                                  r=[K("gre")], w=[("glast", 0, gp)])
                            P.act(lambda e, gp=gp, g_=T[("gim", b)]: e.activation(out=glast[:, 1, gp:gp + 1], in_=g_[:, TC - 1:TC], func=AF.Copy),
                                  r=[K("gim")], w=[("glast", 1, gp)])
                            P.pool(TTf(X("d1"), X("gre"), Ec[:, gp, :], ALU.mult), r=[K("gre"), "Ec"], w=[K("d1")])
                            P.pool(TTf(X("d2"), X("gim"), Es[:, gp, :], ALU.mult), r=[K("gim"), "Es"], w=[K("d2")])
                            P.pool(TTf(Hh[hb][:, gp % 4, 0, :], X("d1"), X("d2"), ALU.subtract), r=[K("d1"), K("d2")], w=[("Hh", hb, gp % 4, 0)])
                            P.dve(TTf(X("d3"), X("gre"), Es[:, gp, :], ALU.mult), r=[K("gre"), "Es"], w=[K("d3")])
                            P.dve(TTf(X("d4"), X("gim"), Ec[:, gp, :], ALU.mult), r=[K("gim"), "Ec"], w=[K("d4")])
                            P.dve(TTf(Hh[hb][:, gp % 4, 1, :], X("d3"), X("d4"), ALU.add), r=[K("d3"), K("d4")], w=[("Hh", hb, gp % 4, 1)])
                            if gp % 4 == 3:
                                for k in range(8):
                                    gl, ri = k // 2, k % 2
                                    P.pe(lambda e, gl=gl, ri=ri, ct=ct, hb=hb, k=k: e.matmul(
                                        ps[4 + hb][:, 0:TC], lhsT=CPad[:, ct * 4 + gl, ri, :], rhs=Hh[hb][:, gl, ri, :],
                                        start=(k == 0), stop=(k == 7)),
                                        r=["CPad", ("Hh", hb, gl, ri)], w=[psk(4 + hb)])
                                P.dve(lambda e, ct=ct, hb=hb, tk=tk: e.scalar_tensor_tensor(
                                    out=ya[hb][:], in0=uT[:, ct, tk], scalar=dsk[:, ct:ct + 1], in1=ps[4 + hb][:, 0:TC],
                                    op0=ALU.mult, op1=ALU.add), r=[psk(4 + hb), ("uT", ct, s_), "dsk"], w=[("ya", hb)])
                                P.act(lambda e, ct=ct, tk=tk, hb=hb: e.activation(out=ga[:, ct, tk], in_=ya[hb][:], func=AF.Gelu_apprx_tanh),
                                      r=[("ya", hb)], w=[("ga", ct, s_)])
                        gk = [("glast", ri, gp) for ri in range(2) for gp in range(16)]
                        P.dve(TTf(ctmp[:, 0, :], glast[:, 0, :], cosN[:], ALU.mult), r=gk + ["cosN"], w=["ctmp0"])
                        P.dve(TTf(ctmp[:, 1, :], glast[:, 1, :], sinN[:], ALU.mult), r=gk + ["sinN"], w=["ctmp1"])
                        P.dve(TTf(ctmp[:, 2, :], glast[:, 0, :], sinN[:], ALU.mult), r=gk + ["sinN"], w=["ctmp2"])
                        P.dve(TTf(ctmp[:, 3, :], glast[:, 1, :], cosN[:], ALU.mult), r=gk + ["cosN"], w=["ctmp3"])
                        P.dve(TTf(carry[:, 0, :], ctmp[:, 0, :], ctmp[:, 1, :], ALU.subtract), r=["ctmp0", "ctmp1"], w=["carry"])
                        P.dve(TTf(carry[:, 1, :], ctmp[:, 2, :], ctmp[:, 3, :], ALU.add), r=["ctmp2", "ctmp3"], w=["carry"])
                if full:
                    tap("uT", uT[:], [("uT", c_, s_) for c_ in range(4) for s_ in range(NSB)], BF16)
                    tap("cvT", cvT[:], [("cvT", c_, s_) for c_ in range(4) for s_ in range(NSB)], BF16)
                    tap("ga", ga[:], [("ga", c_, s_) for c_ in range(4) for s_ in range(NSB)], BF16)
                    tap("carry", carry[:], ["carry"])
                if not full:
                    return
                P.barrier(dummy[:])
                with contextlib.ExitStack() as s4:
                    unC = sb("un", [128, 8, SB], BF16, s4)
                    sqC = [sb("sq%d" % i, [128, SB], BF16, s4) for i in range(2)]
                    sdC = sb("sd", [128, SB], F32, s4)
                    rstdC = sb("rstd", [128, SB], F32, s4)
                    wgl = [WS[:, i * 3584: i * 3584 + 1024].rearrange("p (k j n) -> p k j n", k=4, j=2) for i in range(2)]
                    wco = [WS[:, i * 3584 + 1024: i * 3584 + 1536].rearrange("p (k n) -> p k n", k=4) for i in range(2)]
                    wga = [WS[:, i * 3584 + 1536: i * 3584 + 3584].rearrange("p (k j n) -> p k j n", k=8, j=2) for i in range(2)]
                    glu_v = W["ssm_w_glu"].rearrange("(kc p) n -> p kc n", p=128)
                    co_v = W["conv_w_out"].rearrange("(kc p) n -> p kc n", p=128)
                    win_v = W["w_in"].rearrange("(kc p) n -> p kc n", p=128)
                    wo = WS[:, SLOT:SLOT + 8192].rearrange("p (k n) -> p k n", k=8)
                    mixed = sb("mixed", [128, 8, SB], BF16, s4)
                    ysv = sb("ysv", [128, SB], F32, s4)
                    g1 = sb("g1", [128, SB], F32, s4)
                    g2 = sb("g2", [128, SB], F32, s4)
                    P.dma("pool", lambda e: e.dma_start(out=wo, in_=rows3(W["w_o"])), w=[("WS", 1, 0), ("WS", 1, 1), ("WS", 1, 2)], epoch=False)
                    P.op("pool", lambda e: e.memset(dummy2[:], 0.0), (), [("WS", 0, 0), ("WS", 0, 1), ("WS", 0, 2)], epoch=False)
                    for s_ in range(NSB):
                        tok = slice(s_ * SB, (s_ + 1) * SB)
                        norm_sb(s_, 1, lambda dt_: unC[:, dt_, :], sqC, sdC, rstdC, lambda dt_: ("un", dt_))
                        for dt_ in range(8):
                            ws_ = (s_ * 8 + dt_) % 2
                            P.dma("sp", lambda e, dt_=dt_, ws_=ws_: e.dma_start(
                                out=WS[:, ws_ * 3584:(ws_ + 1) * 3584], in_=cw_scr[dt_, :, :]),
                                r=[("WS", 0, 0), ("WS", 0, 1), ("WS", 0, 2), ("cwscr", dt_, "co")] + [("cwscr", dt_, a_, j_) for a_ in ("gl", "ga") for j_ in range(2)],
                                w=[("WSC", ws_, "gl", 0), ("WSC", ws_, "gl", 1), ("WSC", ws_, "ga", 0), ("WSC", ws_, "ga", 1), ("WSC", ws_, "co")], epoch=False)
                            b0 = 0 if dt_ % 2 == 0 else 7
                            for kc in range(4):
                                P.pe(lambda e, kc=kc, dt_=dt_, tok=tok, ws_=ws_, b0=b0: e.matmul(ps[b0][:, 0:SB], lhsT=wgl[ws_][:, kc, 0, :],
                                                                                  rhs=ga[:, kc, tok], start=(kc == 0), stop=(kc == 3)),
                                     r=[("WS", 0, 0), ("WS", 0, 1), ("WS", 0, 2), ("WSC", ws_, "gl", 0), ("ga", kc, s_)], w=[psk(b0)])
                            for kc in range(4):
                                P.pe(lambda e, kc=kc, dt_=dt_, tok=tok, ws_=ws_: e.matmul(ps[1][:, 0:SB], lhsT=wgl[ws_][:, kc, 1, :],
                                                                                  rhs=ga[:, kc, tok], start=(kc == 0), stop=(kc == 3)),
                                     r=[("WS", 0, 0), ("WS", 0, 1), ("WS", 0, 2), ("WSC", ws_, "gl", 1), ("ga", kc, s_)], w=[psk(1)])
                            for j in range(2):
                                for kc in range(8):
                                    P.pe(lambda e, kc=kc, dt_=dt_, j=j, ws_=ws_: e.matmul(ps[3 + j][:, 0:SB],
                                                                                  lhsT=wga[ws_][:, kc, j, :],
                                                                                  rhs=unC[:, kc, :], start=(kc == 0), stop=(kc == 7)),
                                         r=[("WS", 0, 0), ("WS", 0, 1), ("WS", 0, 2), ("WSC", ws_, "ga", j), ("un", kc)], w=[psk(3 + j)])
                            for kc in range(4):
                                P.pe(lambda e, kc=kc, dt_=dt_, tok=tok, ws_=ws_: e.matmul(ps[2][:, 0:SB], lhsT=wco[ws_][:, kc, :],
                                                                                  rhs=cvT[:, kc, tok], start=(kc == 0), stop=(kc == 3)),
                                     r=[("WS", 0, 0), ("WS", 0, 1), ("WS", 0, 2), ("WSC", ws_, "co"), ("cvT", kc, s_)], w=[psk(2)])
                            P.act(lambda e: e.activation(out=g1[:], in_=ps[1][:, 0:SB], func=AF.Sigmoid), r=[psk(1)], w=["g1"])
                            P.dve(lambda e, b0=b0: e.tensor_tensor(out=ysv[:], in0=ps[b0][:, 0:SB], in1=g1[:], op=ALU.mult), r=[psk(b0), "g1"], w=["ysv"])
                            P.act(lambda e, dt_=dt_: e.activation(out=g1[:], in_=ps[3][:, 0:SB], func=AF.Sigmoid, bias=bgate[:, dt_:dt_ + 1]),
                                  r=[psk(3), "bgate"], w=["g1"])
                            P.act(lambda e, dt_=dt_: e.activation(out=g2[:], in_=ps[4][:, 0:SB], func=AF.Sigmoid, bias=bgate[:, 8 + dt_:9 + dt_]),
                                  r=[psk(4), "bgate"], w=["g2"])
                            P.dve(lambda e: e.tensor_tensor(out=ysv[:], in0=ysv[:], in1=g1[:], op=ALU.mult), r=["ysv", "g1"], w=["ysv"])
                            P.dve(lambda e: e.tensor_tensor(out=g2[:], in0=ps[2][:, 0:SB], in1=g2[:], op=ALU.mult), r=[psk(2), "g2"], w=["g2"])
                            P.dve(lambda e, dt_=dt_: e.tensor_tensor(out=mixed[:, dt_, :], in0=ysv[:], in1=g2[:], op=ALU.add),
                                  r=["ysv", "g2"], w=[("mixed", dt_)])
                        for dt_ in range(8):
                            pb = 5 + (dt_ % 2)
                            for kc in range(8):
                                P.pe(lambda e, kc=kc, dt_=dt_, pb=pb: e.matmul(ps[pb][:, 0:SB], lhsT=wo[:, kc, dt_ * 128:(dt_ + 1) * 128],
                                                                                rhs=mixed[:, kc, :], start=(kc == 0), stop=(kc == 7)),
                                     r=[("WS", 1, 0), ("WS", 1, 1), ("WS", 1, 2), ("mixed", kc)], w=[psk(pb)])
                            P.dve(lambda e, dt_=dt_, pb=pb, tok=tok: e.tensor_tensor(out=hT[:, dt_, tok], in0=ps[pb][:, 0:SB], in1=hT[:, dt_, tok], op=ALU.add),
                                  r=[psk(pb), ("hT", dt_, s_)], w=[("hT", dt_, s_)])
            tap("hT_mix", hT[:], HTK)
            ffn(blk, 2, W["ffn2_w_gate"], W["ffn2_w_up"], W["ffn2_w_down"])
            tap("hT_ffn2", hT[:], HTK)
            P.barrier(dummy[:])
            with contextlib.ExitStack() as s3:
                sqF = [sb("sq%d" % i, [128, SB], BF16, s3) for i in range(2)]
                sdF = sb("sd", [128, SB], F32, s3)
                rstdF = sb("rstd", [128, SB], F32, s3)
                ost = [sb("ost%d" % i, [128, D], F32, s3) for i in range(2)]
                for s_ in range(NSB):
                    norm_sb(s_, 3, lambda dt_, s_=s_: hT[:, dt_, s_ * SB:(s_ + 1) * SB], sqF, sdF, rstdF,
                            lambda dt_, s_=s_: ("hT", dt_, s_))
                for i in range(16):
                    t0 = 16 + i * 128
                    ob = i % 2
                    sbs = sorted(set([t0 // SB, (t0 + 127) // SB]))
                    for h in range(2):
                        bank = 6 + h
                        for q in range(4):
                            dt_ = h * 4 + q
                            P.pe(lambda e, dt_=dt_, q=q, t0=t0, bank=bank: e.transpose(
                                out=ps[bank][:, q * 128:(q + 1) * 128], in_=hT[:, dt_, t0:t0 + 128], identity=ident[:]),
                                r=[("hT", dt_, s_) for s_ in sbs] + ["ident"], w=[psk(bank)])
                        if h == 0:
                            P.act(lambda e, ob=ob, h=h, bank=bank: e.activation(out=ost[ob][:, h * 512:(h + 1) * 512], in_=ps[bank][:, :], func=AF.Copy),
                                  r=[psk(bank)], w=[("ost", ob, h)])
                        else:
                            P.dve(lambda e, ob=ob, h=h, bank=bank: e.tensor_copy(out=ost[ob][:, h * 512:(h + 1) * 512], in_=ps[bank][:, :]),
                                  r=[psk(bank)], w=[("ost", ob, h)])
                    P.dma("sp", lambda e, i=i, ob=ob: e.dma_start(out=out[i * 128:(i + 1) * 128, :], in_=ost[ob][:]),
                          r=[("ost", ob, 0), ("ost", ob, 1)], w=[("out", i)])
        for blk in range(NBLK):
            do_block(blk)
        P.emit()
    return nc


_NC_CACHE = {}


def kernel(**inputs):
    x = np.asarray(inputs["x"], dtype=np.float32)
    meta = np.asarray(inputs["meta_tokens"], dtype=np.float32)
    B, S, _ = x.shape
    n = 8
    if "nc" not in _NC_CACHE:
        _NC_CACHE["nc"] = build_nc()
    nc = _NC_CACHE["nc"]
    f = lambda k: np.ascontiguousarray(np.asarray(inputs[k], dtype=np.float32))
    shared = {
        "g_ffn1": f("g_ffn1")[0], "ffn1_w_gate": f("ffn1_w_gate")[0], "ffn1_w_up": f("ffn1_w_up")[0],
        "ffn1_w_down": f("ffn1_w_down")[0], "g_mix": f("g_mix")[0], "w_in": f("w_in")[0], "b_gate": f("b_gate")[0],
        "ssm_a_re": f("ssm_a_re")[0], "ssm_a_im": f("ssm_a_im")[0], "ssm_log_dt": f("ssm_log_dt")[0],
        "ssm_b_re": f("ssm_b_re")[0], "ssm_b_im": f("ssm_b_im")[0],
        "ssm_c_re": f("ssm_c_re")[0].reshape(512, 64), "ssm_c_im": f("ssm_c_im")[0].reshape(512, 64),
        "ssm_d": f("ssm_d")[0], "ssm_w_glu": f("ssm_w_glu")[0], "conv_w": f("conv_w")[0].reshape(3, 512),
        "conv_w_out": f("conv_w_out")[0], "w_o": f("w_o")[0], "g_ffn2": f("g_ffn2")[0],
        "ffn2_w_gate": f("ffn2_w_gate")[0], "ffn2_w_up": f("ffn2_w_up")[0], "ffn2_w_down": f("ffn2_w_down")[0],
        "g_final": f("g_final"),
    }
    in_maps = []
    total = NBLK * NT
    for c in range(n):
        b, q = c // 4, c % 4
        seq = np.concatenate([meta, x[b, : 2048 * (q + 1)]], axis=0)
        stream = np.zeros((total, D), np.float32)
        stream[total - seq.shape[0]:] = seq
        m = dict(shared)
        m["xs"] = stream
        in_maps.append(m)
    res = run_bass_kernel_spmd(nc, in_maps, core_ids=list(range(n)))
    outp = np.zeros((B, S, D), np.float32)
    for c in range(n):
        b, q = c // 4, c % 4
        outp[b, q * 2048:(q + 1) * 2048] = np.asarray(res.results[c]["out"], dtype=np.float32)
    return outp
```

```python
import contextlib
import math
import numpy as np
import concourse.bass as bass
import concourse.mybir as mybir
from concourse.bass_utils import run_bass_kernel_spmd

F32 = mybir.dt.float32
BF16 = mybir.dt.bfloat16
I32 = mybir.dt.int32
AF = mybir.ActivationFunctionType
ALU = mybir.AluOpType

D = 1024
DFF = 2816
NT = 2064
NBLK = 4
NSB = 6
SB = 344
TC = 172
NF = 22
GROUPS = [(0, 3), (3, 3), (6, 3), (9, 3), (12, 3), (15, 3), (18, 2), (20, 2)]
EPS = 1e-6
ENGS = ("pe", "act", "dve", "pool", "sp")
PI = math.pi


class Prog:
    def __init__(self, nc, dma_ring=8):
        self.nc = nc
        self.ops = []
        self.dma_ring = dma_ring

    def op(self, eng, fn, reads=(), writes=(), dma=False, epoch=True):
        self.ops.append(dict(eng=eng, fn=fn, reads=tuple(reads) + (("EPOCH",) if epoch else ()), writes=tuple(writes), dma=dma))

    def barrier(self, dummy):
        self.ops.append(dict(eng="dve", fn=lambda e: e.memset(dummy, 0.0), reads=(), writes=("EPOCH",), dma=False))

    def pe(self, fn, r=(), w=()):
        self.op("pe", fn, r, w)

    def act(self, fn, r=(), w=()):
        self.op("act", fn, r, w)

    def dve(self, fn, r=(), w=()):
        self.op("dve", fn, r, w)

    def pool(self, fn, r=(), w=()):
        self.op("pool", fn, r, w)

    def dma(self, q, fn, r=(), w=(), epoch=True):
        self.op(q, fn, r, w, dma=True, epoch=epoch)

    def emit(self):
        nc = self.nc
        ops = self.ops
        last_writer = {}
        readers = {}
        for i, o in enumerate(ops):
            deps = set()
            for r in o["reads"]:
                if r in last_writer:
                    deps.add(last_writer[r])
            for w in o["writes"]:
                if w in last_writer:
                    deps.add(last_writer[w])
                deps.update(readers.get(w, ()))
            deps.discard(i)
            o["deps"] = deps
            for r in o["reads"]:
                readers.setdefault(r, []).append(i)
            for w in o["writes"]:
                last_writer[w] = i
                readers[w] = []
        eidx = {e: 0 for e in ENGS}
        dcount = {e: 0 for e in ENGS}
        for o in ops:
            o["eidx"] = eidx[o["eng"]]
            eidx[o["eng"]] += 1
            o["marked"] = False
            if o["dma"]:
                o["dma_n"] = dcount[o["eng"]]
                dcount[o["eng"]] += 1
        R_ = self.dma_ring
        for i, o in enumerate(ops):
            E = o["eng"]
            need = []
            best = {}
            bestd = {}
            for j in o["deps"]:
                p = ops[j]
                if p["dma"]:
                    k = (p["eng"], p["dma_n"] % R_)
                    if k not in bestd or ops[bestd[k]]["dma_n"] < p["dma_n"]:
                        bestd[k] = j
                else:
                    k = p["eng"]
                    if k not in best or ops[best[k]]["eidx"] < p["eidx"]:
                        best[k] = j
            need.extend(bestd.values())
            for k, j in best.items():
                p = ops[j]
                if k != E:
                    need.append(j)
                    p["marked"] = True
                else:
                    if E == "pe" and not o["dma"]:
                        continue
                    need.append(j)
                    p["marked"] = True
            o["need"] = need
        cnt = {e: 0 for e in ENGS}
        for o in ops:
            if o["marked"]:
                cnt[o["eng"]] += 1
                o["cnt"] = cnt[o["eng"]]
        R = self.dma_ring
        with contextlib.ExitStack() as st:
            esem = {e: st.enter_context(nc.semaphore("s_" + e)) for e in ENGS}
            dsem = {e: [st.enter_context(nc.semaphore("d_%s_%d" % (e, k))) for k in range(R)]
                    for e in ("sp", "act", "pool") if dcount[e] > 0}
            block = st.enter_context(nc.Block())

            def run_engine(E, eng):
                waited = {}

                def wait(sem, val):
                    key = id(sem)
                    if waited.get(key, 0) >= val:
                        return
                    waited[key] = val
                    eng.wait_ge(sem, val)

                for o in ops:
                    if o["eng"] != E:
                        continue
                    for j in o["need"]:
                        p = ops[j]
                        if p["dma"]:
                            n = p["dma_n"]
                            wait(dsem[p["eng"]][n % R], 16 * (n // R + 1))
                        else:
                            wait(esem[p["eng"]], p["cnt"])
                    if o["dma"]:
                        n = o["dma_n"]
                        s = dsem[E][n % R]
                        if n // R > 0:
                            wait(s, 16 * (n // R))
                        o["fn"](eng).then_inc(s, 16)
                    else:
                        ins = o["fn"](eng)
                        if o["marked"]:
                            ins.then_inc(esem[E], 1)
                if E in dsem:
                    tot = dcount[E]
                    for k in range(R):
                        uses = (tot - k + R - 1) // R if tot > k else 0
                        if uses > 0:
                            wait(dsem[E][k], 16 * uses)

            @block.tensor
            def _(eng):
                run_engine("pe", eng)

            @block.scalar
            def _(eng):
                run_engine("act", eng)

            @block.vector
            def _(eng):
                run_engine("dve", eng)

            @block.gpsimd
            def _(eng):
                run_engine("pool", eng)

            @block.sync
            def _(eng):
                run_engine("sp", eng)


DEBUG = False


def build_nc():
    nc = bass.Bass("TRN2", target_bir_lowering=False)
    dr = lambda name, shape, kind="ExternalInput", dt=F32: nc.dram_tensor(name, shape, dt, kind=kind).ap()
    xs = dr("xs", [NBLK * NT, D])
    out = dr("out", [2048, D], kind="ExternalOutput")
    W = {}
    for nm, shp in [("g_ffn1", [D]), ("ffn1_w_gate", [D, DFF]), ("ffn1_w_up", [D, DFF]), ("ffn1_w_down", [DFF, D]),
                    ("g_mix", [D]), ("w_in", [D, 4096]), ("b_gate", [2048]), ("ssm_a_re", [32, 64]),
                    ("ssm_a_im", [32, 64]), ("ssm_log_dt", [32]), ("ssm_b_re", [32, 64, 16]),
                    ("ssm_b_im", [32, 64, 16]), ("ssm_c_re", [512, 64]), ("ssm_c_im", [512, 64]),
                    ("ssm_d", [512]), ("ssm_w_glu", [512, 2048]), ("conv_w", [3, 512]),
                    ("conv_w_out", [512, D]), ("w_o", [D, D]), ("g_ffn2", [D]), ("ffn2_w_gate", [D, DFF]),
                    ("ffn2_w_up", [D, DFF]), ("ffn2_w_down", [DFF, D]), ("g_final", [D])]:
        W[nm] = dr(nm, shp)

    P = Prog(nc)
    nc_allow = nc.allow_non_contiguous_dma(reason="small parameter loads")
    with contextlib.ExitStack() as st:
        st.enter_context(nc_allow)

        uid = [0]

        def sb(name, shape, dt, stack=st):
            uid[0] += 1
            return stack.enter_context(nc.sbuf_tensor("%s_%d" % (name, uid[0]), shape, dt))

        ps = [st.enter_context(nc.psum_tensor("ps%d" % i, [128, 512], F32)) for i in range(8)]

        def tap(name, ap, keys, dt=F32):
            if not DEBUG:
                return
            shape = list(ap.shape)
            d = nc.dram_tensor("dbg_" + name, shape, dt, kind="ExternalOutput").ap()
            P.dma("sp", lambda e: e.dma_start(out=d, in_=ap), r=keys, w=[("dbg", name)])

        HTK = [("hT", dt_, s_) for dt_ in range(8) for s_ in range(NSB)]
        psk = lambda i: ("ps", i)

        hT = sb("hT", [128, 8, NT], F32)
        ident = sb("ident", [128, 128], F32)
        dummy = sb("dummy", [128, 1], F32)
        dummy2 = sb("dummy2", [128, 1], F32)
        SLOT = 9216
        WS = sb("WS", [128, 2 * SLOT], BF16)
        wsg = [WS[:, i * SLOT: i * SLOT + 3072].rearrange("p (k n) -> p k n", k=8) for i in range(2)]
        wsu = [WS[:, i * SLOT + 3072: i * SLOT + 6144].rearrange("p (k n) -> p k n", k=8) for i in range(2)]
        wsd = [WS[:, i * SLOT + 6144: i * SLOT + 9216].rearrange("p (f n) -> p f n", f=3) for i in range(2)]
        ALLWS = [("WS", i, j) for i in range(2) for j in range(3)]
        uT = sb("uT", [128, 4, NT], BF16)
        junk = [sb("junk%d" % i, [128, TC], F32) for i in range(2)]
        acc = sb("acc", [128, 4, 16], F32)
        ctm = sb("ctm", [128, 6, 16], F32)
        rows3 = lambda ap: ap.rearrange("(kc p) n -> p kc n", p=128)
        ones_bf = sb("ones_bf", [128, 128], BF16)
        gains = sb("gains", [128, 4, 8], F32)
        bgate = sb("bgate", [128, 16], F32)
        dsk = sb("dsk", [128, 4], F32)
        cw = sb("cw", [128, 4, 3], F32)
        mag = sb("mag", [128, 16], F32)
        carry = sb("carry", [128, 2, 16], F32)
        glast = sb("glast", [128, 2, 16], F32)
        th = sb("th", [128, 16], F32)
        adr = sb("adr", [128, 16], F32)
        lamN = sb("lamN", [128, 2, 16], F32)
        cosN = sb("cosN", [128, 16], F32)
        sinN = sb("sinN", [128, 16], F32)
        Ec = sb("Ec", [128, 16, TC], F32)
        Es = sb("Es", [128, 16, TC], F32)
        BqPad = sb("BqPad", [128, 16, 2, 128], BF16)
        CPad = sb("CPad", [128, 16, 2, 128], BF16)

        P.pool(lambda e: e.memset(ident[:], 0.0), w=["ident"])
        P.pool(lambda e: e.affine_select(out=ident[:], in_=ident[:], compare_op=ALU.not_equal, fill=1.0,
                                         base=0, pattern=[[-1, 128]], channel_multiplier=1), r=["ident"], w=["ident"])
        P.dve(lambda e: e.memset(ones_bf[:], 1.0), w=["ones"])
        for i, nm in enumerate(["g_ffn1", "g_mix", "g_ffn2", "g_final"]):
            P.dma("sp", lambda e, i=i, nm=nm: e.dma_start(out=gains[:, i, :], in_=W[nm].rearrange("(t p) -> p t", p=128)),
                  w=[("gains", i)])
        P.dma("sp", lambda e: e.dma_start(out=bgate[:], in_=W["b_gate"].rearrange("(t p) -> p t", p=128)), w=["bgate"])
        P.dma("sp", lambda e: e.dma_start(out=dsk[:], in_=W["ssm_d"].rearrange("(t p) -> p t", p=128)), w=["dsk"])
        for j in range(3):
            P.dma("sp", lambda e, j=j: e.dma_start(out=cw[:, :, j], in_=W["conv_w"][j, :].rearrange("(t p) -> p t", p=128)), w=["cw"])

        with contextlib.ExitStack() as s2:
            t = lambda name, shape, dt=F32: sb(name, shape, dt, s2)
            are = t("are", [128, 16]); aim = t("aim", [128, 16]); ldt = t("ldt", [128, 16])
            bre = t("bre", [128, 16, 16]); bim = t("bim", [128, 16, 16])
            cn = t("cn", [128, 2, 4, 2, 64])
            CP = t("CP", [128, 2, 16, 16])
            dtv = t("dtv", [128, 16])
            cs = t("cs", [128, 16]); sn = t("sn", [128, 16])
            lre = t("lre", [128, 16]); lim = t("lim", [128, 16])
            den = t("den", [128, 16]); tmpa = t("tmpa", [128, 16]); tmpb = t("tmpb", [128, 16])
            qre = t("qre", [128, 16]); qim = t("qim", [128, 16]); lm1 = t("lm1", [128, 16])
            bq = t("bq", [128, 2, 16, 16]); tb = t("tb", [128, 16])
            angN = t("angN", [128, 16])
            wkn1 = t("wkn1", [128, 16]); wkn2 = t("wkn2", [128, 16]); wkni = t("wkni", [128, 16], I32)

            pair_view = lambda ap: ap.rearrange("(gp two) p -> (two p) gp", two=2)
            P.dma("sp", lambda e: e.dma_start(out=are[:], in_=pair_view(W["ssm_a_re"])), w=["are"])
            P.dma("sp", lambda e: e.dma_start(out=aim[:], in_=pair_view(W["ssm_a_im"])), w=["aim"])
            for gpar in range(2):
                P.dma("sp", lambda e, gpar=gpar: e.dma_start(
                    out=ldt[gpar * 64:(gpar + 1) * 64, :],
                    in_=W["ssm_log_dt"].rearrange("(g two) -> two g", two=2)[gpar:gpar + 1, :].to_broadcast([64, 16])),
                    w=[("ldt", gpar)])
            bview = lambda ap: ap.rearrange("(gp two) p c -> (two p) gp c", two=2)
            P.dma("sp", lambda e: e.dma_start(out=bre[:], in_=bview(W["ssm_b_re"])), w=["bre"])
            P.dma("sp", lambda e: e.dma_start(out=bim[:], in_=bview(W["ssm_b_im"])), w=["bim"])
            for ri, nm in enumerate(["ssm_c_re", "ssm_c_im"]):
                for dup in range(2):
                    P.dma("sp", lambda e, ri=ri, nm=nm, dup=dup: e.dma_start(
                        out=cn[:, ri, :, dup, :], in_=W[nm].rearrange("(t p) s -> p t s", p=128)), w=[("cn", ri, dup)])
            for ri in range(2):
                for ct in range(4):
                    bank = 6 + (ct % 2)
                    P.pe(lambda e, ri=ri, ct=ct, bank=bank: e.transpose(
                        out=ps[bank][:, 0:128], in_=cn[:, ri, ct, :, :].rearrange("p a b -> p (a b)"), identity=ident[:]),
                        r=[("cn", ri, 0), ("cn", ri, 1), "ident"], w=[psk(bank)])
                    for gpar in range(2):
                        P.act(lambda e, ri=ri, ct=ct, bank=bank, gpar=gpar: e.activation(
                            out=CP[gpar * 64:(gpar + 1) * 64, ri, ct * 4:(ct + 1) * 4, :],
                            in_=ps[bank][gpar * 64:(gpar + 1) * 64, 0:128].rearrange(
                                "p (gl two c) -> p gl two c", two=2, c=16)[:, :, gpar, :],
                            func=AF.Identity, scale=(1.0 if ri == 0 else -1.0)),
                            r=[psk(bank)], w=[("CP", ri, ct, gpar)])
            cp_keys = [("CP", ri, ct, gpar) for ri in range(2) for ct in range(4) for gpar in range(2)]
            ldk = [("ldt", 0), ("ldt", 1)]
            P.act(lambda e: e.activation(out=dtv[:], in_=ldt[:], func=AF.Exp), r=ldk, w=["dtv"])
            P.dve(lambda e: e.tensor_tensor(out=adr[:], in0=are[:], in1=dtv[:], op=ALU.mult), r=["are", "dtv"], w=["adr"])
            P.dve(lambda e: e.tensor_tensor(out=th[:], in0=aim[:], in1=dtv[:], op=ALU.mult), r=["aim", "dtv"], w=["th"])
            P.act(lambda e: e.activation(out=mag[:], in_=adr[:], func=AF.Exp), r=["adr"], w=["mag"])

            def sincos(x, xk, o_sin, o_cos, w1, w2, wi, keys):
                for off, o in ((0.0, o_sin), (PI / 2, o_cos)):
                    ok = keys[0] if o is o_sin else keys[1]
                    P.dve(lambda e, off=off: e.tensor_scalar(out=w1, in0=x, scalar1=off, scalar2=1.0 / (2 * PI),
                                                             op0=ALU.add, op1=ALU.mult), r=[xk], w=["w1" + keys[2]])
                    P.dve(lambda e: e.tensor_copy(out=wi, in_=w1), r=["w1" + keys[2]], w=["wi" + keys[2]])
                    P.dve(lambda e: e.tensor_copy(out=w1, in_=wi), r=["wi" + keys[2]], w=["w1" + keys[2]])
                    P.dve(lambda e: e.scalar_tensor_tensor(out=w2, in0=w1, scalar=-6.25, in1=x,
                                                           op0=ALU.mult, op1=ALU.add), r=["w1" + keys[2], xk], w=["w2" + keys[2]])
                    P.dve(lambda e: e.scalar_tensor_tensor(out=w2, in0=w1, scalar=-(2 * PI - 6.25), in1=w2,
                                                           op0=ALU.mult, op1=ALU.add), r=["w1" + keys[2], "w2" + keys[2]], w=["w2" + keys[2]])
                    P.dve(lambda e, off=off: e.tensor_scalar(out=w2, in0=w2, scalar1=off, scalar2=None, op0=ALU.add),
                          r=["w2" + keys[2]], w=["w2" + keys[2]])
                    for lim_, sgn in ((PI, -1.0), (-PI, 1.0)):
                        cmp = ALU.is_gt if sgn < 0 else ALU.is_lt
                        P.dve(lambda e, lim_=lim_, cmp=cmp: e.tensor_scalar(out=w1, in0=w2, scalar1=lim_, scalar2=None, op0=cmp),
                              r=["w2" + keys[2]], w=["w1" + keys[2]])
                        P.dve(lambda e, sgn=sgn: e.scalar_tensor_tensor(out=w2, in0=w1, scalar=sgn * 2 * PI, in1=w2,
                                                                        op0=ALU.mult, op1=ALU.add),
                              r=["w1" + keys[2], "w2" + keys[2]], w=["w2" + keys[2]])
                    P.act(lambda e, o=o: e.activation(out=o, in_=w2, func=AF.Sin), r=["w2" + keys[2]], w=[ok])

            sincos(th[:], "th", sn[:], cs[:], wkn1[:], wkn2[:], wkni[:], ("sn", "cs", "n"))
            P.dve(lambda e: e.tensor_tensor(out=lre[:], in0=mag[:], in1=cs[:], op=ALU.mult), r=["mag", "cs"], w=["lre"])
            P.dve(lambda e: e.tensor_tensor(out=lim[:], in0=mag[:], in1=sn[:], op=ALU.mult), r=["mag", "sn"], w=["lim"])
            P.dve(lambda e: e.tensor_tensor(out=den[:], in0=are[:], in1=are[:], op=ALU.mult), r=["are"], w=["den"])
            P.dve(lambda e: e.tensor_tensor(out=tmpa[:], in0=aim[:], in1=aim[:], op=ALU.mult), r=["aim"], w=["tmpa"])
            P.dve(lambda e: e.tensor_tensor(out=den[:], in0=den[:], in1=tmpa[:], op=ALU.add), r=["den", "tmpa"], w=["den"])
            P.dve(lambda e: e.reciprocal(out=den[:], in_=den[:]), r=["den"], w=["den"])
            P.dve(lambda e: e.tensor_scalar(out=lm1[:], in0=lre[:], scalar1=-1.0, scalar2=None, op0=ALU.add), r=["lre"], w=["lm1"])
            P.dve(lambda e: e.tensor_tensor(out=tmpa[:], in0=lm1[:], in1=are[:], op=ALU.mult), r=["lm1", "are", "den"], w=["tmpa"])
            P.dve(lambda e: e.tensor_tensor(out=tmpb[:], in0=lim[:], in1=aim[:], op=ALU.mult), r=["lim", "aim"], w=["tmpb"])
            P.dve(lambda e: e.tensor_tensor(out=tmpa[:], in0=tmpa[:], in1=tmpb[:], op=ALU.add), r=["tmpa", "tmpb"], w=["tmpa"])
            P.dve(lambda e: e.tensor_tensor(out=qre[:], in0=tmpa[:], in1=den[:], op=ALU.mult), r=["tmpa", "den"], w=["qre"])
            P.dve(lambda e: e.tensor_tensor(out=tmpa[:], in0=lim[:], in1=are[:], op=ALU.mult), r=["lim", "are", "qre"], w=["tmpa"])
            P.dve(lambda e: e.tensor_tensor(out=tmpb[:], in0=lm1[:], in1=aim[:], op=ALU.mult), r=["lm1", "aim"], w=["tmpb"])
            P.dve(lambda e: e.tensor_tensor(out=tmpa[:], in0=tmpa[:], in1=tmpb[:], op=ALU.subtract), r=["tmpa", "tmpb"], w=["tmpa"])
            P.dve(lambda e: e.tensor_tensor(out=qim[:], in0=tmpa[:], in1=den[:], op=ALU.mult), r=["tmpa", "den"], w=["qim"])
            qb = lambda q: q[:, :].unsqueeze(2).to_broadcast([128, 16, 16])
            P.dve(lambda e: e.tensor_tensor(out=bq[:, 0], in0=bre[:], in1=qb(qre), op=ALU.mult), r=["bre", "qre"], w=["bq0"])
            P.dve(lambda e: e.tensor_tensor(out=bq[:, 1], in0=bim[:], in1=qb(qim), op=ALU.mult), r=["bim", "qim"], w=["bq1"])
            P.dve(lambda e: e.tensor_tensor(out=bq[:, 0], in0=bq[:, 0], in1=bq[:, 1], op=ALU.subtract), r=["bq0", "bq1"], w=["bq0"])
            P.dve(lambda e: e.tensor_tensor(out=bq[:, 1], in0=bim[:], in1=qb(qre), op=ALU.mult), r=["bim", "qre", "bq0"], w=["bq1"])
            P.dve(lambda e: e.tensor_tensor(out=bre[:], in0=bre[:], in1=qb(qim), op=ALU.mult), r=["bre", "qim", "bq0"], w=["bre"])
            P.dve(lambda e: e.tensor_tensor(out=bq[:, 1], in0=bq[:, 1], in1=bre[:], op=ALU.add), r=["bq1", "bre"], w=["bq1"])
            sz = contextlib.ExitStack()
            sz.__enter__()
            Z = sb("Z", [128, 16, 2, 128], F32, sz)
            P.pool(lambda e: e.memset(Z[:], 0.0), w=["Z"])
            P.pool(lambda e: e.memset(CPad[:], 0.0), w=["CPad"])
            for ri in range(2):
                for gpar in range(2):
                    for gl in range(4):
                        col = (2 * gl + gpar) * 16
                        rows = slice(gpar * 64, (gpar + 1) * 64)
                        P.dve(lambda e, ri=ri, rows=rows, gl=gl, col=col: e.tensor_copy(
                            out=Z[rows, gl::4, ri, col:col + 16], in_=bq[rows, ri, gl::4, :]),
                            r=["bq0", "bq1", "Z"], w=["Z"])
                        P.act(lambda e, ri=ri, rows=rows, gl=gl, col=col: e.activation(
                            out=CPad[rows, gl::4, ri, col:col + 16], in_=CP[rows, ri, gl::4, :], func=AF.Copy),
                            r=cp_keys + ["CPad"], w=["CPad"])
            for gp in range(16):
                for ri in range(2):
                    bank = 6 + ((gp * 2 + ri) % 2)
                    P.pe(lambda e, gp=gp, ri=ri, bank=bank: e.transpose(out=ps[bank][:, 0:128], in_=Z[:, gp, ri, :], identity=ident[:]),
                         r=["Z", "ident"], w=[psk(bank)])
                    P.act(lambda e, gp=gp, ri=ri, bank=bank: e.activation(out=BqPad[:, gp, ri, :], in_=ps[bank][:, 0:128], func=AF.Copy),
                          r=[psk(bank)], w=["BqPad"])
            sz.__exit__(None, None, None)
            P.barrier(dummy[:])
            P.dve(lambda e: e.tensor_scalar(out=angN[:], in0=th[:], scalar1=float(TC), scalar2=None, op0=ALU.mult), r=["th"], w=["angN"])
            sincos(angN[:], "angN", sinN[:], cosN[:], wkn1[:], wkn2[:], wkni[:], ("sinN", "cosN", "n"))
            P.act(lambda e: e.activation(out=tmpa[:], in_=adr[:], func=AF.Exp, scale=float(TC)), r=["adr", "qre", "qim"], w=["tmpa"])
            P.dve(lambda e: e.tensor_tensor(out=lamN[:, 0, :], in0=tmpa[:], in1=cosN[:], op=ALU.mult), r=["tmpa", "cosN"], w=["lamN"])
            P.dve(lambda e: e.tensor_tensor(out=lamN[:, 1, :], in0=tmpa[:], in1=sinN[:], op=ALU.mult), r=["tmpa", "sinN"], w=["lamN"])
            P.dve(lambda e: e.memset(carry[:], 0.0), w=["carry"])
            tap("mag", mag[:], ["mag"]); tap("lre", lre[:], ["lre"]); tap("lim", lim[:], ["lim"])
            tap("qre", qre[:], ["qre"]); tap("qim", qim[:], ["qim"]); tap("th", th[:], ["th"])
            tap("BqPad", BqPad[:], ["BqPad"], BF16); tap("CPad", CPad[:], ["CPad"], BF16)
            tap("CP", CP[:], cp_keys); tap("bq", bq[:], ["bq0", "bq1"]); tap("ident", ident[:], ["ident"])

        def make_tables(mode):
            P.barrier(dummy[:])
            with contextlib.ExitStack() as sx:
                iot = sb("iot", [128, TC], F32, sx); ioti = sb("ioti", [128, TC], I32, sx)
                ang = sb("ang", [128, 16, TC], F32, sx); wk1 = sb("wk1", [128, 16, TC], F32, sx)
                wk2 = sb("wk2", [128, 16, TC], F32, sx); wki = sb("wki", [128, 16, TC], I32, sx)
                if mode == "E":
                    P.pool(lambda e: e.iota(ioti[:], pattern=[[1, TC]], base=1, channel_multiplier=0), w=["ioti"])
                else:
                    P.pool(lambda e: e.iota(ioti[:], pattern=[[-1, TC]], base=TC - 1, channel_multiplier=0), w=["ioti"])
                P.dve(lambda e: e.tensor_copy(out=iot[:], in_=ioti[:]), r=["ioti"], w=["iot"])
                for gp in range(16):
                    P.dve(lambda e, gp=gp: e.tensor_scalar(out=ang[:, gp, :], in0=iot[:], scalar1=th[:, gp:gp + 1], scalar2=None,
                                                           op0=ALU.mult), r=["iot", "th"], w=["ang"])
                sincos(ang[:], "ang", Es[:], Ec[:], wk1[:], wk2[:], wki[:], ("Es", "Ec", "t"))
                if mode == "D":
                    for gp in range(16):
                        P.act(lambda e, gp=gp: e.activation(out=ang[:, gp, :], in_=iot[:], func=AF.Exp, scale=adr[:, gp:gp + 1]),
                              r=["iot", "adr", "Es", "Ec"], w=[("magp", gp)])
                    mk = [("magp", gp) for gp in range(16)]
                    P.dve(lambda e: e.tensor_tensor(out=Ec[:], in0=Ec[:], in1=ang[:], op=ALU.mult), r=["Ec"] + mk, w=["Ec"])
                    P.dve(lambda e: e.tensor_tensor(out=Es[:], in0=Es[:], in1=ang[:], op=ALU.mult), r=["Es"] + mk, w=["Es"])
                tap("Ec_" + mode, Ec[:], ["Ec"]); tap("Es_" + mode, Es[:], ["Es"])

        make_tables("D")

        cw_scr = nc.dram_tensor("cw_scr", [8, 128, 3584], BF16).ap()
        glu_v = W["ssm_w_glu"].rearrange("(kc p) n -> p kc n", p=128)
        co_v = W["conv_w_out"].rearrange("(kc p) n -> p kc n", p=128)
        win_v = W["w_in"].rearrange("(kc p) n -> p kc n", p=128)
        for dt_ in range(8):
            for j in range(2):
                P.dma("pool", lambda e, j=j, dt_=dt_: e.dma_start(
                    out=cw_scr[dt_, :, 0:1024].rearrange("p (k j n) -> p k j n", k=4, j=2)[:, :, j, :],
                    in_=glu_v[:, :, j * 1024 + dt_ * 128: j * 1024 + (dt_ + 1) * 128]), w=[("cwscr", dt_, "gl", j)], epoch=False)
                P.dma("pool", lambda e, j=j, dt_=dt_: e.dma_start(
                    out=cw_scr[dt_, :, 1536:3584].rearrange("p (k j n) -> p k j n", k=8, j=2)[:, :, j, :],
                    in_=win_v[:, :, 2048 + j * 1024 + dt_ * 128: 2048 + j * 1024 + (dt_ + 1) * 128]), w=[("cwscr", dt_, "ga", j)], epoch=False)
            P.dma("pool", lambda e, dt_=dt_: e.dma_start(
                out=cw_scr[dt_, :, 1024:1536].rearrange("p (k n) -> p k n", k=4),
                in_=co_v[:, :, dt_ * 128:(dt_ + 1) * 128]), w=[("cwscr", dt_, "co")], epoch=False)

        def norm_sb(s_, gi, xn_ap_fn, sq, sd, rstd, keyfn, pbank=6, tag=0):
            tok = slice(s_ * SB, (s_ + 1) * SB)
            for dt_ in range(8):
                P.act(lambda e, dt_=dt_: e.activation(out=sq[dt_ % 2][:], in_=hT[:, dt_, tok], func=AF.Square),
                      r=[("hT", dt_, s_)], w=[("sq", dt_ % 2)])
                P.pe(lambda e, dt_=dt_: e.matmul(ps[pbank][:, 0:SB], lhsT=ones_bf[:], rhs=sq[dt_ % 2][:],
                                                 start=(dt_ == 0), stop=(dt_ == 7)),
                     r=[("sq", dt_ % 2), "ones"], w=[psk(pbank)])
            P.act(lambda e: e.activation(out=sd[:], in_=ps[pbank][:, 0:SB], func=AF.Sqrt, scale=1.0 / D, bias=EPS),
                  r=[psk(pbank)], w=[("sd", tag)])
            P.dve(lambda e: e.reciprocal(out=rstd[:], in_=sd[:]), r=[("sd", tag)], w=[("rstd", tag)])
            for dt_ in range(8):
                P.dve(lambda e, dt_=dt_: e.scalar_tensor_tensor(out=xn_ap_fn(dt_), in0=hT[:, dt_, tok],
                                                                scalar=gains[:, gi, dt_:dt_ + 1], in1=rstd[:],
                                                                op0=ALU.mult, op1=ALU.mult),
                      r=[("hT", dt_, s_), ("rstd", tag), ("gains", gi)], w=[keyfn(dt_)])

        def ffn(blk, gi, wg, wu, wd, bg=None):
            P.barrier(dummy[:])
            with contextlib.ExitStack() as s3:
                xn = sb("xn", [128, 8, NT], BF16, s3)
                sq = [sb("sq%d" % i, [128, SB], BF16, s3) for i in range(2)]
                sd = sb("sd", [128, SB], F32, s3)
                rstd = sb("rstd", [128, SB], F32, s3)
                actb = [sb("actb%d" % i, [128, 3, SB], BF16, s3) for i in range(2)]
                sg = [sb("sg%d" % i, [128, SB], F32, s3) for i in range(2)]
                for s_ in range(NSB):
                    norm_sb(s_, gi, lambda dt_, s_=s_: xn[:, dt_, s_ * SB:(s_ + 1) * SB], sq, sd, rstd,
                            lambda dt_, s_=s_: ("xn", dt_, s_))
                cnt = 0
                prev_down = [None]

                def emit_down(nf, sl, ab, s_, tok):
                    for dt_ in range(8):
                        pb = 4 + (dt_ % 2)
                        for fi in range(nf):
                            P.pe(lambda e, fi=fi, dt_=dt_, pb=pb, sl=sl, ab=ab, nf=nf: e.matmul(
                                ps[pb][:, 0:SB], lhsT=wsd[sl][:, fi, dt_ * 128:(dt_ + 1) * 128], rhs=actb[ab][:, fi, :],
                                start=(fi == 0), stop=(fi == nf - 1)),
                                r=[("WS", sl, 2), ("actb", ab, fi)], w=[psk(pb)])
                        P.dve(lambda e, dt_=dt_, pb=pb, tok=tok: e.scalar_tensor_tensor(
                            out=hT[:, dt_, tok], in0=ps[pb][:, 0:SB], scalar=0.5, in1=hT[:, dt_, tok],
                            op0=ALU.mult, op1=ALU.add),
                            r=[psk(pb), ("hT", dt_, s_)], w=[("hT", dt_, s_)])
                        if bg is not None and dt_ % 4 == 3:
                            next(bg, None)

                for gix, (f0, nf) in enumerate(GROUPS):
                    sl = gix % 2
                    P.dma("pool", lambda e, f0=f0, nf=nf, sl=sl: e.dma_start(
                        out=wsg[sl][:, :, 0:nf * 128], in_=rows3(wg)[:, :, f0 * 128:(f0 + nf) * 128]),
                        w=[("WS", sl, 0)], epoch=False)
                    P.dma("pool", lambda e, f0=f0, nf=nf, sl=sl: e.dma_start(
                        out=wsu[sl][:, :, 0:nf * 128], in_=rows3(wu)[:, :, f0 * 128:(f0 + nf) * 128]),
                        w=[("WS", sl, 1)], epoch=False)
                    P.dma("pool", lambda e, f0=f0, nf=nf, sl=sl: e.dma_start(
                        out=wsd[sl][:, 0:nf, :], in_=rows3(wd)[:, f0:f0 + nf, :]),
                        w=[("WS", sl, 2)], epoch=False)
                    for s_ in range(NSB):
                        tok = slice(s_ * SB, (s_ + 1) * SB)
                        ab = (gix * NSB + s_) % 2
                        for fi in range(nf):
                            par = cnt % 2
                            cnt += 1
                            for kc in range(8):
                                P.pe(lambda e, kc=kc, fi=fi, par=par, sl=sl, tok=tok: e.matmul(
                                    ps[par][:, 0:SB], lhsT=wsg[sl][:, kc, fi * 128:(fi + 1) * 128], rhs=xn[:, kc, tok],
                                    start=(kc == 0), stop=(kc == 7)),
                                    r=[("WS", sl, 0), ("xn", kc, s_)], w=[psk(par)])
                            for kc in range(8):
                                P.pe(lambda e, kc=kc, fi=fi, par=par, sl=sl, tok=tok: e.matmul(
                                    ps[2 + par][:, 0:SB], lhsT=wsu[sl][:, kc, fi * 128:(fi + 1) * 128], rhs=xn[:, kc, tok],
                                    start=(kc == 0), stop=(kc == 7)),
                                    r=[("WS", sl, 1), ("xn", kc, s_)], w=[psk(2 + par)])
                            P.act(lambda e, par=par: e.activation(out=sg[par][:], in_=ps[par][:, 0:SB], func=AF.Silu),
                                  r=[psk(par)], w=[("sg", par)])
                            P.dve(lambda e, par=par, ab=ab, fi=fi: e.tensor_tensor(
                                out=actb[ab][:, fi, :], in0=ps[2 + par][:, 0:SB], in1=sg[par][:], op=ALU.mult),
                                r=[psk(2 + par), ("sg", par)], w=[("actb", ab, fi)])
                            if bg is not None:
                                next(bg, None)
                        if prev_down[0] is not None:
                            prev_down[0]()
                        prev_down[0] = (lambda nf=nf, sl=sl, ab=ab, s_=s_, tok=tok: emit_down(nf, sl, ab, s_, tok))
                if prev_down[0] is not None:
                    prev_down[0]()
                if bg is not None:
                    for _ in bg:
                        pass

        pending = []

        def ssm_hist():
            for c in range(NT // TC):
                tk = slice(c * TC, (c + 1) * TC)
                s_ = (c * TC) // SB
                for gp in range(16):
                    ct = gp // 4
                    bank = 6 + (gp % 2)
                    PA, PB = ps[bank][:, 0:TC], ps[bank][:, TC:2 * TC]
                    P.pe(lambda e, gp=gp, ct=ct, PA=PA, tk=tk: e.matmul(PA, lhsT=BqPad[:, gp, 0, :], rhs=uT[:, ct, tk], start=True, stop=True),
                         r=["BqPad", ("uT", ct, s_)], w=[psk(bank)])
                    P.pe(lambda e, gp=gp, ct=ct, PB=PB, tk=tk: e.matmul(PB, lhsT=BqPad[:, gp, 1, :], rhs=uT[:, ct, tk], start=True, stop=True),
                         r=["BqPad", ("uT", ct, s_)], w=[psk(bank)])
                    for k, (src, tab, tkey) in enumerate([(PA, Ec, "Ec"), (PB, Es, "Es"), (PA, Es, "Es"), (PB, Ec, "Ec")]):
                        P.dve(lambda e, gp=gp, k=k, src=src, tab=tab: e.scalar_tensor_tensor(
                            out=junk[k % 2][:], in0=src, scalar=1.0, in1=tab[:, gp, :],
                            op0=ALU.mult, op1=ALU.mult, accum_out=acc[:, k, gp:gp + 1]),
                            r=[psk(bank), tkey], w=[("junk", k % 2), ("acc", k, gp)])
                    yield
                ak = [("acc", k, gp) for k in range(4) for gp in range(16)]
                TT = lambda o, a_, b_, op: (lambda e: e.tensor_tensor(out=o, in0=a_, in1=b_, op=op))
                P.dve(TT(ctm[:, 0, :], acc[:, 0, :], acc[:, 1, :], ALU.subtract), r=ak, w=["ctm0"])
                P.dve(TT(ctm[:, 1, :], acc[:, 2, :], acc[:, 3, :], ALU.add), r=ak, w=["ctm1"])
                P.dve(TT(ctm[:, 2, :], lamN[:, 0, :], carry[:, 0, :], ALU.mult), r=["lamN", "carry"], w=["ctm2"])
                P.dve(TT(ctm[:, 3, :], lamN[:, 1, :], carry[:, 1, :], ALU.mult), r=["lamN", "carry"], w=["ctm3"])
                P.dve(TT(ctm[:, 4, :], lamN[:, 0, :], carry[:, 1, :], ALU.mult), r=["lamN", "carry"], w=["ctm4"])
                P.dve(TT(ctm[:, 5, :], lamN[:, 1, :], carry[:, 0, :], ALU.mult), r=["lamN", "carry"], w=["ctm5"])
                P.dve(TT(ctm[:, 2, :], ctm[:, 2, :], ctm[:, 3, :], ALU.subtract), r=["ctm2", "ctm3"], w=["ctm2"])
                P.dve(TT(ctm[:, 4, :], ctm[:, 4, :], ctm[:, 5, :], ALU.add), r=["ctm4", "ctm5"], w=["ctm4"])
                P.dve(TT(carry[:, 0, :], ctm[:, 2, :], ctm[:, 0, :], ALU.add), r=["ctm2", "ctm0", "ctm4"], w=["carry"])
                P.dve(TT(carry[:, 1, :], ctm[:, 4, :], ctm[:, 1, :], ALU.add), r=["ctm4", "ctm1"], w=["carry"])
                yield

        def do_block(blk):
            full = (blk == NBLK - 1)
            P.barrier(dummy[:])
            with contextlib.ExitStack() as s3:
                xst = [sb("xst%d" % i, [128, D], F32, s3) for i in range(6)]
                ntile = (NT + 127) // 128
                for i in range(ntile):
                    rows = min(128, NT - i * 128)
                    xb = i % 6
                    P.dma("sp", lambda e, i=i, rows=rows, xb=xb: e.dma_start(
                        out=xst[xb][0:rows, :], in_=xs[blk * NT + i * 128: blk * NT + i * 128 + rows, :]),
                        w=[("xst", xb)])
                    sbs = sorted(set([(i * 128) // SB, (i * 128 + rows - 1) // SB]))
                    for h in range(2):
                        bank = 6 + h
                        for q in range(4):
                            dt_ = h * 4 + q
                            P.pe(lambda e, dt_=dt_, q=q, rows=rows, xb=xb, bank=bank: e.transpose(
                                out=ps[bank][:, q * 128:q * 128 + rows], in_=xst[xb][0:rows, dt_ * 128:(dt_ + 1) * 128],
                                identity=ident[0:rows, 0:rows]),
                                r=[("xst", xb), "ident"], w=[psk(bank)])
                        eng = P.act if h == 0 else P.dve
                        if h == 0:
                            P.act(lambda e, h=h, i=i, rows=rows, bank=bank: e.activation(
                                out=hT[:, h * 4:(h + 1) * 4, i * 128:i * 128 + rows],
                                in_=ps[bank][:, :].rearrange("p (q c) -> p q c", q=4)[:, :, 0:rows], func=AF.Copy),
                                r=[psk(bank)], w=[("hT", dt_, s_) for dt_ in range(h * 4, h * 4 + 4) for s_ in sbs])
                        else:
                            P.dve(lambda e, h=h, i=i, rows=rows, bank=bank: e.tensor_copy(
                                out=hT[:, h * 4:(h + 1) * 4, i * 128:i * 128 + rows],
                                in_=ps[bank][:, :].rearrange("p (q c) -> p q c", q=4)[:, :, 0:rows]),
                                r=[psk(bank)], w=[("hT", dt_, s_) for dt_ in range(h * 4, h * 4 + 4) for s_ in sbs])
            if full:
                tap("hT_load", hT[:], HTK)
            ffn(blk, 0, W["ffn1_w_gate"], W["ffn1_w_up"], W["ffn1_w_down"], bg=(pending.pop() if pending else None))
            if full:
                make_tables("E")
            if full:
                tap("hT_ffn1", hT[:], HTK)
            P.barrier(dummy[:])
            with contextlib.ExitStack() as s3:
                cvT = sb("cvT", [128, 4, NT], BF16, s3) if full else None
                ga = uT
                P.barrier(dummy[:])
                with contextlib.ExitStack() as s4:
                    unA = [sb("un%d" % i, [128, 8, SB], BF16, s4) for i in range(2)]
                    sqA = [sb("sq%d" % i, [128, SB], BF16, s4) for i in range(2)]
                    sdA = [sb("sd%d" % i, [128, SB], F32, s4) for i in range(2)]
                    rstdA = [sb("rstd%d" % i, [128, SB], F32, s4) for i in range(2)]
                    wu_ = WS[:, 0:4096].rearrange("p (k n) -> p k n", k=8)
                    P.dma("pool", lambda e: e.dma_start(out=wu_, in_=rows3(W["w_in"])[:, :, 0:512]), w=ALLWS, epoch=False)
                    if full:
                        wv = WS[:, 4096:16384].rearrange("p (k n) -> p k n", k=8)
                        zs = sb("zs", [128, 4, SB + 2], F32, s4)
                        vv = sb("vv", [128, SB], F32, s4)
                        c1 = sb("c1", [128, SB], F32, s4)
                        P.dma("pool", lambda e: e.dma_start(out=wv, in_=rows3(W["w_in"])[:, :, 512:2048]),
                              w=ALLWS, epoch=False)
                        P.dve(lambda e: e.memset(zs[:], 0.0), w=[("zs", c) for c in range(4)])
                    for s_ in range(NSB):
                        tok = slice(s_ * SB, (s_ + 1) * SB)
                        npar = s_ % 2
                        norm_sb(s_, 1, lambda dt_, npar=npar: unA[npar][:, dt_, :], sqA, sdA[npar], rstdA[npar],
                                lambda dt_, npar=npar: ("un", npar, dt_), pbank=6 + npar, tag=npar)
                        for ot in range(4):
                            pb = 4 + (ot % 2)
                            for kc in range(8):
                                P.pe(lambda e, kc=kc, ot=ot, pb=pb, npar=npar: e.matmul(
                                    ps[pb][:, 0:SB], lhsT=wu_[:, kc, ot * 128:(ot + 1) * 128], rhs=unA[npar][:, kc, :],
                                    start=(kc == 0), stop=(kc == 7)), r=ALLWS + [("un", npar, kc)], w=[psk(pb)])
                            P.act(lambda e, ot=ot, pb=pb, tok=tok: e.activation(out=uT[:, ot, tok], in_=ps[pb][:, 0:SB], func=AF.Copy),
                                  r=[psk(pb)], w=[("uT", ot, s_)])
                        if full:
                            for ot in range(4):
                                for j, bank in ((0, 0), (2, 1), (1, 2)):
                                    for kc in range(8):
                                        P.pe(lambda e, kc=kc, ot=ot, j=j, bank=bank, npar=npar: e.matmul(
                                            ps[bank][:, 0:SB], lhsT=wv[:, kc, j * 512 + ot * 128: j * 512 + (ot + 1) * 128],
                                            rhs=unA[npar][:, kc, :], start=(kc == 0), stop=(kc == 7)),
                                            r=ALLWS + [("un", npar, kc)], w=[psk(bank)])
                                P.act(lambda e: e.activation(out=vv[:], in_=ps[0][:, 0:SB], func=AF.Copy), r=[psk(0)], w=["vv"])
                                P.dve(lambda e, ot=ot: e.tensor_copy(out=zs[:, ot, 0:2], in_=zs[:, ot, SB:SB + 2]),
                                      r=[("zs", ot)], w=[("zs", ot)])
                                P.dve(lambda e, ot=ot: e.tensor_tensor(out=zs[:, ot, 2:SB + 2], in0=ps[1][:, 0:SB], in1=vv[:], op=ALU.mult),
                                      r=[psk(1), "vv", ("zs", ot)], w=[("zs", ot)])
                                P.dve(lambda e, ot=ot: e.tensor_scalar(out=c1[:], in0=zs[:, ot, 0:SB], scalar1=cw[:, ot, 0:1], scalar2=None,
                                                                       op0=ALU.mult), r=[("zs", ot), "cw"], w=["c1"])
                                P.dve(lambda e, ot=ot: e.scalar_tensor_tensor(out=c1[:], in0=zs[:, ot, 1:SB + 1], scalar=cw[:, ot, 1:2], in1=c1[:],
                                                                              op0=ALU.mult, op1=ALU.add), r=[("zs", ot), "cw", "c1"], w=["c1"])
                                P.dve(lambda e, ot=ot: e.scalar_tensor_tensor(out=c1[:], in0=zs[:, ot, 2:SB + 2], scalar=cw[:, ot, 2:3], in1=c1[:],
                                                                              op0=ALU.mult, op1=ALU.add), r=[("zs", ot), "cw", "c1"], w=["c1"])
                                P.dve(lambda e, ot=ot, tok=tok: e.tensor_tensor(out=cvT[:, ot, tok], in0=ps[2][:, 0:SB], in1=c1[:], op=ALU.mult),
                                      r=[psk(2), "c1"], w=[("cvT", ot, s_)])
                if full:
                    tap("uT_A", uT[:], [("uT", c_, s_) for c_ in range(4) for s_ in range(NSB)], BF16)
                if not full:
                    pending.append(ssm_hist())
                    return
                P.barrier(dummy[:])
                with contextlib.ExitStack() as s4:
                    NBUF = 3
                    TN = ["t1", "t2", "t3", "t4", "wre", "wim", "gre", "gim"]
                    T = {}
                    for b_ in range(NBUF):
                        for n_ in TN:
                            T[(n_, b_)] = sb("%s_%d" % (n_, b_), [128, TC], F32, s4)
                        for n_ in ["d1", "d2", "d3", "d4"]:
                            T[(n_, b_)] = sb("%s_%d" % (n_, b_), [128, TC], BF16, s4)
                    ctmp = sb("ctmp", [128, 4, 16], F32, s4)
                    Hh = [sb("Hh%d" % i, [128, 4, 2, TC], BF16, s4) for i in range(2)]
                    ya = [sb("ya%d" % i, [128, TC], F32, s4) for i in range(2)]
                    TTf = lambda o, a_, b_, op: (lambda e: e.tensor_tensor(out=o, in0=a_, in1=b_, op=op))
                    NCH = NT // TC
                    NPAIR = NCH * 16

                    def ctx(i):
                        c, gp = divmod(i, 16)
                        return dict(c=c, gp=gp, ct=gp // 4, b=i % NBUF, pbk=i % NBUF, tk=slice(c * TC, (c + 1) * TC),
                                    s_=(c * TC) // SB, hb=(c * 4 + gp // 4) % 2)

                    def st1(i):
                        q = ctx(i); gp, ct, b, pbk, tk, s_ = q["gp"], q["ct"], q["b"], q["pbk"], q["tk"], q["s_"]
                        PA, PB = ps[pbk][:, 0:TC], ps[pbk][:, TC:2 * TC]
                        K = lambda n_: (n_, b)
                        X = lambda n_: T[(n_, b)][:]
                        P.pe(lambda e: e.matmul(PA, lhsT=BqPad[:, gp, 0, :], rhs=uT[:, ct, tk], start=True, stop=True),
                             r=["BqPad", ("uT", ct, s_)], w=[psk(pbk)])
                        P.pe(lambda e: e.matmul(PB, lhsT=BqPad[:, gp, 1, :], rhs=uT[:, ct, tk], start=True, stop=True),
                             r=["BqPad", ("uT", ct, s_)], w=[psk(pbk)])
                        P.dve(TTf(X("t1"), PA, Ec[:, gp, :], ALU.mult), r=[psk(pbk), "Ec"], w=[K("t1")])
                        P.dve(TTf(X("t2"), PB, Es[:, gp, :], ALU.mult), r=[psk(pbk), "Es"], w=[K("t2")])
                        P.dve(TTf(X("t3"), PB, Ec[:, gp, :], ALU.mult), r=[psk(pbk), "Ec"], w=[K("t3")])
                        P.dve(TTf(X("t4"), PA, Es[:, gp, :], ALU.mult), r=[psk(pbk), "Es"], w=[K("t4")])
                        P.pool(TTf(X("wre"), X("t1"), X("t2"), ALU.add), r=[K("t1"), K("t2")], w=[K("wre")])
                        P.pool(TTf(X("wim"), X("t3"), X("t4"), ALU.subtract), r=[K("t3"), K("t4")], w=[K("wim")])

                    def st2(i):
                        q = ctx(i); gp, b, c = q["gp"], q["b"], q["c"]
                        K = lambda n_: (n_, b)
                        X = lambda n_: T[(n_, b)][:]
                        gre_, gim_, wre_, wim_ = X("gre"), X("gim"), X("wre"), X("wim")
                        P.dve(lambda e: e.tensor_tensor_scan(out=gre_, data0=mag[:, gp:gp + 1].to_broadcast([128, TC]), data1=wre_,
                                                             initial=carry[:, 0, gp:gp + 1], op0=ALU.mult, op1=ALU.add),
                              r=[K("wre"), "carry", "mag"], w=[K("gre")])
                        P.dve(lambda e: e.tensor_tensor_scan(out=gim_, data0=mag[:, gp:gp + 1].to_broadcast([128, TC]), data1=wim_,
                                                             initial=carry[:, 1, gp:gp + 1], op0=ALU.mult, op1=ALU.add),
                              r=[K("wim"), "carry", "mag"], w=[K("gim")])
                        P.act(lambda e: e.activation(out=glast[:, 0, gp:gp + 1], in_=gre_[:, TC - 1:TC], func=AF.Copy),
                              r=[K("gre")], w=[("glast", 0, gp)])
                        P.act(lambda e: e.activation(out=glast[:, 1, gp:gp + 1], in_=gim_[:, TC - 1:TC], func=AF.Copy),
                              r=[K("gim")], w=[("glast", 1, gp)])
                        if gp == 15:
                            gk = [("glast", ri, g_) for ri in range(2) for g_ in range(16)]
                            P.dve(TTf(ctmp[:, 0, :], glast[:, 0, :], cosN[:], ALU.mult), r=gk + ["cosN"], w=["ctmp0"])
                            P.dve(TTf(ctmp[:, 1, :], glast[:, 1, :], sinN[:], ALU.mult), r=gk + ["sinN"], w=["ctmp1"])
                            P.dve(TTf(ctmp[:, 2, :], glast[:, 0, :], sinN[:], ALU.mult), r=gk + ["sinN"], w=["ctmp2"])
                            P.dve(TTf(ctmp[:, 3, :], glast[:, 1, :], cosN[:], ALU.mult), r=gk + ["cosN"], w=["ctmp3"])
                            P.dve(TTf(carry[:, 0, :], ctmp[:, 0, :], ctmp[:, 1, :], ALU.subtract), r=["ctmp0", "ctmp1"], w=["carry"])
                            P.dve(TTf(carry[:, 1, :], ctmp[:, 2, :], ctmp[:, 3, :], ALU.add), r=["ctmp2", "ctmp3"], w=["carry"])

                    def st3(i):
                        q = ctx(i); gp, ct, b, tk, s_, hb = q["gp"], q["ct"], q["b"], q["tk"], q["s_"], q["hb"]
                        K = lambda n_: (n_, b)
                        X = lambda n_: T[(n_, b)][:]
                        P.pool(TTf(X("d1"), X("gre"), Ec[:, gp, :], ALU.mult), r=[K("gre"), "Ec"], w=[K("d1")])
                        P.pool(TTf(X("d2"), X("gim"), Es[:, gp, :], ALU.mult), r=[K("gim"), "Es"], w=[K("d2")])
                        P.pool(TTf(Hh[hb][:, gp % 4, 0, :], X("d1"), X("d2"), ALU.subtract), r=[K("d1"), K("d2")], w=[("Hh", hb, gp % 4, 0)])
                        P.dve(TTf(X("d3"), X("gre"), Es[:, gp, :], ALU.mult), r=[K("gre"), "Es"], w=[K("d3")])
                        P.dve(TTf(X("d4"), X("gim"), Ec[:, gp, :], ALU.mult), r=[K("gim"), "Ec"], w=[K("d4")])
                        P.pool(TTf(Hh[hb][:, gp % 4, 1, :], X("d3"), X("d4"), ALU.add), r=[K("d3"), K("d4")], w=[("Hh", hb, gp % 4, 1)])
                        if gp % 4 == 3:
                            for k in range(8):
                                gl, ri = k // 2, k % 2
                                P.pe(lambda e, gl=gl, ri=ri, k=k: e.matmul(
                                    ps[4 + hb][:, 0:TC], lhsT=CPad[:, ct * 4 + gl, ri, :], rhs=Hh[hb][:, gl, ri, :],
                                    start=(k == 0), stop=(k == 7)),
                                    r=["CPad", ("Hh", hb, gl, ri)], w=[psk(4 + hb)])
                            P.dve(lambda e: e.scalar_tensor_tensor(
                                out=ya[hb][:], in0=uT[:, ct, tk], scalar=dsk[:, ct:ct + 1], in1=ps[4 + hb][:, 0:TC],
                                op0=ALU.mult, op1=ALU.add), r=[psk(4 + hb), ("uT", ct, s_), "dsk"], w=[("ya", hb)])
                            P.act(lambda e: e.activation(out=ga[:, ct, tk], in_=ya[hb][:], func=AF.Gelu_apprx_tanh),
                                  r=[("ya", hb)], w=[("ga", ct, s_)])

                    for i in range(NPAIR + 2):
                        if i < NPAIR:
                            st1(i)
                        if 0 <= i - 1 < NPAIR:
                            st2(i - 1)
                        if 0 <= i - 2 < NPAIR:
                            st3(i - 2)
                if full:
                    tap("uT", uT[:], [("uT", c_, s_) for c_ in range(4) for s_ in range(NSB)], BF16)
                    tap("cvT", cvT[:], [("cvT", c_, s_) for c_ in range(4) for s_ in range(NSB)], BF16)
                    tap("ga", ga[:], [("ga", c_, s_) for c_ in range(4) for s_ in range(NSB)], BF16)
                    tap("carry", carry[:], ["carry"])
                if not full:
                    return
                P.barrier(dummy[:])
                with contextlib.ExitStack() as s4:
                    unC = sb("un", [128, 8, SB], BF16, s4)
                    sqC = [sb("sq%d" % i, [128, SB], BF16, s4) for i in range(2)]
                    sdC = sb("sd", [128, SB], F32, s4)
                    rstdC = sb("rstd", [128, SB], F32, s4)
                    wgl = [WS[:, i * 3584: i * 3584 + 1024].rearrange("p (k j n) -> p k j n", k=4, j=2) for i in range(2)]
                    wco = [WS[:, i * 3584 + 1024: i * 3584 + 1536].rearrange("p (k n) -> p k n", k=4) for i in range(2)]
                    wga = [WS[:, i * 3584 + 1536: i * 3584 + 3584].rearrange("p (k j n) -> p k j n", k=8, j=2) for i in range(2)]
                    glu_v = W["ssm_w_glu"].rearrange("(kc p) n -> p kc n", p=128)
                    co_v = W["conv_w_out"].rearrange("(kc p) n -> p kc n", p=128)
                    win_v = W["w_in"].rearrange("(kc p) n -> p kc n", p=128)
                    wo = WS[:, SLOT:SLOT + 8192].rearrange("p (k n) -> p k n", k=8)
                    mixed = sb("mixed", [128, 8, SB], BF16, s4)
                    ysv = sb("ysv", [128, SB], F32, s4)
                    g1 = sb("g1", [128, SB], F32, s4)
                    g2 = sb("g2", [128, SB], F32, s4)
                    P.dma("pool", lambda e: e.dma_start(out=wo, in_=rows3(W["w_o"])), w=[("WS", 1, 0), ("WS", 1, 1), ("WS", 1, 2)], epoch=False)
                    P.op("pool", lambda e: e.memset(dummy2[:], 0.0), (), [("WS", 0, 0), ("WS", 0, 1), ("WS", 0, 2)], epoch=False)
                    for s_ in range(NSB):
                        tok = slice(s_ * SB, (s_ + 1) * SB)
                        norm_sb(s_, 1, lambda dt_: unC[:, dt_, :], sqC, sdC, rstdC, lambda dt_: ("un", dt_))
                        for dt_ in range(8):
                            ws_ = (s_ * 8 + dt_) % 2
                            P.dma("sp", lambda e, dt_=dt_, ws_=ws_: e.dma_start(
                                out=WS[:, ws_ * 3584:(ws_ + 1) * 3584], in_=cw_scr[dt_, :, :]),
                                r=[("WS", 0, 0), ("WS", 0, 1), ("WS", 0, 2), ("cwscr", dt_, "co")] + [("cwscr", dt_, a_, j_) for a_ in ("gl", "ga") for j_ in range(2)],
                                w=[("WSC", ws_, "gl", 0), ("WSC", ws_, "gl", 1), ("WSC", ws_, "ga", 0), ("WSC", ws_, "ga", 1), ("WSC", ws_, "co")], epoch=False)
                            b0 = 0 if dt_ % 2 == 0 else 7
                            for kc in range(4):
                                P.pe(lambda e, kc=kc, dt_=dt_, tok=tok, ws_=ws_, b0=b0: e.matmul(ps[b0][:, 0:SB], lhsT=wgl[ws_][:, kc, 0, :],
                                                                                  rhs=ga[:, kc, tok], start=(kc == 0), stop=(kc == 3)),
                                     r=[("WS", 0, 0), ("WS", 0, 1), ("WS", 0, 2), ("WSC", ws_, "gl", 0), ("ga", kc, s_)], w=[psk(b0)])
                            for kc in range(4):
                                P.pe(lambda e, kc=kc, dt_=dt_, tok=tok, ws_=ws_: e.matmul(ps[1][:, 0:SB], lhsT=wgl[ws_][:, kc, 1, :],
                                                                                  rhs=ga[:, kc, tok], start=(kc == 0), stop=(kc == 3)),
                                     r=[("WS", 0, 0), ("WS", 0, 1), ("WS", 0, 2), ("WSC", ws_, "gl", 1), ("ga", kc, s_)], w=[psk(1)])
                            for j in range(2):
                                for kc in range(8):
                                    P.pe(lambda e, kc=kc, dt_=dt_, j=j, ws_=ws_: e.matmul(ps[3 + j][:, 0:SB],
                                                                                  lhsT=wga[ws_][:, kc, j, :],
                                                                                  rhs=unC[:, kc, :], start=(kc == 0), stop=(kc == 7)),
                                         r=[("WS", 0, 0), ("WS", 0, 1), ("WS", 0, 2), ("WSC", ws_, "ga", j), ("un", kc)], w=[psk(3 + j)])
                            for kc in range(4):
                                P.pe(lambda e, kc=kc, dt_=dt_, tok=tok, ws_=ws_: e.matmul(ps[2][:, 0:SB], lhsT=wco[ws_][:, kc, :],
                                                                                  rhs=cvT[:, kc, tok], start=(kc == 0), stop=(kc == 3)),
                                     r=[("WS", 0, 0), ("WS", 0, 1), ("WS", 0, 2), ("WSC", ws_, "co"), ("cvT", kc, s_)], w=[psk(2)])
                            P.act(lambda e: e.activation(out=g1[:], in_=ps[1][:, 0:SB], func=AF.Sigmoid), r=[psk(1)], w=["g1"])
                            P.dve(lambda e, b0=b0: e.tensor_tensor(out=ysv[:], in0=ps[b0][:, 0:SB], in1=g1[:], op=ALU.mult), r=[psk(b0), "g1"], w=["ysv"])
                            P.act(lambda e, dt_=dt_: e.activation(out=g1[:], in_=ps[3][:, 0:SB], func=AF.Sigmoid, bias=bgate[:, dt_:dt_ + 1]),
                                  r=[psk(3), "bgate"], w=["g1"])
                            P.act(lambda e, dt_=dt_: e.activation(out=g2[:], in_=ps[4][:, 0:SB], func=AF.Sigmoid, bias=bgate[:, 8 + dt_:9 + dt_]),
                                  r=[psk(4), "bgate"], w=["g2"])
                            P.dve(lambda e: e.tensor_tensor(out=ysv[:], in0=ysv[:], in1=g1[:], op=ALU.mult), r=["ysv", "g1"], w=["ysv"])
                            P.dve(lambda e: e.tensor_tensor(out=g2[:], in0=ps[2][:, 0:SB], in1=g2[:], op=ALU.mult), r=[psk(2), "g2"], w=["g2"])
                            P.dve(lambda e, dt_=dt_: e.tensor_tensor(out=mixed[:, dt_, :], in0=ysv[:], in1=g2[:], op=ALU.add),
                                  r=["ysv", "g2"], w=[("mixed", dt_)])
                        for dt_ in range(8):
                            pb = 5 + (dt_ % 2)
                            for kc in range(8):
                                P.pe(lambda e, kc=kc, dt_=dt_, pb=pb: e.matmul(ps[pb][:, 0:SB], lhsT=wo[:, kc, dt_ * 128:(dt_ + 1) * 128],
                                                                                rhs=mixed[:, kc, :], start=(kc == 0), stop=(kc == 7)),
                                     r=[("WS", 1, 0), ("WS", 1, 1), ("WS", 1, 2), ("mixed", kc)], w=[psk(pb)])
                            P.dve(lambda e, dt_=dt_, pb=pb, tok=tok: e.tensor_tensor(out=hT[:, dt_, tok], in0=ps[pb][:, 0:SB], in1=hT[:, dt_, tok], op=ALU.add),
                                  r=[psk(pb), ("hT", dt_, s_)], w=[("hT", dt_, s_)])
            tap("hT_mix", hT[:], HTK)
            ffn(blk, 2, W["ffn2_w_gate"], W["ffn2_w_up"], W["ffn2_w_down"])
            tap("hT_ffn2", hT[:], HTK)
            P.barrier(dummy[:])
            with contextlib.ExitStack() as s3:
                sqF = [sb("sq%d" % i, [128, SB], BF16, s3) for i in range(2)]
                sdF = sb("sd", [128, SB], F32, s3)
                rstdF = sb("rstd", [128, SB], F32, s3)
                ost = [sb("ost%d" % i, [128, D], F32, s3) for i in range(2)]
                for s_ in range(NSB):
                    norm_sb(s_, 3, lambda dt_, s_=s_: hT[:, dt_, s_ * SB:(s_ + 1) * SB], sqF, sdF, rstdF,
                            lambda dt_, s_=s_: ("hT", dt_, s_))
                for i in range(16):
                    t0 = 16 + i * 128
                    ob = i % 2
                    sbs = sorted(set([t0 // SB, (t0 + 127) // SB]))
                    for h in range(2):
                        bank = 6 + h
                        for q in range(4):
                            dt_ = h * 4 + q
                            P.pe(lambda e, dt_=dt_, q=q, t0=t0, bank=bank: e.transpose(
                                out=ps[bank][:, q * 128:(q + 1) * 128], in_=hT[:, dt_, t0:t0 + 128], identity=ident[:]),
                                r=[("hT", dt_, s_) for s_ in sbs] + ["ident"], w=[psk(bank)])
                        if h == 0:
                            P.act(lambda e, ob=ob, h=h, bank=bank: e.activation(out=ost[ob][:, h * 512:(h + 1) * 512], in_=ps[bank][:, :], func=AF.Copy),
                                  r=[psk(bank)], w=[("ost", ob, h)])
                        else:
                            P.dve(lambda e, ob=ob, h=h, bank=bank: e.tensor_copy(out=ost[ob][:, h * 512:(h + 1) * 512], in_=ps[bank][:, :]),
                                  r=[psk(bank)], w=[("ost", ob, h)])
                    P.dma("sp", lambda e, i=i, ob=ob: e.dma_start(out=out[i * 128:(i + 1) * 128, :], in_=ost[ob][:]),
                          r=[("ost", ob, 0), ("ost", ob, 1)], w=[("out", i)])
        for blk in range(NBLK):
            do_block(blk)
        P.emit()
    return nc


_NC_CACHE = {}


def kernel(**inputs):
    x = np.asarray(inputs["x"], dtype=np.float32)
    meta = np.asarray(inputs["meta_tokens"], dtype=np.float32)
    B, S, _ = x.shape
    n = 8
    if "nc" not in _NC_CACHE:
        _NC_CACHE["nc"] = build_nc()
    nc = _NC_CACHE["nc"]
    f = lambda k: np.ascontiguousarray(np.asarray(inputs[k], dtype=np.float32))
    shared = {
        "g_ffn1": f("g_ffn1")[0], "ffn1_w_gate": f("ffn1_w_gate")[0], "ffn1_w_up": f("ffn1_w_up")[0],
        "ffn1_w_down": f("ffn1_w_down")[0], "g_mix": f("g_mix")[0], "w_in": f("w_in")[0], "b_gate": f("b_gate")[0],
        "ssm_a_re": f("ssm_a_re")[0], "ssm_a_im": f("ssm_a_im")[0], "ssm_log_dt": f("ssm_log_dt")[0],
        "ssm_b_re": f("ssm_b_re")[0], "ssm_b_im": f("ssm_b_im")[0],
        "ssm_c_re": f("ssm_c_re")[0].reshape(512, 64), "ssm_c_im": f("ssm_c_im")[0].reshape(512, 64),
        "ssm_d": f("ssm_d")[0], "ssm_w_glu": f("ssm_w_glu")[0], "conv_w": f("conv_w")[0].reshape(3, 512),
        "conv_w_out": f("conv_w_out")[0], "w_o": f("w_o")[0], "g_ffn2": f("g_ffn2")[0],
        "ffn2_w_gate": f("ffn2_w_gate")[0], "ffn2_w_up": f("ffn2_w_up")[0], "ffn2_w_down": f("ffn2_w_down")[0],
        "g_final": f("g_final"),
    }
    in_maps = []
    total = NBLK * NT
    for c in range(n):
        b, q = c // 4, c % 4
        seq = np.concatenate([meta, x[b, : 2048 * (q + 1)]], axis=0)
        stream = np.zeros((total, D), np.float32)
        stream[total - seq.shape[0]:] = seq
        m = dict(shared)
        m["xs"] = stream
        in_maps.append(m)
    res = run_bass_kernel_spmd(nc, in_maps, core_ids=list(range(n)))
    outp = np.zeros((B, S, D), np.float32)
    for c in range(n):
        b, q = c // 4, c % 4
        outp[b, q * 2048:(q + 1) * 2048] = np.asarray(res.results[c]["out"], dtype=np.float32)
    return outp
```

```python
import contextlib
import math
import numpy as np
import concourse.bass as bass
import concourse.mybir as mybir
from concourse.bass_utils import run_bass_kernel_spmd

F32 = mybir.dt.float32
BF16 = mybir.dt.bfloat16
I32 = mybir.dt.int32
AF = mybir.ActivationFunctionType
ALU = mybir.AluOpType

D = 1024
DFF = 2816
NT = 2064
NBLK = 4
NSB = 6
SB = 344
TC = 172
NF = 22
GROUPS = [(0, 3), (3, 3), (6, 3), (9, 3), (12, 3), (15, 3), (18, 2), (20, 2)]
EPS = 1e-6
ENGS = ("pe", "act", "dve", "pool", "sp")
PI = math.pi


class Prog:
    def __init__(self, nc, dma_ring=8):
        self.nc = nc
        self.ops = []
        self.dma_ring = dma_ring

    def op(self, eng, fn, reads=(), writes=(), dma=False, epoch=True):
        self.ops.append(dict(eng=eng, fn=fn, reads=tuple(reads) + (("EPOCH",) if epoch else ()), writes=tuple(writes), dma=dma))

    def barrier(self, dummy):
        self.ops.append(dict(eng="dve", fn=lambda e: e.memset(dummy, 0.0), reads=(), writes=("EPOCH",), dma=False))

    def pe(self, fn, r=(), w=()):
        self.op("pe", fn, r, w)

    def act(self, fn, r=(), w=()):
        self.op("act", fn, r, w)

    def dve(self, fn, r=(), w=()):
        self.op("dve", fn, r, w)

    def pool(self, fn, r=(), w=()):
        self.op("pool", fn, r, w)

    def dma(self, q, fn, r=(), w=(), epoch=True):
        self.op(q, fn, r, w, dma=True, epoch=epoch)

    def emit(self):
        nc = self.nc
        ops = self.ops
        last_writer = {}
        readers = {}
        for i, o in enumerate(ops):
            deps = set()
            for r in o["reads"]:
                if r in last_writer:
                    deps.add(last_writer[r])
            for w in o["writes"]:
                if w in last_writer:
                    deps.add(last_writer[w])
                deps.update(readers.get(w, ()))
            deps.discard(i)
            o["deps"] = deps
            for r in o["reads"]:
                readers.setdefault(r, []).append(i)
            for w in o["writes"]:
                last_writer[w] = i
                readers[w] = []
        eidx = {e: 0 for e in ENGS}
        dcount = {e: 0 for e in ENGS}
        for o in ops:
            o["eidx"] = eidx[o["eng"]]
            eidx[o["eng"]] += 1
            o["marked"] = False
            if o["dma"]:
                o["dma_n"] = dcount[o["eng"]]
                dcount[o["eng"]] += 1
        R_ = self.dma_ring
        for i, o in enumerate(ops):
            E = o["eng"]
            need = []
            best = {}
            bestd = {}
            for j in o["deps"]:
                p = ops[j]
                if p["dma"]:
                    k = (p["eng"], p["dma_n"] % R_)
                    if k not in bestd or ops[bestd[k]]["dma_n"] < p["dma_n"]:
                        bestd[k] = j
                else:
                    k = p["eng"]
                    if k not in best or ops[best[k]]["eidx"] < p["eidx"]:
                        best[k] = j
            need.extend(bestd.values())
            for k, j in best.items():
                p = ops[j]
                if k != E:
                    need.append(j)
                    p["marked"] = True
                else:
                    if E == "pe" and not o["dma"]:
                        continue
                    need.append(j)
                    p["marked"] = True
            o["need"] = need
        cnt = {e: 0 for e in ENGS}
        for o in ops:
            if o["marked"]:
                cnt[o["eng"]] += 1
                o["cnt"] = cnt[o["eng"]]
        R = self.dma_ring
        with contextlib.ExitStack() as st:
            esem = {e: st.enter_context(nc.semaphore("s_" + e)) for e in ENGS}
            dsem = {e: [st.enter_context(nc.semaphore("d_%s_%d" % (e, k))) for k in range(R)]
                    for e in ("sp", "act", "pool") if dcount[e] > 0}
            block = st.enter_context(nc.Block())

            def run_engine(E, eng):
                waited = {}

                def wait(sem, val):
                    key = id(sem)
                    if waited.get(key, 0) >= val:
                        return
                    waited[key] = val
                    eng.wait_ge(sem, val)

                for o in ops:
                    if o["eng"] != E:
                        continue
                    for j in o["need"]:
                        p = ops[j]
                        if p["dma"]:
                            n = p["dma_n"]
                            wait(dsem[p["eng"]][n % R], 16 * (n // R + 1))
                        else:
                            wait(esem[p["eng"]], p["cnt"])
                    if o["dma"]:
                        n = o["dma_n"]
                        s = dsem[E][n % R]
                        if n // R > 0:
                            wait(s, 16 * (n // R))
                        o["fn"](eng).then_inc(s, 16)
                    else:
                        ins = o["fn"](eng)
                        if o["marked"]:
                            ins.then_inc(esem[E], 1)
                if E in dsem:
                    tot = dcount[E]
                    for k in range(R):
                        uses = (tot - k + R - 1) // R if tot > k else 0
                        if uses > 0:
                            wait(dsem[E][k], 16 * uses)

            @block.tensor
            def _(eng):
                run_engine("pe", eng)

            @block.scalar
            def _(eng):
                run_engine("act", eng)

            @block.vector
            def _(eng):
                run_engine("dve", eng)

            @block.gpsimd
            def _(eng):
                run_engine("pool", eng)

            @block.sync
            def _(eng):
                run_engine("sp", eng)


DEBUG = False


def build_nc():
    nc = bass.Bass("TRN2", target_bir_lowering=False)
    dr = lambda name, shape, kind="ExternalInput", dt=F32: nc.dram_tensor(name, shape, dt, kind=kind).ap()
    xs = dr("xs", [NBLK * NT, D])
    out = dr("out", [2048, D], kind="ExternalOutput")
    W = {}
    for nm, shp in [("g_ffn1", [D]), ("ffn1_w_gate", [D, DFF]), ("ffn1_w_up", [D, DFF]), ("ffn1_w_down", [DFF, D]),
                    ("g_mix", [D]), ("w_in", [D, 4096]), ("b_gate", [2048]), ("ssm_a_re", [32, 64]),
                    ("ssm_a_im", [32, 64]), ("ssm_log_dt", [32]), ("ssm_b_re", [32, 64, 16]),
                    ("ssm_b_im", [32, 64, 16]), ("ssm_c_re", [512, 64]), ("ssm_c_im", [512, 64]),
                    ("ssm_d", [512]), ("ssm_w_glu", [512, 2048]), ("conv_w", [3, 512]),
                    ("conv_w_out", [512, D]), ("w_o", [D, D]), ("g_ffn2", [D]), ("ffn2_w_gate", [D, DFF]),
                    ("ffn2_w_up", [D, DFF]), ("ffn2_w_down", [DFF, D]), ("g_final", [D])]:
        W[nm] = dr(nm, shp)

    P = Prog(nc)
    nc_allow = nc.allow_non_contiguous_dma(reason="small parameter loads")
    with contextlib.ExitStack() as st:
        st.enter_context(nc_allow)

        uid = [0]

        def sb(name, shape, dt, stack=st):
            uid[0] += 1
            return stack.enter_context(nc.sbuf_tensor("%s_%d" % (name, uid[0]), shape, dt))

        ps = [st.enter_context(nc.psum_tensor("ps%d" % i, [128, 512], F32)) for i in range(8)]

        def tap(name, ap, keys, dt=F32):
            if not DEBUG:
                return
            shape = list(ap.shape)
            d = nc.dram_tensor("dbg_" + name, shape, dt, kind="ExternalOutput").ap()
            P.dma("sp", lambda e: e.dma_start(out=d, in_=ap), r=keys, w=[("dbg", name)])

        HTK = [("hT", dt_, s_) for dt_ in range(8) for s_ in range(NSB)]
        psk = lambda i: ("ps", i)

        hT = sb("hT", [128, 8, NT], F32)
        ident = sb("ident", [128, 128], F32)
        dummy = sb("dummy", [128, 1], F32)
        dummy2 = sb("dummy2", [128, 1], F32)
        SLOT = 9216
        WS = sb("WS", [128, 2 * SLOT], BF16)
        wsg = [WS[:, i * SLOT: i * SLOT + 3072].rearrange("p (k n) -> p k n", k=8) for i in range(2)]
        wsu = [WS[:, i * SLOT + 3072: i * SLOT + 6144].rearrange("p (k n) -> p k n", k=8) for i in range(2)]
        wsd = [WS[:, i * SLOT + 6144: i * SLOT + 9216].rearrange("p (f n) -> p f n", f=3) for i in range(2)]
        ALLWS = [("WS", i, j) for i in range(2) for j in range(3)]
        uT = sb("uT", [128, 4, NT], BF16)
        junk = [sb("junk%d" % i, [128, TC], F32) for i in range(2)]
        acc = sb("acc", [128, 4, 16], F32)
        ctm = sb("ctm", [128, 6, 16], F32)
        rows3 = lambda ap: ap.rearrange("(kc p) n -> p kc n", p=128)
        ones_bf = sb("ones_bf", [128, 128], BF16)
        gains = sb("gains", [128, 4, 8], F32)
        bgate = sb("bgate", [128, 16], F32)
        dsk = sb("dsk", [128, 4], F32)
        cw = sb("cw", [128, 4, 3], F32)
        mag = sb("mag", [128, 16], F32)
        carry = sb("carry", [128, 2, 16], F32)
        glast = sb("glast", [128, 2, 16], F32)
        th = sb("th", [128, 16], F32)
        adr = sb("adr", [128, 16], F32)
        lamN = sb("lamN", [128, 2, 16], F32)
        cosN = sb("cosN", [128, 16], F32)
        sinN = sb("sinN", [128, 16], F32)
        Ec = sb("Ec", [128, 16, TC], F32)
        Es = sb("Es", [128, 16, TC], F32)
        BqPad = sb("BqPad", [128, 16, 2, 128], BF16)
        CPad = sb("CPad", [128, 16, 2, 128], BF16)

        P.pool(lambda e: e.memset(ident[:], 0.0), w=["ident"])
        P.pool(lambda e: e.affine_select(out=ident[:], in_=ident[:], compare_op=ALU.not_equal, fill=1.0,
                                         base=0, pattern=[[-1, 128]], channel_multiplier=1), r=["ident"], w=["ident"])
        P.dve(lambda e: e.memset(ones_bf[:], 1.0), w=["ones"])
        for i, nm in enumerate(["g_ffn1", "g_mix", "g_ffn2", "g_final"]):
            P.dma("sp", lambda e, i=i, nm=nm: e.dma_start(out=gains[:, i, :], in_=W[nm].rearrange("(t p) -> p t", p=128)),
                  w=[("gains", i)])
        P.dma("sp", lambda e: e.dma_start(out=bgate[:], in_=W["b_gate"].rearrange("(t p) -> p t", p=128)), w=["bgate"])
        P.dma("sp", lambda e: e.dma_start(out=dsk[:], in_=W["ssm_d"].rearrange("(t p) -> p t", p=128)), w=["dsk"])
        for j in range(3):
            P.dma("sp", lambda e, j=j: e.dma_start(out=cw[:, :, j], in_=W["conv_w"][j, :].rearrange("(t p) -> p t", p=128)), w=["cw"])

        with contextlib.ExitStack() as s2:
            t = lambda name, shape, dt=F32: sb(name, shape, dt, s2)
            are = t("are", [128, 16]); aim = t("aim", [128, 16]); ldt = t("ldt", [128, 16])
            bre = t("bre", [128, 16, 16]); bim = t("bim", [128, 16, 16])
            cn = t("cn", [128, 2, 4, 2, 64])
            CP = t("CP", [128, 2, 16, 16])
            dtv = t("dtv", [128, 16])
            cs = t("cs", [128, 16]); sn = t("sn", [128, 16])
            lre = t("lre", [128, 16]); lim = t("lim", [128, 16])
            den = t("den", [128, 16]); tmpa = t("tmpa", [128, 16]); tmpb = t("tmpb", [128, 16])
            qre = t("qre", [128, 16]); qim = t("qim", [128, 16]); lm1 = t("lm1", [128, 16])
            bq = t("bq", [128, 2, 16, 16]); tb = t("tb", [128, 16])
            angN = t("angN", [128, 16])
            wkn1 = t("wkn1", [128, 16]); wkn2 = t("wkn2", [128, 16]); wkni = t("wkni", [128, 16], I32)

            pair_view = lambda ap: ap.rearrange("(gp two) p -> (two p) gp", two=2)
            P.dma("sp", lambda e: e.dma_start(out=are[:], in_=pair_view(W["ssm_a_re"])), w=["are"])
            P.dma("sp", lambda e: e.dma_start(out=aim[:], in_=pair_view(W["ssm_a_im"])), w=["aim"])
            for gpar in range(2):
                P.dma("sp", lambda e, gpar=gpar: e.dma_start(
                    out=ldt[gpar * 64:(gpar + 1) * 64, :],
                    in_=W["ssm_log_dt"].rearrange("(g two) -> two g", two=2)[gpar:gpar + 1, :].to_broadcast([64, 16])),
                    w=[("ldt", gpar)])
            bview = lambda ap: ap.rearrange("(gp two) p c -> (two p) gp c", two=2)
            P.dma("sp", lambda e: e.dma_start(out=bre[:], in_=bview(W["ssm_b_re"])), w=["bre"])
            P.dma("sp", lambda e: e.dma_start(out=bim[:], in_=bview(W["ssm_b_im"])), w=["bim"])
            for ri, nm in enumerate(["ssm_c_re", "ssm_c_im"]):
                for dup in range(2):
                    P.dma("sp", lambda e, ri=ri, nm=nm, dup=dup: e.dma_start(
                        out=cn[:, ri, :, dup, :], in_=W[nm].rearrange("(t p) s -> p t s", p=128)), w=[("cn", ri, dup)])
            for ri in range(2):
                for ct in range(4):
                    bank = 6 + (ct % 2)
                    P.pe(lambda e, ri=ri, ct=ct, bank=bank: e.transpose(
                        out=ps[bank][:, 0:128], in_=cn[:, ri, ct, :, :].rearrange("p a b -> p (a b)"), identity=ident[:]),
                        r=[("cn", ri, 0), ("cn", ri, 1), "ident"], w=[psk(bank)])
                    for gpar in range(2):
                        P.act(lambda e, ri=ri, ct=ct, bank=bank, gpar=gpar: e.activation(
                            out=CP[gpar * 64:(gpar + 1) * 64, ri, ct * 4:(ct + 1) * 4, :],
                            in_=ps[bank][gpar * 64:(gpar + 1) * 64, 0:128].rearrange(
                                "p (gl two c) -> p gl two c", two=2, c=16)[:, :, gpar, :],
                            func=AF.Identity, scale=(1.0 if ri == 0 else -1.0)),
                            r=[psk(bank)], w=[("CP", ri, ct, gpar)])
            cp_keys = [("CP", ri, ct, gpar) for ri in range(2) for ct in range(4) for gpar in range(2)]
            ldk = [("ldt", 0), ("ldt", 1)]
            P.act(lambda e: e.activation(out=dtv[:], in_=ldt[:], func=AF.Exp), r=ldk, w=["dtv"])
            P.dve(lambda e: e.tensor_tensor(out=adr[:], in0=are[:], in1=dtv[:], op=ALU.mult), r=["are", "dtv"], w=["adr"])
            P.dve(lambda e: e.tensor_tensor(out=th[:], in0=aim[:], in1=dtv[:], op=ALU.mult), r=["aim", "dtv"], w=["th"])
            P.act(lambda e: e.activation(out=mag[:], in_=adr[:], func=AF.Exp), r=["adr"], w=["mag"])

            def sincos(x, xk, o_sin, o_cos, w1, w2, wi, keys):
                for off, o in ((0.0, o_sin), (PI / 2, o_cos)):
                    ok = keys[0] if o is o_sin else keys[1]
                    P.dve(lambda e, off=off: e.tensor_scalar(out=w1, in0=x, scalar1=off, scalar2=1.0 / (2 * PI),
                                                             op0=ALU.add, op1=ALU.mult), r=[xk], w=["w1" + keys[2]])
                    P.dve(lambda e: e.tensor_copy(out=wi, in_=w1), r=["w1" + keys[2]], w=["wi" + keys[2]])
                    P.dve(lambda e: e.tensor_copy(out=w1, in_=wi), r=["wi" + keys[2]], w=["w1" + keys[2]])
                    P.dve(lambda e: e.scalar_tensor_tensor(out=w2, in0=w1, scalar=-6.25, in1=x,
                                                           op0=ALU.mult, op1=ALU.add), r=["w1" + keys[2], xk], w=["w2" + keys[2]])
                    P.dve(lambda e: e.scalar_tensor_tensor(out=w2, in0=w1, scalar=-(2 * PI - 6.25), in1=w2,
                                                           op0=ALU.mult, op1=ALU.add), r=["w1" + keys[2], "w2" + keys[2]], w=["w2" + keys[2]])
                    P.dve(lambda e, off=off: e.tensor_scalar(out=w2, in0=w2, scalar1=off, scalar2=None, op0=ALU.add),
                          r=["w2" + keys[2]], w=["w2" + keys[2]])
                    for lim_, sgn in ((PI, -1.0), (-PI, 1.0)):
                        cmp = ALU.is_gt if sgn < 0 else ALU.is_lt
                        P.dve(lambda e, lim_=lim_, cmp=cmp: e.tensor_scalar(out=w1, in0=w2, scalar1=lim_, scalar2=None, op0=cmp),
                              r=["w2" + keys[2]], w=["w1" + keys[2]])
                        P.dve(lambda e, sgn=sgn: e.scalar_tensor_tensor(out=w2, in0=w1, scalar=sgn * 2 * PI, in1=w2,
                                                                        op0=ALU.mult, op1=ALU.add),
                              r=["w1" + keys[2], "w2" + keys[2]], w=["w2" + keys[2]])
                    P.act(lambda e, o=o: e.activation(out=o, in_=w2, func=AF.Sin), r=["w2" + keys[2]], w=[ok])

            sincos(th[:], "th", sn[:], cs[:], wkn1[:], wkn2[:], wkni[:], ("sn", "cs", "n"))
            P.dve(lambda e: e.tensor_tensor(out=lre[:], in0=mag[:], in1=cs[:], op=ALU.mult), r=["mag", "cs"], w=["lre"])
            P.dve(lambda e: e.tensor_tensor(out=lim[:], in0=mag[:], in1=sn[:], op=ALU.mult), r=["mag", "sn"], w=["lim"])
            P.dve(lambda e: e.tensor_tensor(out=den[:], in0=are[:], in1=are[:], op=ALU.mult), r=["are"], w=["den"])
            P.dve(lambda e: e.tensor_tensor(out=tmpa[:], in0=aim[:], in1=aim[:], op=ALU.mult), r=["aim"], w=["tmpa"])
            P.dve(lambda e: e.tensor_tensor(out=den[:], in0=den[:], in1=tmpa[:], op=ALU.add), r=["den", "tmpa"], w=["den"])
            P.dve(lambda e: e.reciprocal(out=den[:], in_=den[:]), r=["den"], w=["den"])
            P.dve(lambda e: e.tensor_scalar(out=lm1[:], in0=lre[:], scalar1=-1.0, scalar2=None, op0=ALU.add), r=["lre"], w=["lm1"])
            P.dve(lambda e: e.tensor_tensor(out=tmpa[:], in0=lm1[:], in1=are[:], op=ALU.mult), r=["lm1", "are", "den"], w=["tmpa"])
            P.dve(lambda e: e.tensor_tensor(out=tmpb[:], in0=lim[:], in1=aim[:], op=ALU.mult), r=["lim", "aim"], w=["tmpb"])
            P.dve(lambda e: e.tensor_tensor(out=tmpa[:], in0=tmpa[:], in1=tmpb[:], op=ALU.add), r=["tmpa", "tmpb"], w=["tmpa"])
            P.dve(lambda e: e.tensor_tensor(out=qre[:], in0=tmpa[:], in1=den[:], op=ALU.mult), r=["tmpa", "den"], w=["qre"])
            P.dve(lambda e: e.tensor_tensor(out=tmpa[:], in0=lim[:], in1=are[:], op=ALU.mult), r=["lim", "are", "qre"], w=["tmpa"])
            P.dve(lambda e: e.tensor_tensor(out=tmpb[:], in0=lm1[:], in1=aim[:], op=ALU.mult), r=["lm1", "aim"], w=["tmpb"])
            P.dve(lambda e: e.tensor_tensor(out=tmpa[:], in0=tmpa[:], in1=tmpb[:], op=ALU.subtract), r=["tmpa", "tmpb"], w=["tmpa"])
            P.dve(lambda e: e.tensor_tensor(out=qim[:], in0=tmpa[:], in1=den[:], op=ALU.mult), r=["tmpa", "den"], w=["qim"])
            qb = lambda q: q[:, :].unsqueeze(2).to_broadcast([128, 16, 16])
            P.dve(lambda e: e.tensor_tensor(out=bq[:, 0], in0=bre[:], in1=qb(qre), op=ALU.mult), r=["bre", "qre"], w=["bq0"])
            P.dve(lambda e: e.tensor_tensor(out=bq[:, 1], in0=bim[:], in1=qb(qim), op=ALU.mult), r=["bim", "qim"], w=["bq1"])
            P.dve(lambda e: e.tensor_tensor(out=bq[:, 0], in0=bq[:, 0], in1=bq[:, 1], op=ALU.subtract), r=["bq0", "bq1"], w=["bq0"])
            P.dve(lambda e: e.tensor_tensor(out=bq[:, 1], in0=bim[:], in1=qb(qre), op=ALU.mult), r=["bim", "qre", "bq0"], w=["bq1"])
            P.dve(lambda e: e.tensor_tensor(out=bre[:], in0=bre[:], in1=qb(qim), op=ALU.mult), r=["bre", "qim", "bq0"], w=["bre"])
            P.dve(lambda e: e.tensor_tensor(out=bq[:, 1], in0=bq[:, 1], in1=bre[:], op=ALU.add), r=["bq1", "bre"], w=["bq1"])
            sz = contextlib.ExitStack()
            sz.__enter__()
            Z = sb("Z", [128, 16, 2, 128], F32, sz)
            P.pool(lambda e: e.memset(Z[:], 0.0), w=["Z"])
            P.pool(lambda e: e.memset(CPad[:], 0.0), w=["CPad"])
            for ri in range(2):
                for gpar in range(2):
                    for gl in range(4):
                        col = (2 * gl + gpar) * 16
                        rows = slice(gpar * 64, (gpar + 1) * 64)
                        P.dve(lambda e, ri=ri, rows=rows, gl=gl, col=col: e.tensor_copy(
                            out=Z[rows, gl::4, ri, col:col + 16], in_=bq[rows, ri, gl::4, :]),
                            r=["bq0", "bq1", "Z"], w=["Z"])
                        P.act(lambda e, ri=ri, rows=rows, gl=gl, col=col: e.activation(
                            out=CPad[rows, gl::4, ri, col:col + 16], in_=CP[rows, ri, gl::4, :], func=AF.Copy),
                            r=cp_keys + ["CPad"], w=["CPad"])
            for gp in range(16):
                for ri in range(2):
                    bank = 6 + ((gp * 2 + ri) % 2)
                    P.pe(lambda e, gp=gp, ri=ri, bank=bank: e.transpose(out=ps[bank][:, 0:128], in_=Z[:, gp, ri, :], identity=ident[:]),
                         r=["Z", "ident"], w=[psk(bank)])
                    P.act(lambda e, gp=gp, ri=ri, bank=bank: e.activation(out=BqPad[:, gp, ri, :], in_=ps[bank][:, 0:128], func=AF.Copy),
                          r=[psk(bank)], w=["BqPad"])
            sz.__exit__(None, None, None)
            P.barrier(dummy[:])
            P.dve(lambda e: e.tensor_scalar(out=angN[:], in0=th[:], scalar1=float(TC), scalar2=None, op0=ALU.mult), r=["th"], w=["angN"])
            sincos(angN[:], "angN", sinN[:], cosN[:], wkn1[:], wkn2[:], wkni[:], ("sinN", "cosN", "n"))
            P.act(lambda e: e.activation(out=tmpa[:], in_=adr[:], func=AF.Exp, scale=float(TC)), r=["adr", "qre", "qim"], w=["tmpa"])
            P.dve(lambda e: e.tensor_tensor(out=lamN[:, 0, :], in0=tmpa[:], in1=cosN[:], op=ALU.mult), r=["tmpa", "cosN"], w=["lamN"])
            P.dve(lambda e: e.tensor_tensor(out=lamN[:, 1, :], in0=tmpa[:], in1=sinN[:], op=ALU.mult), r=["tmpa", "sinN"], w=["lamN"])
            P.dve(lambda e: e.memset(carry[:], 0.0), w=["carry"])
            tap("mag", mag[:], ["mag"]); tap("lre", lre[:], ["lre"]); tap("lim", lim[:], ["lim"])
            tap("qre", qre[:], ["qre"]); tap("qim", qim[:], ["qim"]); tap("th", th[:], ["th"])
            tap("BqPad", BqPad[:], ["BqPad"], BF16); tap("CPad", CPad[:], ["CPad"], BF16)
            tap("CP", CP[:], cp_keys); tap("bq", bq[:], ["bq0", "bq1"]); tap("ident", ident[:], ["ident"])

        def make_tables(mode):
            P.barrier(dummy[:])
            with contextlib.ExitStack() as sx:
                iot = sb("iot", [128, TC], F32, sx); ioti = sb("ioti", [128, TC], I32, sx)
                ang = sb("ang", [128, 16, TC], F32, sx); wk1 = sb("wk1", [128, 16, TC], F32, sx)
                wk2 = sb("wk2", [128, 16, TC], F32, sx); wki = sb("wki", [128, 16, TC], I32, sx)
                if mode == "E":
                    P.pool(lambda e: e.iota(ioti[:], pattern=[[1, TC]], base=1, channel_multiplier=0), w=["ioti"])
                else:
                    P.pool(lambda e: e.iota(ioti[:], pattern=[[-1, TC]], base=TC - 1, channel_multiplier=0), w=["ioti"])
                P.dve(lambda e: e.tensor_copy(out=iot[:], in_=ioti[:]), r=["ioti"], w=["iot"])
                for gp in range(16):
                    P.dve(lambda e, gp=gp: e.tensor_scalar(out=ang[:, gp, :], in0=iot[:], scalar1=th[:, gp:gp + 1], scalar2=None,
                                                           op0=ALU.mult), r=["iot", "th"], w=["ang"])
                sincos(ang[:], "ang", Es[:], Ec[:], wk1[:], wk2[:], wki[:], ("Es", "Ec", "t"))
                if mode == "D":
                    for gp in range(16):
                        P.act(lambda e, gp=gp: e.activation(out=ang[:, gp, :], in_=iot[:], func=AF.Exp, scale=adr[:, gp:gp + 1]),
                              r=["iot", "adr", "Es", "Ec"], w=[("magp", gp)])
                    mk = [("magp", gp) for gp in range(16)]
                    P.dve(lambda e: e.tensor_tensor(out=Ec[:], in0=Ec[:], in1=ang[:], op=ALU.mult), r=["Ec"] + mk, w=["Ec"])
                    P.dve(lambda e: e.tensor_tensor(out=Es[:], in0=Es[:], in1=ang[:], op=ALU.mult), r=["Es"] + mk, w=["Es"])
                tap("Ec_" + mode, Ec[:], ["Ec"]); tap("Es_" + mode, Es[:], ["Es"])

        make_tables("D")

        cw_scr = nc.dram_tensor("cw_scr", [8, 128, 3584], BF16).ap()
        glu_v = W["ssm_w_glu"].rearrange("(kc p) n -> p kc n", p=128)
        co_v = W["conv_w_out"].rearrange("(kc p) n -> p kc n", p=128)
        win_v = W["w_in"].rearrange("(kc p) n -> p kc n", p=128)
        for dt_ in range(8):
            for j in range(2):
                P.dma("pool", lambda e, j=j, dt_=dt_: e.dma_start(
                    out=cw_scr[dt_, :, 0:1024].rearrange("p (k j n) -> p k j n", k=4, j=2)[:, :, j, :],
                    in_=glu_v[:, :, j * 1024 + dt_ * 128: j * 1024 + (dt_ + 1) * 128]), w=[("cwscr", dt_, "gl", j)], epoch=False)
                P.dma("pool", lambda e, j=j, dt_=dt_: e.dma_start(
                    out=cw_scr[dt_, :, 1536:3584].rearrange("p (k j n) -> p k j n", k=8, j=2)[:, :, j, :],
                    in_=win_v[:, :, 2048 + j * 1024 + dt_ * 128: 2048 + j * 1024 + (dt_ + 1) * 128]), w=[("cwscr", dt_, "ga", j)], epoch=False)
            P.dma("pool", lambda e, dt_=dt_: e.dma_start(
                out=cw_scr[dt_, :, 1024:1536].rearrange("p (k n) -> p k n", k=4),
                in_=co_v[:, :, dt_ * 128:(dt_ + 1) * 128]), w=[("cwscr", dt_, "co")], epoch=False)

        def norm_sb(s_, gi, xn_ap_fn, sq, sd, rstd, keyfn, pbank=6, tag=0):
            tok = slice(s_ * SB, (s_ + 1) * SB)
            for dt_ in range(8):
                P.act(lambda e, dt_=dt_: e.activation(out=sq[dt_ % 2][:], in_=hT[:, dt_, tok], func=AF.Square),
                      r=[("hT", dt_, s_)], w=[("sq", dt_ % 2)])
                P.pe(lambda e, dt_=dt_: e.matmul(ps[pbank][:, 0:SB], lhsT=ones_bf[:], rhs=sq[dt_ % 2][:],
                                                 start=(dt_ == 0), stop=(dt_ == 7)),
                     r=[("sq", dt_ % 2), "ones"], w=[psk(pbank)])
            P.act(lambda e: e.activation(out=sd[:], in_=ps[pbank][:, 0:SB], func=AF.Sqrt, scale=1.0 / D, bias=EPS),
                  r=[psk(pbank)], w=[("sd", tag)])
            P.dve(lambda e: e.reciprocal(out=rstd[:], in_=sd[:]), r=[("sd", tag)], w=[("rstd", tag)])
            for dt_ in range(8):
                P.dve(lambda e, dt_=dt_: e.scalar_tensor_tensor(out=xn_ap_fn(dt_), in0=hT[:, dt_, tok],
                                                                scalar=gains[:, gi, dt_:dt_ + 1], in1=rstd[:],
                                                                op0=ALU.mult, op1=ALU.mult),
                      r=[("hT", dt_, s_), ("rstd", tag), ("gains", gi)], w=[keyfn(dt_)])

        def ffn(blk, gi, wg, wu, wd, bg=None):
            P.barrier(dummy[:])
            with contextlib.ExitStack() as s3:
                xn = sb("xn", [128, 8, NT], BF16, s3)
                sq = [sb("sq%d" % i, [128, SB], BF16, s3) for i in range(2)]
                sd = sb("sd", [128, SB], F32, s3)
                rstd = sb("rstd", [128, SB], F32, s3)
                actb = [sb("actb%d" % i, [128, 3, SB], BF16, s3) for i in range(2)]
                sg = [sb("sg%d" % i, [128, SB], F32, s3) for i in range(2)]
                for s_ in range(NSB):
                    norm_sb(s_, gi, lambda dt_, s_=s_: xn[:, dt_, s_ * SB:(s_ + 1) * SB], sq, sd, rstd,
                            lambda dt_, s_=s_: ("xn", dt_, s_))
                cnt = 0
                prev_down = [None]

                def emit_down(nf, sl, ab, s_, tok):
                    for dt_ in range(8):
                        pb = 4 + (dt_ % 2)
                        for fi in range(nf):
                            P.pe(lambda e, fi=fi, dt_=dt_, pb=pb, sl=sl, ab=ab, nf=nf: e.matmul(
                                ps[pb][:, 0:SB], lhsT=wsd[sl][:, fi, dt_ * 128:(dt_ + 1) * 128], rhs=actb[ab][:, fi, :],
                                start=(fi == 0), stop=(fi == nf - 1)),
                                r=[("WS", sl, 2), ("actb", ab, fi)], w=[psk(pb)])
                        P.dve(lambda e, dt_=dt_, pb=pb, tok=tok: e.scalar_tensor_tensor(
                            out=hT[:, dt_, tok], in0=ps[pb][:, 0:SB], scalar=0.5, in1=hT[:, dt_, tok],
                            op0=ALU.mult, op1=ALU.add),
                            r=[psk(pb), ("hT", dt_, s_)], w=[("hT", dt_, s_)])
                        if bg is not None and dt_ % 4 == 3:
                            next(bg, None)

                for gix, (f0, nf) in enumerate(GROUPS):
                    sl = gix % 2
                    P.dma("pool", lambda e, f0=f0, nf=nf, sl=sl: e.dma_start(
                        out=wsg[sl][:, :, 0:nf * 128], in_=rows3(wg)[:, :, f0 * 128:(f0 + nf) * 128]),
                        w=[("WS", sl, 0)], epoch=False)
                    P.dma("pool", lambda e, f0=f0, nf=nf, sl=sl: e.dma_start(
                        out=wsu[sl][:, :, 0:nf * 128], in_=rows3(wu)[:, :, f0 * 128:(f0 + nf) * 128]),
                        w=[("WS", sl, 1)], epoch=False)
                    P.dma("pool", lambda e, f0=f0, nf=nf, sl=sl: e.dma_start(
                        out=wsd[sl][:, 0:nf, :], in_=rows3(wd)[:, f0:f0 + nf, :]),
                        w=[("WS", sl, 2)], epoch=False)
                    for s_ in range(NSB):
                        tok = slice(s_ * SB, (s_ + 1) * SB)
                        ab = (gix * NSB + s_) % 2
                        for fi in range(nf):
                            par = cnt % 2
                            cnt += 1
                            for kc in range(8):
                                P.pe(lambda e, kc=kc, fi=fi, par=par, sl=sl, tok=tok: e.matmul(
                                    ps[par][:, 0:SB], lhsT=wsg[sl][:, kc, fi * 128:(fi + 1) * 128], rhs=xn[:, kc, tok],
                                    start=(kc == 0), stop=(kc == 7)),
                                    r=[("WS", sl, 0), ("xn", kc, s_)], w=[psk(par)])
                            for kc in range(8):
                                P.pe(lambda e, kc=kc, fi=fi, par=par, sl=sl, tok=tok: e.matmul(
                                    ps[2 + par][:, 0:SB], lhsT=wsu[sl][:, kc, fi * 128:(fi + 1) * 128], rhs=xn[:, kc, tok],
                                    start=(kc == 0), stop=(kc == 7)),
                                    r=[("WS", sl, 1), ("xn", kc, s_)], w=[psk(2 + par)])
                            P.act(lambda e, par=par: e.activation(out=sg[par][:], in_=ps[par][:, 0:SB], func=AF.Silu),
                                  r=[psk(par)], w=[("sg", par)])
                            P.dve(lambda e, par=par, ab=ab, fi=fi: e.tensor_tensor(
                                out=actb[ab][:, fi, :], in0=ps[2 + par][:, 0:SB], in1=sg[par][:], op=ALU.mult),
                                r=[psk(2 + par), ("sg", par)], w=[("actb", ab, fi)])
                            if bg is not None:
                                next(bg, None)
                        if prev_down[0] is not None:
                            prev_down[0]()
                        prev_down[0] = (lambda nf=nf, sl=sl, ab=ab, s_=s_, tok=tok: emit_down(nf, sl, ab, s_, tok))
                if prev_down[0] is not None:
                    prev_down[0]()
                if bg is not None:
                    for _ in bg:
                        pass

        pending = []

        def ssm_hist():
            for c in range(NT // TC):
                tk = slice(c * TC, (c + 1) * TC)
                s_ = (c * TC) // SB
                for gp in range(16):
                    ct = gp // 4
                    bank = 6 + (gp % 2)
                    PA, PB = ps[bank][:, 0:TC], ps[bank][:, TC:2 * TC]
                    P.pe(lambda e, gp=gp, ct=ct, PA=PA, tk=tk: e.matmul(PA, lhsT=BqPad[:, gp, 0, :], rhs=uT[:, ct, tk], start=True, stop=True),
                         r=["BqPad", ("uT", ct, s_)], w=[psk(bank)])
                    P.pe(lambda e, gp=gp, ct=ct, PB=PB, tk=tk: e.matmul(PB, lhsT=BqPad[:, gp, 1, :], rhs=uT[:, ct, tk], start=True, stop=True),
                         r=["BqPad", ("uT", ct, s_)], w=[psk(bank)])
                    for k, (src, tab, tkey) in enumerate([(PA, Ec, "Ec"), (PB, Es, "Es"), (PA, Es, "Es"), (PB, Ec, "Ec")]):
                        P.dve(lambda e, gp=gp, k=k, src=src, tab=tab: e.scalar_tensor_tensor(
                            out=junk[k % 2][:], in0=src, scalar=1.0, in1=tab[:, gp, :],
                            op0=ALU.mult, op1=ALU.mult, accum_out=acc[:, k, gp:gp + 1]),
                            r=[psk(bank), tkey], w=[("junk", k % 2), ("acc", k, gp)])
                    yield
                ak = [("acc", k, gp) for k in range(4) for gp in range(16)]
                TT = lambda o, a_, b_, op: (lambda e: e.tensor_tensor(out=o, in0=a_, in1=b_, op=op))
                P.dve(TT(ctm[:, 0, :], acc[:, 0, :], acc[:, 1, :], ALU.subtract), r=ak, w=["ctm0"])
                P.dve(TT(ctm[:, 1, :], acc[:, 2, :], acc[:, 3, :], ALU.add), r=ak, w=["ctm1"])
                P.dve(TT(ctm[:, 2, :], lamN[:, 0, :], carry[:, 0, :], ALU.mult), r=["lamN", "carry"], w=["ctm2"])
                P.dve(TT(ctm[:, 3, :], lamN[:, 1, :], carry[:, 1, :], ALU.mult), r=["lamN", "carry"], w=["ctm3"])
                P.dve(TT(ctm[:, 4, :], lamN[:, 0, :], carry[:, 1, :], ALU.mult), r=["lamN", "carry"], w=["ctm4"])
                P.dve(TT(ctm[:, 5, :], lamN[:, 1, :], carry[:, 0, :], ALU.mult), r=["lamN", "carry"], w=["ctm5"])
                P.dve(TT(ctm[:, 2, :], ctm[:, 2, :], ctm[:, 3, :], ALU.subtract), r=["ctm2", "ctm3"], w=["ctm2"])
                P.dve(TT(ctm[:, 4, :], ctm[:, 4, :], ctm[:, 5, :], ALU.add), r=["ctm4", "ctm5"], w=["ctm4"])
                P.dve(TT(carry[:, 0, :], ctm[:, 2, :], ctm[:, 0, :], ALU.add), r=["ctm2", "ctm0", "ctm4"], w=["carry"])
                P.dve(TT(carry[:, 1, :], ctm[:, 4, :], ctm[:, 1, :], ALU.add), r=["ctm4", "ctm1"], w=["carry"])
                yield

        def do_block(blk):
            full = (blk == NBLK - 1)
            P.barrier(dummy[:])
            with contextlib.ExitStack() as s3:
                xst = [sb("xst%d" % i, [128, D], F32, s3) for i in range(6)]
                ntile = (NT + 127) // 128
                for i in range(ntile):
                    rows = min(128, NT - i * 128)
                    xb = i % 6
                    P.dma("sp", lambda e, i=i, rows=rows, xb=xb: e.dma_start(
                        out=xst[xb][0:rows, :], in_=xs[blk * NT + i * 128: blk * NT + i * 128 + rows, :]),
                        w=[("xst", xb)])
                    sbs = sorted(set([(i * 128) // SB, (i * 128 + rows - 1) // SB]))
                    for h in range(2):
                        bank = 6 + h
                        for q in range(4):
                            dt_ = h * 4 + q
                            P.pe(lambda e, dt_=dt_, q=q, rows=rows, xb=xb, bank=bank: e.transpose(
                                out=ps[bank][:, q * 128:q * 128 + rows], in_=xst[xb][0:rows, dt_ * 128:(dt_ + 1) * 128],
                                identity=ident[0:rows, 0:rows]),
                                r=[("xst", xb), "ident"], w=[psk(bank)])
                        eng = P.act if h == 0 else P.dve
                        if h == 0:
                            P.act(lambda e, h=h, i=i, rows=rows, bank=bank: e.activation(
                                out=hT[:, h * 4:(h + 1) * 4, i * 128:i * 128 + rows],
                                in_=ps[bank][:, :].rearrange("p (q c) -> p q c", q=4)[:, :, 0:rows], func=AF.Copy),
                                r=[psk(bank)], w=[("hT", dt_, s_) for dt_ in range(h * 4, h * 4 + 4) for s_ in sbs])
                        else:
                            P.dve(lambda e, h=h, i=i, rows=rows, bank=bank: e.tensor_copy(
                                out=hT[:, h * 4:(h + 1) * 4, i * 128:i * 128 + rows],
                                in_=ps[bank][:, :].rearrange("p (q c) -> p q c", q=4)[:, :, 0:rows]),
                                r=[psk(bank)], w=[("hT", dt_, s_) for dt_ in range(h * 4, h * 4 + 4) for s_ in sbs])
            if full:
                tap("hT_load", hT[:], HTK)
            ffn(blk, 0, W["ffn1_w_gate"], W["ffn1_w_up"], W["ffn1_w_down"], bg=(pending.pop() if pending else None))
            if full:
                make_tables("E")
            if full:
                tap("hT_ffn1", hT[:], HTK)
            P.barrier(dummy[:])
            with contextlib.ExitStack() as s3:
                cvT = sb("cvT", [128, 4, NT], BF16, s3) if full else None
                ga = uT
                P.barrier(dummy[:])
                with contextlib.ExitStack() as s4:
                    unA = [sb("un%d" % i, [128, 8, SB], BF16, s4) for i in range(2)]
                    sqA = [sb("sq%d" % i, [128, SB], BF16, s4) for i in range(2)]
                    sdA = [sb("sd%d" % i, [128, SB], F32, s4) for i in range(2)]
                    rstdA = [sb("rstd%d" % i, [128, SB], F32, s4) for i in range(2)]
                    wu_ = WS[:, 0:4096].rearrange("p (k n) -> p k n", k=8)
                    P.dma("pool", lambda e: e.dma_start(out=wu_, in_=rows3(W["w_in"])[:, :, 0:512]), w=ALLWS, epoch=False)
                    if full:
                        wv = WS[:, 4096:16384].rearrange("p (k n) -> p k n", k=8)
                        zs = sb("zs", [128, 4, SB + 2], F32, s4)
                        vv = sb("vv", [128, SB], F32, s4)
                        c1 = sb("c1", [128, SB], F32, s4)
                        P.dma("pool", lambda e: e.dma_start(out=wv, in_=rows3(W["w_in"])[:, :, 512:2048]),
                              w=ALLWS, epoch=False)
                        P.dve(lambda e: e.memset(zs[:], 0.0), w=[("zs", c) for c in range(4)])
                    for s_ in range(NSB):
                        tok = slice(s_ * SB, (s_ + 1) * SB)
                        npar = s_ % 2
                        norm_sb(s_, 1, lambda dt_, npar=npar: unA[npar][:, dt_, :], sqA, sdA[npar], rstdA[npar],
                                lambda dt_, npar=npar: ("un", npar, dt_), pbank=6 + npar, tag=npar)
                        for ot in range(4):
                            pb = 4 + (ot % 2)
                            for kc in range(8):
                                P.pe(lambda e, kc=kc, ot=ot, pb=pb, npar=npar: e.matmul(
                                    ps[pb][:, 0:SB], lhsT=wu_[:, kc, ot * 128:(ot + 1) * 128], rhs=unA[npar][:, kc, :],
                                    start=(kc == 0), stop=(kc == 7)), r=ALLWS + [("un", npar, kc)], w=[psk(pb)])
                            P.act(lambda e, ot=ot, pb=pb, tok=tok: e.activation(out=uT[:, ot, tok], in_=ps[pb][:, 0:SB], func=AF.Copy),
                                  r=[psk(pb)], w=[("uT", ot, s_)])
                        if full:
                            for ot in range(4):
                                for j, bank in ((0, 0), (2, 1), (1, 2)):
                                    for kc in range(8):
                                        P.pe(lambda e, kc=kc, ot=ot, j=j, bank=bank, npar=npar: e.matmul(
                                            ps[bank][:, 0:SB], lhsT=wv[:, kc, j * 512 + ot * 128: j * 512 + (ot + 1) * 128],
                                            rhs=unA[npar][:, kc, :], start=(kc == 0), stop=(kc == 7)),
                                            r=ALLWS + [("un", npar, kc)], w=[psk(bank)])
                                P.act(lambda e: e.activation(out=vv[:], in_=ps[0][:, 0:SB], func=AF.Copy), r=[psk(0)], w=["vv"])
                                P.dve(lambda e, ot=ot: e.tensor_copy(out=zs[:, ot, 0:2], in_=zs[:, ot, SB:SB + 2]),
                                      r=[("zs", ot)], w=[("zs", ot)])
                                P.dve(lambda e, ot=ot: e.tensor_tensor(out=zs[:, ot, 2:SB + 2], in0=ps[1][:, 0:SB], in1=vv[:], op=ALU.mult),
                                      r=[psk(1), "vv", ("zs", ot)], w=[("zs", ot)])
                                P.dve(lambda e, ot=ot: e.tensor_scalar(out=c1[:], in0=zs[:, ot, 0:SB], scalar1=cw[:, ot, 0:1], scalar2=None,
                                                                       op0=ALU.mult), r=[("zs", ot), "cw"], w=["c1"])
                                P.dve(lambda e, ot=ot: e.scalar_tensor_tensor(out=c1[:], in0=zs[:, ot, 1:SB + 1], scalar=cw[:, ot, 1:2], in1=c1[:],
                                                                              op0=ALU.mult, op1=ALU.add), r=[("zs", ot), "cw", "c1"], w=["c1"])
                                P.dve(lambda e, ot=ot: e.scalar_tensor_tensor(out=c1[:], in0=zs[:, ot, 2:SB + 2], scalar=cw[:, ot, 2:3], in1=c1[:],
                                                                              op0=ALU.mult, op1=ALU.add), r=[("zs", ot), "cw", "c1"], w=["c1"])
                                P.dve(lambda e, ot=ot, tok=tok: e.tensor_tensor(out=cvT[:, ot, tok], in0=ps[2][:, 0:SB], in1=c1[:], op=ALU.mult),
                                      r=[psk(2), "c1"], w=[("cvT", ot, s_)])
                if full:
                    tap("uT_A", uT[:], [("uT", c_, s_) for c_ in range(4) for s_ in range(NSB)], BF16)
                if not full:
                    pending.append(ssm_hist())
                    return
                P.barrier(dummy[:])
                with contextlib.ExitStack() as s4:
                    NBUF = 3
                    TN = ["t1", "t2", "t3", "t4", "gre", "gim"]
                    T = {}
                    for b_ in range(NBUF):
                        for n_ in TN:
                            T[(n_, b_)] = sb("%s_%d" % (n_, b_), [128, TC], F32, s4)
                        for n_ in ["d1", "d2", "d3", "d4"]:
                            T[(n_, b_)] = sb("%s_%d" % (n_, b_), [128, TC], BF16, s4)
                    ctmp = sb("ctmp", [128, 4, 16], F32, s4)
                    Hh = [sb("Hh%d" % i, [128, 4, 2, TC], BF16, s4) for i in range(2)]
                    ya = [sb("ya%d" % i, [128, TC], F32, s4) for i in range(2)]
                    TTf = lambda o, a_, b_, op: (lambda e: e.tensor_tensor(out=o, in0=a_, in1=b_, op=op))
                    NCH = NT // TC
                    NPAIR = NCH * 16

                    def ctx(i):
                        c, gp = divmod(i, 16)
                        return dict(c=c, gp=gp, ct=gp // 4, b=i % NBUF, pbk=i % NBUF, sbk=(3, 6, 7)[i % 3], tk=slice(c * TC, (c + 1) * TC),
                                    s_=(c * TC) // SB, hb=(c * 4 + gp // 4) % 2)

                    def st1(i):
                        q = ctx(i); gp, ct, b, pbk, tk, s_ = q["gp"], q["ct"], q["b"], q["pbk"], q["tk"], q["s_"]
                        PA, PB = ps[pbk][:, 0:TC], ps[pbk][:, TC:2 * TC]
                        K = lambda n_: (n_, b)
                        X = lambda n_: T[(n_, b)][:]
                        P.pe(lambda e: e.matmul(PA, lhsT=BqPad[:, gp, 0, :], rhs=uT[:, ct, tk], start=True, stop=True),
                             r=["BqPad", ("uT", ct, s_)], w=[psk(pbk)])
                        P.pe(lambda e: e.matmul(PB, lhsT=BqPad[:, gp, 1, :], rhs=uT[:, ct, tk], start=True, stop=True),
                             r=["BqPad", ("uT", ct, s_)], w=[psk(pbk)])
                        P.dve(TTf(X("t1"), PA, Ec[:, gp, :], ALU.mult), r=[psk(pbk), "Ec"], w=[K("t1")])
                        P.dve(TTf(X("t2"), PB, Es[:, gp, :], ALU.mult), r=[psk(pbk), "Es"], w=[K("t2")])
                        P.dve(TTf(X("t3"), PB, Ec[:, gp, :], ALU.mult), r=[psk(pbk), "Ec"], w=[K("t3")])
                        t4_ = X("t4")
                        P.dve(lambda e: e.scalar_tensor_tensor(out=t4_, in0=PA, scalar=-1.0, in1=Es[:, gp, :], op0=ALU.mult, op1=ALU.mult),
                              r=[psk(pbk), "Es"], w=[K("t4")])
                        sbk = q["sbk"]
                        WR, WI = ps[sbk][:, 0:TC], ps[sbk][:, TC:2 * TC]
                        t1_, t2_, t3_ = X("t1"), X("t2"), X("t3")
                        P.pe(lambda e: e.matmul(WR, lhsT=ident[:], rhs=t1_, start=True, stop=False), r=[K("t1"), "ident"], w=[psk(sbk)])
                        P.pe(lambda e: e.matmul(WR, lhsT=ident[:], rhs=t2_, start=False, stop=True), r=[K("t2"), "ident"], w=[psk(sbk)])
                        P.pe(lambda e: e.matmul(WI, lhsT=ident[:], rhs=t3_, start=True, stop=False), r=[K("t3"), "ident"], w=[psk(sbk)])
                        P.pe(lambda e: e.matmul(WI, lhsT=ident[:], rhs=t4_, start=False, stop=True), r=[K("t4"), "ident"], w=[psk(sbk)])

                    def st2(i):
                        q = ctx(i); gp, b, c = q["gp"], q["b"], q["c"]
                        K = lambda n_: (n_, b)
                        X = lambda n_: T[(n_, b)][:]
                        sbk = q["sbk"]
                        gre_, gim_, wre_, wim_ = X("gre"), X("gim"), ps[sbk][:, 0:TC], ps[sbk][:, TC:2 * TC]
                        P.dve(lambda e: e.tensor_tensor_scan(out=gre_, data0=mag[:, gp:gp + 1].to_broadcast([128, TC]), data1=wre_,
                                                             initial=carry[:, 0, gp:gp + 1], op0=ALU.mult, op1=ALU.add),
                              r=[psk(sbk), "carry", "mag"], w=[K("gre")])
                        P.dve(lambda e: e.tensor_tensor_scan(out=gim_, data0=mag[:, gp:gp + 1].to_broadcast([128, TC]), data1=wim_,
                                                             initial=carry[:, 1, gp:gp + 1], op0=ALU.mult, op1=ALU.add),
                              r=[psk(sbk), "carry", "mag"], w=[K("gim")])
                        P.act(lambda e: e.activation(out=glast[:, 0, gp:gp + 1], in_=gre_[:, TC - 1:TC], func=AF.Copy),
                              r=[K("gre")], w=[("glast", 0, gp)])
                        P.act(lambda e: e.activation(out=glast[:, 1, gp:gp + 1], in_=gim_[:, TC - 1:TC], func=AF.Copy),
                              r=[K("gim")], w=[("glast", 1, gp)])
                        if gp == 15:
                            gk = [("glast", ri, g_) for ri in range(2) for g_ in range(16)]
                            P.dve(TTf(ctmp[:, 0, :], glast[:, 0, :], cosN[:], ALU.mult), r=gk + ["cosN"], w=["ctmp0"])
                            P.dve(TTf(ctmp[:, 1, :], glast[:, 1, :], sinN[:], ALU.mult), r=gk + ["sinN"], w=["ctmp1"])
                            P.dve(TTf(ctmp[:, 2, :], glast[:, 0, :], sinN[:], ALU.mult), r=gk + ["sinN"], w=["ctmp2"])
                            P.dve(TTf(ctmp[:, 3, :], glast[:, 1, :], cosN[:], ALU.mult), r=gk + ["cosN"], w=["ctmp3"])
                            P.dve(TTf(carry[:, 0, :], ctmp[:, 0, :], ctmp[:, 1, :], ALU.subtract), r=["ctmp0", "ctmp1"], w=["carry"])
                            P.dve(TTf(carry[:, 1, :], ctmp[:, 2, :], ctmp[:, 3, :], ALU.add), r=["ctmp2", "ctmp3"], w=["carry"])

                    def st3(i):
                        q = ctx(i); gp, ct, b, tk, s_, hb = q["gp"], q["ct"], q["b"], q["tk"], q["s_"], q["hb"]
                        K = lambda n_: (n_, b)
                        X = lambda n_: T[(n_, b)][:]
                        P.pool(TTf(X("d1"), X("gre"), Ec[:, gp, :], ALU.mult), r=[K("gre"), "Ec"], w=[K("d1")])
                        P.pool(TTf(X("d2"), X("gim"), Es[:, gp, :], ALU.mult), r=[K("gim"), "Es"], w=[K("d2")])
                        P.pool(TTf(Hh[hb][:, gp % 4, 0, :], X("d1"), X("d2"), ALU.subtract), r=[K("d1"), K("d2")], w=[("Hh", hb, gp % 4, 0)])
                        P.pool(TTf(X("d3"), X("gre"), Es[:, gp, :], ALU.mult), r=[K("gre"), "Es"], w=[K("d3")])
                        P.dve(TTf(X("d4"), X("gim"), Ec[:, gp, :], ALU.mult), r=[K("gim"), "Ec"], w=[K("d4")])
                        P.pool(TTf(Hh[hb][:, gp % 4, 1, :], X("d3"), X("d4"), ALU.add), r=[K("d3"), K("d4")], w=[("Hh", hb, gp % 4, 1)])
                        if gp % 4 == 3:
                            for k in range(8):
                                gl, ri = k // 2, k % 2
                                P.pe(lambda e, gl=gl, ri=ri, k=k: e.matmul(
                                    ps[4 + hb][:, 0:TC], lhsT=CPad[:, ct * 4 + gl, ri, :], rhs=Hh[hb][:, gl, ri, :],
                                    start=(k == 0), stop=(k == 7)),
                                    r=["CPad", ("Hh", hb, gl, ri)], w=[psk(4 + hb)])
                            P.dve(lambda e: e.scalar_tensor_tensor(
                                out=ya[hb][:], in0=uT[:, ct, tk], scalar=dsk[:, ct:ct + 1], in1=ps[4 + hb][:, 0:TC],
                                op0=ALU.mult, op1=ALU.add), r=[psk(4 + hb), ("uT", ct, s_), "dsk"], w=[("ya", hb)])
                            P.act(lambda e: e.activation(out=ga[:, ct, tk], in_=ya[hb][:], func=AF.Gelu_apprx_tanh),
                                  r=[("ya", hb)], w=[("ga", ct, s_)])

                    for i in range(NPAIR + 2):
                        if i < NPAIR:
                            st1(i)
                        if 0 <= i - 1 < NPAIR:
                            st2(i - 1)
                        if 0 <= i - 2 < NPAIR:
                            st3(i - 2)
                if full:
                    tap("uT", uT[:], [("uT", c_, s_) for c_ in range(4) for s_ in range(NSB)], BF16)
                    tap("cvT", cvT[:], [("cvT", c_, s_) for c_ in range(4) for s_ in range(NSB)], BF16)
                    tap("ga", ga[:], [("ga", c_, s_) for c_ in range(4) for s_ in range(NSB)], BF16)
                    tap("carry", carry[:], ["carry"])
                if not full:
                    return
                P.barrier(dummy[:])
                with contextlib.ExitStack() as s4:
                    unC = sb("un", [128, 8, SB], BF16, s4)
                    sqC = [sb("sq%d" % i, [128, SB], BF16, s4) for i in range(2)]
                    sdC = sb("sd", [128, SB], F32, s4)
                    rstdC = sb("rstd", [128, SB], F32, s4)
                    wgl = [WS[:, i * 3584: i * 3584 + 1024].rearrange("p (k j n) -> p k j n", k=4, j=2) for i in range(2)]
                    wco = [WS[:, i * 3584 + 1024: i * 3584 + 1536].rearrange("p (k n) -> p k n", k=4) for i in range(2)]
                    wga = [WS[:, i * 3584 + 1536: i * 3584 + 3584].rearrange("p (k j n) -> p k j n", k=8, j=2) for i in range(2)]
                    glu_v = W["ssm_w_glu"].rearrange("(kc p) n -> p kc n", p=128)
                    co_v = W["conv_w_out"].rearrange("(kc p) n -> p kc n", p=128)
                    win_v = W["w_in"].rearrange("(kc p) n -> p kc n", p=128)
                    wo = WS[:, SLOT:SLOT + 8192].rearrange("p (k n) -> p k n", k=8)
                    mixed = sb("mixed", [128, 8, SB], BF16, s4)
                    ysv = sb("ysv", [128, SB], F32, s4)
                    g1 = sb("g1", [128, SB], F32, s4)
                    g2 = sb("g2", [128, SB], F32, s4)
                    P.dma("pool", lambda e: e.dma_start(out=wo, in_=rows3(W["w_o"])), w=[("WS", 1, 0), ("WS", 1, 1), ("WS", 1, 2)], epoch=False)
                    P.op("pool", lambda e: e.memset(dummy2[:], 0.0), (), [("WS", 0, 0), ("WS", 0, 1), ("WS", 0, 2)], epoch=False)
                    for s_ in range(NSB):
                        tok = slice(s_ * SB, (s_ + 1) * SB)
                        norm_sb(s_, 1, lambda dt_: unC[:, dt_, :], sqC, sdC, rstdC, lambda dt_: ("un", dt_))
                        for dt_ in range(8):
                            ws_ = (s_ * 8 + dt_) % 2
                            P.dma("sp", lambda e, dt_=dt_, ws_=ws_: e.dma_start(
                                out=WS[:, ws_ * 3584:(ws_ + 1) * 3584], in_=cw_scr[dt_, :, :]),
                                r=[("WS", 0, 0), ("WS", 0, 1), ("WS", 0, 2), ("cwscr", dt_, "co")] + [("cwscr", dt_, a_, j_) for a_ in ("gl", "ga") for j_ in range(2)],
                                w=[("WSC", ws_, "gl", 0), ("WSC", ws_, "gl", 1), ("WSC", ws_, "ga", 0), ("WSC", ws_, "ga", 1), ("WSC", ws_, "co")], epoch=False)
                            b0 = 0 if dt_ % 2 == 0 else 7
                            for kc in range(4):
                                P.pe(lambda e, kc=kc, dt_=dt_, tok=tok, ws_=ws_, b0=b0: e.matmul(ps[b0][:, 0:SB], lhsT=wgl[ws_][:, kc, 0, :],
                                                                                  rhs=ga[:, kc, tok], start=(kc == 0), stop=(kc == 3)),
                                     r=[("WS", 0, 0), ("WS", 0, 1), ("WS", 0, 2), ("WSC", ws_, "gl", 0), ("ga", kc, s_)], w=[psk(b0)])
                            for kc in range(4):
                                P.pe(lambda e, kc=kc, dt_=dt_, tok=tok, ws_=ws_: e.matmul(ps[1][:, 0:SB], lhsT=wgl[ws_][:, kc, 1, :],
                                                                                  rhs=ga[:, kc, tok], start=(kc == 0), stop=(kc == 3)),
                                     r=[("WS", 0, 0), ("WS", 0, 1), ("WS", 0, 2), ("WSC", ws_, "gl", 1), ("ga", kc, s_)], w=[psk(1)])
                            for j in range(2):
                                for kc in range(8):
                                    P.pe(lambda e, kc=kc, dt_=dt_, j=j, ws_=ws_: e.matmul(ps[3 + j][:, 0:SB],
                                                                                  lhsT=wga[ws_][:, kc, j, :],
                                                                                  rhs=unC[:, kc, :], start=(kc == 0), stop=(kc == 7)),
                                         r=[("WS", 0, 0), ("WS", 0, 1), ("WS", 0, 2), ("WSC", ws_, "ga", j), ("un", kc)], w=[psk(3 + j)])
                            for kc in range(4):
                                P.pe(lambda e, kc=kc, dt_=dt_, tok=tok, ws_=ws_: e.matmul(ps[2][:, 0:SB], lhsT=wco[ws_][:, kc, :],
                                                                                  rhs=cvT[:, kc, tok], start=(kc == 0), stop=(kc == 3)),
                                     r=[("WS", 0, 0), ("WS", 0, 1), ("WS", 0, 2), ("WSC", ws_, "co"), ("cvT", kc, s_)], w=[psk(2)])
                            P.act(lambda e: e.activation(out=g1[:], in_=ps[1][:, 0:SB], func=AF.Sigmoid), r=[psk(1)], w=["g1"])
                            P.dve(lambda e, b0=b0: e.tensor_tensor(out=ysv[:], in0=ps[b0][:, 0:SB], in1=g1[:], op=ALU.mult), r=[psk(b0), "g1"], w=["ysv"])
                            P.act(lambda e, dt_=dt_: e.activation(out=g1[:], in_=ps[3][:, 0:SB], func=AF.Sigmoid, bias=bgate[:, dt_:dt_ + 1]),
                                  r=[psk(3), "bgate"], w=["g1"])
                            P.act(lambda e, dt_=dt_: e.activation(out=g2[:], in_=ps[4][:, 0:SB], func=AF.Sigmoid, bias=bgate[:, 8 + dt_:9 + dt_]),
                                  r=[psk(4), "bgate"], w=["g2"])
                            P.dve(lambda e: e.tensor_tensor(out=ysv[:], in0=ysv[:], in1=g1[:], op=ALU.mult), r=["ysv", "g1"], w=["ysv"])
                            P.dve(lambda e: e.tensor_tensor(out=g2[:], in0=ps[2][:, 0:SB], in1=g2[:], op=ALU.mult), r=[psk(2), "g2"], w=["g2"])
                            P.dve(lambda e, dt_=dt_: e.tensor_tensor(out=mixed[:, dt_, :], in0=ysv[:], in1=g2[:], op=ALU.add),
                                  r=["ysv", "g2"], w=[("mixed", dt_)])
                        for dt_ in range(8):
                            pb = 5 + (dt_ % 2)
                            for kc in range(8):
                                P.pe(lambda e, kc=kc, dt_=dt_, pb=pb: e.matmul(ps[pb][:, 0:SB], lhsT=wo[:, kc, dt_ * 128:(dt_ + 1) * 128],
                                                                                rhs=mixed[:, kc, :], start=(kc == 0), stop=(kc == 7)),
                                     r=[("WS", 1, 0), ("WS", 1, 1), ("WS", 1, 2), ("mixed", kc)], w=[psk(pb)])
                            P.dve(lambda e, dt_=dt_, pb=pb, tok=tok: e.tensor_tensor(out=hT[:, dt_, tok], in0=ps[pb][:, 0:SB], in1=hT[:, dt_, tok], op=ALU.add),
                                  r=[psk(pb), ("hT", dt_, s_)], w=[("hT", dt_, s_)])
            tap("hT_mix", hT[:], HTK)
            ffn(blk, 2, W["ffn2_w_gate"], W["ffn2_w_up"], W["ffn2_w_down"])
            tap("hT_ffn2", hT[:], HTK)
            P.barrier(dummy[:])
            with contextlib.ExitStack() as s3:
                sqF = [sb("sq%d" % i, [128, SB], BF16, s3) for i in range(2)]
                sdF = sb("sd", [128, SB], F32, s3)
                rstdF = sb("rstd", [128, SB], F32, s3)
                ost = [sb("ost%d" % i, [128, D], F32, s3) for i in range(2)]
                for s_ in range(NSB):
                    norm_sb(s_, 3, lambda dt_, s_=s_: hT[:, dt_, s_ * SB:(s_ + 1) * SB], sqF, sdF, rstdF,
                            lambda dt_, s_=s_: ("hT", dt_, s_))
                for i in range(16):
                    t0 = 16 + i * 128
                    ob = i % 2
                    sbs = sorted(set([t0 // SB, (t0 + 127) // SB]))
                    for h in range(2):
                        bank = 6 + h
                        for q in range(4):
                            dt_ = h * 4 + q
                            P.pe(lambda e, dt_=dt_, q=q, t0=t0, bank=bank: e.transpose(
                                out=ps[bank][:, q * 128:(q + 1) * 128], in_=hT[:, dt_, t0:t0 + 128], identity=ident[:]),
                                r=[("hT", dt_, s_) for s_ in sbs] + ["ident"], w=[psk(bank)])
                        if h == 0:
                            P.act(lambda e, ob=ob, h=h, bank=bank: e.activation(out=ost[ob][:, h * 512:(h + 1) * 512], in_=ps[bank][:, :], func=AF.Copy),
                                  r=[psk(bank)], w=[("ost", ob, h)])
                        else:
                            P.dve(lambda e, ob=ob, h=h, bank=bank: e.tensor_copy(out=ost[ob][:, h * 512:(h + 1) * 512], in_=ps[bank][:, :]),
                                  r=[psk(bank)], w=[("ost", ob, h)])
                    P.dma("sp", lambda e, i=i, ob=ob: e.dma_start(out=out[i * 128:(i + 1) * 128, :], in_=ost[ob][:]),
                          r=[("ost", ob, 0), ("ost", ob, 1)], w=[("out", i)])
        for blk in range(NBLK):
            do_block(blk)
        P.emit()
    return nc


_NC_CACHE = {}


def kernel(**inputs):
    x = np.asarray(inputs["x"], dtype=np.float32)
    meta = np.asarray(inputs["meta_tokens"], dtype=np.float32)
    B, S, _ = x.shape
    n = 8
    if "nc" not in _NC_CACHE:
        _NC_CACHE["nc"] = build_nc()
    nc = _NC_CACHE["nc"]
    f = lambda k: np.ascontiguousarray(np.asarray(inputs[k], dtype=np.float32))
    shared = {
        "g_ffn1": f("g_ffn1")[0], "ffn1_w_gate": f("ffn1_w_gate")[0], "ffn1_w_up": f("ffn1_w_up")[0],
        "ffn1_w_down": f("ffn1_w_down")[0], "g_mix": f("g_mix")[0], "w_in": f("w_in")[0], "b_gate": f("b_gate")[0],
        "ssm_a_re": f("ssm_a_re")[0], "ssm_a_im": f("ssm_a_im")[0], "ssm_log_dt": f("ssm_log_dt")[0],
        "ssm_b_re": f("ssm_b_re")[0], "ssm_b_im": f("ssm_b_im")[0],
        "ssm_c_re": f("ssm_c_re")[0].reshape(512, 64), "ssm_c_im": f("ssm_c_im")[0].reshape(512, 64),
        "ssm_d": f("ssm_d")[0], "ssm_w_glu": f("ssm_w_glu")[0], "conv_w": f("conv_w")[0].reshape(3, 512),
        "conv_w_out": f("conv_w_out")[0], "w_o": f("w_o")[0], "g_ffn2": f("g_ffn2")[0],
        "ffn2_w_gate": f("ffn2_w_gate")[0], "ffn2_w_up": f("ffn2_w_up")[0], "ffn2_w_down": f("ffn2_w_down")[0],
        "g_final": f("g_final"),
    }
    in_maps = []
    total = NBLK * NT
    for c in range(n):
        b, q = c // 4, c % 4
        seq = np.concatenate([meta, x[b, : 2048 * (q + 1)]], axis=0)
        stream = np.zeros((total, D), np.float32)
        stream[total - seq.shape[0]:] = seq
        m = dict(shared)
        m["xs"] = stream
        in_maps.append(m)
    res = run_bass_kernel_spmd(nc, in_maps, core_ids=list(range(n)))
    outp = np.zeros((B, S, D), np.float32)
    for c in range(n):
        b, q = c // 4, c % 4
        outp[b, q * 2048:(q + 1) * 2048] = np.asarray(res.results[c]["out"], dtype=np.float32)
    return outp
```

```python
import contextlib
import math
import numpy as np
import concourse.bass as bass
import concourse.mybir as mybir
from concourse.bass_utils import run_bass_kernel_spmd

F32 = mybir.dt.float32
BF16 = mybir.dt.bfloat16
I32 = mybir.dt.int32
AF = mybir.ActivationFunctionType
ALU = mybir.AluOpType

D = 1024
DFF = 2816
NT = 2064
NBLK = 4
NSB = 6
SB = 344
TC = 172
NF = 22
GROUPS = [(0, 3), (3, 3), (6, 3), (9, 3), (12, 3), (15, 3), (18, 2), (20, 2)]
EPS = 1e-6
ENGS = ("pe", "act", "dve", "pool", "sp")
PI = math.pi


class Prog:
    def __init__(self, nc, dma_ring=8):
        self.nc = nc
        self.ops = []
        self.dma_ring = dma_ring

    def op(self, eng, fn, reads=(), writes=(), dma=False, epoch=True):
        self.ops.append(dict(eng=eng, fn=fn, reads=tuple(reads) + (("EPOCH",) if epoch else ()), writes=tuple(writes), dma=dma))

    def barrier(self, dummy):
        self.ops.append(dict(eng="dve", fn=lambda e: e.memset(dummy, 0.0), reads=(), writes=("EPOCH",), dma=False))

    def pe(self, fn, r=(), w=()):
        self.op("pe", fn, r, w)

    def act(self, fn, r=(), w=()):
        self.op("act", fn, r, w)

    def dve(self, fn, r=(), w=()):
        self.op("dve", fn, r, w)

    def pool(self, fn, r=(), w=()):
        self.op("pool", fn, r, w)

    def dma(self, q, fn, r=(), w=(), epoch=True):
        self.op(q, fn, r, w, dma=True, epoch=epoch)

    def emit(self):
        nc = self.nc
        ops = self.ops
        last_writer = {}
        readers = {}
        for i, o in enumerate(ops):
            deps = set()
            for r in o["reads"]:
                if r in last_writer:
                    deps.add(last_writer[r])
            for w in o["writes"]:
                if w in last_writer:
                    deps.add(last_writer[w])
                deps.update(readers.get(w, ()))
            deps.discard(i)
            o["deps"] = deps
            for r in o["reads"]:
                readers.setdefault(r, []).append(i)
            for w in o["writes"]:
                last_writer[w] = i
                readers[w] = []
        eidx = {e: 0 for e in ENGS}
        dcount = {e: 0 for e in ENGS}
        for o in ops:
            o["eidx"] = eidx[o["eng"]]
            eidx[o["eng"]] += 1
            o["marked"] = False
            if o["dma"]:
                o["dma_n"] = dcount[o["eng"]]
                dcount[o["eng"]] += 1
        R_ = self.dma_ring
        for i, o in enumerate(ops):
            E = o["eng"]
            need = []
            best = {}
            bestd = {}
            for j in o["deps"]:
                p = ops[j]
                if p["dma"]:
                    k = (p["eng"], p["dma_n"] % R_)
                    if k not in bestd or ops[bestd[k]]["dma_n"] < p["dma_n"]:
                        bestd[k] = j
                else:
                    k = p["eng"]
                    if k not in best or ops[best[k]]["eidx"] < p["eidx"]:
                        best[k] = j
            need.extend(bestd.values())
            for k, j in best.items():
                p = ops[j]
                if k != E:
                    need.append(j)
                    p["marked"] = True
                else:
                    if E == "pe" and not o["dma"]:
                        continue
                    need.append(j)
                    p["marked"] = True
            o["need"] = need
        cnt = {e: 0 for e in ENGS}
        for o in ops:
            if o["marked"]:
                cnt[o["eng"]] += 1
                o["cnt"] = cnt[o["eng"]]
        R = self.dma_ring
        with contextlib.ExitStack() as st:
            esem = {e: st.enter_context(nc.semaphore("s_" + e)) for e in ENGS}
            dsem = {e: [st.enter_context(nc.semaphore("d_%s_%d" % (e, k))) for k in range(R)]
                    for e in ("sp", "act", "pool") if dcount[e] > 0}
            block = st.enter_context(nc.Block())

            def run_engine(E, eng):
                waited = {}

                def wait(sem, val):
                    key = id(sem)
                    if waited.get(key, 0) >= val:
                        return
                    waited[key] = val
                    eng.wait_ge(sem, val)

                for o in ops:
                    if o["eng"] != E:
                        continue
                    for j in o["need"]:
                        p = ops[j]
                        if p["dma"]:
                            n = p["dma_n"]
                            wait(dsem[p["eng"]][n % R], 16 * (n // R + 1))
                        else:
                            wait(esem[p["eng"]], p["cnt"])
                    if o["dma"]:
                        n = o["dma_n"]
                        s = dsem[E][n % R]
                        if n // R > 0:
                            wait(s, 16 * (n // R))
                        o["fn"](eng).then_inc(s, 16)
                    else:
                        ins = o["fn"](eng)
                        if o["marked"]:
                            ins.then_inc(esem[E], 1)
                if E in dsem:
                    tot = dcount[E]
                    for k in range(R):
                        uses = (tot - k + R - 1) // R if tot > k else 0
                        if uses > 0:
                            wait(dsem[E][k], 16 * uses)

            @block.tensor
            def _(eng):
                run_engine("pe", eng)

            @block.scalar
            def _(eng):
                run_engine("act", eng)

            @block.vector
            def _(eng):
                run_engine("dve", eng)

            @block.gpsimd
            def _(eng):
                run_engine("pool", eng)

            @block.sync
            def _(eng):
                run_engine("sp", eng)


DEBUG = False


def build_nc():
    nc = bass.Bass("TRN2", target_bir_lowering=False)
    dr = lambda name, shape, kind="ExternalInput", dt=F32: nc.dram_tensor(name, shape, dt, kind=kind).ap()
    xs = dr("xs", [NBLK * NT, D])
    out = dr("out", [2048, D], kind="ExternalOutput")
    W = {}
    for nm, shp in [("g_ffn1", [D]), ("ffn1_w_gate", [D, DFF]), ("ffn1_w_up", [D, DFF]), ("ffn1_w_down", [DFF, D]),
                    ("g_mix", [D]), ("w_in", [D, 4096]), ("b_gate", [2048]), ("ssm_a_re", [32, 64]),
                    ("ssm_a_im", [32, 64]), ("ssm_log_dt", [32]), ("ssm_b_re", [32, 64, 16]),
                    ("ssm_b_im", [32, 64, 16]), ("ssm_c_re", [512, 64]), ("ssm_c_im", [512, 64]),
                    ("ssm_d", [512]), ("ssm_w_glu", [512, 2048]), ("conv_w", [3, 512]),
                    ("conv_w_out", [512, D]), ("w_o", [D, D]), ("g_ffn2", [D]), ("ffn2_w_gate", [D, DFF]),
                    ("ffn2_w_up", [D, DFF]), ("ffn2_w_down", [DFF, D]), ("g_final", [D])]:
        W[nm] = dr(nm, shp)

    P = Prog(nc)
    nc_allow = nc.allow_non_contiguous_dma(reason="small parameter loads")
    with contextlib.ExitStack() as st:
        st.enter_context(nc_allow)

        uid = [0]

        def sb(name, shape, dt, stack=st):
            uid[0] += 1
            return stack.enter_context(nc.sbuf_tensor("%s_%d" % (name, uid[0]), shape, dt))

        ps = [st.enter_context(nc.psum_tensor("ps%d" % i, [128, 512], F32)) for i in range(8)]

        def tap(name, ap, keys, dt=F32):
            if not DEBUG:
                return
            shape = list(ap.shape)
            d = nc.dram_tensor("dbg_" + name, shape, dt, kind="ExternalOutput").ap()
            P.dma("sp", lambda e: e.dma_start(out=d, in_=ap), r=keys, w=[("dbg", name)])

        HTK = [("hT", dt_, s_) for dt_ in range(8) for s_ in range(NSB)]
        psk = lambda i: ("ps", i)

        hT = sb("hT", [128, 8, NT], F32)
        ident = sb("ident", [128, 128], F32)
        dummy = sb("dummy", [128, 1], F32)
        dummy2 = sb("dummy2", [128, 1], F32)
        identb = sb("identb", [128, 128], BF16)
        negIb = sb("negIb", [128, 128], BF16)
        SLOT = 9216
        WS = sb("WS", [128, 2 * SLOT], BF16)
        wsg = [WS[:, i * SLOT: i * SLOT + 3072].rearrange("p (k n) -> p k n", k=8) for i in range(2)]
        wsu = [WS[:, i * SLOT + 3072: i * SLOT + 6144].rearrange("p (k n) -> p k n", k=8) for i in range(2)]
        wsd = [WS[:, i * SLOT + 6144: i * SLOT + 9216].rearrange("p (f n) -> p f n", f=3) for i in range(2)]
        ALLWS = [("WS", i, j) for i in range(2) for j in range(3)]
        uT = sb("uT", [128, 4, NT], BF16)
        junk = [sb("junk%d" % i, [128, TC], F32) for i in range(2)]
        acc = sb("acc", [128, 4, 16], F32)
        ctm = sb("ctm", [128, 6, 16], F32)
        rows3 = lambda ap: ap.rearrange("(kc p) n -> p kc n", p=128)
        ones_bf = sb("ones_bf", [128, 128], BF16)
        gains = sb("gains", [128, 4, 8], F32)
        bgate = sb("bgate", [128, 16], F32)
        dsk = sb("dsk", [128, 4], F32)
        cw = sb("cw", [128, 4, 3], F32)
        mag = sb("mag", [128, 16], F32)
        carry = sb("carry", [128, 2, 16], F32)
        glast = sb("glast", [128, 2, 16], F32)
        th = sb("th", [128, 16], F32)
        adr = sb("adr", [128, 16], F32)
        lamN = sb("lamN", [128, 2, 16], F32)
        cosN = sb("cosN", [128, 16], F32)
        sinN = sb("sinN", [128, 16], F32)
        Ec = sb("Ec", [128, 16, TC], F32)
        Es = sb("Es", [128, 16, TC], F32)
        BqPad = sb("BqPad", [128, 16, 2, 128], BF16)
        CPad = sb("CPad", [128, 16, 2, 128], BF16)

        P.pool(lambda e: e.memset(ident[:], 0.0), w=["ident"])
        P.pool(lambda e: e.affine_select(out=ident[:], in_=ident[:], compare_op=ALU.not_equal, fill=1.0,
                                         base=0, pattern=[[-1, 128]], channel_multiplier=1), r=["ident"], w=["ident"])
        P.dve(lambda e: e.memset(ones_bf[:], 1.0), w=["ones"])
        P.act(lambda e: e.activation(out=identb[:], in_=ident[:], func=AF.Identity, scale=1.0), r=["ident"], w=["identb"])
        P.act(lambda e: e.activation(out=negIb[:], in_=ident[:], func=AF.Identity, scale=-1.0), r=["ident"], w=["negIb"])
        for i, nm in enumerate(["g_ffn1", "g_mix", "g_ffn2", "g_final"]):
            P.dma("sp", lambda e, i=i, nm=nm: e.dma_start(out=gains[:, i, :], in_=W[nm].rearrange("(t p) -> p t", p=128)),
                  w=[("gains", i)])
        P.dma("sp", lambda e: e.dma_start(out=bgate[:], in_=W["b_gate"].rearrange("(t p) -> p t", p=128)), w=["bgate"])
        P.dma("sp", lambda e: e.dma_start(out=dsk[:], in_=W["ssm_d"].rearrange("(t p) -> p t", p=128)), w=["dsk"])
        for j in range(3):
            P.dma("sp", lambda e, j=j: e.dma_start(out=cw[:, :, j], in_=W["conv_w"][j, :].rearrange("(t p) -> p t", p=128)), w=["cw"])

        with contextlib.ExitStack() as s2:
            t = lambda name, shape, dt=F32: sb(name, shape, dt, s2)
            are = t("are", [128, 16]); aim = t("aim", [128, 16]); ldt = t("ldt", [128, 16])
            bre = t("bre", [128, 16, 16]); bim = t("bim", [128, 16, 16])
            cn = t("cn", [128, 2, 4, 2, 64])
            CP = t("CP", [128, 2, 16, 16])
            dtv = t("dtv", [128, 16])
            cs = t("cs", [128, 16]); sn = t("sn", [128, 16])
            lre = t("lre", [128, 16]); lim = t("lim", [128, 16])
            den = t("den", [128, 16]); tmpa = t("tmpa", [128, 16]); tmpb = t("tmpb", [128, 16])
            qre = t("qre", [128, 16]); qim = t("qim", [128, 16]); lm1 = t("lm1", [128, 16])
            bq = t("bq", [128, 2, 16, 16]); tb = t("tb", [128, 16])
            angN = t("angN", [128, 16])
            wkn1 = t("wkn1", [128, 16]); wkn2 = t("wkn2", [128, 16]); wkni = t("wkni", [128, 16], I32)

            pair_view = lambda ap: ap.rearrange("(gp two) p -> (two p) gp", two=2)
            P.dma("sp", lambda e: e.dma_start(out=are[:], in_=pair_view(W["ssm_a_re"])), w=["are"])
            P.dma("sp", lambda e: e.dma_start(out=aim[:], in_=pair_view(W["ssm_a_im"])), w=["aim"])
            for gpar in range(2):
                P.dma("sp", lambda e, gpar=gpar: e.dma_start(
                    out=ldt[gpar * 64:(gpar + 1) * 64, :],
                    in_=W["ssm_log_dt"].rearrange("(g two) -> two g", two=2)[gpar:gpar + 1, :].to_broadcast([64, 16])),
                    w=[("ldt", gpar)])
            bview = lambda ap: ap.rearrange("(gp two) p c -> (two p) gp c", two=2)
            P.dma("sp", lambda e: e.dma_start(out=bre[:], in_=bview(W["ssm_b_re"])), w=["bre"])
            P.dma("sp", lambda e: e.dma_start(out=bim[:], in_=bview(W["ssm_b_im"])), w=["bim"])
            for ri, nm in enumerate(["ssm_c_re", "ssm_c_im"]):
                for dup in range(2):
                    P.dma("sp", lambda e, ri=ri, nm=nm, dup=dup: e.dma_start(
                        out=cn[:, ri, :, dup, :], in_=W[nm].rearrange("(t p) s -> p t s", p=128)), w=[("cn", ri, dup)])
            for ri in range(2):
                for ct in range(4):
                    bank = 6 + (ct % 2)
                    P.pe(lambda e, ri=ri, ct=ct, bank=bank: e.transpose(
                        out=ps[bank][:, 0:128], in_=cn[:, ri, ct, :, :].rearrange("p a b -> p (a b)"), identity=ident[:]),
                        r=[("cn", ri, 0), ("cn", ri, 1), "ident"], w=[psk(bank)])
                    for gpar in range(2):
                        P.act(lambda e, ri=ri, ct=ct, bank=bank, gpar=gpar: e.activation(
                            out=CP[gpar * 64:(gpar + 1) * 64, ri, ct * 4:(ct + 1) * 4, :],
                            in_=ps[bank][gpar * 64:(gpar + 1) * 64, 0:128].rearrange(
                                "p (gl two c) -> p gl two c", two=2, c=16)[:, :, gpar, :],
                            func=AF.Identity, scale=(1.0 if ri == 0 else -1.0)),
                            r=[psk(bank)], w=[("CP", ri, ct, gpar)])
            cp_keys = [("CP", ri, ct, gpar) for ri in range(2) for ct in range(4) for gpar in range(2)]
            ldk = [("ldt", 0), ("ldt", 1)]
            P.act(lambda e: e.activation(out=dtv[:], in_=ldt[:], func=AF.Exp), r=ldk, w=["dtv"])
            P.dve(lambda e: e.tensor_tensor(out=adr[:], in0=are[:], in1=dtv[:], op=ALU.mult), r=["are", "dtv"], w=["adr"])
            P.dve(lambda e: e.tensor_tensor(out=th[:], in0=aim[:], in1=dtv[:], op=ALU.mult), r=["aim", "dtv"], w=["th"])
            P.act(lambda e: e.activation(out=mag[:], in_=adr[:], func=AF.Exp), r=["adr"], w=["mag"])

            def sincos(x, xk, o_sin, o_cos, w1, w2, wi, keys):
                for off, o in ((0.0, o_sin), (PI / 2, o_cos)):
                    ok = keys[0] if o is o_sin else keys[1]
                    P.dve(lambda e, off=off: e.tensor_scalar(out=w1, in0=x, scalar1=off, scalar2=1.0 / (2 * PI),
                                                             op0=ALU.add, op1=ALU.mult), r=[xk], w=["w1" + keys[2]])
                    P.dve(lambda e: e.tensor_copy(out=wi, in_=w1), r=["w1" + keys[2]], w=["wi" + keys[2]])
                    P.dve(lambda e: e.tensor_copy(out=w1, in_=wi), r=["wi" + keys[2]], w=["w1" + keys[2]])
                    P.dve(lambda e: e.scalar_tensor_tensor(out=w2, in0=w1, scalar=-6.25, in1=x,
                                                           op0=ALU.mult, op1=ALU.add), r=["w1" + keys[2], xk], w=["w2" + keys[2]])
                    P.dve(lambda e: e.scalar_tensor_tensor(out=w2, in0=w1, scalar=-(2 * PI - 6.25), in1=w2,
                                                           op0=ALU.mult, op1=ALU.add), r=["w1" + keys[2], "w2" + keys[2]], w=["w2" + keys[2]])
                    P.dve(lambda e, off=off: e.tensor_scalar(out=w2, in0=w2, scalar1=off, scalar2=None, op0=ALU.add),
                          r=["w2" + keys[2]], w=["w2" + keys[2]])
                    for lim_, sgn in ((PI, -1.0), (-PI, 1.0)):
                        cmp = ALU.is_gt if sgn < 0 else ALU.is_lt
                        P.dve(lambda e, lim_=lim_, cmp=cmp: e.tensor_scalar(out=w1, in0=w2, scalar1=lim_, scalar2=None, op0=cmp),
                              r=["w2" + keys[2]], w=["w1" + keys[2]])
                        P.dve(lambda e, sgn=sgn: e.scalar_tensor_tensor(out=w2, in0=w1, scalar=sgn * 2 * PI, in1=w2,
                                                                        op0=ALU.mult, op1=ALU.add),
                              r=["w1" + keys[2], "w2" + keys[2]], w=["w2" + keys[2]])
                    P.act(lambda e, o=o: e.activation(out=o, in_=w2, func=AF.Sin), r=["w2" + keys[2]], w=[ok])

            sincos(th[:], "th", sn[:], cs[:], wkn1[:], wkn2[:], wkni[:], ("sn", "cs", "n"))
            P.dve(lambda e: e.tensor_tensor(out=lre[:], in0=mag[:], in1=cs[:], op=ALU.mult), r=["mag", "cs"], w=["lre"])
            P.dve(lambda e: e.tensor_tensor(out=lim[:], in0=mag[:], in1=sn[:], op=ALU.mult), r=["mag", "sn"], w=["lim"])
            P.dve(lambda e: e.tensor_tensor(out=den[:], in0=are[:], in1=are[:], op=ALU.mult), r=["are"], w=["den"])
            P.dve(lambda e: e.tensor_tensor(out=tmpa[:], in0=aim[:], in1=aim[:], op=ALU.mult), r=["aim"], w=["tmpa"])
            P.dve(lambda e: e.tensor_tensor(out=den[:], in0=den[:], in1=tmpa[:], op=ALU.add), r=["den", "tmpa"], w=["den"])
            P.dve(lambda e: e.reciprocal(out=den[:], in_=den[:]), r=["den"], w=["den"])
            P.dve(lambda e: e.tensor_scalar(out=lm1[:], in0=lre[:], scalar1=-1.0, scalar2=None, op0=ALU.add), r=["lre"], w=["lm1"])
            P.dve(lambda e: e.tensor_tensor(out=tmpa[:], in0=lm1[:], in1=are[:], op=ALU.mult), r=["lm1", "are", "den"], w=["tmpa"])
            P.dve(lambda e: e.tensor_tensor(out=tmpb[:], in0=lim[:], in1=aim[:], op=ALU.mult), r=["lim", "aim"], w=["tmpb"])
            P.dve(lambda e: e.tensor_tensor(out=tmpa[:], in0=tmpa[:], in1=tmpb[:], op=ALU.add), r=["tmpa", "tmpb"], w=["tmpa"])
            P.dve(lambda e: e.tensor_tensor(out=qre[:], in0=tmpa[:], in1=den[:], op=ALU.mult), r=["tmpa", "den"], w=["qre"])
            P.dve(lambda e: e.tensor_tensor(out=tmpa[:], in0=lim[:], in1=are[:], op=ALU.mult), r=["lim", "are", "qre"], w=["tmpa"])
            P.dve(lambda e: e.tensor_tensor(out=tmpb[:], in0=lm1[:], in1=aim[:], op=ALU.mult), r=["lm1", "aim"], w=["tmpb"])
            P.dve(lambda e: e.tensor_tensor(out=tmpa[:], in0=tmpa[:], in1=tmpb[:], op=ALU.subtract), r=["tmpa", "tmpb"], w=["tmpa"])
            P.dve(lambda e: e.tensor_tensor(out=qim[:], in0=tmpa[:], in1=den[:], op=ALU.mult), r=["tmpa", "den"], w=["qim"])
            qb = lambda q: q[:, :].unsqueeze(2).to_broadcast([128, 16, 16])
            P.dve(lambda e: e.tensor_tensor(out=bq[:, 0], in0=bre[:], in1=qb(qre), op=ALU.mult), r=["bre", "qre"], w=["bq0"])
            P.dve(lambda e: e.tensor_tensor(out=bq[:, 1], in0=bim[:], in1=qb(qim), op=ALU.mult), r=["bim", "qim"], w=["bq1"])
            P.dve(lambda e: e.tensor_tensor(out=bq[:, 0], in0=bq[:, 0], in1=bq[:, 1], op=ALU.subtract), r=["bq0", "bq1"], w=["bq0"])
            P.dve(lambda e: e.tensor_tensor(out=bq[:, 1], in0=bim[:], in1=qb(qre), op=ALU.mult), r=["bim", "qre", "bq0"], w=["bq1"])
            P.dve(lambda e: e.tensor_tensor(out=bre[:], in0=bre[:], in1=qb(qim), op=ALU.mult), r=["bre", "qim", "bq0"], w=["bre"])
            P.dve(lambda e: e.tensor_tensor(out=bq[:, 1], in0=bq[:, 1], in1=bre[:], op=ALU.add), r=["bq1", "bre"], w=["bq1"])
            sz = contextlib.ExitStack()
            sz.__enter__()
            Z = sb("Z", [128, 16, 2, 128], F32, sz)
            P.pool(lambda e: e.memset(Z[:], 0.0), w=["Z"])
            P.pool(lambda e: e.memset(CPad[:], 0.0), w=["CPad"])
            for ri in range(2):
                for gpar in range(2):
                    for gl in range(4):
                        col = (2 * gl + gpar) * 16
                        rows = slice(gpar * 64, (gpar + 1) * 64)
                        P.dve(lambda e, ri=ri, rows=rows, gl=gl, col=col: e.tensor_copy(
                            out=Z[rows, gl::4, ri, col:col + 16], in_=bq[rows, ri, gl::4, :]),
                            r=["bq0", "bq1", "Z"], w=["Z"])
                        P.act(lambda e, ri=ri, rows=rows, gl=gl, col=col: e.activation(
                            out=CPad[rows, gl::4, ri, col:col + 16], in_=CP[rows, ri, gl::4, :], func=AF.Copy),
                            r=cp_keys + ["CPad"], w=["CPad"])
            for gp in range(16):
                for ri in range(2):
                    bank = 6 + ((gp * 2 + ri) % 2)
                    P.pe(lambda e, gp=gp, ri=ri, bank=bank: e.transpose(out=ps[bank][:, 0:128], in_=Z[:, gp, ri, :], identity=ident[:]),
                         r=["Z", "ident"], w=[psk(bank)])
                    P.act(lambda e, gp=gp, ri=ri, bank=bank: e.activation(out=BqPad[:, gp, ri, :], in_=ps[bank][:, 0:128], func=AF.Copy),
                          r=[psk(bank)], w=["BqPad"])
            sz.__exit__(None, None, None)
            P.barrier(dummy[:])
            P.dve(lambda e: e.tensor_scalar(out=angN[:], in0=th[:], scalar1=float(TC), scalar2=None, op0=ALU.mult), r=["th"], w=["angN"])
            sincos(angN[:], "angN", sinN[:], cosN[:], wkn1[:], wkn2[:], wkni[:], ("sinN", "cosN", "n"))
            P.act(lambda e: e.activation(out=tmpa[:], in_=adr[:], func=AF.Exp, scale=float(TC)), r=["adr", "qre", "qim"], w=["tmpa"])
            P.dve(lambda e: e.tensor_tensor(out=lamN[:, 0, :], in0=tmpa[:], in1=cosN[:], op=ALU.mult), r=["tmpa", "cosN"], w=["lamN"])
            P.dve(lambda e: e.tensor_tensor(out=lamN[:, 1, :], in0=tmpa[:], in1=sinN[:], op=ALU.mult), r=["tmpa", "sinN"], w=["lamN"])
            P.dve(lambda e: e.memset(carry[:], 0.0), w=["carry"])
            tap("mag", mag[:], ["mag"]); tap("lre", lre[:], ["lre"]); tap("lim", lim[:], ["lim"])
            tap("qre", qre[:], ["qre"]); tap("qim", qim[:], ["qim"]); tap("th", th[:], ["th"])
            tap("BqPad", BqPad[:], ["BqPad"], BF16); tap("CPad", CPad[:], ["CPad"], BF16)
            tap("CP", CP[:], cp_keys); tap("bq", bq[:], ["bq0", "bq1"]); tap("ident", ident[:], ["ident"])

        def make_tables(mode):
            P.barrier(dummy[:])
            with contextlib.ExitStack() as sx:
                iot = sb("iot", [128, TC], F32, sx); ioti = sb("ioti", [128, TC], I32, sx)
                ang = sb("ang", [128, 16, TC], F32, sx); wk1 = sb("wk1", [128, 16, TC], F32, sx)
                wk2 = sb("wk2", [128, 16, TC], F32, sx); wki = sb("wki", [128, 16, TC], I32, sx)
                if mode == "E":
                    P.pool(lambda e: e.iota(ioti[:], pattern=[[1, TC]], base=1, channel_multiplier=0), w=["ioti"])
                else:
                    P.pool(lambda e: e.iota(ioti[:], pattern=[[-1, TC]], base=TC - 1, channel_multiplier=0), w=["ioti"])
                P.dve(lambda e: e.tensor_copy(out=iot[:], in_=ioti[:]), r=["ioti"], w=["iot"])
                for gp in range(16):
                    P.dve(lambda e, gp=gp: e.tensor_scalar(out=ang[:, gp, :], in0=iot[:], scalar1=th[:, gp:gp + 1], scalar2=None,
                                                           op0=ALU.mult), r=["iot", "th"], w=["ang"])
                sincos(ang[:], "ang", Es[:], Ec[:], wk1[:], wk2[:], wki[:], ("Es", "Ec", "t"))
                if mode == "D":
                    for gp in range(16):
                        P.act(lambda e, gp=gp: e.activation(out=ang[:, gp, :], in_=iot[:], func=AF.Exp, scale=adr[:, gp:gp + 1]),
                              r=["iot", "adr", "Es", "Ec"], w=[("magp", gp)])
                    mk = [("magp", gp) for gp in range(16)]
                    P.dve(lambda e: e.tensor_tensor(out=Ec[:], in0=Ec[:], in1=ang[:], op=ALU.mult), r=["Ec"] + mk, w=["Ec"])
                    P.dve(lambda e: e.tensor_tensor(out=Es[:], in0=Es[:], in1=ang[:], op=ALU.mult), r=["Es"] + mk, w=["Es"])
                tap("Ec_" + mode, Ec[:], ["Ec"]); tap("Es_" + mode, Es[:], ["Es"])

        make_tables("D")

        cw_scr = nc.dram_tensor("cw_scr", [8, 128, 3584], BF16).ap()
        glu_v = W["ssm_w_glu"].rearrange("(kc p) n -> p kc n", p=128)
        co_v = W["conv_w_out"].rearrange("(kc p) n -> p kc n", p=128)
        win_v = W["w_in"].rearrange("(kc p) n -> p kc n", p=128)
        for dt_ in range(8):
            for j in range(2):
                P.dma("pool", lambda e, j=j, dt_=dt_: e.dma_start(
                    out=cw_scr[dt_, :, 0:1024].rearrange("p (k j n) -> p k j n", k=4, j=2)[:, :, j, :],
                    in_=glu_v[:, :, j * 1024 + dt_ * 128: j * 1024 + (dt_ + 1) * 128]), w=[("cwscr", dt_, "gl", j)], epoch=False)
                P.dma("pool", lambda e, j=j, dt_=dt_: e.dma_start(
                    out=cw_scr[dt_, :, 1536:3584].rearrange("p (k j n) -> p k j n", k=8, j=2)[:, :, j, :],
                    in_=win_v[:, :, 2048 + j * 1024 + dt_ * 128: 2048 + j * 1024 + (dt_ + 1) * 128]), w=[("cwscr", dt_, "ga", j)], epoch=False)
            P.dma("pool", lambda e, dt_=dt_: e.dma_start(
                out=cw_scr[dt_, :, 1024:1536].rearrange("p (k n) -> p k n", k=4),
                in_=co_v[:, :, dt_ * 128:(dt_ + 1) * 128]), w=[("cwscr", dt_, "co")], epoch=False)

        def norm_sb(s_, gi, xn_ap_fn, sq, sd, rstd, keyfn, pbank=6, tag=0):
            tok = slice(s_ * SB, (s_ + 1) * SB)
            for dt_ in range(8):
                P.act(lambda e, dt_=dt_: e.activation(out=sq[dt_ % 2][:], in_=hT[:, dt_, tok], func=AF.Square),
                      r=[("hT", dt_, s_)], w=[("sq", dt_ % 2)])
                P.pe(lambda e, dt_=dt_: e.matmul(ps[pbank][:, 0:SB], lhsT=ones_bf[:], rhs=sq[dt_ % 2][:],
                                                 start=(dt_ == 0), stop=(dt_ == 7)),
                     r=[("sq", dt_ % 2), "ones"], w=[psk(pbank)])
            P.act(lambda e: e.activation(out=sd[:], in_=ps[pbank][:, 0:SB], func=AF.Sqrt, scale=1.0 / D, bias=EPS),
                  r=[psk(pbank)], w=[("sd", tag)])
            P.dve(lambda e: e.reciprocal(out=rstd[:], in_=sd[:]), r=[("sd", tag)], w=[("rstd", tag)])
            for dt_ in range(8):
                P.dve(lambda e, dt_=dt_: e.scalar_tensor_tensor(out=xn_ap_fn(dt_), in0=hT[:, dt_, tok],
                                                                scalar=gains[:, gi, dt_:dt_ + 1], in1=rstd[:],
                                                                op0=ALU.mult, op1=ALU.mult),
                      r=[("hT", dt_, s_), ("rstd", tag), ("gains", gi)], w=[keyfn(dt_)])

        def ffn(blk, gi, wg, wu, wd, bg=None):
            P.barrier(dummy[:])
            with contextlib.ExitStack() as s3:
                xn = sb("xn", [128, 8, NT], BF16, s3)
                sq = [sb("sq%d" % i, [128, SB], BF16, s3) for i in range(2)]
                sd = sb("sd", [128, SB], F32, s3)
                rstd = sb("rstd", [128, SB], F32, s3)
                actb = [sb("actb%d" % i, [128, 3, SB], BF16, s3) for i in range(2)]
                sg = [sb("sg%d" % i, [128, SB], F32, s3) for i in range(2)]
                for s_ in range(NSB):
                    norm_sb(s_, gi, lambda dt_, s_=s_: xn[:, dt_, s_ * SB:(s_ + 1) * SB], sq, sd, rstd,
                            lambda dt_, s_=s_: ("xn", dt_, s_))
                cnt = 0
                prev_down = [None]

                def emit_down(nf, sl, ab, s_, tok):
                    for dt_ in range(8):
                        pb = 4 + (dt_ % 2)
                        for fi in range(nf):
                            P.pe(lambda e, fi=fi, dt_=dt_, pb=pb, sl=sl, ab=ab, nf=nf: e.matmul(
                                ps[pb][:, 0:SB], lhsT=wsd[sl][:, fi, dt_ * 128:(dt_ + 1) * 128], rhs=actb[ab][:, fi, :],
                                start=(fi == 0), stop=(fi == nf - 1)),
                                r=[("WS", sl, 2), ("actb", ab, fi)], w=[psk(pb)])
                        P.dve(lambda e, dt_=dt_, pb=pb, tok=tok: e.scalar_tensor_tensor(
                            out=hT[:, dt_, tok], in0=ps[pb][:, 0:SB], scalar=0.5, in1=hT[:, dt_, tok],
                            op0=ALU.mult, op1=ALU.add),
                            r=[psk(pb), ("hT", dt_, s_)], w=[("hT", dt_, s_)])
                        if bg is not None and dt_ % 4 == 3:
                            next(bg, None)

                for gix, (f0, nf) in enumerate(GROUPS):
                    sl = gix % 2
                    P.dma("pool", lambda e, f0=f0, nf=nf, sl=sl: e.dma_start(
                        out=wsg[sl][:, :, 0:nf * 128], in_=rows3(wg)[:, :, f0 * 128:(f0 + nf) * 128]),
                        w=[("WS", sl, 0)], epoch=False)
                    P.dma("pool", lambda e, f0=f0, nf=nf, sl=sl: e.dma_start(
                        out=wsu[sl][:, :, 0:nf * 128], in_=rows3(wu)[:, :, f0 * 128:(f0 + nf) * 128]),
                        w=[("WS", sl, 1)], epoch=False)
                    P.dma("pool", lambda e, f0=f0, nf=nf, sl=sl: e.dma_start(
                        out=wsd[sl][:, 0:nf, :], in_=rows3(wd)[:, f0:f0 + nf, :]),
                        w=[("WS", sl, 2)], epoch=False)
                    for s_ in range(NSB):
                        tok = slice(s_ * SB, (s_ + 1) * SB)
                        ab = (gix * NSB + s_) % 2
                        for fi in range(nf):
                            par = cnt % 2
                            cnt += 1
                            for kc in range(8):
                                P.pe(lambda e, kc=kc, fi=fi, par=par, sl=sl, tok=tok: e.matmul(
                                    ps[par][:, 0:SB], lhsT=wsg[sl][:, kc, fi * 128:(fi + 1) * 128], rhs=xn[:, kc, tok],
                                    start=(kc == 0), stop=(kc == 7)),
                                    r=[("WS", sl, 0), ("xn", kc, s_)], w=[psk(par)])
                            for kc in range(8):
                                P.pe(lambda e, kc=kc, fi=fi, par=par, sl=sl, tok=tok: e.matmul(
                                    ps[2 + par][:, 0:SB], lhsT=wsu[sl][:, kc, fi * 128:(fi + 1) * 128], rhs=xn[:, kc, tok],
                                    start=(kc == 0), stop=(kc == 7)),
                                    r=[("WS", sl, 1), ("xn", kc, s_)], w=[psk(2 + par)])
                            P.act(lambda e, par=par: e.activation(out=sg[par][:], in_=ps[par][:, 0:SB], func=AF.Silu),
                                  r=[psk(par)], w=[("sg", par)])
                            P.dve(lambda e, par=par, ab=ab, fi=fi: e.tensor_tensor(
                                out=actb[ab][:, fi, :], in0=ps[2 + par][:, 0:SB], in1=sg[par][:], op=ALU.mult),
                                r=[psk(2 + par), ("sg", par)], w=[("actb", ab, fi)])
                            if bg is not None:
                                next(bg, None)
                        if prev_down[0] is not None:
                            prev_down[0]()
                        prev_down[0] = (lambda nf=nf, sl=sl, ab=ab, s_=s_, tok=tok: emit_down(nf, sl, ab, s_, tok))
                if prev_down[0] is not None:
                    prev_down[0]()
                if bg is not None:
                    for _ in bg:
                        pass

        pending = []

        def ssm_hist():
            for c in range(NT // TC):
                tk = slice(c * TC, (c + 1) * TC)
                s_ = (c * TC) // SB
                for gp in range(16):
                    ct = gp // 4
                    bank = 6 + (gp % 2)
                    PA, PB = ps[bank][:, 0:TC], ps[bank][:, TC:2 * TC]
                    P.pe(lambda e, gp=gp, ct=ct, PA=PA, tk=tk: e.matmul(PA, lhsT=BqPad[:, gp, 0, :], rhs=uT[:, ct, tk], start=True, stop=True),
                         r=["BqPad", ("uT", ct, s_)], w=[psk(bank)])
                    P.pe(lambda e, gp=gp, ct=ct, PB=PB, tk=tk: e.matmul(PB, lhsT=BqPad[:, gp, 1, :], rhs=uT[:, ct, tk], start=True, stop=True),
                         r=["BqPad", ("uT", ct, s_)], w=[psk(bank)])
                    for k, (src, tab, tkey) in enumerate([(PA, Ec, "Ec"), (PB, Es, "Es"), (PA, Es, "Es"), (PB, Ec, "Ec")]):
                        P.dve(lambda e, gp=gp, k=k, src=src, tab=tab: e.scalar_tensor_tensor(
                            out=junk[k % 2][:], in0=src, scalar=1.0, in1=tab[:, gp, :],
                            op0=ALU.mult, op1=ALU.mult, accum_out=acc[:, k, gp:gp + 1]),
                            r=[psk(bank), tkey], w=[("junk", k % 2), ("acc", k, gp)])
                    yield
                ak = [("acc", k, gp) for k in range(4) for gp in range(16)]
                TT = lambda o, a_, b_, op: (lambda e: e.tensor_tensor(out=o, in0=a_, in1=b_, op=op))
                P.dve(TT(ctm[:, 0, :], acc[:, 0, :], acc[:, 1, :], ALU.subtract), r=ak, w=["ctm0"])
                P.dve(TT(ctm[:, 1, :], acc[:, 2, :], acc[:, 3, :], ALU.add), r=ak, w=["ctm1"])
                P.dve(TT(ctm[:, 2, :], lamN[:, 0, :], carry[:, 0, :], ALU.mult), r=["lamN", "carry"], w=["ctm2"])
                P.dve(TT(ctm[:, 3, :], lamN[:, 1, :], carry[:, 1, :], ALU.mult), r=["lamN", "carry"], w=["ctm3"])
                P.dve(TT(ctm[:, 4, :], lamN[:, 0, :], carry[:, 1, :], ALU.mult), r=["lamN", "carry"], w=["ctm4"])
                P.dve(TT(ctm[:, 5, :], lamN[:, 1, :], carry[:, 0, :], ALU.mult), r=["lamN", "carry"], w=["ctm5"])
                P.dve(TT(ctm[:, 2, :], ctm[:, 2, :], ctm[:, 3, :], ALU.subtract), r=["ctm2", "ctm3"], w=["ctm2"])
                P.dve(TT(ctm[:, 4, :], ctm[:, 4, :], ctm[:, 5, :], ALU.add), r=["ctm4", "ctm5"], w=["ctm4"])
                P.dve(TT(carry[:, 0, :], ctm[:, 2, :], ctm[:, 0, :], ALU.add), r=["ctm2", "ctm0", "ctm4"], w=["carry"])
                P.dve(TT(carry[:, 1, :], ctm[:, 4, :], ctm[:, 1, :], ALU.add), r=["ctm4", "ctm1"], w=["carry"])
                yield

        def do_block(blk):
            full = (blk == NBLK - 1)
            P.barrier(dummy[:])
            with contextlib.ExitStack() as s3:
                xst = [sb("xst%d" % i, [128, D], F32, s3) for i in range(6)]
                ntile = (NT + 127) // 128
                for i in range(ntile):
                    rows = min(128, NT - i * 128)
                    xb = i % 6
                    P.dma("sp", lambda e, i=i, rows=rows, xb=xb: e.dma_start(
                        out=xst[xb][0:rows, :], in_=xs[blk * NT + i * 128: blk * NT + i * 128 + rows, :]),
                        w=[("xst", xb)])
                    sbs = sorted(set([(i * 128) // SB, (i * 128 + rows - 1) // SB]))
                    for h in range(2):
                        bank = 6 + h
                        for q in range(4):
                            dt_ = h * 4 + q
                            P.pe(lambda e, dt_=dt_, q=q, rows=rows, xb=xb, bank=bank: e.transpose(
                                out=ps[bank][:, q * 128:q * 128 + rows], in_=xst[xb][0:rows, dt_ * 128:(dt_ + 1) * 128],
                                identity=ident[0:rows, 0:rows]),
                                r=[("xst", xb), "ident"], w=[psk(bank)])
                        eng = P.act if h == 0 else P.dve
                        if h == 0:
                            P.act(lambda e, h=h, i=i, rows=rows, bank=bank: e.activation(
                                out=hT[:, h * 4:(h + 1) * 4, i * 128:i * 128 + rows],
                                in_=ps[bank][:, :].rearrange("p (q c) -> p q c", q=4)[:, :, 0:rows], func=AF.Copy),
                                r=[psk(bank)], w=[("hT", dt_, s_) for dt_ in range(h * 4, h * 4 + 4) for s_ in sbs])
                        else:
                            P.dve(lambda e, h=h, i=i, rows=rows, bank=bank: e.tensor_copy(
                                out=hT[:, h * 4:(h + 1) * 4, i * 128:i * 128 + rows],
                                in_=ps[bank][:, :].rearrange("p (q c) -> p q c", q=4)[:, :, 0:rows]),
                                r=[psk(bank)], w=[("hT", dt_, s_) for dt_ in range(h * 4, h * 4 + 4) for s_ in sbs])
            if full:
                tap("hT_load", hT[:], HTK)
            ffn(blk, 0, W["ffn1_w_gate"], W["ffn1_w_up"], W["ffn1_w_down"], bg=(pending.pop() if pending else None))
            if full:
                make_tables("E")
            if full:
                tap("hT_ffn1", hT[:], HTK)
            P.barrier(dummy[:])
            with contextlib.ExitStack() as s3:
                cvT = sb("cvT", [128, 4, NT], BF16, s3) if full else None
                ga = uT
                P.barrier(dummy[:])
                with contextlib.ExitStack() as s4:
                    unA = [sb("un%d" % i, [128, 8, SB], BF16, s4) for i in range(2)]
                    sqA = [sb("sq%d" % i, [128, SB], BF16, s4) for i in range(2)]
                    sdA = [sb("sd%d" % i, [128, SB], F32, s4) for i in range(2)]
                    rstdA = [sb("rstd%d" % i, [128, SB], F32, s4) for i in range(2)]
                    wu_ = WS[:, 0:4096].rearrange("p (k n) -> p k n", k=8)
                    P.dma("pool", lambda e: e.dma_start(out=wu_, in_=rows3(W["w_in"])[:, :, 0:512]), w=ALLWS, epoch=False)
                    if full:
                        wv = WS[:, 4096:16384].rearrange("p (k n) -> p k n", k=8)
                        zs = sb("zs", [128, 4, SB + 2], F32, s4)
                        vv = sb("vv", [128, SB], F32, s4)
                        c1 = sb("c1", [128, SB], F32, s4)
                        P.dma("pool", lambda e: e.dma_start(out=wv, in_=rows3(W["w_in"])[:, :, 512:2048]),
                              w=ALLWS, epoch=False)
                        P.dve(lambda e: e.memset(zs[:], 0.0), w=[("zs", c) for c in range(4)])
                    for s_ in range(NSB):
                        tok = slice(s_ * SB, (s_ + 1) * SB)
                        npar = s_ % 2
                        norm_sb(s_, 1, lambda dt_, npar=npar: unA[npar][:, dt_, :], sqA, sdA[npar], rstdA[npar],
                                lambda dt_, npar=npar: ("un", npar, dt_), pbank=6 + npar, tag=npar)
                        for ot in range(4):
                            pb = 4 + (ot % 2)
                            for kc in range(8):
                                P.pe(lambda e, kc=kc, ot=ot, pb=pb, npar=npar: e.matmul(
                                    ps[pb][:, 0:SB], lhsT=wu_[:, kc, ot * 128:(ot + 1) * 128], rhs=unA[npar][:, kc, :],
                                    start=(kc == 0), stop=(kc == 7)), r=ALLWS + [("un", npar, kc)], w=[psk(pb)])
                            P.act(lambda e, ot=ot, pb=pb, tok=tok: e.activation(out=uT[:, ot, tok], in_=ps[pb][:, 0:SB], func=AF.Copy),
                                  r=[psk(pb)], w=[("uT", ot, s_)])
                        if full:
                            for ot in range(4):
                                for j, bank in ((0, 0), (2, 1), (1, 2)):
                                    for kc in range(8):
                                        P.pe(lambda e, kc=kc, ot=ot, j=j, bank=bank, npar=npar: e.matmul(
                                            ps[bank][:, 0:SB], lhsT=wv[:, kc, j * 512 + ot * 128: j * 512 + (ot + 1) * 128],
                                            rhs=unA[npar][:, kc, :], start=(kc == 0), stop=(kc == 7)),
                                            r=ALLWS + [("un", npar, kc)], w=[psk(bank)])
                                P.act(lambda e: e.activation(out=vv[:], in_=ps[0][:, 0:SB], func=AF.Copy), r=[psk(0)], w=["vv"])
                                P.dve(lambda e, ot=ot: e.tensor_copy(out=zs[:, ot, 0:2], in_=zs[:, ot, SB:SB + 2]),
                                      r=[("zs", ot)], w=[("zs", ot)])
                                P.dve(lambda e, ot=ot: e.tensor_tensor(out=zs[:, ot, 2:SB + 2], in0=ps[1][:, 0:SB], in1=vv[:], op=ALU.mult),
                                      r=[psk(1), "vv", ("zs", ot)], w=[("zs", ot)])
                                P.dve(lambda e, ot=ot: e.tensor_scalar(out=c1[:], in0=zs[:, ot, 0:SB], scalar1=cw[:, ot, 0:1], scalar2=None,
                                                                       op0=ALU.mult), r=[("zs", ot), "cw"], w=["c1"])
                                P.dve(lambda e, ot=ot: e.scalar_tensor_tensor(out=c1[:], in0=zs[:, ot, 1:SB + 1], scalar=cw[:, ot, 1:2], in1=c1[:],
                                                                              op0=ALU.mult, op1=ALU.add), r=[("zs", ot), "cw", "c1"], w=["c1"])
                                P.dve(lambda e, ot=ot: e.scalar_tensor_tensor(out=c1[:], in0=zs[:, ot, 2:SB + 2], scalar=cw[:, ot, 2:3], in1=c1[:],
                                                                              op0=ALU.mult, op1=ALU.add), r=[("zs", ot), "cw", "c1"], w=["c1"])
                                P.dve(lambda e, ot=ot, tok=tok: e.tensor_tensor(out=cvT[:, ot, tok], in0=ps[2][:, 0:SB], in1=c1[:], op=ALU.mult),
                                      r=[psk(2), "c1"], w=[("cvT", ot, s_)])
                if full:
                    tap("uT_A", uT[:], [("uT", c_, s_) for c_ in range(4) for s_ in range(NSB)], BF16)
                if not full:
                    pending.append(ssm_hist())
                    return
                P.barrier(dummy[:])
                with contextlib.ExitStack() as s4:
                    NBUF = 3
                    TN = ["t1", "t2", "t3", "t4", "gre", "gim"]
                    T = {}
                    for b_ in range(NBUF):
                        for n_ in TN:
                            T[(n_, b_)] = sb("%s_%d" % (n_, b_), [128, TC], F32, s4)
                        for n_ in ["d1", "d2", "d3", "d4"]:
                            T[(n_, b_)] = sb("%s_%d" % (n_, b_), [128, TC], BF16, s4)
                    ctmp = sb("ctmp", [128, 4, 16], F32, s4)
                    Hh = [sb("Hh%d" % i, [128, 4, 2, TC], BF16, s4) for i in range(2)]
                    ya = [sb("ya%d" % i, [128, TC], F32, s4) for i in range(2)]
                    TTf = lambda o, a_, b_, op: (lambda e: e.tensor_tensor(out=o, in0=a_, in1=b_, op=op))
                    NCH = NT // TC
                    NPAIR = NCH * 16

                    def ctx(i):
                        c, gp = divmod(i, 16)
                        return dict(c=c, gp=gp, ct=gp // 4, b=i % NBUF, pbk=i % 2, sbk=(3, 6)[i % 2], hbk=(2, 7)[i % 2], tk=slice(c * TC, (c + 1) * TC),
                                    s_=(c * TC) // SB, hb=(c * 4 + gp // 4) % 2)

                    def st1(i):
                        q = ctx(i); gp, ct, b, pbk, tk, s_ = q["gp"], q["ct"], q["b"], q["pbk"], q["tk"], q["s_"]
                        PA, PB = ps[pbk][:, 0:TC], ps[pbk][:, TC:2 * TC]
                        K = lambda n_: (n_, b)
                        X = lambda n_: T[(n_, b)][:]
                        P.pe(lambda e: e.matmul(PA, lhsT=BqPad[:, gp, 0, :], rhs=uT[:, ct, tk], start=True, stop=True),
                             r=["BqPad", ("uT", ct, s_)], w=[psk(pbk)])
                        P.pe(lambda e: e.matmul(PB, lhsT=BqPad[:, gp, 1, :], rhs=uT[:, ct, tk], start=True, stop=True),
                             r=["BqPad", ("uT", ct, s_)], w=[psk(pbk)])
                        P.dve(TTf(X("t1"), PA, Ec[:, gp, :], ALU.mult), r=[psk(pbk), "Ec"], w=[K("t1")])
                        P.dve(TTf(X("t2"), PB, Es[:, gp, :], ALU.mult), r=[psk(pbk), "Es"], w=[K("t2")])
                        P.dve(TTf(X("t3"), PB, Ec[:, gp, :], ALU.mult), r=[psk(pbk), "Ec"], w=[K("t3")])
                        t4_ = X("t4")
                        P.dve(lambda e: e.scalar_tensor_tensor(out=t4_, in0=PA, scalar=-1.0, in1=Es[:, gp, :], op0=ALU.mult, op1=ALU.mult),
                              r=[psk(pbk), "Es"], w=[K("t4")])
                        sbk = q["sbk"]
                        WR, WI = ps[sbk][:, 0:TC], ps[sbk][:, TC:2 * TC]
                        t1_, t2_, t3_ = X("t1"), X("t2"), X("t3")
                        P.pe(lambda e: e.matmul(WR, lhsT=ident[:], rhs=t1_, start=True, stop=False), r=[K("t1"), "ident"], w=[psk(sbk)])
                        P.pe(lambda e: e.matmul(WR, lhsT=ident[:], rhs=t2_, start=False, stop=True), r=[K("t2"), "ident"], w=[psk(sbk)])
                        P.pe(lambda e: e.matmul(WI, lhsT=ident[:], rhs=t3_, start=True, stop=False), r=[K("t3"), "ident"], w=[psk(sbk)])
                        P.pe(lambda e: e.matmul(WI, lhsT=ident[:], rhs=t4_, start=False, stop=True), r=[K("t4"), "ident"], w=[psk(sbk)])

                    def st2(i):
                        q = ctx(i); gp, b, c = q["gp"], q["b"], q["c"]
                        K = lambda n_: (n_, b)
                        X = lambda n_: T[(n_, b)][:]
                        sbk = q["sbk"]
                        gre_, gim_, wre_, wim_ = X("gre"), X("gim"), ps[sbk][:, 0:TC], ps[sbk][:, TC:2 * TC]
                        P.dve(lambda e: e.tensor_tensor_scan(out=gre_, data0=mag[:, gp:gp + 1].to_broadcast([128, TC]), data1=wre_,
                                                             initial=carry[:, 0, gp:gp + 1], op0=ALU.mult, op1=ALU.add),
                              r=[psk(sbk), "carry", "mag"], w=[K("gre")])
                        P.dve(lambda e: e.tensor_tensor_scan(out=gim_, data0=mag[:, gp:gp + 1].to_broadcast([128, TC]), data1=wim_,
                                                             initial=carry[:, 1, gp:gp + 1], op0=ALU.mult, op1=ALU.add),
                              r=[psk(sbk), "carry", "mag"], w=[K("gim")])
                        P.act(lambda e: e.activation(out=glast[:, 0, gp:gp + 1], in_=gre_[:, TC - 1:TC], func=AF.Copy),
                              r=[K("gre")], w=[("glast", 0, gp)])
                        P.act(lambda e: e.activation(out=glast[:, 1, gp:gp + 1], in_=gim_[:, TC - 1:TC], func=AF.Copy),
                              r=[K("gim")], w=[("glast", 1, gp)])
                        if gp == 15:
                            gk = [("glast", ri, g_) for ri in range(2) for g_ in range(16)]
                            P.dve(TTf(ctmp[:, 0, :], glast[:, 0, :], cosN[:], ALU.mult), r=gk + ["cosN"], w=["ctmp0"])
                            P.dve(TTf(ctmp[:, 1, :], glast[:, 1, :], sinN[:], ALU.mult), r=gk + ["sinN"], w=["ctmp1"])
                            P.dve(TTf(ctmp[:, 2, :], glast[:, 0, :], sinN[:], ALU.mult), r=gk + ["sinN"], w=["ctmp2"])
                            P.dve(TTf(ctmp[:, 3, :], glast[:, 1, :], cosN[:], ALU.mult), r=gk + ["cosN"], w=["ctmp3"])
                            P.dve(TTf(carry[:, 0, :], ctmp[:, 0, :], ctmp[:, 1, :], ALU.subtract), r=["ctmp0", "ctmp1"], w=["carry"])
                            P.dve(TTf(carry[:, 1, :], ctmp[:, 2, :], ctmp[:, 3, :], ALU.add), r=["ctmp2", "ctmp3"], w=["carry"])

                    def st3(i):
                        q = ctx(i); gp, ct, b, tk, s_, hb = q["gp"], q["ct"], q["b"], q["tk"], q["s_"], q["hb"]
                        K = lambda n_: (n_, b)
                        X = lambda n_: T[(n_, b)][:]
                        hbk = q["hbk"]
                        HR, HI = ps[hbk][:, 0:TC], ps[hbk][:, TC:2 * TC]
                        d1_, d2_, d3_, d4_ = X("d1"), X("d2"), X("d3"), X("d4")
                        P.pool(TTf(d1_, X("gre"), Ec[:, gp, :], ALU.mult), r=[K("gre"), "Ec"], w=[K("d1")])
                        P.pool(TTf(d2_, X("gim"), Es[:, gp, :], ALU.mult), r=[K("gim"), "Es"], w=[K("d2")])
                        P.pool(TTf(d3_, X("gre"), Es[:, gp, :], ALU.mult), r=[K("gre"), "Es"], w=[K("d3")])
                        P.pool(TTf(d4_, X("gim"), Ec[:, gp, :], ALU.mult), r=[K("gim"), "Ec"], w=[K("d4")])
                        P.pe(lambda e: e.matmul(HR, lhsT=identb[:], rhs=d1_, start=True, stop=False), r=[K("d1"), "identb"], w=[psk(hbk)])
                        P.pe(lambda e: e.matmul(HR, lhsT=negIb[:], rhs=d2_, start=False, stop=True), r=[K("d2"), "negIb"], w=[psk(hbk)])
                        P.pe(lambda e: e.matmul(HI, lhsT=identb[:], rhs=d3_, start=True, stop=False), r=[K("d3"), "identb"], w=[psk(hbk)])
                        P.pe(lambda e: e.matmul(HI, lhsT=identb[:], rhs=d4_, start=False, stop=True), r=[K("d4"), "identb"], w=[psk(hbk)])
                        P.act(lambda e: e.activation(out=Hh[hb][:, gp % 4, 0, :], in_=HR, func=AF.Copy), r=[psk(hbk)], w=[("Hh", hb, gp % 4, 0)])
                        P.act(lambda e: e.activation(out=Hh[hb][:, gp % 4, 1, :], in_=HI, func=AF.Copy), r=[psk(hbk)], w=[("Hh", hb, gp % 4, 1)])
                        if gp % 4 == 3:
                            for k in range(8):
                                gl, ri = k // 2, k % 2
                                P.pe(lambda e, gl=gl, ri=ri, k=k: e.matmul(
                                    ps[4 + hb][:, 0:TC], lhsT=CPad[:, ct * 4 + gl, ri, :], rhs=Hh[hb][:, gl, ri, :],
                                    start=(k == 0), stop=(k == 7)),
                                    r=["CPad", ("Hh", hb, gl, ri)], w=[psk(4 + hb)])
                            P.dve(lambda e: e.scalar_tensor_tensor(
                                out=ya[hb][:], in0=uT[:, ct, tk], scalar=dsk[:, ct:ct + 1], in1=ps[4 + hb][:, 0:TC],
                                op0=ALU.mult, op1=ALU.add), r=[psk(4 + hb), ("uT", ct, s_), "dsk"], w=[("ya", hb)])
                            P.act(lambda e: e.activation(out=ga[:, ct, tk], in_=ya[hb][:], func=AF.Gelu_apprx_tanh),
                                  r=[("ya", hb)], w=[("ga", ct, s_)])

                    for i in range(NPAIR + 2):
                        if i < NPAIR:
                            st1(i)
                        if 0 <= i - 1 < NPAIR:
                            st2(i - 1)
                        if 0 <= i - 2 < NPAIR:
                            st3(i - 2)
                if full:
                    tap("uT", uT[:], [("uT", c_, s_) for c_ in range(4) for s_ in range(NSB)], BF16)
                    tap("cvT", cvT[:], [("cvT", c_, s_) for c_ in range(4) for s_ in range(NSB)], BF16)
                    tap("ga", ga[:], [("ga", c_, s_) for c_ in range(4) for s_ in range(NSB)], BF16)
                    tap("carry", carry[:], ["carry"])
                if not full:
                    return
                P.barrier(dummy[:])
                with contextlib.ExitStack() as s4:
                    unC = sb("un", [128, 8, SB], BF16, s4)
                    sqC = [sb("sq%d" % i, [128, SB], BF16, s4) for i in range(2)]
                    sdC = sb("sd", [128, SB], F32, s4)
                    rstdC = sb("rstd", [128, SB], F32, s4)
                    wgl = [WS[:, i * 3584: i * 3584 + 1024].rearrange("p (k j n) -> p k j n", k=4, j=2) for i in range(2)]
                    wco = [WS[:, i * 3584 + 1024: i * 3584 + 1536].rearrange("p (k n) -> p k n", k=4) for i in range(2)]
                    wga = [WS[:, i * 3584 + 1536: i * 3584 + 3584].rearrange("p (k j n) -> p k j n", k=8, j=2) for i in range(2)]
                    glu_v = W["ssm_w_glu"].rearrange("(kc p) n -> p kc n", p=128)
                    co_v = W["conv_w_out"].rearrange("(kc p) n -> p kc n", p=128)
                    win_v = W["w_in"].rearrange("(kc p) n -> p kc n", p=128)
                    wo = WS[:, SLOT:SLOT + 8192].rearrange("p (k n) -> p k n", k=8)
                    mixed = sb("mixed", [128, 8, SB], BF16, s4)
                    ysv = sb("ysv", [128, SB], F32, s4)
                    g1 = sb("g1", [128, SB], F32, s4)
                    g2 = sb("g2", [128, SB], F32, s4)
                    P.dma("pool", lambda e: e.dma_start(out=wo, in_=rows3(W["w_o"])), w=[("WS", 1, 0), ("WS", 1, 1), ("WS", 1, 2)], epoch=False)
                    P.op("pool", lambda e: e.memset(dummy2[:], 0.0), (), [("WS", 0, 0), ("WS", 0, 1), ("WS", 0, 2)], epoch=False)
                    for s_ in range(NSB):
                        tok = slice(s_ * SB, (s_ + 1) * SB)
                        norm_sb(s_, 1, lambda dt_: unC[:, dt_, :], sqC, sdC, rstdC, lambda dt_: ("un", dt_))
                        for dt_ in range(8):
                            ws_ = (s_ * 8 + dt_) % 2
                            P.dma("sp", lambda e, dt_=dt_, ws_=ws_: e.dma_start(
                                out=WS[:, ws_ * 3584:(ws_ + 1) * 3584], in_=cw_scr[dt_, :, :]),
                                r=[("WS", 0, 0), ("WS", 0, 1), ("WS", 0, 2), ("cwscr", dt_, "co")] + [("cwscr", dt_, a_, j_) for a_ in ("gl", "ga") for j_ in range(2)],
                                w=[("WSC", ws_, "gl", 0), ("WSC", ws_, "gl", 1), ("WSC", ws_, "ga", 0), ("WSC", ws_, "ga", 1), ("WSC", ws_, "co")], epoch=False)
                            b0 = 0 if dt_ % 2 == 0 else 7
                            for kc in range(4):
                                P.pe(lambda e, kc=kc, dt_=dt_, tok=tok, ws_=ws_, b0=b0: e.matmul(ps[b0][:, 0:SB], lhsT=wgl[ws_][:, kc, 0, :],
                                                                                  rhs=ga[:, kc, tok], start=(kc == 0), stop=(kc == 3)),
                                     r=[("WS", 0, 0), ("WS", 0, 1), ("WS", 0, 2), ("WSC", ws_, "gl", 0), ("ga", kc, s_)], w=[psk(b0)])
                            for kc in range(4):
                                P.pe(lambda e, kc=kc, dt_=dt_, tok=tok, ws_=ws_: e.matmul(ps[1][:, 0:SB], lhsT=wgl[ws_][:, kc, 1, :],
                                                                                  rhs=ga[:, kc, tok], start=(kc == 0), stop=(kc == 3)),
                                     r=[("WS", 0, 0), ("WS", 0, 1), ("WS", 0, 2), ("WSC", ws_, "gl", 1), ("ga", kc, s_)], w=[psk(1)])
                            for j in range(2):
                                for kc in range(8):
                                    P.pe(lambda e, kc=kc, dt_=dt_, j=j, ws_=ws_: e.matmul(ps[3 + j][:, 0:SB],
                                                                                  lhsT=wga[ws_][:, kc, j, :],
                                                                                  rhs=unC[:, kc, :], start=(kc == 0), stop=(kc == 7)),
                                         r=[("WS", 0, 0), ("WS", 0, 1), ("WS", 0, 2), ("WSC", ws_, "ga", j), ("un", kc)], w=[psk(3 + j)])
                            for kc in range(4):
                                P.pe(lambda e, kc=kc, dt_=dt_, tok=tok, ws_=ws_: e.matmul(ps[2][:, 0:SB], lhsT=wco[ws_][:, kc, :],
                                                                                  rhs=cvT[:, kc, tok], start=(kc == 0), stop=(kc == 3)),
                                     r=[("WS", 0, 0), ("WS", 0, 1), ("WS", 0, 2), ("WSC", ws_, "co"), ("cvT", kc, s_)], w=[psk(2)])
                            P.act(lambda e: e.activation(out=g1[:], in_=ps[1][:, 0:SB], func=AF.Sigmoid), r=[psk(1)], w=["g1"])
                            P.dve(lambda e, b0=b0: e.tensor_tensor(out=ysv[:], in0=ps[b0][:, 0:SB], in1=g1[:], op=ALU.mult), r=[psk(b0), "g1"], w=["ysv"])
                            P.act(lambda e, dt_=dt_: e.activation(out=g1[:], in_=ps[3][:, 0:SB], func=AF.Sigmoid, bias=bgate[:, dt_:dt_ + 1]),
                                  r=[psk(3), "bgate"], w=["g1"])
                            P.act(lambda e, dt_=dt_: e.activation(out=g2[:], in_=ps[4][:, 0:SB], func=AF.Sigmoid, bias=bgate[:, 8 + dt_:9 + dt_]),
                                  r=[psk(4), "bgate"], w=["g2"])
                            P.dve(lambda e: e.tensor_tensor(out=ysv[:], in0=ysv[:], in1=g1[:], op=ALU.mult), r=["ysv", "g1"], w=["ysv"])
                            P.dve(lambda e: e.tensor_tensor(out=g2[:], in0=ps[2][:, 0:SB], in1=g2[:], op=ALU.mult), r=[psk(2), "g2"], w=["g2"])
                            P.dve(lambda e, dt_=dt_: e.tensor_tensor(out=mixed[:, dt_, :], in0=ysv[:], in1=g2[:], op=ALU.add),
                                  r=["ysv", "g2"], w=[("mixed", dt_)])
                        for dt_ in range(8):
                            pb = 5 + (dt_ % 2)
                            for kc in range(8):
                                P.pe(lambda e, kc=kc, dt_=dt_, pb=pb: e.matmul(ps[pb][:, 0:SB], lhsT=wo[:, kc, dt_ * 128:(dt_ + 1) * 128],
                                                                                rhs=mixed[:, kc, :], start=(kc == 0), stop=(kc == 7)),
                                     r=[("WS", 1, 0), ("WS", 1, 1), ("WS", 1, 2), ("mixed", kc)], w=[psk(pb)])
                            P.dve(lambda e, dt_=dt_, pb=pb, tok=tok: e.tensor_tensor(out=hT[:, dt_, tok], in0=ps[pb][:, 0:SB], in1=hT[:, dt_, tok], op=ALU.add),
                                  r=[psk(pb), ("hT", dt_, s_)], w=[("hT", dt_, s_)])
            tap("hT_mix", hT[:], HTK)
            ffn(blk, 2, W["ffn2_w_gate"], W["ffn2_w_up"], W["ffn2_w_down"])
            tap("hT_ffn2", hT[:], HTK)
            P.barrier(dummy[:])
            with contextlib.ExitStack() as s3:
                sqF = [sb("sq%d" % i, [128, SB], BF16, s3) for i in range(2)]
                sdF = sb("sd", [128, SB], F32, s3)
                rstdF = sb("rstd", [128, SB], F32, s3)
                ost = [sb("ost%d" % i, [128, D], F32, s3) for i in range(2)]
                for s_ in range(NSB):
                    norm_sb(s_, 3, lambda dt_, s_=s_: hT[:, dt_, s_ * SB:(s_ + 1) * SB], sqF, sdF, rstdF,
                            lambda dt_, s_=s_: ("hT", dt_, s_))
                for i in range(16):
                    t0 = 16 + i * 128
                    ob = i % 2
                    sbs = sorted(set([t0 // SB, (t0 + 127) // SB]))
                    for h in range(2):
                        bank = 6 + h
                        for q in range(4):
                            dt_ = h * 4 + q
                            P.pe(lambda e, dt_=dt_, q=q, t0=t0, bank=bank: e.transpose(
                                out=ps[bank][:, q * 128:(q + 1) * 128], in_=hT[:, dt_, t0:t0 + 128], identity=ident[:]),
                                r=[("hT", dt_, s_) for s_ in sbs] + ["ident"], w=[psk(bank)])
                        if h == 0:
                            P.act(lambda e, ob=ob, h=h, bank=bank: e.activation(out=ost[ob][:, h * 512:(h + 1) * 512], in_=ps[bank][:, :], func=AF.Copy),
                                  r=[psk(bank)], w=[("ost", ob, h)])
                        else:
                            P.dve(lambda e, ob=ob, h=h, bank=bank: e.tensor_copy(out=ost[ob][:, h * 512:(h + 1) * 512], in_=ps[bank][:, :]),
                                  r=[psk(bank)], w=[("ost", ob, h)])
                    P.dma("sp", lambda e, i=i, ob=ob: e.dma_start(out=out[i * 128:(i + 1) * 128, :], in_=ost[ob][:]),
                          r=[("ost", ob, 0), ("ost", ob, 1)], w=[("out", i)])
        for blk in range(NBLK):
            do_block(blk)
        P.emit()
    return nc


_NC_CACHE = {}


def kernel(**inputs):
    x = np.asarray(inputs["x"], dtype=np.float32)
    meta = np.asarray(inputs["meta_tokens"], dtype=np.float32)
    B, S, _ = x.shape
    n = 8
    if "nc" not in _NC_CACHE:
        _NC_CACHE["nc"] = build_nc()
    nc = _NC_CACHE["nc"]
    f = lambda k: np.ascontiguousarray(np.asarray(inputs[k], dtype=np.float32))
    shared = {
        "g_ffn1": f("g_ffn1")[0], "ffn1_w_gate": f("ffn1_w_gate")[0], "ffn1_w_up": f("ffn1_w_up")[0],
        "ffn1_w_down": f("ffn1_w_down")[0], "g_mix": f("g_mix")[0], "w_in": f("w_in")[0], "b_gate": f("b_gate")[0],
        "ssm_a_re": f("ssm_a_re")[0], "ssm_a_im": f("ssm_a_im")[0], "ssm_log_dt": f("ssm_log_dt")[0],
        "ssm_b_re": f("ssm_b_re")[0], "ssm_b_im": f("ssm_b_im")[0],
        "ssm_c_re": f("ssm_c_re")[0].reshape(512, 64), "ssm_c_im": f("ssm_c_im")[0].reshape(512, 64),
        "ssm_d": f("ssm_d")[0], "ssm_w_glu": f("ssm_w_glu")[0], "conv_w": f("conv_w")[0].reshape(3, 512),
        "conv_w_out": f("conv_w_out")[0], "w_o": f("w_o")[0], "g_ffn2": f("g_ffn2")[0],
        "ffn2_w_gate": f("ffn2_w_gate")[0], "ffn2_w_up": f("ffn2_w_up")[0], "ffn2_w_down": f("ffn2_w_down")[0],
        "g_final": f("g_final"),
    }
    in_maps = []
    total = NBLK * NT
    for c in range(n):
        b, q = c // 4, c % 4
        seq = np.concatenate([meta, x[b, : 2048 * (q + 1)]], axis=0)
        stream = np.zeros((total, D), np.float32)
        stream[total - seq.shape[0]:] = seq
        m = dict(shared)
        m["xs"] = stream
        in_maps.append(m)
    res = run_bass_kernel_spmd(nc, in_maps, core_ids=list(range(n)))
    outp = np.zeros((B, S, D), np.float32)
    for c in range(n):
        b, q = c // 4, c % 4
        outp[b, q * 2048:(q + 1) * 2048] = np.asarray(res.results[c]["out"], dtype=np.float32)
    return outp
```
